# Optimizing a Trainium2 kernel written in Bass

```python
import math
import jax, jax.numpy as jnp
from jax import lax
import numpy as np

D_MODEL = 1024
BATCH = 4
SEQ = 4096
DEPTH = 4
DEC_BATCH = 8
DEC_SEQ = 64
PAST_LEN = 2048

CHUNK = 64
N_PREV_CHUNKS = 8
HEAD_DIM = 64
A_HEADS = 4
REL_CLIP = 128
B_HEADS = 4
B_V_DIM = 2 * HEAD_DIM
C_HEADS = 4
C_Q_LORA = 384
C_KV_LORA = 256
C_NOPE = 64
C_ROPE = 32
C_V = 64
ROPE_THETA = 10000.0
N_MEM = 256
M_HEADS = 4
M_HEAD_DIM = 128
D_FF = 2816
EPS = 1e-6
Q_BLOCK = 128
NEG_INF = -1e30

A_WIDTH = A_HEADS * HEAD_DIM
B_QK_COLS = 4 * B_HEADS * HEAD_DIM
B_WIDTH = B_HEADS * B_V_DIM
C_WIDTH = C_HEADS * C_V
MIX_WIDTH = A_WIDTH + B_WIDTH + C_WIDTH
A_COLS = 3 * A_WIDTH
B_COLS = B_QK_COLS + B_WIDTH
C_COLS = C_Q_LORA + C_KV_LORA + C_ROPE
IN_COLS = A_COLS + B_COLS + C_COLS
M_WIDTH = M_HEADS * M_HEAD_DIM

kernel_name = 'hybrid_streaming_encoder_step'


def rms_norm(x, g):
    xf = x.astype(jnp.float32)
    y = xf * lax.rsqrt(jnp.mean(xf * xf, axis=-1, keepdims=True) + EPS)
    return (y * g.astype(jnp.float32)).astype(x.dtype)


def half_ffn(x, g, w_gate, w_up, w_down):
    h = rms_norm(x, g)
    return 0.5 * ((jax.nn.silu(h @ w_gate) * (h @ w_up)) @ w_down)


def rope(x, pos):
    half = C_ROPE // 2
    inv = ROPE_THETA ** (-jnp.arange(half, dtype=jnp.float32) / half)
    ang = pos.astype(jnp.float32)[:, None] * inv[None, :]
    shape = (ang.shape[0],) + (1,) * (x.ndim - 3) + (half,)
    cos = jnp.cos(ang).reshape(shape)
    sin = jnp.sin(ang).reshape(shape)
    xf = x.astype(jnp.float32)
    x1, x2 = xf[..., :half], xf[..., half:]
    return jnp.concatenate([x1 * cos - x2 * sin, x1 * sin + x2 * cos], axis=-1).astype(x.dtype)


def alibi_slopes(n):
    return jnp.exp2(-8.0 * (jnp.arange(n, dtype=jnp.float32) + 1.0) / n)


def chunk_causal(q_pos, k_pos):
    return (k_pos[None, :] // CHUNK) <= (q_pos[:, None] // CHUNK)


def map_query_blocks(fn, q, q_pos):
    b, s = q.shape[:2]
    nb = s // Q_BLOCK
    qb = jnp.moveaxis(q.reshape((b, nb, Q_BLOCK) + q.shape[2:]), 1, 0)
    pb = q_pos.reshape(nb, Q_BLOCK)
    out = lax.map(lambda a: fn(a[0], a[1]), (qb, pb))
    out = jnp.moveaxis(out, 0, 1)
    return out.reshape((b, s) + out.shape[3:])


def mix_projections(h, pos, p):
    b, t = h.shape[:2]
    u = h @ p['w_in']
    a = u[..., :A_COLS].reshape(b, t, 3, A_HEADS, HEAD_DIM)
    qa = rms_norm(a[:, :, 0], p['a_q_norm'])
    ka = rms_norm(a[:, :, 1], p['a_k_norm'])
    va = a[:, :, 2]
    bqk = u[..., A_COLS:A_COLS + B_QK_COLS].reshape(b, t, 2, B_HEADS, 2, HEAD_DIM)
    qb = rms_norm(bqk[:, :, 0], p['b_q_norm'])
    kb = rms_norm(bqk[:, :, 1], p['b_k_norm'])
    vb = u[..., A_COLS + B_QK_COLS:A_COLS + B_COLS].reshape(b, t, B_HEADS, B_V_DIM)
    c = u[..., A_COLS + B_COLS:]
    c_q = rms_norm(c[..., :C_Q_LORA], p['c_q_lat_norm'])
    c_kv = rms_norm(c[..., C_Q_LORA:C_Q_LORA + C_KV_LORA], p['c_kv_lat_norm'])
    k_pe = rope(c[..., C_Q_LORA + C_KV_LORA:], pos)
    qc = (c_q @ p['c_w_uq']).reshape(b, t, C_HEADS, C_NOPE + C_ROPE)
    qc = jnp.concatenate([qc[..., :C_NOPE], rope(qc[..., C_NOPE:], pos)], axis=-1)
    qc = rms_norm(qc, p['c_q_norm'])
    return qa, ka, va, qb, kb, vb, qc, c_kv, k_pe


def band_attention(q, k, v, q_pos, k_pos, rel_table):
    s = jnp.einsum('bqhd,bkhd->bhqk', q, k).astype(jnp.float32) * (HEAD_DIM ** -0.5)
    rel = jnp.clip(q_pos[:, None] - k_pos[None, :], -REL_CLIP, REL_CLIP) + REL_CLIP
    s = s + rel_table.astype(jnp.float32)[:, rel]
    qc = q_pos[:, None] // CHUNK
    kc = k_pos[None, :] // CHUNK
    vis = (k_pos[None, :] >= 0) & (kc <= qc) & (kc >= qc - N_PREV_CHUNKS)
    prob = jax.nn.softmax(jnp.where(vis, s, NEG_INF), axis=-1)
    return jnp.einsum('bhqk,bkhd->bqhd', prob.astype(v.dtype), v)


def band_attention_prompt(q, k, v, rel_table):
    b, s = q.shape[:2]
    nc = s // CHUNK
    reach = N_PREV_CHUNKS * CHUNK
    band = reach + CHUNK
    pad = ((0, 0), (reach, 0), (0, 0), (0, 0))
    kp = jnp.pad(k, pad)
    vp = jnp.pad(v, pad)
    idx = jnp.arange(nc)[:, None] * CHUNK + jnp.arange(band)[None, :]
    kb = kp[:, idx]
    vb = vp[:, idx]
    qb = q.reshape(b, nc, CHUNK, A_HEADS, HEAD_DIM)
    q_pos = jnp.arange(s).reshape(nc, CHUNK)
    k_pos = idx - reach
    out = jax.vmap(band_attention, in_axes=(1, 1, 1, 0, 0, None), out_axes=1)(qb, kb, vb, q_pos, k_pos, rel_table)
    return out.reshape(b, s, A_HEADS, HEAD_DIM)


def diff_lambda(b_lambda, lam_init):
    lf = b_lambda.astype(jnp.float32)
    return jnp.exp(jnp.sum(lf[0] * lf[1])) - jnp.exp(jnp.sum(lf[2] * lf[3])) + lam_init


def diff_attention(q, k, v, q_pos, k_pos, lam):
    s = jnp.einsum('bqhjd,bkhjd->bjhqk', q, k).astype(jnp.float32) * (HEAD_DIM ** -0.5)
    dist = jnp.abs(q_pos[:, None] - k_pos[None, :]).astype(jnp.float32)
    s = s - alibi_slopes(B_HEADS)[:, None, None] * dist
    s = jnp.where(chunk_causal(q_pos, k_pos), s, NEG_INF)
    prob = jax.nn.softmax(s, axis=-1)
    a = prob[:, 0] - lam * prob[:, 1]
    return jnp.einsum('bhqk,bkhe->bqhe', a.astype(v.dtype), v)


def mla_keys(c_kv, k_pe, w_ukv, k_norm):
    b, t = c_kv.shape[:2]
    kv = (c_kv @ w_ukv).reshape(b, t, C_HEADS, C_NOPE + C_V)
    k = jnp.concatenate([kv[..., :C_NOPE], jnp.broadcast_to(k_pe[:, :, None, :], (b, t, C_HEADS, C_ROPE))], axis=-1)
    return rms_norm(k, k_norm), kv[..., C_NOPE:]


def latent_attention(q, k, v, q_pos, k_pos):
    s = jnp.einsum('bqhd,bkhd->bhqk', q, k).astype(jnp.float32) * ((C_NOPE + C_ROPE) ** -0.5)
    prob = jax.nn.softmax(jnp.where(chunk_causal(q_pos, k_pos), s, NEG_INF), axis=-1)
    return jnp.einsum('bhqk,bkhd->bqhd', prob.astype(v.dtype), v)


def merge_heads(oa, ob, oc, b_sub_norm, w_out, lam_init):
    b, t = oa.shape[:2]
    ob = rms_norm(ob, b_sub_norm) * (1.0 - lam_init)
    o = jnp.concatenate([oa.reshape(b, t, A_WIDTH), ob.reshape(b, t, B_WIDTH), oc.reshape(b, t, C_WIDTH)], axis=-1)
    return o @ w_out


def memory_kv(mem, p):
    b, m = mem.shape[:2]
    mn = rms_norm(mem, p['mem_norm_m'])
    k = rms_norm((mn @ p['mem_w_k']).reshape(b, m, M_HEADS, M_HEAD_DIM), p['mem_k_norm'])
    v = (mn @ p['mem_w_v']).reshape(b, m, M_HEADS, M_HEAD_DIM)
    return k, v


def memory_attention(x, k, v, p):
    b, t = x.shape[:2]
    q = (rms_norm(x, p['mem_norm_x']) @ p['mem_w_q']).reshape(b, t, M_HEADS, M_HEAD_DIM)
    q = rms_norm(q, p['mem_q_norm'])
    s = jnp.einsum('bqhd,bmhd->bhqm', q, k).astype(jnp.float32) * (M_HEAD_DIM ** -0.5)
    prob = jax.nn.softmax(s, axis=-1)
    o = jnp.einsum('bhqm,bmhd->bqhd', prob.astype(v.dtype), v)
    return o.reshape(b, t, M_WIDTH) @ p['mem_w_o']


def prompt_layer(x, mem, p, lam_init):
    s = x.shape[1]
    pos = jnp.arange(s)
    x = x + half_ffn(x, p['ffn1_norm'], p['ffn1_w_gate'], p['ffn1_w_up'], p['ffn1_w_down'])
    h = rms_norm(x, p['mix_norm'])
    qa, ka, va, qb, kb, vb, qc, c_kv, k_pe = mix_projections(h, pos, p)
    oa = band_attention_prompt(qa, ka, va, p['a_rel_bias'])
    lam = diff_lambda(p['b_lambda'], lam_init)
    ob = map_query_blocks(lambda qq, qp: diff_attention(qq, kb, vb, qp, pos, lam), qb, pos)
    kc, vc = mla_keys(c_kv, k_pe, p['c_w_ukv'], p['c_k_norm'])
    oc = map_query_blocks(lambda qq, qp: latent_attention(qq, kc, vc, qp, pos), qc, pos)
    x = x + merge_heads(oa, ob, oc, p['b_sub_norm'], p['w_out'], lam_init)
    mk, mv = memory_kv(mem, p)
    x = x + memory_attention(x, mk, mv, p)
    x = x + half_ffn(x, p['ffn2_norm'], p['ffn2_w_gate'], p['ffn2_w_up'], p['ffn2_w_down'])
    reach = min(N_PREV_CHUNKS * CHUNK, s)
    return x, (ka[:, s - reach:], va[:, s - reach:], kb, vb, c_kv, k_pe, mk, mv)


def sample_layer(x, p, lam_init, ca_k, ca_v, cb_k, cb_v, cc_kv, cc_kpe, cm_k, cm_v):
    n = x.shape[1]
    past = cb_k.shape[1]
    a_len = ca_k.shape[1]
    pos = past + jnp.arange(n)
    x = x + half_ffn(x, p['ffn1_norm'], p['ffn1_w_gate'], p['ffn1_w_up'], p['ffn1_w_down'])
    h = rms_norm(x, p['mix_norm'])
    qa, ka, va, qb, kb, vb, qc, c_kv, k_pe = mix_projections(h, pos, p)
    a_pos = jnp.concatenate([past - a_len + jnp.arange(a_len), pos])
    oa = band_attention(qa, jnp.concatenate([ca_k, ka], axis=1), jnp.concatenate([ca_v, va], axis=1), pos, a_pos, p['a_rel_bias'])
    k_pos = jnp.arange(past + n)
    lam = diff_lambda(p['b_lambda'], lam_init)
    ob = diff_attention(qb, jnp.concatenate([cb_k, kb], axis=1), jnp.concatenate([cb_v, vb], axis=1), pos, k_pos, lam)
    kc, vc = mla_keys(jnp.concatenate([cc_kv, c_kv], axis=1), jnp.concatenate([cc_kpe, k_pe], axis=1), p['c_w_ukv'], p['c_k_norm'])
    oc = latent_attention(qc, kc, vc, pos, k_pos)
    x = x + merge_heads(oa, ob, oc, p['b_sub_norm'], p['w_out'], lam_init)
    x = x + memory_attention(x, cm_k, cm_v, p)
    x = x + half_ffn(x, p['ffn2_norm'], p['ffn2_w_gate'], p['ffn2_w_up'], p['ffn2_w_down'])
    return x, (ka, va, kb, vb, c_kv, k_pe)


def setup_inputs(seed: int = 0) -> dict:
    key = jax.random.key(seed)
    keys = jax.random.split(key, 64)
    counter = [0]

    def nrm(shape, scale=1.0):
        k = keys[counter[0]]
        counter[0] += 1
        return scale * jax.random.normal(k, shape, jnp.float32)

    def gain(n):
        return 1.0 + nrm((DEPTH, n), 0.02)

    def mat(fan_in, fan_out):
        return nrm((DEPTH, fan_in, fan_out), fan_in ** -0.5)

    a_len = min(N_PREV_CHUNKS * CHUNK, PAST_LEN)
    return {
        'x_prompt': nrm((BATCH, SEQ, D_MODEL)),
        'x_sample': nrm((DEC_BATCH, DEC_SEQ, D_MODEL)),
        'mem_prompt': nrm((BATCH, N_MEM, D_MODEL)),
        'cache_a_k': nrm((DEPTH, DEC_BATCH, a_len, A_HEADS, HEAD_DIM)),
        'cache_a_v': nrm((DEPTH, DEC_BATCH, a_len, A_HEADS, HEAD_DIM)),
        'cache_b_k': nrm((DEPTH, DEC_BATCH, PAST_LEN, B_HEADS, 2, HEAD_DIM)),
        'cache_b_v': nrm((DEPTH, DEC_BATCH, PAST_LEN, B_HEADS, B_V_DIM)),
        'cache_c_kv': nrm((DEPTH, DEC_BATCH, PAST_LEN, C_KV_LORA)),
        'cache_c_kpe': nrm((DEPTH, DEC_BATCH, PAST_LEN, C_ROPE)),
        'cache_mem_k': nrm((DEPTH, DEC_BATCH, N_MEM, M_HEADS, M_HEAD_DIM)),
        'cache_mem_v': nrm((DEPTH, DEC_BATCH, N_MEM, M_HEADS, M_HEAD_DIM)),
        'ffn1_norm': gain(D_MODEL),
        'ffn1_w_gate': mat(D_MODEL, D_FF),
        'ffn1_w_up': mat(D_MODEL, D_FF),
        'ffn1_w_down': mat(D_FF, D_MODEL),
        'mix_norm': gain(D_MODEL),
        'w_in': mat(D_MODEL, IN_COLS),
        'a_q_norm': gain(HEAD_DIM),
        'a_k_norm': gain(HEAD_DIM),
        'a_rel_bias': nrm((DEPTH, A_HEADS, 2 * REL_CLIP + 1), 0.1),
        'b_q_norm': gain(HEAD_DIM),
        'b_k_norm': gain(HEAD_DIM),
        'b_lambda': nrm((DEPTH, 4, HEAD_DIM), 0.1),
        'b_sub_norm': gain(B_V_DIM),
        'c_q_lat_norm': gain(C_Q_LORA),
        'c_kv_lat_norm': gain(C_KV_LORA),
        'c_w_uq': mat(C_Q_LORA, C_HEADS * (C_NOPE + C_ROPE)),
        'c_w_ukv': mat(C_KV_LORA, C_HEADS * (C_NOPE + C_V)),
        'c_q_norm': gain(C_NOPE + C_ROPE),
        'c_k_norm': gain(C_NOPE + C_ROPE),
        'w_out': mat(MIX_WIDTH, D_MODEL),
        'mem_norm_x': gain(D_MODEL),
        'mem_w_q': mat(D_MODEL, M_WIDTH),
        'mem_q_norm': gain(M_HEAD_DIM),
        'mem_norm_m': gain(D_MODEL),
        'mem_w_k': mat(D_MODEL, M_WIDTH),
        'mem_w_v': mat(D_MODEL, M_WIDTH),
        'mem_k_norm': gain(M_HEAD_DIM),
        'mem_w_o': mat(M_WIDTH, D_MODEL),
        'ffn2_norm': gain(D_MODEL),
        'ffn2_w_gate': mat(D_MODEL, D_FF),
        'ffn2_w_up': mat(D_MODEL, D_FF),
        'ffn2_w_down': mat(D_FF, D_MODEL),
    }


def reference(x_prompt, x_sample, mem_prompt, cache_a_k, cache_a_v, cache_b_k, cache_b_v,
              cache_c_kv, cache_c_kpe, cache_mem_k, cache_mem_v,
              ffn1_norm, ffn1_w_gate, ffn1_w_up, ffn1_w_down, mix_norm, w_in,
              a_q_norm, a_k_norm, a_rel_bias, b_q_norm, b_k_norm, b_lambda, b_sub_norm,
              c_q_lat_norm, c_kv_lat_norm, c_w_uq, c_w_ukv, c_q_norm, c_k_norm, w_out,
              mem_norm_x, mem_w_q, mem_q_norm, mem_norm_m, mem_w_k, mem_w_v, mem_k_norm, mem_w_o,
              ffn2_norm, ffn2_w_gate, ffn2_w_up, ffn2_w_down):
    xp, xs = x_prompt, x_sample
    prompt_states = [[] for _ in range(8)]
    sample_states = [[] for _ in range(6)]
    for l in range(DEPTH):
        p = {
            'ffn1_norm': ffn1_norm[l], 'ffn1_w_gate': ffn1_w_gate[l], 'ffn1_w_up': ffn1_w_up[l],
            'ffn1_w_down': ffn1_w_down[l], 'mix_norm': mix_norm[l], 'w_in': w_in[l],
            'a_q_norm': a_q_norm[l], 'a_k_norm': a_k_norm[l], 'a_rel_bias': a_rel_bias[l],
            'b_q_norm': b_q_norm[l], 'b_k_norm': b_k_norm[l], 'b_lambda': b_lambda[l],
            'b_sub_norm': b_sub_norm[l], 'c_q_lat_norm': c_q_lat_norm[l],
            'c_kv_lat_norm': c_kv_lat_norm[l], 'c_w_uq': c_w_uq[l], 'c_w_ukv': c_w_ukv[l],
            'c_q_norm': c_q_norm[l], 'c_k_norm': c_k_norm[l], 'w_out': w_out[l],
            'mem_norm_x': mem_norm_x[l], 'mem_w_q': mem_w_q[l], 'mem_q_norm': mem_q_norm[l],
            'mem_norm_m': mem_norm_m[l], 'mem_w_k': mem_w_k[l], 'mem_w_v': mem_w_v[l],
            'mem_k_norm': mem_k_norm[l], 'mem_w_o': mem_w_o[l],
            'ffn2_norm': ffn2_norm[l], 'ffn2_w_gate': ffn2_w_gate[l], 'ffn2_w_up': ffn2_w_up[l],
            'ffn2_w_down': ffn2_w_down[l],
        }
        lam_init = 0.8 - 0.6 * math.exp(-0.3 * l)
        xp, sp = prompt_layer(xp, mem_prompt, p, lam_init)
        xs, ss = sample_layer(xs, p, lam_init, cache_a_k[l], cache_a_v[l], cache_b_k[l], cache_b_v[l],
                              cache_c_kv[l], cache_c_kpe[l], cache_mem_k[l], cache_mem_v[l])
        for lst, arr in zip(prompt_states, sp):
            lst.append(arr)
        for lst, arr in zip(sample_states, ss):
            lst.append(arr)
    pak, pav, pbk, pbv, pckv, pckpe, pmk, pmv = [jnp.stack(s) for s in prompt_states]
    sak, sav, sbk, sbv, sckv, sckpe = [jnp.stack(s) for s in sample_states]
    return (xp, xs, pak, pav, pbk, pbv, pckv, pckpe, pmk, pmv, sak, sav, sbk, sbv, sckv, sckpe)
```

```python
import math
import os
from contextlib import ExitStack
import numpy as np
import concourse.bass as bass
import concourse.mybir as mybir
from concourse.bass_utils import run_bass_kernel_spmd

F32 = mybir.dt.float32
BF16 = mybir.dt.bfloat16
ALU = mybir.AluOpType
AF = mybir.ActivationFunctionType
AX = mybir.AxisListType

D = 1024
DFF = 2816
HD = 64
NMEM = 256
EPS = 1e-6
IN_COLS = 2976
NEG = -30000.0
ENGS = ("pe", "act", "dve", "pool", "sp")

O_KA, O_VA, O_KB, O_VB, O_KC, O_VC, RBE = 0, 32768, 65536, 135168, 200704, 249856, 282624


class GSync:
    def __init__(self, nc):
        self.nc = nc
        self.esem = {e: nc.alloc_semaphore(name=f"es_{e}") for e in ("pe", "act", "dve", "pool")}
        self.ecnt = {e: 0 for e in self.esem}
        self.dsem = {}
        self.dcnt = {}
        self.kmap = {}
        self.nops = 0

    def kid(self, key):
        if key not in self.kmap:
            self.kmap[key] = len(self.kmap) % 56
        return self.kmap[key]

    def dkey(self, key):
        if key not in self.dsem:
            self.dsem[key] = self.nc.alloc_semaphore(name=f"ds_{len(self.dsem)}")
            self.dcnt[key] = 0
        return self.dsem[key]


class Phase:
    def __init__(self, g):
        self.g = g
        self.raw = []
        self.stream = 0
        self.ops = []

    def _add(self, eng, fn, r, w, dma_key=None, inc=16):
        if dma_key is not None:
            dma_key = self.g.kid(dma_key)
        self.raw.append(dict(eng=eng, fn=fn, r=r, w=w, dma=dma_key, inc=inc, stream=self.stream))
        return len(self.raw) - 1

    def _finalize(self):
        streams = {}
        for o in self.raw:
            streams.setdefault(o["stream"], []).append(o)
        keys = sorted(streams)
        if len(keys) == 1:
            order = streams[keys[0]]
        else:
            order = []
            pos = {k: 0 for k in keys}
            tot = {k: len(streams[k]) for k in keys}
            while any(pos[k] < tot[k] for k in keys):
                k = min((k for k in keys if pos[k] < tot[k]), key=lambda k: (pos[k] + 1) / tot[k])
                order.append(streams[k][pos[k]])
                pos[k] += 1
        lastw, readers, lastdma = {}, {}, {}
        self.ops = []
        for raw in order:
            idx = len(self.ops)
            eng, dma_key = raw["eng"], raw["dma"]
            deps = {}
            for x in raw["r"]:
                if x in lastw:
                    deps.setdefault(lastw[x], set()).add("raw")
            for x in raw["w"]:
                if x in lastw:
                    deps.setdefault(lastw[x], set()).add("waw")
                for rd in readers.get(x, ()):
                    deps.setdefault(rd, set()).add("war")
            if dma_key is not None and dma_key in lastdma:
                deps.setdefault(lastdma[dma_key], set()).add("raw")
            op = dict(idx=idx, eng=eng, fn=raw["fn"], dma=dma_key, waits=[], signal=False, cnt=None, inc=raw["inc"])
            for p, kinds in deps.items():
                P = self.ops[p]
                if P["dma"] is not None:
                    op["waits"].append(p)
                elif P["eng"] == eng and dma_key is None and "raw" not in kinds:
                    continue
                else:
                    P["signal"] = True
                    op["waits"].append(p)
            self.ops.append(op)
            for x in raw["r"]:
                readers.setdefault(x, []).append(idx)
            for x in raw["w"]:
                lastw[x] = idx
                readers[x] = []
            if dma_key is not None:
                lastdma[dma_key] = idx

    def op(self, eng, fn, r=(), w=()):
        return self._add(eng, fn, tuple(r), tuple(w))

    def dma(self, q, out, in_, r=(), w=(), key=None, **kw):
        return self._add(q, lambda e: e.dma_start(out=out, in_=in_, **kw), tuple(r), tuple(w), dma_key=key)

    def custom(self, q, fn, r=(), w=(), key=None, inc=16):
        return self._add(q, fn, tuple(r), tuple(w), dma_key=key, inc=inc)

    def run(self):
        g = self.g
        nc = g.nc
        self._finalize()
        for op in self.ops:
            if op["dma"] is not None:
                g.dkey(op["dma"])
                g.dcnt[op["dma"]] += op["inc"]
                op["cnt"] = g.dcnt[op["dma"]]
            elif op["signal"]:
                g.ecnt[op["eng"]] += 1
                op["cnt"] = g.ecnt[op["eng"]]
        per = {e: [o for o in self.ops if o["eng"] == e] for e in ENGS}
        ops = self.ops
        g.nops += len(ops)

        def emit(e, lst):
            def body(engine):
                waited = {}
                lastkeys = {}
                for o in lst:
                    need = {}
                    for p in o["waits"]:
                        P = ops[p]
                        s = g.dsem[P["dma"]] if P["dma"] is not None else g.esem[P["eng"]]
                        k = id(s)
                        if k not in need or need[k][1] < P["cnt"]:
                            need[k] = (s, P["cnt"])
                    for k, (s, v) in need.items():
                        if waited.get(k, -1) >= v:
                            continue
                        engine.wait_ge(s, v)
                        waited[k] = v
                    ins = o["fn"](engine)
                    if o["dma"] is not None:
                        ins.then_inc(g.dsem[o["dma"]], o["inc"])
                        lastkeys[o["dma"]] = o["cnt"]
                    elif o["signal"]:
                        ins.then_inc(g.esem[e], 1)
                for key, v in lastkeys.items():
                    engine.wait_ge(g.dsem[key], v)
            return body

        with nc.Block() as block:
            for e in ENGS:
                if not per[e]:
                    continue
                reg = {"pe": block.tensor, "act": block.scalar, "dve": block.vector,
                       "pool": block.gpsimd, "sp": block.sync}[e]
                reg(emit(e, per[e]))


def build(NBLK, DEPTH, PAST):
    STAGE = int(os.environ.get('KSTAGE', '99'))
    NB = NBLK + 1
    T = NBLK * 128 + 64
    NCB = PAST // 128
    NSB = NCB + 1
    NKB = 2 * NBLK
    SC = NBLK * 128
    nc = bass.Bass("TRN2", target_bir_lowering=False)
    g = GSync(nc)

    def din(name, shape, dt=F32):
        return nc.dram_tensor(name, list(shape), dt, kind="ExternalInput")

    def dout(name, shape):
        return nc.dram_tensor(name, list(shape), F32, kind="ExternalOutput")

    xin = din("xin", [T, D])
    mem = din("mem", [NMEM, D])
    ca_k = din("ca_k", [DEPTH, 512, 256]); ca_v = din("ca_v", [DEPTH, 512, 256])
    cb_k = din("cb_k", [DEPTH, PAST, 512]); cb_v = din("cb_v", [DEPTH, PAST, 512])
    cc_kv = din("cc_kv", [DEPTH, PAST, 256]); cc_kpe = din("cc_kpe", [DEPTH, PAST, 32])
    cm_k = din("cm_k", [DEPTH, NMEM, 512]); cm_v = din("cm_v", [DEPTH, NMEM, 512])
    W = {}
    for nm, shp in [("ffn1_norm", [DEPTH, D]), ("ffn1_w_gate", [DEPTH, D, DFF]), ("ffn1_w_up", [DEPTH, D, DFF]),
                    ("ffn1_w_down", [DEPTH, DFF, D]), ("mix_norm", [DEPTH, D]), ("w_in", [DEPTH, D, IN_COLS]),
                    ("a_q_norm", [DEPTH, 64]), ("a_k_norm", [DEPTH, 64]), ("a_rel_bias", [DEPTH, 4, 257]),
                    ("b_q_norm", [DEPTH, 64]), ("b_k_norm", [DEPTH, 64]), ("b_lambda", [DEPTH, 4, 64]),
                    ("b_sub_norm", [DEPTH, 128]), ("c_q_lat_norm", [DEPTH, 384]), ("c_kv_lat_norm", [DEPTH, 256]),
                    ("c_w_uq", [DEPTH, 384, 384]), ("c_w_ukv", [DEPTH, 256, 512]), ("c_q_norm", [DEPTH, 96]),
                    ("c_k_norm", [DEPTH, 96]), ("w_out", [DEPTH, D, D]), ("mem_norm_x", [DEPTH, D]),
                    ("mem_w_q", [DEPTH, D, 512]), ("mem_q_norm", [DEPTH, 128]), ("mem_norm_m", [DEPTH, D]),
                    ("mem_w_k", [DEPTH, D, 512]), ("mem_w_v", [DEPTH, D, 512]), ("mem_k_norm", [DEPTH, 128]),
                    ("mem_w_o", [DEPTH, 512, D]), ("ffn2_norm", [DEPTH, D]), ("ffn2_w_gate", [DEPTH, D, DFF]),
                    ("ffn2_w_up", [DEPTH, D, DFF]), ("ffn2_w_down", [DEPTH, DFF, D])]:
        W[nm] = din(nm, shp)
    c_ident = din("c_ident", [128, 128])
    c_cs = din("c_cs", [T, 32])
    c_augq = din("c_augq", [T, 32])
    c_augk = din("c_augk", [T, 32])
    c_augkc = din("c_augkc", [PAST, 32])
    c_corrB = din("c_corrB", [128, 2, 4, 128])
    c_corrBs = din("c_corrBs", [64, 4, 64])
    c_maskC = din("c_maskC", [128, 2, 128])
    c_maskA = din("c_maskA", [128, 6, 128])
    c_w01 = din("c_w01", [128, 2])
    c_lam = din("c_lam", [128, 2 * DEPTH])
    y = dout("y", [T, D])
    o_ak = dout("o_ak", [DEPTH, 320, 256]); o_av = dout("o_av", [DEPTH, 320, 256])
    o_bk = dout("o_bk", [DEPTH, T, 512]); o_bv = dout("o_bv", [DEPTH, T, 512])
    o_ckv = dout("o_ckv", [DEPTH, T, 256]); o_kpe = dout("o_kpe", [DEPTH, T, 32])
    o_mk = dout("o_mk", [DEPTH, NMEM, 512]); o_mv = dout("o_mv", [DEPTH, NMEM, 512])
    qa_d = nc.dram_tensor("qa_d", [NB, 64, 4, 128], BF16)
    qb_d = nc.dram_tensor("qb_d", [NB, 68, 8, 128], BF16)
    qc_d = nc.dram_tensor("qc_d", [NB, 96, 4, 128], BF16)
    ksrc = nc.dram_tensor("ksrc", [NBLK * RBE // 128, 128], BF16)
    kdst = nc.dram_tensor("kdst", [2 * NBLK * RBE // 128, 128], BF16)
    srec = nc.dram_tensor("srec", [NSB * RBE // 128, 128], BF16)
    Rtoe = nc.dram_tensor("Rtoe", [4, 128, 1024], F32)
    Etoe = nc.dram_tensor("Etoe", [4, 1024], F32)

    uid = [0]

    def SB(name, shape, dt):
        uid[0] += 1
        return nc.sbuf_tensor(f"{name}_{uid[0]}", shape, dt)

    def PS(name, shape, dt):
        uid[0] += 1
        return nc.psum_tensor(f"{name}_{uid[0]}", shape, dt)

    def rec_ap(tensor, base, dims):
        return bass.AP(tensor, base, [list(d) for d in dims])

    def blk_cols(bi):
        return (bi * 128, 128) if bi < NBLK else (SC, 64)

    groups = [(t0, min(512, SC - t0)) for t0 in range(0, SC, 512)] + [(SC, 64)]

    with ExitStack() as es1:
        xT = es1.enter_context(SB("xT", [128, 8, T], F32))
        idf = es1.enter_context(SB("idf", [128, 128], F32))
        idb = es1.enter_context(SB("idb", [128, 128], BF16))
        ones = es1.enter_context(SB("ones", [128, 128], BF16))
        zerob = es1.enter_context(SB("zerob", [128, 128], BF16))
        gcols = es1.enter_context(SB("gcols", [128, 4 * DEPTH, 8], F32))
        lamc = es1.enter_context(SB("lamc", [128, 2 * DEPTH], F32))

        with ExitStack() as es2:
            xtok = es2.enter_context(SB("xtok", [128, 2, D], F32))
            grow = es2.enter_context(SB("grow", [4 * DEPTH * 8, 128], F32))
            pt = es2.enter_context(PS("pt", [128, 2, 512], F32))
            ph = Phase(g)
            ph.dma("sp", idf[:, :], c_ident[:, :], w=["idf"], key="c0")
            ph.dma("pool", idb[:, :], c_ident[:, :], w=["idb"], key="c1")
            ph.dma("sp", lamc[:, :], c_lam[:, :], w=["lamc"], key="c2")
            ph.op("pool", lambda e: e.memset(ones[:, :], 1.0), w=["ones"])
            ph.op("pool", lambda e: e.memset(zerob[:, :], 0.0), w=["zerob"])
            for k, nm in enumerate(["ffn1_norm", "mix_norm", "mem_norm_x", "ffn2_norm"]):
                for l in range(DEPTH):
                    r0 = (l * 4 + k) * 8
                    ph.dma("sp", grow[r0:r0 + 8, :], W[nm][l, :].rearrange("(c p) -> c p", p=128), w=["grow"], key="c3")
            nr = 4 * DEPTH * 8
            ph.op("pe", lambda e: e.transpose(out=pt[:, 0, 0:nr], in_=grow[:, :], identity=idf[0:nr, 0:nr]), r=["grow", "idf"], w=[("pt", 0)])
            ph.op("dve", lambda e: e.tensor_copy(out=gcols[:, :, :].rearrange("p a b -> p (a b)"), in_=pt[:, 0, 0:nr]), w=[("pt", 0), "gcols"])
            for bi in range(NB):
                c0, nb = blk_cols(bi)
                s = bi % 2
                ph.dma("sp", xtok[:nb, s, :], xin[c0:c0 + nb, :], w=[("xtok", s)], key=("xtok", s))
                for c in range(8):
                    ps = c % 2
                    ph.op("pe", lambda e, s=s, c=c, ps=ps, nb=nb: e.transpose(out=pt[:, ps, 0:nb], in_=xtok[:nb, s, c * 128:(c + 1) * 128], identity=idf[:nb, :nb]),
                          r=[("xtok", s), "idf"], w=[("pt", ps)])
                    ph.op("dve", lambda e, c=c, ps=ps, c0=c0, nb=nb: e.tensor_copy(out=xT[:, c, c0:c0 + nb], in_=pt[:, ps, 0:nb]),
                          w=[("pt", ps), ("xT", c)])
            ph.run()

        def norm_to_hT(ph, hT, xsq, rinv, pn, gidx, only=None, pres=lambda b: ("pn", b)):
            for gi, (t0, n) in enumerate(groups):
                if only is not None and gi != only:
                    continue
                h0 = 0 if only is not None else t0
                s = 0
                ps_ = gi % 2
                ph.op("act", lambda e, t0=t0, n=n, s=s: e.activation(out=xsq[:, s, :, 0:n], in_=xT[:, :, t0:t0 + n], func=AF.Square),
                      r=[("xT", c) for c in range(8)], w=[("xsq", s)])
                for c in range(8):
                    ph.op("pe", lambda e, c=c, n=n, s=s, ps_=ps_: e.matmul(pn[:, ps_, 0:n], lhsT=ones[:, :], rhs=xsq[:, s, c, 0:n], start=(c == 0), stop=(c == 7)),
                          r=[("xsq", s), "ones"], w=[pres(ps_)])
                ph.op("act", lambda e, n=n, s=s, ps_=ps_: e.activation(out=rinv[:, s, 0:n], in_=pn[:, ps_, 0:n], func=AF.Sqrt, scale=1.0 / D, bias=EPS),
                      w=[pres(ps_), ("rinv", s)])
                ph.op("dve", lambda e, n=n, s=s: e.reciprocal(out=rinv[:, s, 0:n], in_=rinv[:, s, 0:n]), r=[("rinv", s)], w=[("rinv", s)])
                for c in range(8):
                    ph.op("dve", lambda e, c=c, t0=t0, n=n, s=s, h0=h0: e.scalar_tensor_tensor(out=hT[:, c, h0:h0 + n], in0=xT[:, c, t0:t0 + n], scalar=gcols[:, gidx, c:c + 1],
                                                                                         in1=rinv[:, s, 0:n], op0=ALU.mult, op1=ALU.mult),
                          r=[("rinv", s), "gcols", ("xT", c)], w=[("hT", c)])

        def ffn(l, which, prefetch=None):
            wg_d, wu_d, wd_d = W[f"ffn{which}_w_gate"], W[f"ffn{which}_w_up"], W[f"ffn{which}_w_down"]
            gidx = l * 4 + (0 if which == 1 else 3)
            NFG = DFF // 256
            with ExitStack() as es3:
                hT = es3.enter_context(SB("hT", [128, 8, T], BF16))
                xsq = es3.enter_context(SB("xsq", [128, 1, 8, 512], BF16))
                rinv = es3.enter_context(SB("rinv", [128, 1, 512], F32))
                wgs = es3.enter_context(SB("wgs", [128, 2, 8, 256], BF16))
                wus = es3.enter_context(SB("wus", [128, 2, 8, 256], BF16))
                wds = es3.enter_context(SB("wds", [128, 2, 2, D], BF16))
                sg = es3.enter_context(SB("sg", [128, 2, 512], F32))
                actT = es3.enter_context(SB("actT", [128, 2, 2, T], BF16))
                pn = es3.enter_context(PS("pn", [128, 2, 512], F32))
                pgt = es3.enter_context(PS("pgt", [128, 2, 512], F32))
                put = es3.enter_context(PS("put", [128, 2, 512], F32))
                pd = es3.enter_context(PS("pd", [128, 2, 512], F32))
                ph = Phase(g)
                norm_to_hT(ph, hT, xsq, rinv, pn, gidx)
                it = 0
                dit = [0]
                pending = []

                def emit_down(fg, ws, t0, n):
                    for dc in range(8):
                        pb = dit[0] % 2
                        dit[0] += 1
                        for j in range(2):
                            ph.op("pe", lambda e, j=j, dc=dc, pb=pb: e.matmul(pd[:, pb, 0:n], lhsT=wds[:, ws, j, dc * 128:(dc + 1) * 128], rhs=actT[:, ws, j, t0:t0 + n], start=(j == 0), stop=(j == 1)),
                                  r=[("wds", ws), ("actT", ws, j, t0)], w=[("pd", pb)])
                        ph.op("dve", lambda e, dc=dc, pb=pb: e.scalar_tensor_tensor(out=xT[:, dc, t0:t0 + n], in0=pd[:, pb, 0:n], scalar=0.5, in1=xT[:, dc, t0:t0 + n], op0=ALU.mult, op1=ALU.add),
                              r=[("xT", dc)], w=[("pd", pb), ("xT", dc)])

                for fg in range(NFG):
                    ws = fg % 2
                    if prefetch is not None and fg in (2, 4, 6, 8):
                        prefetch(ph, fg // 2 - 1)
                    ph.dma("pool", wgs[:, ws, :, :], wg_d[l, :, fg * 256:(fg + 1) * 256].rearrange("(c p) n -> p c n", p=128), w=[("wgs", ws)], key=("wgs", ws))
                    ph.dma("pool", wus[:, ws, :, :], wu_d[l, :, fg * 256:(fg + 1) * 256].rearrange("(c p) n -> p c n", p=128), w=[("wus", ws)], key=("wus", ws))
                    ph.dma("pool", wds[:, ws, :, :], wd_d[l, fg * 256:(fg + 1) * 256, :].rearrange("(c p) n -> p c n", p=128), w=[("wds", ws)], key=("wds", ws))
                    for (t0, n) in groups:
                        for j in range(2):
                            pb = it % 2
                            it += 1
                            for c in range(8):
                                ph.op("pe", lambda e, c=c, j=j, pb=pb, ws=ws, t0=t0, n=n: e.matmul(pgt[:, pb, 0:n], lhsT=wgs[:, ws, c, j * 128:(j + 1) * 128], rhs=hT[:, c, t0:t0 + n], start=(c == 0), stop=(c == 7)),
                                      r=[("wgs", ws), ("hT", c)], w=[("pgt", pb)])
                            for c in range(8):
                                ph.op("pe", lambda e, c=c, j=j, pb=pb, ws=ws, t0=t0, n=n: e.matmul(put[:, pb, 0:n], lhsT=wus[:, ws, c, j * 128:(j + 1) * 128], rhs=hT[:, c, t0:t0 + n], start=(c == 0), stop=(c == 7)),
                                      r=[("wus", ws), ("hT", c)], w=[("put", pb)])
                            ph.op("act", lambda e, pb=pb, n=n: e.activation(out=sg[:, pb, 0:n], in_=pgt[:, pb, 0:n], func=AF.Silu), w=[("pgt", pb), ("sg", pb)])
                            ph.op("dve", lambda e, pb=pb, ws=ws, j=j, t0=t0, n=n: e.tensor_tensor(out=actT[:, ws, j, t0:t0 + n], in0=put[:, pb, 0:n], in1=sg[:, pb, 0:n], op=ALU.mult),
                                  r=[("sg", pb)], w=[("put", pb), ("actT", ws, j, t0)])
                        if pending:
                            emit_down(*pending.pop(0))
                        pending.append((fg, ws, t0, n))
                while pending:
                    emit_down(*pending.pop(0))
                ph.run()

        def bcast_row(ph, dst_ap, src_1d, n, reps, key, wres):
            ph.dma("sp", dst_ap, src_1d.rearrange("(o h n) -> o h n", o=1, h=1).to_broadcast([128, reps, n]), w=[wres], key=key)

        def rms_rows(ph, src, nb, H, dh, gain, out, sq, ss, tag, src_res, out_res, gain_res):
            ph.op("act", lambda e: e.activation(out=sq[:nb, 0:H * dh].rearrange("p (h d) -> p h d", h=H), in_=src, func=AF.Square), r=[], w=list(src_res) + [("sq", tag)])
            ph.op("dve", lambda e: e.tensor_reduce(out=ss[:nb, 0:H], in_=sq[:nb, 0:H * dh].rearrange("p (h d) -> p h d", h=H), axis=AX.X, op=ALU.add), r=[("sq", tag)], w=[("ss", tag)])
            ph.op("act", lambda e: e.activation(out=ss[:nb, 0:H], in_=ss[:nb, 0:H], func=AF.Sqrt, scale=1.0 / dh, bias=EPS), r=[("ss", tag)], w=[("ss", tag)])
            ph.op("dve", lambda e: e.reciprocal(out=ss[:nb, 0:H], in_=ss[:nb, 0:H]), r=[("ss", tag)], w=[("ss", tag)])
            ph.op("dve", lambda e: e.tensor_tensor(out=out, in0=src, in1=ss[:nb, 0:H].unsqueeze(2).to_broadcast([nb, H, dh]), op=ALU.mult), r=[("ss", tag)], w=list(src_res) + list(out_res))
            ph.op("dve", lambda e: e.tensor_tensor(out=out, in0=out, in1=gain, op=ALU.mult), r=list(out_res) + list(gain_res), w=list(out_res))

        def attn_core(ph, sp, pT, qT, nq, tiles, E, outp, out_res, cnt):
            ngr = (len(tiles) + 3) // 4
            bufs = []
            for gi in range(ngr):
                bufs.append(cnt[0] % 2)
                cnt[0] += 1

            def emit_S(gi):
                b = bufs[gi]
                for ti, (kT, v, nk, corrs, kres, vres) in enumerate(tiles[gi * 4:(gi + 1) * 4]):
                    ph.op("pe", lambda e, kT=kT, ti=ti, nk=nk, last=(len(corrs) == 0): e.matmul(sp[:nk, b, ti * 128:ti * 128 + nq], lhsT=kT, rhs=qT, start=True, stop=last),
                          r=list(kres) + ["qt"], w=[("sp", b)])
                    for ci, cr in enumerate(corrs):
                        ph.op("pe", lambda e, cr=cr, ti=ti, nk=nk, last=(ci == len(corrs) - 1): e.matmul(sp[:nk, b, ti * 128:ti * 128 + nq], lhsT=idb[:nk, :nk], rhs=cr, start=False, stop=last),
                              r=["corr", "idb"], w=[("sp", b)])

            def emit_rest(gi):
                b = bufs[gi]
                grp = tiles[gi * 4:(gi + 1) * 4]
                ng = len(grp)
                ph.op("act", lambda e: e.activation(out=pT[:, b, 0:ng, 0:nq], in_=sp[:, b, 0:ng * 128].rearrange("p (g q) -> p g q", g=ng)[:, :, 0:nq], func=AF.Exp),
                      w=[("sp", b), ("pT", b)])
                for ti, (kT, v, nk, corrs, kres, vres) in enumerate(grp):
                    first = (gi == 0 and ti == 0)
                    lastmm = (gi == ngr - 1) and (ti == ng - 1)
                    ph.op("pe", lambda e, v=v, ti=ti, nk=nk, first=first, lastmm=lastmm: e.matmul(outp, lhsT=pT[:nk, b, ti, 0:nq], rhs=v, start=first, stop=lastmm),
                          r=[("pT", b)] + list(vres), w=list(out_res))

            emit_S(0)
            for gi in range(ngr):
                if gi + 1 < ngr:
                    emit_S(gi + 1)
                emit_rest(gi)

        def attn_T(ph, sp, pTb, opp, a, qflat, ncols, tiles, MP, cnt, kres, vres, zero_init=False, qres="qt"):
            nt = len(tiles)
            nbuf = sp.shape[1]
            bufs = []
            for ti in range(nt):
                bufs.append(cnt[0] % nbuf)
                cnt[0] += 1

            if zero_init:
                R0 = qflat.shape[0]
                for k_ in range(2):
                    ph.op("pe", lambda e, k_=k_: e.matmul(opp[:MP, 2 * a + k_, 0:ncols], lhsT=zerob[0:R0, 0:MP], rhs=qflat[:, 0:ncols], start=True, stop=False),
                          r=["zerob", qres], w=[("op", 2 * a + k_)])

            def emit_S(ti):
                kT, vl, nk, c_lo, corrs = tiles[ti][:5]
                c_hi = tiles[ti][5] if len(tiles[ti]) > 5 else ncols
                b = bufs[ti]
                ph.op("pe", lambda e: e.matmul(sp[:nk, b, c_lo:c_hi], lhsT=kT, rhs=qflat[:, c_lo:c_hi], start=True, stop=(len(corrs) == 0)),
                      r=list(kres) + [qres], w=[("sp", b)])
                for ci, (lo, hi, cap) in enumerate(corrs):
                    ph.op("pe", lambda e, lo=lo, hi=hi, cap=cap, last=(ci == len(corrs) - 1): e.matmul(sp[:nk, b, lo:hi], lhsT=idb[:nk, :nk], rhs=cap, start=False, stop=last),
                          r=["corr", "idb"], w=[("sp", b)])

            def emit_rest(ti):
                kT, vl, nk, c_lo, corrs = tiles[ti][:5]
                c_hi = tiles[ti][5] if len(tiles[ti]) > 5 else ncols
                b = bufs[ti]
                st_ = (ti == 0) and not zero_init
                ph.op("act", lambda e: e.activation(out=pTb[:nk, b, c_lo:c_hi], in_=sp[:nk, b, c_lo:c_hi], func=AF.Exp), w=[("sp", b), ("pT", b)])
                ph.op("pe", lambda e: e.matmul(opp[:MP, 2 * a, c_lo:c_hi], lhsT=vl, rhs=pTb[:nk, b, c_lo:c_hi], start=st_, stop=(ti == nt - 1)),
                      r=[("pT", b)] + list(vres), w=[("op", 2 * a)])
                ph.op("pe", lambda e: e.matmul(opp[:MP, 2 * a + 1, c_lo:c_hi], lhsT=ones[:nk, :MP], rhs=pTb[:nk, b, c_lo:c_hi], start=st_, stop=(ti == nt - 1)),
                      r=[("pT", b), "ones"], w=[("op", 2 * a + 1)])

            for ti in range(min(nbuf - 1, nt)):
                emit_S(ti)
            for ti in range(nt):
                if ti + nbuf - 1 < nt:
                    emit_S(ti + nbuf - 1)
                emit_rest(ti)

        def load_kv(ph, kt, vt, R, E, off_k, off_v, h, nhk, HV, res):
            HK = {O_KA: 4, O_KB: 8, O_KC: 4}[off_k]
            for jj in range(nhk):
                ph.dma("sp", kt[0:R, jj, :, :, :], rec_ap(kdst, off_k + (h * nhk + jj) * 128, [[HK * 128, R], [RBE, NKB], [1, 128]]), w=[res + "k"], key=("Kk", jj))
            ph.dma("act", vt[:, :, :, 0:E], rec_ap(kdst, off_v + h * E, [[HV * E, 128], [RBE, NKB], [1, E]]), w=[res + "v"], key="Kv")

        def load_kv_s(ph, kts, vts, R, E, off_k, off_v, h, nhk, HV, res, b0, nbk):
            HK = {O_KA: 4, O_KB: 8, O_KC: 4}[off_k]
            for jj in range(nhk):
                ph.dma("sp", kts[0:R, jj, 0:nbk, :], rec_ap(srec, b0 * RBE + off_k + (h * nhk + jj) * 128, [[HK * 128, R], [RBE, nbk], [1, 128]]), w=[res + "ks"], key=("Kks", jj))
            ph.dma("act", vts[:, 0:nbk, 0:E], rec_ap(srec, b0 * RBE + off_v + h * E, [[HV * E, 128], [RBE, nbk], [1, E]]), w=[res + "vs"], key="Kvs")

        def attention(l, Oall):
            with ExitStack() as es4:
                rc = es4.enter_context(SB("rc", [128, 8], F32))
                sp3 = es4.enter_context(PS("sp", [128, 3, 512], F32))
                sp = sp3[:, 0:2, :]
                opp = es4.enter_context(PS("opp", [128, 4, 512], F32))
                pss = es4.enter_context(PS("pss", [128, 1, 512], F32))
                OT = es4.enter_context(SB("OT", [128, 8, T], BF16))
                with ExitStack() as es5:
                    tb = es5.enter_context(SB("tb", [4, 257], F32))
                    ng = es5.enter_context(SB("ng", [4, 1], F32))
                    ext = es5.enter_context(SB("ext", [4, 1024], F32))
                    extb = es5.enter_context(SB("extb", [128, 1024], F32))
                    t7 = es5.enter_context(SB("t7", [128, 7, 128], F32))
                    tmpa = es5.enter_context(SB("tmpa", [128, 6, 128], F32))
                    mka = es5.enter_context(SB("mka", [128, 6, 128], F32))
                    w01 = es5.enter_context(SB("w01", [128, 2], F32))
                    biasA = es5.enter_context(SB("biasA", [128, 4, 6, 128], BF16))
                    biasS = es5.enter_context(SB("biasS", [128, 4, 5, 64], BF16))
                    kt = es5.enter_context(SB("kt", [64, NKB, 128], BF16))
                    vtp = es5.enter_context(SB("vtp", [128, 2, NKB, 128], BF16))
                    kts = es5.enter_context(SB("kts", [64, 5, 128], BF16))
                    vtsp = es5.enter_context(SB("vtsp", [128, 2, 5, 128], BF16))
                    qt = es5.enter_context(SB("qt", [64, NB * 128], BF16))
                    pTb = es5.enter_context(SB("pTb", [128, 3, 512], BF16))
                    rsA = es5.enter_context(SB("rsA", [128, 2, 512], F32))
                    ph = Phase(g)
                    ph.dma("sp", tb[:, :], W["a_rel_bias"][l, :, :], w=["tb"], key="tb")
                    ph.dma("sp", mka[:, :, :], c_maskA[:, :, :], w=["mka"], key="mka")
                    ph.dma("sp", w01[:, :], c_w01[:, :], w=["w01"], key="w01")
                    ph.op("dve", lambda e: e.tensor_scalar(out=ng[:, :], in0=tb[:, 256:257], scalar1=-1.0, scalar2=None, op0=ALU.mult), r=["tb"], w=["ng"])
                    ph.op("pool", lambda e: e.memset(ext[:, :], 0.0), w=["ext"])
                    ph.op("dve", lambda e: e.tensor_scalar(out=ext[:, 127:384], in0=tb[:, :], scalar1=ng[:, 0:1], scalar2=None, op0=ALU.add), r=["tb", "ng"], w=["ext"])
                    ph.op("dve", lambda e: e.tensor_copy(out=ext[:, 0:127], in_=ext[:, 127:128].to_broadcast([4, 127])), r=["ext"], w=["ext"])
                    ph.dma("sp", Etoe[:, :], ext[:, :], r=["ext"], w=["Etoe"], key="Etoe")
                    for h in range(4):
                        ph.dma("sp", extb[:, :], Etoe[h:h + 1, :].to_broadcast([128, 1024]), r=["Etoe"], w=["extb"], key="extb")
                        ph.dma("sp", Rtoe[h, :, :], extb[:, :], r=["extb"], w=[("Rtoe", h)], key="Rtoe")
                    ph.op("pool", lambda e: e.memset(vtp[:, :, :, :], 0.0), w=["Av"])
                    ph.op("pool", lambda e: e.memset(vtsp[:, :, :, :], 0.0), w=["Avs"])
                    cnt = [0]
                    acc = 0
                    qgroupsA = [list(range(b0_, min(b0_ + 4, NBLK))) for b0_ in range(0, NBLK, 4)] + [[NBLK]]
                    for h in range(4):
                        ph.dma("sp", t7[:, :, :], rec_ap(Rtoe, h * 128 * 1024 + 127, [[1023, 128], [128, 7], [1, 128]]), r=[("Rtoe", h)], w=["t7"], key="t7")
                        ph.op("dve", lambda e: e.tensor_scalar(out=tmpa[:, :, :], in0=t7[:, 0:6, :], scalar1=w01[:, 0:1], scalar2=None, op0=ALU.mult), r=["t7", "w01"], w=["tmpa"])
                        ph.op("dve", lambda e: e.scalar_tensor_tensor(out=tmpa[:, :, :], in0=t7[:, 1:7, :], scalar=w01[:, 1:2], in1=tmpa[:, :, :], op0=ALU.mult, op1=ALU.add), r=["t7", "w01", "tmpa"], w=["tmpa"])
                        ph.op("dve", lambda e, h=h: e.tensor_tensor(out=biasA[:, h, :, :], in0=tmpa[:, :, :], in1=mka[:, :, :], op=ALU.add), r=["tmpa", "mka"], w=["corr"])
                        ph.op("dve", lambda e, h=h: e.tensor_copy(out=biasS[:, h, :, :], in_=t7[:, 1:6, 0:64]), r=["t7"], w=["corr"])
                        par = h % 2
                        ph.dma("sp", kt[0:64, :, :], rec_ap(kdst, O_KA + h * 128, [[4 * 128, 64], [RBE, NKB], [1, 128]]), w=["Ak"], key=("Kk", 0))
                        ph.dma("sp", kts[0:64, :, :], rec_ap(srec, (NCB - 4) * RBE + O_KA + h * 128, [[4 * 128, 64], [RBE, 5], [1, 128]]), w=["Aks"], key=("Kks", 0))
                        ph.dma("sp", qt[:, :].rearrange("p (b t) -> p b t", b=NB), qa_d[:, :, h, :].rearrange("b d t -> d b t"), w=["qt"], key=("Kq", 0))
                        ph.dma("act", vtp[:, par, :, par * 64:par * 64 + 64], rec_ap(kdst, O_VA + h * 64, [[256, 128], [RBE, NKB], [1, 64]]), w=["Av"], key="Kv")
                        ph.dma("act", vtsp[:, par, :, par * 64:par * 64 + 64], rec_ap(srec, (NCB - 4) * RBE + O_VA + h * 64, [[256, 128], [RBE, 5], [1, 64]]), w=["Avs"], key="Kvs")
                        for qg in qgroupsA:
                            b0, b1 = qg[0], qg[-1]
                            tiles = []
                            if b0 < NBLK:
                                ncols = len(qg) * 128
                                q0 = b0 * 128
                                for kb in range(max(0, 2 * b0 - 4), 2 * b1 + 2):
                                    ilo = max(b0, kb // 2)
                                    ihi = min(b1, (kb + 4) // 2)
                                    corrs = []
                                    for i in range(ilo, ihi + 1):
                                        j = kb - (2 * i - 4)
                                        corrs.append(((i - b0) * 128, (i - b0 + 1) * 128, biasA[:, h, 5 - j, :]))
                                    tiles.append((kt[0:64, kb, :], vtp[:, par, kb, :], 128, (ilo - b0) * 128, corrs, (ihi - b0 + 1) * 128))
                                kres, vres = ["Ak"], ["Av"]
                            else:
                                ncols = 64
                                q0 = SC
                                for cbi in range(4):
                                    tiles.append((kts[0:64, cbi, :], vtsp[:, par, cbi, :], 128, 0, [(0, 64, biasS[:, h, 4 - cbi, :])]))
                                tiles.append((kts[0:64, 4, 0:64], vtsp[0:64, par, 4, :], 64, 0, [(0, 64, biasS[0:64, h, 0, :])]))
                                kres, vres = ["Aks"], ["Avs"]
                            a = acc % 2
                            acc += 1
                            attn_T(ph, sp3, pTb, opp, a, qt[0:64, q0:q0 + ncols], ncols, tiles, 128, cnt, kres, vres, zero_init=True)
                            ph.op("dve", lambda e, a=a, ncols=ncols: e.reciprocal(out=rsA[:, a, 0:ncols], in_=opp[:, 2 * a + 1, 0:ncols]), w=[("op", 2 * a + 1), ("rs", a)])
                            ph.op("dve", lambda e, a=a, ncols=ncols, q0=q0, h=h, par=par: e.tensor_tensor(out=OT[par * 64:par * 64 + 64, h // 2, q0:q0 + ncols], in0=opp[par * 64:par * 64 + 64, 2 * a, 0:ncols],
                                                                                                          in1=rsA[par * 64:par * 64 + 64, a, 0:ncols], op=ALU.mult),
                                  r=[("rs", a)], w=[("op", 2 * a), ("OT", h // 2)])
                    ph.run()
                if STAGE < 5:
                    return
                qgroups = [list(range(b0, min(b0 + 4, NBLK))) for b0 in range(0, NBLK, 4)] + [[NBLK]]
                with ExitStack() as es6:
                    kt2 = es6.enter_context(SB("kt", [68, 4, NKB, 128], BF16))
                    vt2 = es6.enter_context(SB("vt", [128, 2, NKB, 128], BF16))
                    kts = es6.enter_context(SB("kts", [68, 2, NSB, 128], BF16))
                    vts = es6.enter_context(SB("vts", [128, NSB, 128], BF16))
                    qt2 = es6.enter_context(SB("qt", [68, 4, NB * 128], BF16))
                    pTb = es6.enter_context(SB("pTb", [128, 3, 512], BF16))
                    corrB = es6.enter_context(SB("corrB", [128, 2, 4, 128], BF16))
                    corrBs = es6.enter_context(SB("corrBs", [64, 4, 64], BF16))
                    lamb = es6.enter_context(SB("lamb", [128, 4, 64], F32))
                    lp = es6.enter_context(SB("lp", [128, 2, 64], F32))
                    lv = es6.enter_context(SB("lv", [128, 4], F32))
                    gsc = es6.enter_context(SB("gsc", [128, 1], F32))
                    rs = es6.enter_context(SB("rs", [128, 2, 512], F32))
                    t0b = es6.enter_context(SB("t0", [128, 2, 512], F32))
                    t1b = es6.enter_context(SB("t1", [128, 2, 512], F32))
                    sqb = es6.enter_context(SB("sqb", [128, 512], BF16))
                    ph = Phase(g)
                    ph.dma("pool", corrB[:, :, :, :], c_corrB[:, :, :, :], w=["corr"], key="corrB")
                    ph.dma("pool", corrBs[:, :, :], c_corrBs[:, :, :], w=["corr"], key="corrBs")
                    ph.dma("sp", lamb[:, :, :], W["b_lambda"][l, :, :].rearrange("(o a) n -> o a n", o=1).to_broadcast([128, 4, 64]), w=["lamb"], key="lamb")
                    ph.dma("sp", gsc[:, :], W["b_sub_norm"][l, :].rearrange("(p o) -> p o", o=1), w=["gsc"], key="gsub")
                    ph.op("dve", lambda e: e.tensor_scalar(out=gsc[:, :], in0=gsc[:, :], scalar1=lamc[:, 2 * l + 1:2 * l + 2], scalar2=None, op0=ALU.mult), r=["gsc", "lamc"], w=["gsc"])
                    for k in range(2):
                        ph.op("dve", lambda e, k=k: e.tensor_tensor(out=lp[:, k, :], in0=lamb[:, 2 * k, :], in1=lamb[:, 2 * k + 1, :], op=ALU.mult), r=["lamb"], w=["lp"])
                    ph.op("dve", lambda e: e.tensor_reduce(out=lv[:, 0:2], in_=lp[:, :, :], axis=AX.X, op=ALU.add), r=["lp"], w=["lv"])
                    ph.op("act", lambda e: e.activation(out=lv[:, 0:2], in_=lv[:, 0:2], func=AF.Exp), r=["lv"], w=["lv"])
                    ph.op("dve", lambda e: e.tensor_tensor(out=lv[:, 2:3], in0=lv[:, 1:2], in1=lv[:, 0:1], op=ALU.subtract), r=["lv"], w=["lv2"])
                    ph.op("dve", lambda e: e.tensor_tensor(out=lv[:, 3:4], in0=lv[:, 2:3], in1=lamc[:, 2 * l:2 * l + 1], op=ALU.subtract), r=["lv2", "lamc"], w=["lv3"])
                    cnt = [0]
                    acc = 0
                    pendingB = []
                    gcount = [0]

                    def finalB(gb, ncols, q0, h):
                        t0, t1 = t0b[:, gb, :], t1b[:, gb, :]
                        ph.op("pool", lambda e: e.tensor_tensor(out=t0[:, 0:ncols], in0=t0[:, 0:ncols], in1=t1[:, 0:ncols], op=ALU.add), r=[("t0", gb), ("t1", gb)], w=[("t0", gb)])
                        ph.op("act", lambda e: e.activation(out=sqb[:, 0:ncols], in_=t0[:, 0:ncols], func=AF.Square), r=[("t0", gb)], w=["sqb"])
                        ph.op("pe", lambda e: e.matmul(pss[:, 0, 0:ncols], lhsT=ones[:, :], rhs=sqb[:, 0:ncols], start=True, stop=True), r=["sqb", "ones"], w=["pss"])
                        ph.op("act", lambda e: e.activation(out=t1[:, 0:ncols], in_=pss[:, 0, 0:ncols], func=AF.Sqrt, scale=1.0 / 128, bias=EPS), w=["pss", ("t1", gb)])
                        ph.op("dve", lambda e: e.reciprocal(out=t1[:, 0:ncols], in_=t1[:, 0:ncols]), r=[("t1", gb)], w=[("t1", gb)])
                        ph.op("dve", lambda e: e.scalar_tensor_tensor(out=OT[:, 2 + h, q0:q0 + ncols], in0=t0[:, 0:ncols], scalar=gsc[:, 0:1], in1=t1[:, 0:ncols], op0=ALU.mult, op1=ALU.mult),
                              r=[("t0", gb), ("t1", gb), "gsc"], w=[("OT", 2 + h)])

                    def loadB(h):
                        sl = h % 2
                        for jj in range(2):
                            ph.dma("sp", kt2[0:68, 2 * sl + jj, :, :], rec_ap(kdst, O_KB + (h * 2 + jj) * 128, [[8 * 128, 68], [RBE, NKB], [1, 128]]), w=[("Bk", sl)], key=("Kk", jj, sl))
                            ph.dma("sp", qt2[:, 2 * sl + jj, :].rearrange("p (b t) -> p b t", b=NB), qb_d[:, :, 2 * h + jj, :].rearrange("b d t -> d b t"), w=[("qt", sl)], key=("Kq", jj, sl))
                        ph.dma("sp", vt2[:, sl, :, :], rec_ap(kdst, O_VB + h * 128, [[512, 128], [RBE, NKB], [1, 128]]), w=[("Bv", sl)], key=("Kv", sl))

                    loadB(0)
                    for h in range(4):
                        sl = h % 2
                        kt = kt2[:, 2 * sl:2 * sl + 2, :, :]
                        vt = vt2[:, sl, :, :]
                        qt = qt2[:, 2 * sl:2 * sl + 2, :]
                        if h + 1 < 4:
                            loadB(h + 1)
                        for jj in range(2):
                            ph.dma("sp", kts[0:68, jj, :, :], rec_ap(srec, O_KB + (h * 2 + jj) * 128, [[8 * 128, 68], [RBE, NSB], [1, 128]]), w=["Bks"], key=("Kks", jj))
                        ph.dma("sp", vts[:, :, :], rec_ap(srec, O_VB + h * 128, [[512, 128], [RBE, NSB], [1, 128]]), w=["Bvs"], key="Kvs")
                        for qg in qgroups:
                            b0 = qg[0]
                            if b0 < NBLK:
                                ncols = len(qg) * 128
                                q0 = b0 * 128
                            else:
                                ncols = 64
                                q0 = SC
                            gb = gcount[0] % 2
                            gcount[0] += 1
                            t0, t1 = t0b[:, gb, :], t1b[:, gb, :]
                            for j in range(2):
                                tiles = []
                                if b0 < NBLK:
                                    for kb in range(2 * qg[-1] + 2):
                                        imin = max(b0, kb // 2)
                                        c_lo = (imin - b0) * 128
                                        corrs = []
                                        if kb // 2 >= b0:
                                            lo = (kb // 2 - b0) * 128
                                            corrs = [(lo, lo + 128, corrB[:, kb % 2, h, :])]
                                        tiles.append((kt[0:68, j, kb, :], vt[:, kb, :], 128, c_lo, corrs))
                                    kres, vres = [("Bk", sl)], [("Bv", sl)]
                                else:
                                    for cbi in range(NCB):
                                        tiles.append((kts[0:68, j, cbi, :], vts[:, cbi, :], 128, 0, []))
                                    tiles.append((kts[0:68, j, NCB, 0:64], vts[0:64, NCB, :], 64, 0, [(0, 64, corrBs[:, h, :])]))
                                    kres, vres = ["Bks"], ["Bvs"]
                                a = acc % 2
                                acc += 1
                                attn_T(ph, sp3, pTb, opp, a, qt[0:68, j, q0:q0 + ncols], ncols, tiles, 128, cnt, kres, vres, qres=("qt", sl))
                                ph.op("dve", lambda e, a=a, j=j, ncols=ncols: e.reciprocal(out=rs[:, j, 0:ncols], in_=opp[:, 2 * a + 1, 0:ncols]), w=[("op", 2 * a + 1), ("rs", j)])
                                if j == 0:
                                    ph.op("dve", lambda e, a=a, ncols=ncols, t0=t0: e.tensor_tensor(out=t0[:, 0:ncols], in0=opp[:, 2 * a, 0:ncols], in1=rs[:, 0, 0:ncols], op=ALU.mult), r=[("rs", 0)], w=[("op", 2 * a), ("t0", gb)])
                                    if pendingB:
                                        finalB(*pendingB.pop(0))
                                else:
                                    ph.op("dve", lambda e, ncols=ncols: e.tensor_scalar(out=rs[:, 1, 0:ncols], in0=rs[:, 1, 0:ncols], scalar1=lv[:, 3:4], scalar2=None, op0=ALU.mult), r=[("rs", 1), "lv3"], w=[("rs", 1)])
                                    ph.op("dve", lambda e, a=a, ncols=ncols, t1=t1: e.tensor_tensor(out=t1[:, 0:ncols], in0=opp[:, 2 * a, 0:ncols], in1=rs[:, 1, 0:ncols], op=ALU.mult), r=[("rs", 1)], w=[("op", 2 * a), ("t1", gb)])
                            pendingB.append((gb, ncols, q0, h))
                    while pendingB:
                        finalB(*pendingB.pop(0))
                    ph.run()
                if STAGE < 6:
                    return
                wout = es4.enter_context(SB("wout", [128, 8, D], BF16))
                with ExitStack() as es7:
                    kt = es7.enter_context(SB("kt", [96, NKB, 128], BF16))
                    vtp = es7.enter_context(SB("vtp", [128, 2, NKB, 128], BF16))
                    kts = es7.enter_context(SB("kts", [96, NSB, 128], BF16))
                    vtsp = es7.enter_context(SB("vtsp", [128, 2, NSB, 128], BF16))
                    qt = es7.enter_context(SB("qt", [96, NB * 128], BF16))
                    pTb = es7.enter_context(SB("pTb", [128, 3, 512], BF16))
                    maskC = es7.enter_context(SB("maskC", [128, 2, 128], BF16))
                    rs = es7.enter_context(SB("rs", [128, 2, 512], F32))
                    ph = Phase(g)
                    ph.dma("pool", maskC[:, :, :], c_maskC[:, :, :], w=["corr"], key="maskC")
                    for c4 in range(4):
                        ph.dma("pool", wout[:, 2 * c4:2 * c4 + 2, :], W["w_out"][l, c4 * 256:(c4 + 1) * 256, :].rearrange("(c p) n -> p c n", p=128), w=["wout"], key=("wout", c4))
                    ph.op("pool", lambda e: e.memset(vtp[:, :, :, :], 0.0), w=["Cv"])
                    ph.op("pool", lambda e: e.memset(vtsp[:, :, :, :], 0.0), w=["Cvs"])
                    cnt = [0]
                    acc = 0
                    for h in range(4):
                        par = h % 2
                        ph.dma("sp", kt[0:96, :, :], rec_ap(kdst, O_KC + h * 128, [[4 * 128, 96], [RBE, NKB], [1, 128]]), w=["Ck"], key=("Kk", 0))
                        ph.dma("sp", kts[0:96, :, :], rec_ap(srec, O_KC + h * 128, [[4 * 128, 96], [RBE, NSB], [1, 128]]), w=["Cks"], key=("Kks", 0))
                        ph.dma("sp", qt[:, :].rearrange("p (b t) -> p b t", b=NB), qc_d[:, :, h, :].rearrange("b d t -> d b t"), w=["qt"], key=("Kq", 0))
                        ph.dma("act", vtp[:, par, :, par * 64:par * 64 + 64], rec_ap(kdst, O_VC + h * 64, [[256, 128], [RBE, NKB], [1, 64]]), w=["Cv"], key="Kv")
                        ph.dma("act", vtsp[:, par, :, par * 64:par * 64 + 64], rec_ap(srec, O_VC + h * 64, [[256, 128], [RBE, NSB], [1, 64]]), w=["Cvs"], key="Kvs")
                        for qg in qgroups:
                            b0 = qg[0]
                            tiles = []
                            if b0 < NBLK:
                                ncols = len(qg) * 128
                                q0 = b0 * 128
                                for kb in range(2 * qg[-1] + 2):
                                    imin = max(b0, kb // 2)
                                    c_lo = (imin - b0) * 128
                                    corrs = []
                                    if kb // 2 >= b0:
                                        lo = (kb // 2 - b0) * 128
                                        corrs = [(lo, lo + 128, maskC[:, kb % 2, :])]
                                    tiles.append((kt[0:96, kb, :], vtp[:, par, kb, :], 128, c_lo, corrs))
                                kres, vres = ["Ck"], ["Cv"]
                            else:
                                ncols = 64
                                q0 = SC
                                for cbi in range(NCB):
                                    tiles.append((kts[0:96, cbi, :], vtsp[:, par, cbi, :], 128, 0, []))
                                tiles.append((kts[0:96, NCB, 0:64], vtsp[0:64, par, NCB, :], 64, 0, []))
                                kres, vres = ["Cks"], ["Cvs"]
                            a = acc % 2
                            acc += 1
                            attn_T(ph, sp3, pTb, opp, a, qt[0:96, q0:q0 + ncols], ncols, tiles, 128, cnt, kres, vres)
                            ph.op("dve", lambda e, a=a, ncols=ncols: e.reciprocal(out=rs[:, a, 0:ncols], in_=opp[:, 2 * a + 1, 0:ncols]), w=[("op", 2 * a + 1), ("rs", a)])
                            ph.op("dve", lambda e, a=a, ncols=ncols, q0=q0, h=h, par=par: e.tensor_tensor(out=OT[par * 64:par * 64 + 64, 6 + h // 2, q0:q0 + ncols], in0=opp[par * 64:par * 64 + 64, 2 * a, 0:ncols],
                                                                                                          in1=rs[par * 64:par * 64 + 64, a, 0:ncols], op=ALU.mult),
                                  r=[("rs", a)], w=[("op", 2 * a), ("OT", 6 + h // 2)])
                    ph.run()
                if STAGE < 7:
                    return
                ph = Phase(g)
                for gi, (t0_, n) in enumerate(groups):
                    for dc in range(8):
                        ob = dc % 4
                        for c in range(8):
                            ph.op("pe", lambda e, c=c, dc=dc, ob=ob, n=n, t0_=t0_: e.matmul(opp[:, ob, 0:n], lhsT=wout[:, c, dc * 128:(dc + 1) * 128], rhs=OT[:, c, t0_:t0_ + n], start=(c == 0), stop=(c == 7)),
                                  r=["wout"], w=[("op", ob)])
                        ph.op("dve", lambda e, dc=dc, ob=ob, t0_=t0_, n=n: e.tensor_tensor(out=xT[:, dc, t0_:t0_ + n], in0=opp[:, ob, 0:n], in1=xT[:, dc, t0_:t0_ + n], op=ALU.add),
                              r=[("xT", dc)], w=[("op", ob), ("xT", dc)])
                ph.run()

        def mem_attention(l):
            with ExitStack() as es8:
                hT = es8.enter_context(SB("hT", [128, 8, 512], BF16))
                xsq = es8.enter_context(SB("xsq", [128, 1, 8, 512], BF16))
                rinv = es8.enter_context(SB("rinv", [128, 1, 512], F32))
                wq = es8.enter_context(SB("wq", [128, 8, 512], BF16))
                wk = es8.enter_context(SB("wk", [128, 8, 512], BF16))
                wv = es8.enter_context(SB("wv", [128, 8, 512], BF16))
                wo = es8.enter_context(SB("wo", [128, 4, D], BF16))
                gm = es8.enter_context(SB("gm", [128, 1, D], F32))
                gkn = es8.enter_context(SB("gkn", [128, 4, 128], F32))
                mtok = es8.enter_context(SB("mtok", [128, 2, D], F32))
                mnb = es8.enter_context(SB("mnb", [128, 2, D], BF16))
                mnT = es8.enter_context(SB("mnT", [128, 8, 256], BF16))
                kf = es8.enter_context(SB("kf", [128, 2, 512], F32))
                vf = es8.enter_context(SB("vf", [128, 2, 512], F32))
                kb16 = es8.enter_context(SB("kb16", [128, 512], BF16))
                mK = es8.enter_context(SB("mK", [128, 2, 4, 256], BF16))
                mV = es8.enter_context(SB("mV", [128, 2, 2, 4, 129], BF16))
                sq = es8.enter_context(SB("sq", [128, 1024], F32))
                ss = es8.enter_context(SB("ss", [128, 8], F32))
                qTn = es8.enter_context(SB("qTn", [128, 4, 512], BF16))
                pTb = es8.enter_context(SB("pTb", [128, 2, 512], BF16))
                omT = es8.enter_context(SB("omT", [128, 4, 512], BF16))
                sqh = es8.enter_context(SB("sqh", [128, 512], BF16))
                rn = es8.enter_context(SB("rn", [128, 512], F32))
                rsm = es8.enter_context(SB("rsm", [128, 512], F32))
                gqc = es8.enter_context(SB("gqc", [128, 1], F32))
                pn = es8.enter_context(PS("pn", [128, 2, 512], F32))
                pq = es8.enter_context(PS("pq", [128, 2, 512], F32))
                sp = es8.enter_context(PS("sp", [128, 2, 512], F32))
                opp = es8.enter_context(PS("opp", [128, 2, 512], F32))
                ph = Phase(g)
                for nm, t in [("mem_w_q", wq), ("mem_w_k", wk), ("mem_w_v", wv)]:
                    for c4 in range(2):
                        ph.dma("pool", t[:, 4 * c4:4 * c4 + 4, :], W[nm][l, c4 * 512:(c4 + 1) * 512, :].rearrange("(c p) n -> p c n", p=128), w=[nm], key=(nm, c4))
                ph.dma("pool", wo[:, :, :], W["mem_w_o"][l, :, :].rearrange("(c p) n -> p c n", p=128), w=["wo"], key="wo")
                bcast_row(ph, gm[:, :, :], W["mem_norm_m"][l, :], D, 1, "gm", "gm")
                ph.dma("sp", gqc[:, :], W["mem_q_norm"][l, :].rearrange("(p o) -> p o", o=1), w=["gqc"], key="gqn")
                bcast_row(ph, gkn[:, :, :], W["mem_k_norm"][l, :], 128, 4, "gkn", "gkn")
                ph.op("dve", lambda e: e.tensor_scalar(out=gqc[:, :], in0=gqc[:, :], scalar1=128.0 ** -0.5, scalar2=None, op0=ALU.mult), r=["gqc"], w=["gqc"])
                cnt = [0]
                ph.op("pool", lambda e: e.memset(mV[:, :, :, :, 128:129], 1.0), w=["mV"])
                spb = sp[:, :, :].bitcast(BF16)
                for mb in range(2):
                    ph.dma("sp", mtok[:, mb, :], mem[mb * 128:(mb + 1) * 128, :], w=[("mtok", mb)], key=("mtok", mb))
                    rms_rows(ph, mtok[:, mb, :].rearrange("p (h d) -> p h d", h=1), 128, 1, D, gm[:, :, :], mtok[:, mb, :].rearrange("p (h d) -> p h d", h=1), sq, ss, "m", [("mtok", mb)], [("mtok", mb)], ["gm"])
                    ph.op("act", lambda e, mb=mb: e.copy(out=mnb[:, mb, :], in_=mtok[:, mb, :]), r=[("mtok", mb)], w=[("mnb", mb)])
                    for c in range(8):
                        b = c % 2
                        ph.op("pe", lambda e, c=c, b=b, mb=mb: e.transpose(out=spb[:, b, 0:128], in_=mnb[:, mb, c * 128:(c + 1) * 128], identity=idb[:, :]), r=[("mnb", mb), "idb"], w=[("sp", b)])
                        ph.op("dve", lambda e, c=c, b=b, mb=mb: e.tensor_copy(out=mnT[:, c, mb * 128:(mb + 1) * 128], in_=spb[:, b, 0:128]), w=[("sp", b), "mnT"])
                    for c in range(8):
                        ph.op("pe", lambda e, c=c, mb=mb: e.matmul(pq[:, 0, :], lhsT=mnT[:, c, mb * 128:(mb + 1) * 128], rhs=wk[:, c, :], start=(c == 0), stop=(c == 7)), r=["mnT", "mem_w_k"], w=[("pq", 0)])
                    for c in range(8):
                        ph.op("pe", lambda e, c=c, mb=mb: e.matmul(pq[:, 1, :], lhsT=mnT[:, c, mb * 128:(mb + 1) * 128], rhs=wv[:, c, :], start=(c == 0), stop=(c == 7)), r=["mnT", "mem_w_v"], w=[("pq", 1)])
                    rms_rows(ph, pq[:, 0, :].rearrange("p (h d) -> p h d", h=4), 128, 4, 128, gkn[:, :, :], kf[:, mb, :].rearrange("p (h d) -> p h d", h=4), sq, ss, "mk", [("pq", 0)], [("kf", mb)], ["gkn"])
                    ph.op("act", lambda e, mb=mb: e.copy(out=vf[:, mb, :], in_=pq[:, 1, :]), w=[("pq", 1), ("vf", mb)])
                    ph.dma("sp", o_mk[l, mb * 128:(mb + 1) * 128, :], kf[:, mb, :], r=[("kf", mb)], key=("o_mk", mb))
                    ph.dma("sp", o_mv[l, mb * 128:(mb + 1) * 128, :], vf[:, mb, :], r=[("vf", mb)], key=("o_mv", mb))
                for st in range(2):
                    for mb in range(2):
                        if st == 1:
                            ph.dma("sp", kf[:, mb, :], cm_k[l, mb * 128:(mb + 1) * 128, :], w=[("kf", mb)], key=("cmk", mb))
                            ph.dma("sp", vf[:, mb, :], cm_v[l, mb * 128:(mb + 1) * 128, :], w=[("vf", mb)], key=("cmv", mb))
                        ph.op("act", lambda e, mb=mb: e.copy(out=kb16[:, :], in_=kf[:, mb, :]), r=[("kf", mb)], w=["kb16"])
                        for h in range(4):
                            b = h % 2
                            ph.op("pe", lambda e, h=h, b=b: e.transpose(out=spb[:, b, 0:128], in_=kb16[:, h * 128:(h + 1) * 128], identity=idb[:, :]), r=["kb16", "idb"], w=[("sp", b)])
                            ph.op("dve", lambda e, h=h, b=b, st=st, mb=mb: e.tensor_copy(out=mK[:, st, h, mb * 128:(mb + 1) * 128], in_=spb[:, b, 0:128]), w=[("sp", b), "mK"])
                        ph.op("act", lambda e, st=st, mb=mb: e.copy(out=mV[:, st, mb, :, 0:128], in_=vf[:, mb, :].rearrange("p (h d) -> p h d", h=4)), r=[("vf", mb)], w=["mV"])
                for gi, (t0_, n) in enumerate(groups):
                    st = 0 if t0_ < SC else 1
                    norm_to_hT(ph, hT, xsq, rinv, pn, l * 4 + 2, only=gi)
                    for h in range(4):
                        pb = h % 2
                        for c in range(8):
                            ph.op("pe", lambda e, c=c, pb=pb, h=h, n=n: e.matmul(pq[:, pb, 0:n], lhsT=wq[:, c, h * 128:(h + 1) * 128], rhs=hT[:, c, 0:n], start=(c == 0), stop=(c == 7)),
                                  r=[("hT", c), "mem_w_q"], w=[("pq", pb)])
                        ph.op("act", lambda e, pb=pb, n=n: e.activation(out=sqh[:, 0:n], in_=pq[:, pb, 0:n], func=AF.Square), w=[("pq", pb), "sqh"])
                        ph.op("pe", lambda e, n=n: e.matmul(pn[:, 0, 0:n], lhsT=ones[:, :], rhs=sqh[:, 0:n], start=True, stop=True), r=["sqh", "ones"], w=[("pn", 0)])
                        ph.op("act", lambda e, n=n: e.activation(out=rn[:, 0:n], in_=pn[:, 0, 0:n], func=AF.Sqrt, scale=1.0 / 128, bias=EPS), w=[("pn", 0), "rn"])
                        ph.op("dve", lambda e, n=n: e.reciprocal(out=rn[:, 0:n], in_=rn[:, 0:n]), r=["rn"], w=["rn"])
                        ph.op("dve", lambda e, pb=pb, h=h, n=n: e.scalar_tensor_tensor(out=qTn[:, h, 0:n], in0=pq[:, pb, 0:n], scalar=gqc[:, 0:1], in1=rn[:, 0:n], op0=ALU.mult, op1=ALU.mult),
                              r=["rn", "gqc"], w=[("pq", pb), "qt"])
                    for h in range(4):
                        tiles = [(mK[:, st, h, mb * 128:(mb + 1) * 128], mV[:, st, mb, h, 0:128], 128, 0, []) for mb in range(2)]
                        attn_T(ph, sp, pTb, opp, 0, qTn[:, h, 0:n], n, tiles, 128, cnt, ["mK"], ["mV"])
                        ph.op("dve", lambda e, n=n: e.reciprocal(out=rsm[:, 0:n], in_=opp[:, 1, 0:n]), w=[("op", 1), "rsm"])
                        ph.op("dve", lambda e, h=h, n=n: e.tensor_tensor(out=omT[:, h, 0:n], in0=opp[:, 0, 0:n], in1=rsm[:, 0:n], op=ALU.mult), r=["rsm"], w=[("op", 0), "omT"])
                    for dc in range(8):
                        pb2 = dc % 2
                        for c in range(4):
                            ph.op("pe", lambda e, c=c, dc=dc, pb2=pb2, n=n: e.matmul(pn[:, pb2, 0:n], lhsT=wo[:, c, dc * 128:(dc + 1) * 128], rhs=omT[:, c, 0:n], start=(c == 0), stop=(c == 3)), r=["omT", "wo"], w=[("pn", pb2)])
                        ph.op("dve", lambda e, dc=dc, pb2=pb2, t0_=t0_, n=n: e.tensor_tensor(out=xT[:, dc, t0_:t0_ + n], in0=pn[:, pb2, 0:n], in1=xT[:, dc, t0_:t0_ + n], op=ALU.add), r=[("xT", dc)], w=[("pn", pb2), ("xT", dc)])
                ph.run()

        def rms_rows_multi(ph, items):
            for (src, nb, H, dh, gain, out, sqv, ssv, tag, src_res, out_res, gain_res) in items:
                ph.op("act", lambda e, src=src, sqv=sqv, nb=nb, H=H, dh=dh: e.activation(out=sqv[:nb, 0:H * dh].rearrange("p (h d) -> p h d", h=H), in_=src, func=AF.Square), r=[], w=list(src_res) + [("sq", tag)])
            for (src, nb, H, dh, gain, out, sqv, ssv, tag, src_res, out_res, gain_res) in items:
                ph.op("dve", lambda e, sqv=sqv, ssv=ssv, nb=nb, H=H, dh=dh: e.tensor_reduce(out=ssv[:nb, 0:H], in_=sqv[:nb, 0:H * dh].rearrange("p (h d) -> p h d", h=H), axis=AX.X, op=ALU.add), r=[("sq", tag)], w=[("ss", tag)])
            for (src, nb, H, dh, gain, out, sqv, ssv, tag, src_res, out_res, gain_res) in items:
                ph.op("act", lambda e, ssv=ssv, nb=nb, H=H, dh=dh: e.activation(out=ssv[:nb, 0:H], in_=ssv[:nb, 0:H], func=AF.Sqrt, scale=1.0 / dh, bias=EPS), r=[("ss", tag)], w=[("ss", tag)])
            for (src, nb, H, dh, gain, out, sqv, ssv, tag, src_res, out_res, gain_res) in items:
                ph.op("dve", lambda e, ssv=ssv, nb=nb, H=H: e.reciprocal(out=ssv[:nb, 0:H], in_=ssv[:nb, 0:H]), r=[("ss", tag)], w=[("ss", tag)])
            for (src, nb, H, dh, gain, out, sqv, ssv, tag, src_res, out_res, gain_res) in items:
                ph.op("dve", lambda e, src=src, out=out, ssv=ssv, nb=nb, H=H, dh=dh: e.tensor_tensor(out=out, in0=src, in1=ssv[:nb, 0:H].unsqueeze(2).to_broadcast([nb, H, dh]), op=ALU.mult), r=[("ss", tag)], w=list(src_res) + list(out_res))
            for (src, nb, H, dh, gain, out, sqv, ssv, tag, src_res, out_res, gain_res) in items:
                ph.op("dve", lambda e, out=out, gain=gain: e.tensor_tensor(out=out, in0=out, in1=gain, op=ALU.mult), r=list(out_res) + list(gain_res), w=list(out_res))

        for l in range(DEPTH):
            es_win = ExitStack()
            win = es_win.enter_context(SB("win", [128, 8, IN_COLS], BF16))

            def pf_win(ph, c4, l=l, win=win):
                ph.dma("pool", win[:, 2 * c4:2 * c4 + 2, :], W["w_in"][l, c4 * 256:(c4 + 1) * 256, :].rearrange("(c p) n -> p c n", p=128), w=["win"], key=("win", c4))
            ffn(l, 1, prefetch=pf_win)
            if STAGE < 2:
                continue

            with ExitStack() as es9:
                hT = es9.enter_context(SB("hT", [128, 8, 512], BF16))
                wuq = es9.enter_context(SB("wuq", [128, 3, 384], BF16))
                wukv = es9.enter_context(SB("wukv", [128, 2, 512], BF16))
                gall = es9.enter_context(SB("gall", [128, 28, 64], F32))
                gcq = es9.enter_context(SB("gcq", [128, 1, 384], F32))
                gckv = es9.enter_context(SB("gckv", [128, 1, 256], F32))
                gq96 = es9.enter_context(SB("gq96", [128, 4, 96], F32))
                gk96 = es9.enter_context(SB("gk96", [128, 4, 96], F32))
                Nf = es9.enter_context(SB("Nf", [128, 1, IN_COLS], F32))
                tokb2 = es9.enter_context(SB("tokb2", [128, 2464], BF16))
                stg2 = es9.enter_context(SB("stg2", [128, 2048], BF16))
                latT2 = es9.enter_context(SB("latT2", [128, 2, 128], BF16))
                kcf2 = es9.enter_context(SB("kcf2", [128, 4, 96], F32))
                sq2 = es9.enter_context(SB("sq2", [128, 384], F32))
                ss2 = es9.enter_context(SB("ss2", [128, 8], F32))
                ak2 = es9.enter_context(SB("ak2", [128, 32], F32))
                kp2 = es9.enter_context(SB("kp2", [128, 32], F32))
                sq = es9.enter_context(SB("sq", [128, 2560], F32))
                ss = es9.enter_context(SB("ss", [128, 32], F32))
                cs = es9.enter_context(SB("cs", [128, 32], F32))
                aq = es9.enter_context(SB("aq", [128, 32], F32))
                ak = es9.enter_context(SB("ak", [128, 32], F32))
                tokb = es9.enter_context(SB("tokb", [128, 3360], BF16))
                latT = es9.enter_context(SB("latT", [128, 5, 128], BF16))
                qcf = es9.enter_context(SB("qcf", [128, 4, 96], F32))
                kcf = es9.enter_context(SB("kcf", [128, 4, 96], F32))
                rt = es9.enter_context(SB("rt", [128, 4, 16], F32))
                rt2 = es9.enter_context(SB("rt2", [128, 4, 16], F32))
                stg = es9.enter_context(SB("stg", [128, 2, 2816], BF16))
                xsq = es9.enter_context(SB("xsq", [128, 1, 8, 512], BF16))
                rinv = es9.enter_context(SB("rinv", [128, 1, 512], F32))
                pu = es9.enter_context(PS("pu", [128, 6, 512], F32))
                pn = es9.enter_context(PS("pn", [128, 2, 512], F32))
                ptr = pu[:, :, :].bitcast(BF16)
                ph = Phase(g)
                ph.dma("pool", wuq[:, :, :], W["c_w_uq"][l, :, :].rearrange("(c p) n -> p c n", p=128), w=["wuq"], key="wuq")
                ph.dma("pool", wukv[:, :, :], W["c_w_ukv"][l, :, :].rearrange("(c p) n -> p c n", p=128), w=["wukv"], key="wukv")
                ph.op("pool", lambda e: e.memset(gall[:, 8:12, :], 1.0), w=["gall"])
                bcast_row(ph, gall[:, 0:4, :], W["a_q_norm"][l, :], 64, 4, "g0", "gall")
                bcast_row(ph, gall[:, 4:8, :], W["a_k_norm"][l, :], 64, 4, "g1", "gall")
                bcast_row(ph, gall[:, 12:20, :], W["b_q_norm"][l, :], 64, 8, "g2", "gall")
                bcast_row(ph, gall[:, 20:28, :], W["b_k_norm"][l, :], 64, 8, "g3", "gall")
                bcast_row(ph, gcq[:, :, :], W["c_q_lat_norm"][l, :], 384, 1, "g4", "gcq")
                bcast_row(ph, gckv[:, :, :], W["c_kv_lat_norm"][l, :], 256, 1, "g5", "gckv")
                bcast_row(ph, gq96[:, :, :], W["c_q_norm"][l, :], 96, 4, "g6", "gq96")
                bcast_row(ph, gk96[:, :, :], W["c_k_norm"][l, :], 96, 4, "g7", "gk96")
                ph.op("dve", lambda e: e.tensor_scalar(out=gall[:, 0:4, :], in0=gall[:, 0:4, :], scalar1=0.125, scalar2=None, op0=ALU.mult), r=["gall"], w=["gall"])
                ph.op("dve", lambda e: e.tensor_scalar(out=gall[:, 12:20, :], in0=gall[:, 12:20, :], scalar1=0.125, scalar2=None, op0=ALU.mult), r=["gall"], w=["gall"])
                ph.op("dve", lambda e: e.tensor_scalar(out=gq96[:, :, :], in0=gq96[:, :, :], scalar1=96.0 ** -0.5, scalar2=None, op0=ALU.mult), r=["gq96"], w=["gq96"])

                def kv_tail(ph, S, si, nb, rec_t, rec_base, bs, rr=None):
                    P_ = bs["pfx"]
                    tokb_, stg_, latT_, kcf_, sq_, ss_, ak_ = bs["tokb"], bs["stg"], bs["latT"], bs["kcf"], bs["sq"], bs["ss"], bs["ak"]
                    pA, rA = bs["pA"]
                    (pB0, rB0), (pB1, rB1) = bs["pB"]
                    pL, rL = bs["pL"]
                    pKV, rKV = bs["pKV"]
                    pC, rC = bs["pC"]
                    kpsrc = bs["kp"]
                    ks = bs["ksfx"]
                    if S is not None:
                        srcr = [("src", id(S), si)]
                        ph.op("act", lambda e: e.copy(out=tokb_[:nb, 0:512], in_=S[:nb, si, 256:768]), r=srcr, w=[P_ + "tokb_a"])
                        ph.op("act", lambda e: e.copy(out=tokb_[:nb, 512:512 + 544].rearrange("p (a d) -> p a d", a=8)[:, :, 0:64], in_=S[:nb, si, 1280:1792].rearrange("p (a d) -> p a d", a=8)), r=srcr, w=[P_ + "tokb_b"])
                        ph.op("act", lambda e: e.copy(out=tokb_[:nb, 1056:1568], in_=S[:nb, si, 1792:2304]), r=srcr, w=[P_ + "tokb_bv"])
                        ph.op("act", lambda e: e.copy(out=tokb_[:nb, 1568:1824], in_=S[:nb, si, 2688:2944]), r=srcr, w=[P_ + "tokb_c"])
                    for h in range(4):
                        ph.op("pe", lambda e, h=h: e.transpose(out=pA[0:64, h * 128:h * 128 + nb], in_=tokb_[:nb, h * 64:(h + 1) * 64], identity=idb[:nb, :nb]),
                              r=[P_ + "tokb_a", "idb"], w=[rA])
                    ph.op("dve", lambda e: e.tensor_copy(out=stg_[0:64, 0:512].rearrange("p (h t) -> p h t", h=4)[:, :, 0:nb], in_=pA[0:64, 0:512].rearrange("p (h t) -> p h t", h=4)[:, :, 0:nb]),
                          w=[rA, P_ + "stg_ka"])
                    ph.dma("sp", rec_ap(rec_t, rec_base + O_KA, [[512, 64], [128, 4], [1, nb]]), stg_[0:64, 0:512].rearrange("p (h t) -> p h t", h=4)[:, :, 0:nb], r=[P_ + "stg_ka"], w=([(rr, 0)] if rr is not None else []), key="st_ka" + ks)
                    ph.dma("act", rec_ap(rec_t, rec_base + O_VA, [[256, nb], [1, 256]]), tokb_[:nb, 256:512], r=[P_ + "tokb_a"], w=([(rr, 1)] if rr is not None else []), key="st_va" + ks)
                    ph.op("dve", lambda e: e.tensor_copy(out=tokb_[:nb, 512:512 + 544].rearrange("p (a d) -> p a d", a=8)[:, :, 64:68], in_=ak_[:nb, :].rearrange("p (a d) -> p a d", a=8)), r=[P_ + "ak"], w=[P_ + "tokb_b"])
                    for hj in range(8):
                        pBx, rBx = (pB0, rB0) if hj < 4 else (pB1, rB1)
                        col = (hj % 4) * 128
                        ph.op("pe", lambda e, hj=hj, pBx=pBx, col=col: e.transpose(out=pBx[0:68, col:col + nb], in_=tokb_[:nb, 512 + hj * 68:512 + (hj + 1) * 68], identity=idb[:nb, :nb]),
                              r=[P_ + "tokb_b", "idb"], w=[rBx])
                    for half, (pBx, rBx) in enumerate([(pB0, rB0), (pB1, rB1)]):
                        ph.op("dve", lambda e, half=half, pBx=pBx: e.tensor_copy(out=stg_[0:68, 512 + half * 512:1024 + half * 512].rearrange("p (h t) -> p h t", h=4)[:, :, 0:nb],
                                                                                  in_=pBx[0:68, 0:512].rearrange("p (h t) -> p h t", h=4)[:, :, 0:nb]),
                              w=[rBx, P_ + "stg_kb"])
                    ph.dma("sp", rec_ap(rec_t, rec_base + O_KB, [[1024, 68], [128, 8], [1, nb]]), stg_[0:68, 512:1536].rearrange("p (h t) -> p h t", h=8)[:, :, 0:nb], r=[P_ + "stg_kb"], w=([(rr, 2)] if rr is not None else []), key="st_kb" + ks)
                    ph.dma("act", rec_ap(rec_t, rec_base + O_VB, [[512, nb], [1, 512]]), tokb_[:nb, 1056:1568], r=[P_ + "tokb_bv"], w=([(rr, 3)] if rr is not None else []), key="st_vb" + ks)
                    for c in range(2):
                        ph.op("pe", lambda e, c=c: e.transpose(out=pL[:, c * 128:c * 128 + nb], in_=tokb_[:nb, 1568 + c * 128:1568 + (c + 1) * 128], identity=idb[:nb, :nb]),
                              r=[P_ + "tokb_c", "idb"], w=[rL])
                    ph.op("dve", lambda e: e.tensor_copy(out=latT_[:, 0:2, 0:nb], in_=pL[:, 0:256].rearrange("p (c t) -> p c t", c=2)[:, :, 0:nb]), w=[rL, P_ + "latT_kv"])
                    for c in range(2):
                        ph.op("pe", lambda e, c=c: e.matmul(pKV[:nb, 0:512], lhsT=latT_[:, c, 0:nb], rhs=wukv[:, c, :], start=(c == 0), stop=(c == 1)),
                              r=[P_ + "latT_kv", "wukv"], w=[rKV])
                    ph.op("act", lambda e: e.copy(out=kcf_[:nb, :, 0:64], in_=pKV[:nb, 0:512].rearrange("p (h d) -> p h d", h=4)[:, :, 0:64]), w=[rKV, P_ + "kcf"])
                    ph.op("pool", lambda e: e.tensor_copy(out=kcf_[:nb, :, 64:96], in_=kpsrc.unsqueeze(1).to_broadcast([nb, 4, 32])), r=bs["kpres"], w=[P_ + "kcf"])
                    ph.op("act", lambda e: e.copy(out=tokb_[:nb, 1824:2080].rearrange("p (h d) -> p h d", h=4), in_=pKV[:nb, 0:512].rearrange("p (h d) -> p h d", h=4)[:, :, 64:128]), w=[rKV, P_ + "tokb_cv"])
                    rms_rows(ph, kcf_[:nb, :, :], nb, 4, 96, gk96[:nb, :, :], kcf_[:nb, :, :], sq_, ss_, bs["sqtag"], [P_ + "kcf"], [P_ + "kcf"], ["gk96"])
                    ph.op("act", lambda e: e.copy(out=tokb_[:nb, 2080:2464].rearrange("p (h d) -> p h d", h=4), in_=kcf_[:nb, :, :]), r=[P_ + "kcf"], w=[P_ + "tokb_ck"])
                    for h in range(4):
                        ph.op("pe", lambda e, h=h: e.transpose(out=pC[0:96, h * 128:h * 128 + nb], in_=tokb_[:nb, 2080 + h * 96:2080 + (h + 1) * 96], identity=idb[:nb, :nb]),
                              r=[P_ + "tokb_ck", "idb"], w=[rC])
                    ph.op("dve", lambda e: e.tensor_copy(out=stg_[0:96, 1536:2048].rearrange("p (h t) -> p h t", h=4)[:, :, 0:nb], in_=pC[0:96, 0:512].rearrange("p (h t) -> p h t", h=4)[:, :, 0:nb]),
                          w=[rC, P_ + "stg_kc"])
                    ph.dma("sp", rec_ap(rec_t, rec_base + O_KC, [[512, 96], [128, 4], [1, nb]]), stg_[0:96, 1536:2048].rearrange("p (h t) -> p h t", h=4)[:, :, 0:nb], r=[P_ + "stg_kc"], w=([(rr, 4)] if rr is not None else []), key="st_kc" + ks)
                    ph.dma("act", rec_ap(rec_t, rec_base + O_VC, [[256, nb], [1, 256]]), tokb_[:nb, 1824:2080], r=[P_ + "tokb_cv"], w=([(rr, 5)] if rr is not None else []), key="st_vc" + ks)

                pnb = pn[:, :, :].bitcast(BF16)
                bs1 = dict(pfx="s1", tokb=tokb, stg=stg[:, 0, :], latT=latT, kcf=kcf, sq=sq[:, 2176:2560], ss=ss[:, 26:30], sqtag="x1", ak=ak, ksfx="",
                           pA=(ptr[:, 0, 0:512], ("pu", 0)), pB=((ptr[:, 1, 0:512], ("pu", 1)), (ptr[:, 2, 0:512], ("pu", 2))),
                           pL=(ptr[:, 3, 0:256], ("pu", 3)), pKV=(pu[:, 3, 0:512], ("pu", 3)), pC=(ptr[:, 4, 0:512], ("pu", 4)))
                bs2 = dict(pfx="s2", tokb=tokb2, stg=stg2, latT=latT2, kcf=kcf2, sq=sq2, ss=ss2, sqtag="s2k", ak=ak2, ksfx="2",
                           pA=(pnb[:, 0, 0:512], ("pn", 0)), pB=((pnb[:, 1, 0:512], ("pn", 1)), (pnb[:, 1, 512:1024], ("pn", 1))),
                           pL=(pnb[:, 0, 512:768], ("pn", 0)), pKV=(pn[:, 0, 0:512], ("pn", 0)), pC=(pnb[:, 1, 0:512], ("pn", 1)),
                           kp=kp2[:, :], kpres=["s2kp"])

                def proj_block(bi, hc0):
                    c0, nb = blk_cols(bi)
                    si = 0
                    nres = [("src", id(Nf), si)]
                    ph.dma("act", cs[:nb, :], c_cs[c0:c0 + nb, :], w=["cs"], key="cs")
                    ph.dma("act", aq[:nb, :], c_augq[c0:c0 + nb, :], w=["aq"], key="aq")
                    ph.dma("act", ak[:nb, :], c_augk[c0:c0 + nb, :], w=["s1ak"], key="ak")
                    for c in range(8):
                        for cg in range(6):
                            w0 = cg * 512
                            wn = min(512, IN_COLS - w0)
                            ph.op("pe", lambda e, c=c, cg=cg, w0=w0, wn=wn: e.matmul(pu[:nb, cg, 0:wn], lhsT=hT[:, c, hc0:hc0 + nb], rhs=win[:, c, w0:w0 + wn], start=(c == 0), stop=(c == 7)),
                                  r=[("hT", c), "win"], w=[("pu", cg)])
                    pur = [("pu", k) for k in range(6)]
                    puf = pu[:nb, :, :].rearrange("p a b -> p (a b)")
                    ph.op("act", lambda e: e.copy(out=Nf[:nb, si, :], in_=puf[:, 0:IN_COLS]), w=pur + nres)
                    rms_rows_multi(ph, [
                        (puf[:, 0:512].rearrange("p (h d) -> p h d", h=8), nb, 8, 64, gall[:nb, 0:8, :], Nf[:nb, si, 0:512].rearrange("p (h d) -> p h d", h=8), sq[:, 0:512], ss[:, 0:8], "a", pur, nres, ["gall"]),
                        (puf[:, 768:1792].rearrange("p (h d) -> p h d", h=16), nb, 16, 64, gall[:nb, 12:28, :], Nf[:nb, si, 768:1792].rearrange("p (h d) -> p h d", h=16), sq[:, 512:1536], ss[:, 8:24], "b", pur, nres, ["gall"]),
                        (puf[:, 2304:2688].rearrange("p (h d) -> p h d", h=1), nb, 1, 384, gcq[:nb, :, :], Nf[:nb, si, 2304:2688].rearrange("p (h d) -> p h d", h=1), sq[:, 1536:1920], ss[:, 24:25], "cq", pur, nres, ["gcq"]),
                        (puf[:, 2688:2944].rearrange("p (h d) -> p h d", h=1), nb, 1, 256, gckv[:nb, :, :], Nf[:nb, si, 2688:2944].rearrange("p (h d) -> p h d", h=1), sq[:, 1920:2176], ss[:, 25:26], "ckv", pur, nres, ["gckv"]),
                    ])
                    kp = puf[:, 2944:2976]
                    ph.op("dve", lambda e: e.tensor_tensor(out=rt[:nb, 0, :], in0=kp[:, 0:16], in1=cs[:nb, 0:16], op=ALU.mult), r=["cs"], w=pur + ["rt"])
                    ph.op("dve", lambda e: e.tensor_tensor(out=rt[:nb, 1, :], in0=kp[:, 16:32], in1=cs[:nb, 16:32], op=ALU.mult), r=["cs"], w=pur + ["rt"])
                    ph.op("dve", lambda e: e.tensor_tensor(out=Nf[:nb, si, 2944:2960], in0=rt[:nb, 0, :], in1=rt[:nb, 1, :], op=ALU.subtract), r=["rt"], w=nres)
                    ph.op("dve", lambda e: e.tensor_tensor(out=rt[:nb, 2, :], in0=kp[:, 0:16], in1=cs[:nb, 16:32], op=ALU.mult), r=["cs"], w=pur + ["rt"])
                    ph.op("dve", lambda e: e.tensor_tensor(out=rt[:nb, 3, :], in0=kp[:, 16:32], in1=cs[:nb, 0:16], op=ALU.mult), r=["cs"], w=pur + ["rt"])
                    ph.op("dve", lambda e: e.tensor_tensor(out=Nf[:nb, si, 2960:2976], in0=rt[:nb, 2, :], in1=rt[:nb, 3, :], op=ALU.add), r=["rt"], w=nres)
                    ph.dma("sp", o_bk[l, c0:c0 + nb, :], Nf[:nb, si, 1280:1792], r=nres, key="o_bk")
                    ph.dma("sp", o_bv[l, c0:c0 + nb, :], Nf[:nb, si, 1792:2304], r=nres, key="o_bv")
                    ph.dma("sp", o_ckv[l, c0:c0 + nb, :], Nf[:nb, si, 2688:2944], r=nres, key="o_ckv")
                    ph.dma("sp", o_kpe[l, c0:c0 + nb, :], Nf[:nb, si, 2944:2976], r=nres, key="o_kpe")
                    if bi >= NBLK - 2:
                        r0 = (bi - (NBLK - 2)) * 128
                        ph.dma("sp", o_ak[l, r0:r0 + nb, :], Nf[:nb, si, 256:512], r=nres, key="o_ak")
                        ph.dma("sp", o_av[l, r0:r0 + nb, :], Nf[:nb, si, 512:768], r=nres, key="o_av")
                    ph.op("act", lambda e: e.copy(out=tokb[:nb, 2464:2720], in_=Nf[:nb, si, 0:256]), r=nres, w=["tokb_qa"])
                    for h in range(4):
                        ph.op("pe", lambda e, h=h: e.transpose(out=ptr[0:64, 5, h * 128:h * 128 + nb], in_=tokb[:nb, 2464 + h * 64:2464 + (h + 1) * 64], identity=idb[:nb, :nb]),
                              r=["tokb_qa", "idb"], w=[("pu", 5)])
                    ph.op("dve", lambda e: e.tensor_copy(out=stg[0:64, 1, 0:512].rearrange("p (h t) -> p h t", h=4)[:, :, 0:nb], in_=ptr[0:64, 5, 0:512].rearrange("p (h t) -> p h t", h=4)[:, :, 0:nb]),
                          w=[("pu", 5), "stg_qa"])
                    ph.dma("sp", qa_d[bi, :, :, 0:nb], stg[0:64, 1, 0:512].rearrange("p (h t) -> p h t", h=4)[:, :, 0:nb], r=["stg_qa"], key="st_qa")
                    ph.op("act", lambda e: e.copy(out=tokb[:nb, 2720:3264].rearrange("p (a d) -> p a d", a=8)[:, :, 0:64], in_=Nf[:nb, si, 768:1280].rearrange("p (a d) -> p a d", a=8)), r=nres, w=["tokb_qb"])
                    ph.op("dve", lambda e: e.tensor_copy(out=tokb[:nb, 2720:3264].rearrange("p (a d) -> p a d", a=8)[:, :, 64:68], in_=aq[:nb, :].rearrange("p (a d) -> p a d", a=8)), r=["aq"], w=["tokb_qb"])
                    for hj in range(8):
                        bank, col = hj // 4, (hj % 4) * 128
                        ph.op("pe", lambda e, hj=hj, bank=bank, col=col: e.transpose(out=ptr[0:68, bank, col:col + nb], in_=tokb[:nb, 2720 + hj * 68:2720 + (hj + 1) * 68], identity=idb[:nb, :nb]),
                              r=["tokb_qb", "idb"], w=[("pu", bank)])
                    for half in range(2):
                        ph.op("dve", lambda e, half=half: e.tensor_copy(out=stg[0:68, 1, 512 + half * 512:1024 + half * 512].rearrange("p (h t) -> p h t", h=4)[:, :, 0:nb],
                                                                         in_=ptr[0:68, half, 0:512].rearrange("p (h t) -> p h t", h=4)[:, :, 0:nb]),
                              w=[("pu", half), "stg_qb"])
                    ph.dma("sp", qb_d[bi, :, :, 0:nb], stg[0:68, 1, 512:1536].rearrange("p (h t) -> p h t", h=8)[:, :, 0:nb], r=["stg_qb"], key="st_qb")
                    ph.op("act", lambda e: e.copy(out=stg[:nb, 1, 2048:2432], in_=Nf[:nb, si, 2304:2688]), r=nres, w=["cq_b"])
                    for c in range(3):
                        ph.op("pe", lambda e, c=c: e.transpose(out=ptr[:, 2, c * 128:c * 128 + nb], in_=stg[:nb, 1, 2048 + c * 128:2048 + (c + 1) * 128], identity=idb[:nb, :nb]),
                              r=["cq_b", "idb"], w=[("pu", 2)])
                    ph.op("dve", lambda e: e.tensor_copy(out=latT[:, 2:5, 0:nb], in_=ptr[:, 2, 0:384].rearrange("p (c t) -> p c t", c=3)[:, :, 0:nb]), w=[("pu", 2), "latT_q"])
                    for c in range(3):
                        ph.op("pe", lambda e, c=c: e.matmul(pu[:nb, 2, 0:384], lhsT=latT[:, 2 + c, 0:nb], rhs=wuq[:, c, :], start=(c == 0), stop=(c == 2)),
                              r=["latT_q", "wuq"], w=[("pu", 2)])
                    pq = pu[:nb, 2, 0:384].rearrange("p (h d) -> p h d", h=4)
                    ph.op("act", lambda e: e.copy(out=qcf[:nb, :, 0:64], in_=pq[:, :, 0:64]), w=[("pu", 2), "qcf"])
                    cosb = cs[:nb, 0:16].unsqueeze(1).to_broadcast([nb, 4, 16])
                    sinb = cs[:nb, 16:32].unsqueeze(1).to_broadcast([nb, 4, 16])
                    ph.op("dve", lambda e: e.tensor_tensor(out=rt[:nb, :, :], in0=pq[:, :, 64:80], in1=cosb, op=ALU.mult), r=["cs"], w=[("pu", 2), "rt"])
                    ph.op("dve", lambda e: e.tensor_tensor(out=rt2[:nb, :, :], in0=pq[:, :, 80:96], in1=sinb, op=ALU.mult), r=["cs"], w=[("pu", 2), "rt2"])
                    ph.op("dve", lambda e: e.tensor_tensor(out=qcf[:nb, :, 64:80], in0=rt[:nb, :, :], in1=rt2[:nb, :, :], op=ALU.subtract), r=["rt", "rt2"], w=["qcf"])
                    ph.op("dve", lambda e: e.tensor_tensor(out=rt[:nb, :, :], in0=pq[:, :, 64:80], in1=sinb, op=ALU.mult), r=["cs"], w=[("pu", 2), "rt"])
                    ph.op("dve", lambda e: e.tensor_tensor(out=rt2[:nb, :, :], in0=pq[:, :, 80:96], in1=cosb, op=ALU.mult), r=["cs"], w=[("pu", 2), "rt2"])
                    ph.op("dve", lambda e: e.tensor_tensor(out=qcf[:nb, :, 80:96], in0=rt[:nb, :, :], in1=rt2[:nb, :, :], op=ALU.add), r=["rt", "rt2"], w=["qcf"])
                    rms_rows(ph, qcf[:nb, :, :], nb, 4, 96, gq96[:nb, :, :], qcf[:nb, :, :], sq[:, 2176:2560], ss[:, 26:30], "x1", ["qcf"], ["qcf"], ["gq96"])
                    ph.op("act", lambda e: e.copy(out=stg[:nb, 1, 2432:2816].rearrange("p (h d) -> p h d", h=4), in_=qcf[:nb, :, :]), r=["qcf"], w=["qc_b"])
                    for h in range(4):
                        ph.op("pe", lambda e, h=h: e.transpose(out=ptr[0:96, 5, h * 128:h * 128 + nb], in_=stg[:nb, 1, 2432 + h * 96:2432 + (h + 1) * 96], identity=idb[:nb, :nb]),
                              r=["qc_b", "idb"], w=[("pu", 5)])
                    ph.op("dve", lambda e: e.tensor_copy(out=stg[0:96, 0, 2048:2560].rearrange("p (h t) -> p h t", h=4)[:, :, 0:nb], in_=ptr[0:96, 5, 0:512].rearrange("p (h t) -> p h t", h=4)[:, :, 0:nb]),
                          w=[("pu", 5), "stg_qc"])
                    ph.dma("sp", qc_d[bi, :, :, 0:nb], stg[0:96, 0, 2048:2560].rearrange("p (h t) -> p h t", h=4)[:, :, 0:nb], r=["stg_qc"], key="st_qc")
                    bs1["kp"] = Nf[:nb, si, 2944:2976]
                    bs1["kpres"] = nres
                    bs1["ak"] = ak
                    if bi < NBLK:
                        kv_tail(ph, Nf, si, nb, ksrc, bi * RBE, dict(bs1), rr=("rec", bi))
                        RB = RBE // 128
                        ph.custom("pool", lambda e: e.collective_compute("AllGather", ALU.bypass, replica_groups=[[0, 1], [2, 3], [4, 5], [6, 7]],
                                                                         ins=[ksrc[bi * RB:(bi + 1) * RB, :].opt()], outs=[kdst[2 * bi * RB:(2 * bi + 2) * RB, :].opt()]),
                                  r=[(("rec", bi), k) for k in range(6)], key="cc", inc=1)
                    else:
                        kv_tail(ph, Nf, si, nb, srec, NCB * RBE, dict(bs1))
                for gi, (gt0, gn) in enumerate(groups):
                    norm_to_hT(ph, hT, xsq, rinv, pu[:, 4:6, :], l * 4 + 1, only=gi, pres=lambda b: ("pu", 4 + b))
                    for bi in ([NBLK] if gt0 >= SC else range(gt0 // 128, (gt0 + gn) // 128)):
                        proj_block(bi, blk_cols(bi)[0] - gt0)
                ph.stream = 1
                for cb in range(NCB):
                    r0, r1 = cb * 128, (cb + 1) * 128
                    ab = cb - (NCB - 4)
                    if ab >= 0:
                        ph.dma("pool", tokb2[:, 0:256], ca_k[l, ab * 128:(ab + 1) * 128, :], w=["s2tokb_a"], key="ci0")
                        ph.dma("pool", tokb2[:, 256:512], ca_v[l, ab * 128:(ab + 1) * 128, :], w=["s2tokb_a"], key="ci1")
                    ph.dma("pool", tokb2[:, 512:512 + 544].rearrange("p (a d) -> p a d", a=8)[:, :, 0:64], cb_k[l, r0:r1, :].rearrange("p (a d) -> p a d", a=8), w=["s2tokb_b"], key="ci2")
                    ph.dma("pool", tokb2[:, 1056:1568], cb_v[l, r0:r1, :], w=["s2tokb_bv"], key="ci3")
                    ph.dma("pool", tokb2[:, 1568:1824], cc_kv[l, r0:r1, :], w=["s2tokb_c"], key="ci4")
                    ph.dma("sp", kp2[:, :], cc_kpe[l, r0:r1, :], w=["s2kp"], key="ci5")
                    ph.dma("act", ak2[:, :], c_augkc[r0:r1, :], w=["s2ak"], key="ak2")
                    kv_tail(ph, None, 0, 128, srec, cb * RBE, bs2)
                ph.stream = 0
                ph.run()
            es_win.close()

            if STAGE < 3:
                continue
            if STAGE < 4:
                continue
            attention(l, None)
            if STAGE < 8:
                continue
            mem_attention(l)
            if STAGE < 9:
                continue
            ffn(l, 2)

        with ExitStack() as es10:
            ytok = es10.enter_context(SB("ytok", [128, 2, D], F32))
            pt2 = es10.enter_context(PS("pt2", [128, 2, 512], F32))
            ph = Phase(g)
            for bi in range(NB):
                c0, nb = blk_cols(bi)
                s = bi % 2
                for c in range(8):
                    ps = c % 2
                    ph.op("pe", lambda e, c=c, ps=ps, c0=c0, nb=nb: e.transpose(out=pt2[:nb, ps, 0:128], in_=xT[:, c, c0:c0 + nb], identity=idf[:, :]),
                          r=[("xT", c), "idf"], w=[("pt2", ps)])
                    ph.op("act", lambda e, s=s, c=c, ps=ps, nb=nb: e.copy(out=ytok[:nb, s, c * 128:(c + 1) * 128], in_=pt2[:nb, ps, 0:128]),
                          w=[("pt2", ps), ("ytok", s)])
                ph.dma("sp", y[c0:c0 + nb, :], ytok[:nb, s, :], r=[("ytok", s)], key=("yst", s))
            ph.run()
    return nc


WNAMES = ["ffn1_norm", "ffn1_w_gate", "ffn1_w_up", "ffn1_w_down", "mix_norm", "w_in", "a_q_norm", "a_k_norm",
          "a_rel_bias", "b_q_norm", "b_k_norm", "b_lambda", "b_sub_norm", "c_q_lat_norm", "c_kv_lat_norm",
          "c_w_uq", "c_w_ukv", "c_q_norm", "c_k_norm", "w_out", "mem_norm_x", "mem_w_q", "mem_q_norm",
          "mem_norm_m", "mem_w_k", "mem_w_v", "mem_k_norm", "mem_w_o", "ffn2_norm", "ffn2_w_gate",
          "ffn2_w_up", "ffn2_w_down"]


def _consts(p, NBLK, DEPTH, PAST):
    SC = NBLK * 128
    T = SC + 64
    t = np.arange(SC)
    pos = np.concatenate([(2 * (t // 128) + p) * 128 + t % 128, PAST + np.arange(64)]).astype(np.int64)
    inv = (10000.0 ** (-np.arange(16, dtype=np.float32) / 16)).astype(np.float32)
    ang = pos.astype(np.float32)[:, None] * inv[None, :]
    cs = np.concatenate([np.cos(ang), np.sin(ang)], axis=1).astype(np.float32)
    slopes = np.exp2(-8.0 * (np.arange(4, dtype=np.float32) + 1.0) / 4).astype(np.float32)

    def aug(posv, qside):
        lo = (posv % 128).astype(np.float32)
        hi = (posv - posv % 128).astype(np.float32)
        out = np.zeros((len(posv), 8, 4), np.float32)
        for h in range(4):
            for j in range(2):
                if qside:
                    out[:, 2 * h + j] = np.stack([-slopes[h] * hi, -slopes[h] * lo, np.ones_like(lo), np.ones_like(lo)], 1)
                else:
                    out[:, 2 * h + j] = np.stack([np.ones_like(lo), np.ones_like(lo), slopes[h] * hi, slopes[h] * lo], 1)
        return out.reshape(len(posv), 32)

    k = np.arange(128)[:, None]
    q = np.arange(128)[None, :]
    kc, qc = k // 64, q // 64
    diag = np.zeros((4, 128, 128), np.float32)
    for h in range(4):
        d = np.where((kc == qc) & (k > q), -2.0 * slopes[h] * (k - q), 0.0)
        diag[h] = np.where(kc > qc, NEG, d)
    dmask = np.where(kc > qc, NEG, 0.0).astype(np.float32)
    full = np.full((128, 128), NEG, np.float32)
    zero = np.zeros((128, 128), np.float32)
    corrB = np.zeros((128, 2, 4, 128), np.float32)
    maskC = np.zeros((128, 2, 128), np.float32)
    for h in range(4):
        corrB[:, 0, h, :] = diag[h] if p == 0 else zero
        corrB[:, 1, h, :] = full if p == 0 else diag[h]
    maskC[:, 0, :] = dmask if p == 0 else zero
    maskC[:, 1, :] = full if p == 0 else dmask
    k64 = np.arange(64)[:, None]
    q64 = np.arange(64)[None, :]
    corrBs = np.zeros((64, 4, 64), np.float32)
    for h in range(4):
        corrBs[:, h, :] = np.where(k64 > q64, -2.0 * slopes[h] * (k64 - q64), 0.0)
    maskA = np.zeros((128, 6, 128), np.float32)
    for r in range(6):
        delta = r - 1 + p
        rel = -2 * delta + kc - qc
        maskA[:, r, :] = np.where((rel <= 0) & (rel >= -8), 0.0, NEG)
    w01 = np.tile(np.array([[1.0 - p, float(p)]], np.float32), (128, 1))
    lam = np.zeros((128, 2 * DEPTH), np.float32)
    for l in range(DEPTH):
        li = 0.8 - 0.6 * math.exp(-0.3 * l)
        lam[:, 2 * l] = li
        lam[:, 2 * l + 1] = 1.0 - li
    return dict(c_ident=np.eye(128, dtype=np.float32), c_cs=cs, c_augq=aug(pos, True), c_augk=aug(pos, False),
                c_augkc=aug(np.arange(PAST), False), c_corrB=corrB, c_corrBs=corrBs, c_maskC=maskC,
                c_maskA=maskA, c_w01=w01, c_lam=lam)


_CACHE = {}


def kernel(**inputs):
    x_prompt = np.asarray(inputs["x_prompt"], np.float32)
    x_sample = np.asarray(inputs["x_sample"], np.float32)
    B, SEQ, _ = x_prompt.shape
    DB = x_sample.shape[0]
    DEPTH = inputs["w_in"].shape[0]
    PAST = inputs["cache_b_k"].shape[2]
    assert B * 2 == 8 and DB == 8 and x_sample.shape[1] == 64 and inputs["cache_a_k"].shape[2] == 512
    NBLK = SEQ // 256
    SC = NBLK * 128
    key = (NBLK, DEPTH, PAST)
    if key not in _CACHE:
        _CACHE[key] = build(NBLK, DEPTH, PAST)
    nc = _CACHE[key]
    wts = {nm: np.ascontiguousarray(np.asarray(inputs[nm], np.float32)) for nm in WNAMES}
    in_maps = []
    for c in range(8):
        b, p = c // 2, c % 2
        xb = x_prompt[b].reshape(SEQ // 128, 128, D)[p::2].reshape(SC, D)
        m = dict(wts)
        m["xin"] = np.ascontiguousarray(np.concatenate([xb, x_sample[c]], axis=0))
        m["mem"] = np.ascontiguousarray(np.asarray(inputs["mem_prompt"], np.float32)[b])
        m["ca_k"] = np.ascontiguousarray(np.asarray(inputs["cache_a_k"], np.float32)[:, c].reshape(DEPTH, 512, 256))
        m["ca_v"] = np.ascontiguousarray(np.asarray(inputs["cache_a_v"], np.float32)[:, c].reshape(DEPTH, 512, 256))
        m["cb_k"] = np.ascontiguousarray(np.asarray(inputs["cache_b_k"], np.float32)[:, c].reshape(DEPTH, PAST, 512))
        m["cb_v"] = np.ascontiguousarray(np.asarray(inputs["cache_b_v"], np.float32)[:, c].reshape(DEPTH, PAST, 512))
        m["cc_kv"] = np.ascontiguousarray(np.asarray(inputs["cache_c_kv"], np.float32)[:, c])
        m["cc_kpe"] = np.ascontiguousarray(np.asarray(inputs["cache_c_kpe"], np.float32)[:, c])
        m["cm_k"] = np.ascontiguousarray(np.asarray(inputs["cache_mem_k"], np.float32)[:, c].reshape(DEPTH, NMEM, 512))
        m["cm_v"] = np.ascontiguousarray(np.asarray(inputs["cache_mem_v"], np.float32)[:, c].reshape(DEPTH, NMEM, 512))
        m.update(_consts(p, NBLK, DEPTH, PAST))
        in_maps.append(m)
    res = run_bass_kernel_spmd(nc, in_maps, core_ids=list(range(8))).results

    def unzig(name, width):
        out = np.zeros((DEPTH, B, SEQ // 128, 128, width), np.float32)
        for c in range(8):
            b, p = c // 2, c % 2
            out[:, b, p::2] = res[c][name][:, :SC].reshape(DEPTH, NBLK, 128, width)
        return out.reshape(DEPTH, B, SEQ, width)

    yp = np.zeros((B, SEQ // 128, 128, D), np.float32)
    ys = np.zeros((DB, 64, D), np.float32)
    for c in range(8):
        b, p = c // 2, c % 2
        yp[b, p::2] = res[c]["y"][:SC].reshape(NBLK, 128, D)
        ys[c] = res[c]["y"][SC:]
    yp = yp.reshape(B, SEQ, D)
    pak = np.zeros((DEPTH, B, 4, 128, 256), np.float32)
    pav = np.zeros((DEPTH, B, 4, 128, 256), np.float32)
    for c in range(8):
        b, p = c // 2, c % 2
        for mb in range(4):
            if mb % 2 == p:
                pak[:, b, mb] = res[c]["o_ak"][:, (mb // 2) * 128:(mb // 2 + 1) * 128]
                pav[:, b, mb] = res[c]["o_av"][:, (mb // 2) * 128:(mb // 2 + 1) * 128]
    pak = pak.reshape(DEPTH, B, 512, 4, 64)
    pav = pav.reshape(DEPTH, B, 512, 4, 64)
    pbk = unzig("o_bk", 512).reshape(DEPTH, B, SEQ, 4, 2, 64)
    pbv = unzig("o_bv", 512).reshape(DEPTH, B, SEQ, 4, 128)
    pckv = unzig("o_ckv", 256)
    pkpe = unzig("o_kpe", 32)
    pmk = np.stack([res[2 * b]["o_mk"] for b in range(B)], axis=1).reshape(DEPTH, B, NMEM, 4, 128)
    pmv = np.stack([res[2 * b]["o_mv"] for b in range(B)], axis=1).reshape(DEPTH, B, NMEM, 4, 128)
    sak = np.stack([res[c]["o_ak"][:, 256:320] for c in range(8)], axis=1).reshape(DEPTH, DB, 64, 4, 64)
    sav = np.stack([res[c]["o_av"][:, 256:320] for c in range(8)], axis=1).reshape(DEPTH, DB, 64, 4, 64)
    sbk = np.stack([res[c]["o_bk"][:, SC:] for c in range(8)], axis=1).reshape(DEPTH, DB, 64, 4, 2, 64)
    sbv = np.stack([res[c]["o_bv"][:, SC:] for c in range(8)], axis=1).reshape(DEPTH, DB, 64, 4, 128)
    sckv = np.stack([res[c]["o_ckv"][:, SC:] for c in range(8)], axis=1)
    skpe = np.stack([res[c]["o_kpe"][:, SC:] for c in range(8)], axis=1)
    return (yp, ys, pak, pav, pbk, pbv, pckv, pkpe, pmk, pmv, sak, sav, sbk, sbv, sckv, skpe)
```

```python
import math
import os
from contextlib import ExitStack
import numpy as np
import concourse.bass as bass
import concourse.mybir as mybir
from concourse.bass_utils import run_bass_kernel_spmd

F32 = mybir.dt.float32
BF16 = mybir.dt.bfloat16
ALU = mybir.AluOpType
AF = mybir.ActivationFunctionType
AX = mybir.AxisListType

D = 1024
DFF = 2816
HD = 64
NMEM = 256
EPS = 1e-6
IN_COLS = 2976
NEG = -30000.0
ENGS = ("pe", "act", "dve", "pool", "sp")

O_KA, O_VA, O_KB, O_VB, O_KC, O_VC, RBE = 0, 32768, 65536, 135168, 200704, 249856, 282624


class GSync:
    def __init__(self, nc):
        self.nc = nc
        self.esem = {e: nc.alloc_semaphore(name=f"es_{e}") for e in ("pe", "act", "dve", "pool")}
        self.ecnt = {e: 0 for e in self.esem}
        self.dsem = {}
        self.dcnt = {}
        self.kmap = {}
        self.nops = 0

    def kid(self, key):
        if key not in self.kmap:
            self.kmap[key] = len(self.kmap) % 56
        return self.kmap[key]

    def dkey(self, key):
        if key not in self.dsem:
            self.dsem[key] = self.nc.alloc_semaphore(name=f"ds_{len(self.dsem)}")
            self.dcnt[key] = 0
        return self.dsem[key]


class Phase:
    def __init__(self, g):
        self.g = g
        self.raw = []
        self.stream = 0
        self.segment = 0
        self.ops = []

    def _add(self, eng, fn, r, w, dma_key=None, inc=16):
        if dma_key is not None:
            dma_key = self.g.kid(dma_key)
        self.raw.append(dict(eng=eng, fn=fn, r=r, w=w, dma=dma_key, inc=inc, stream=self.stream, seg=self.segment))
        return len(self.raw) - 1

    def _finalize(self):
        order = []
        for seg in sorted({o["seg"] for o in self.raw}):
            streams = {}
            for o in self.raw:
                if o["seg"] == seg:
                    streams.setdefault(o["stream"], []).append(o)
            keys = sorted(streams)
            if len(keys) == 1:
                order.extend(streams[keys[0]])
                continue
            pos = {k: 0 for k in keys}
            tot = {k: len(streams[k]) for k in keys}
            while any(pos[k] < tot[k] for k in keys):
                k = min((k for k in keys if pos[k] < tot[k]), key=lambda k: (pos[k] + 1) / tot[k])
                order.append(streams[k][pos[k]])
                pos[k] += 1
        lastw, readers, lastdma = {}, {}, {}
        self.ops = []
        for raw in order:
            idx = len(self.ops)
            eng, dma_key = raw["eng"], raw["dma"]
            deps = {}
            for x in raw["r"]:
                if x in lastw:
                    deps.setdefault(lastw[x], set()).add("raw")
            for x in raw["w"]:
                if x in lastw:
                    deps.setdefault(lastw[x], set()).add("waw")
                for rd in readers.get(x, ()):
                    deps.setdefault(rd, set()).add("war")
            if dma_key is not None and dma_key in lastdma:
                deps.setdefault(lastdma[dma_key], set()).add("raw")
            op = dict(idx=idx, eng=eng, fn=raw["fn"], dma=dma_key, waits=[], signal=False, cnt=None, inc=raw["inc"])
            for p, kinds in deps.items():
                P = self.ops[p]
                if P["dma"] is not None:
                    op["waits"].append(p)
                elif P["eng"] == eng and dma_key is None and "raw" not in kinds:
                    continue
                else:
                    P["signal"] = True
                    op["waits"].append(p)
            self.ops.append(op)
            for x in raw["r"]:
                readers.setdefault(x, []).append(idx)
            for x in raw["w"]:
                lastw[x] = idx
                readers[x] = []
            if dma_key is not None:
                lastdma[dma_key] = idx

    def op(self, eng, fn, r=(), w=()):
        return self._add(eng, fn, tuple(r), tuple(w))

    def dma(self, q, out, in_, r=(), w=(), key=None, **kw):
        return self._add(q, lambda e: e.dma_start(out=out, in_=in_, **kw), tuple(r), tuple(w), dma_key=key)

    def custom(self, q, fn, r=(), w=(), key=None, inc=16):
        return self._add(q, fn, tuple(r), tuple(w), dma_key=key, inc=inc)

    def run(self):
        g = self.g
        nc = g.nc
        self._finalize()
        for op in self.ops:
            if op["dma"] is not None:
                g.dkey(op["dma"])
                g.dcnt[op["dma"]] += op["inc"]
                op["cnt"] = g.dcnt[op["dma"]]
            elif op["signal"]:
                g.ecnt[op["eng"]] += 1
                op["cnt"] = g.ecnt[op["eng"]]
        per = {e: [o for o in self.ops if o["eng"] == e] for e in ENGS}
        ops = self.ops
        g.nops += len(ops)

        def emit(e, lst):
            def body(engine):
                waited = {}
                lastkeys = {}
                for o in lst:
                    need = {}
                    for p in o["waits"]:
                        P = ops[p]
                        s = g.dsem[P["dma"]] if P["dma"] is not None else g.esem[P["eng"]]
                        k = id(s)
                        if k not in need or need[k][1] < P["cnt"]:
                            need[k] = (s, P["cnt"])
                    for k, (s, v) in need.items():
                        if waited.get(k, -1) >= v:
                            continue
                        engine.wait_ge(s, v)
                        waited[k] = v
                    ins = o["fn"](engine)
                    if o["dma"] is not None:
                        ins.then_inc(g.dsem[o["dma"]], o["inc"])
                        lastkeys[o["dma"]] = o["cnt"]
                    elif o["signal"]:
                        ins.then_inc(g.esem[e], 1)
                for key, v in lastkeys.items():
                    engine.wait_ge(g.dsem[key], v)
            return body

        with nc.Block() as block:
            for e in ENGS:
                if not per[e]:
                    continue
                reg = {"pe": block.tensor, "act": block.scalar, "dve": block.vector,
                       "pool": block.gpsimd, "sp": block.sync}[e]
                reg(emit(e, per[e]))


def build(NBLK, DEPTH, PAST):
    STAGE = int(os.environ.get('KSTAGE', '99'))
    NB = NBLK + 1
    T = NBLK * 128 + 64
    NCB = PAST // 128
    NSB = NCB + 1
    NKB = 2 * NBLK
    SC = NBLK * 128
    nc = bass.Bass("TRN2", target_bir_lowering=False)
    g = GSync(nc)

    def din(name, shape, dt=F32):
        return nc.dram_tensor(name, list(shape), dt, kind="ExternalInput")

    def dout(name, shape):
        return nc.dram_tensor(name, list(shape), F32, kind="ExternalOutput")

    xin = din("xin", [T, D])
    mem = din("mem", [NMEM, D])
    ca_k = din("ca_k", [DEPTH, 512, 256]); ca_v = din("ca_v", [DEPTH, 512, 256])
    cb_k = din("cb_k", [DEPTH, PAST, 512]); cb_v = din("cb_v", [DEPTH, PAST, 512])
    cc_kv = din("cc_kv", [DEPTH, PAST, 256]); cc_kpe = din("cc_kpe", [DEPTH, PAST, 32])
    cm_k = din("cm_k", [DEPTH, NMEM, 512]); cm_v = din("cm_v", [DEPTH, NMEM, 512])
    W = {}
    for nm, shp in [("ffn1_norm", [DEPTH, D]), ("ffn1_w_gate", [DEPTH, D, DFF]), ("ffn1_w_up", [DEPTH, D, DFF]),
                    ("ffn1_w_down", [DEPTH, DFF, D]), ("mix_norm", [DEPTH, D]), ("w_in", [DEPTH, D, IN_COLS]),
                    ("a_q_norm", [DEPTH, 64]), ("a_k_norm", [DEPTH, 64]), ("a_rel_bias", [DEPTH, 4, 257]),
                    ("b_q_norm", [DEPTH, 64]), ("b_k_norm", [DEPTH, 64]), ("b_lambda", [DEPTH, 4, 64]),
                    ("b_sub_norm", [DEPTH, 128]), ("c_q_lat_norm", [DEPTH, 384]), ("c_kv_lat_norm", [DEPTH, 256]),
                    ("c_w_uq", [DEPTH, 384, 384]), ("c_w_ukv", [DEPTH, 256, 512]), ("c_q_norm", [DEPTH, 96]),
                    ("c_k_norm", [DEPTH, 96]), ("w_out", [DEPTH, D, D]), ("mem_norm_x", [DEPTH, D]),
                    ("mem_w_q", [DEPTH, D, 512]), ("mem_q_norm", [DEPTH, 128]), ("mem_norm_m", [DEPTH, D]),
                    ("mem_w_k", [DEPTH, D, 512]), ("mem_w_v", [DEPTH, D, 512]), ("mem_k_norm", [DEPTH, 128]),
                    ("mem_w_o", [DEPTH, 512, D]), ("ffn2_norm", [DEPTH, D]), ("ffn2_w_gate", [DEPTH, D, DFF]),
                    ("ffn2_w_up", [DEPTH, D, DFF]), ("ffn2_w_down", [DEPTH, DFF, D])]:
        W[nm] = din(nm, shp)
    c_ident = din("c_ident", [128, 128])
    c_cs = din("c_cs", [T, 32])
    c_augq = din("c_augq", [T, 32])
    c_augk = din("c_augk", [T, 32])
    c_augkc = din("c_augkc", [PAST, 32])
    c_corrB = din("c_corrB", [128, 2, 4, 128])
    c_corrBs = din("c_corrBs", [64, 4, 64])
    c_maskC = din("c_maskC", [128, 2, 128])
    c_maskA = din("c_maskA", [128, 6, 128])
    c_w01 = din("c_w01", [128, 2])
    c_lam = din("c_lam", [128, 2 * DEPTH])
    y = dout("y", [T, D])
    o_ak = dout("o_ak", [DEPTH, 320, 256]); o_av = dout("o_av", [DEPTH, 320, 256])
    o_bk = dout("o_bk", [DEPTH, T, 512]); o_bv = dout("o_bv", [DEPTH, T, 512])
    o_ckv = dout("o_ckv", [DEPTH, T, 256]); o_kpe = dout("o_kpe", [DEPTH, T, 32])
    o_mk = dout("o_mk", [DEPTH, NMEM, 512]); o_mv = dout("o_mv", [DEPTH, NMEM, 512])
    qa_d = nc.dram_tensor("qa_d", [NB, 64, 4, 128], BF16)
    qb_d = nc.dram_tensor("qb_d", [NB, 68, 8, 128], BF16)
    qc_d = nc.dram_tensor("qc_d", [NB, 96, 4, 128], BF16)
    ksrc = nc.dram_tensor("ksrc", [NBLK * RBE // 128, 128], BF16)
    kdst = nc.dram_tensor("kdst", [2 * NBLK * RBE // 128, 128], BF16)
    srec = nc.dram_tensor("srec", [NSB * RBE // 128, 128], BF16)
    Rtoe = nc.dram_tensor("Rtoe", [4, 128, 1024], F32)
    Etoe = nc.dram_tensor("Etoe", [4, 1024], F32)

    uid = [0]

    def SB(name, shape, dt):
        uid[0] += 1
        return nc.sbuf_tensor(f"{name}_{uid[0]}", shape, dt)

    def PS(name, shape, dt):
        uid[0] += 1
        return nc.psum_tensor(f"{name}_{uid[0]}", shape, dt)

    def rec_ap(tensor, base, dims):
        return bass.AP(tensor, base, [list(d) for d in dims])

    def blk_cols(bi):
        return (bi * 128, 128) if bi < NBLK else (SC, 64)

    groups = [(t0, min(512, SC - t0)) for t0 in range(0, SC, 512)] + [(SC, 64)]

    with ExitStack() as es1:
        xT = es1.enter_context(SB("xT", [128, 8, T], F32))
        idf = es1.enter_context(SB("idf", [128, 128], F32))
        idb = es1.enter_context(SB("idb", [128, 128], BF16))
        ones = es1.enter_context(SB("ones", [128, 128], BF16))
        zerob = es1.enter_context(SB("zerob", [128, 128], BF16))
        gcols = es1.enter_context(SB("gcols", [128, 4 * DEPTH, 8], F32))
        lamc = es1.enter_context(SB("lamc", [128, 2 * DEPTH], F32))

        with ExitStack() as es2:
            xtok = es2.enter_context(SB("xtok", [128, 2, D], F32))
            grow = es2.enter_context(SB("grow", [4 * DEPTH * 8, 128], F32))
            pt = es2.enter_context(PS("pt", [128, 2, 512], F32))
            ph = Phase(g)
            ph.dma("sp", idf[:, :], c_ident[:, :], w=["idf"], key="c0")
            ph.dma("pool", idb[:, :], c_ident[:, :], w=["idb"], key="c1")
            ph.dma("sp", lamc[:, :], c_lam[:, :], w=["lamc"], key="c2")
            ph.op("pool", lambda e: e.memset(ones[:, :], 1.0), w=["ones"])
            ph.op("pool", lambda e: e.memset(zerob[:, :], 0.0), w=["zerob"])
            for k, nm in enumerate(["ffn1_norm", "mix_norm", "mem_norm_x", "ffn2_norm"]):
                for l in range(DEPTH):
                    r0 = (l * 4 + k) * 8
                    ph.dma("sp", grow[r0:r0 + 8, :], W[nm][l, :].rearrange("(c p) -> c p", p=128), w=["grow"], key="c3")
            nr = 4 * DEPTH * 8
            ph.op("pe", lambda e: e.transpose(out=pt[:, 0, 0:nr], in_=grow[:, :], identity=idf[0:nr, 0:nr]), r=["grow", "idf"], w=[("pt", 0)])
            ph.op("dve", lambda e: e.tensor_copy(out=gcols[:, :, :].rearrange("p a b -> p (a b)"), in_=pt[:, 0, 0:nr]), w=[("pt", 0), "gcols"])
            for bi in range(NB):
                c0, nb = blk_cols(bi)
                s = bi % 2
                ph.dma("sp", xtok[:nb, s, :], xin[c0:c0 + nb, :], w=[("xtok", s)], key=("xtok", s))
                for c in range(8):
                    ps = c % 2
                    ph.op("pe", lambda e, s=s, c=c, ps=ps, nb=nb: e.transpose(out=pt[:, ps, 0:nb], in_=xtok[:nb, s, c * 128:(c + 1) * 128], identity=idf[:nb, :nb]),
                          r=[("xtok", s), "idf"], w=[("pt", ps)])
                    ph.op("dve", lambda e, c=c, ps=ps, c0=c0, nb=nb: e.tensor_copy(out=xT[:, c, c0:c0 + nb], in_=pt[:, ps, 0:nb]),
                          w=[("pt", ps), ("xT", c)])
            ph.run()

        def norm_to_hT(ph, hT, xsq, rinv, pn, gidx, only=None, pres=lambda b: ("pn", b)):
            for gi, (t0, n) in enumerate(groups):
                if only is not None and gi != only:
                    continue
                h0 = 0 if only is not None else t0
                s = 0
                ps_ = gi % 2
                ph.op("act", lambda e, t0=t0, n=n, s=s: e.activation(out=xsq[:, s, :, 0:n], in_=xT[:, :, t0:t0 + n], func=AF.Square),
                      r=[("xT", c) for c in range(8)], w=[("xsq", s)])
                for c in range(8):
                    ph.op("pe", lambda e, c=c, n=n, s=s, ps_=ps_: e.matmul(pn[:, ps_, 0:n], lhsT=ones[:, :], rhs=xsq[:, s, c, 0:n], start=(c == 0), stop=(c == 7)),
                          r=[("xsq", s), "ones"], w=[pres(ps_)])
                ph.op("act", lambda e, n=n, s=s, ps_=ps_: e.activation(out=rinv[:, s, 0:n], in_=pn[:, ps_, 0:n], func=AF.Sqrt, scale=1.0 / D, bias=EPS),
                      w=[pres(ps_), ("rinv", s)])
                ph.op("dve", lambda e, n=n, s=s: e.reciprocal(out=rinv[:, s, 0:n], in_=rinv[:, s, 0:n]), r=[("rinv", s)], w=[("rinv", s)])
                for c in range(8):
                    ph.op("dve", lambda e, c=c, t0=t0, n=n, s=s, h0=h0: e.scalar_tensor_tensor(out=hT[:, c, h0:h0 + n], in0=xT[:, c, t0:t0 + n], scalar=gcols[:, gidx, c:c + 1],
                                                                                         in1=rinv[:, s, 0:n], op0=ALU.mult, op1=ALU.mult),
                          r=[("rinv", s), "gcols", ("xT", c)], w=[("hT", c)])

        def ffn(l, which, prefetch=None):
            wg_d, wu_d, wd_d = W[f"ffn{which}_w_gate"], W[f"ffn{which}_w_up"], W[f"ffn{which}_w_down"]
            gidx = l * 4 + (0 if which == 1 else 3)
            NFG = DFF // 256
            with ExitStack() as es3:
                hT = es3.enter_context(SB("hT", [128, 8, T], BF16))
                xsq = es3.enter_context(SB("xsq", [128, 1, 8, 512], BF16))
                rinv = es3.enter_context(SB("rinv", [128, 1, 512], F32))
                wgs = es3.enter_context(SB("wgs", [128, 2, 8, 256], BF16))
                wus = es3.enter_context(SB("wus", [128, 2, 8, 256], BF16))
                wds = es3.enter_context(SB("wds", [128, 2, 2, D], BF16))
                sg = es3.enter_context(SB("sg", [128, 2, 512], F32))
                actT = es3.enter_context(SB("actT", [128, 2, 2, T], BF16))
                pn = es3.enter_context(PS("pn", [128, 2, 512], F32))
                pgt = es3.enter_context(PS("pgt", [128, 2, 512], F32))
                put = es3.enter_context(PS("put", [128, 2, 512], F32))
                pd = es3.enter_context(PS("pd", [128, 2, 512], F32))
                ph = Phase(g)
                norm_to_hT(ph, hT, xsq, rinv, pn, gidx)
                it = 0
                dit = [0]
                pending = []

                def emit_down(fg, ws, t0, n):
                    for dc in range(8):
                        pb = dit[0] % 2
                        dit[0] += 1
                        for j in range(2):
                            ph.op("pe", lambda e, j=j, dc=dc, pb=pb: e.matmul(pd[:, pb, 0:n], lhsT=wds[:, ws, j, dc * 128:(dc + 1) * 128], rhs=actT[:, ws, j, t0:t0 + n], start=(j == 0), stop=(j == 1)),
                                  r=[("wds", ws), ("actT", ws, j, t0)], w=[("pd", pb)])
                        ph.op("dve", lambda e, dc=dc, pb=pb: e.scalar_tensor_tensor(out=xT[:, dc, t0:t0 + n], in0=pd[:, pb, 0:n], scalar=0.5, in1=xT[:, dc, t0:t0 + n], op0=ALU.mult, op1=ALU.add),
                              r=[("xT", dc)], w=[("pd", pb), ("xT", dc)])

                for fg in range(NFG):
                    ws = fg % 2
                    if prefetch is not None and fg in (2, 4, 6, 8):
                        prefetch(ph, fg // 2 - 1)
                    ph.dma("pool", wgs[:, ws, :, :], wg_d[l, :, fg * 256:(fg + 1) * 256].rearrange("(c p) n -> p c n", p=128), w=[("wgs", ws)], key=("wgs", ws))
                    ph.dma("pool", wus[:, ws, :, :], wu_d[l, :, fg * 256:(fg + 1) * 256].rearrange("(c p) n -> p c n", p=128), w=[("wus", ws)], key=("wus", ws))
                    ph.dma("pool", wds[:, ws, :, :], wd_d[l, fg * 256:(fg + 1) * 256, :].rearrange("(c p) n -> p c n", p=128), w=[("wds", ws)], key=("wds", ws))
                    for (t0, n) in groups:
                        for j in range(2):
                            pb = it % 2
                            it += 1
                            for c in range(8):
                                ph.op("pe", lambda e, c=c, j=j, pb=pb, ws=ws, t0=t0, n=n: e.matmul(pgt[:, pb, 0:n], lhsT=wgs[:, ws, c, j * 128:(j + 1) * 128], rhs=hT[:, c, t0:t0 + n], start=(c == 0), stop=(c == 7)),
                                      r=[("wgs", ws), ("hT", c)], w=[("pgt", pb)])
                            for c in range(8):
                                ph.op("pe", lambda e, c=c, j=j, pb=pb, ws=ws, t0=t0, n=n: e.matmul(put[:, pb, 0:n], lhsT=wus[:, ws, c, j * 128:(j + 1) * 128], rhs=hT[:, c, t0:t0 + n], start=(c == 0), stop=(c == 7)),
                                      r=[("wus", ws), ("hT", c)], w=[("put", pb)])
                            ph.op("act", lambda e, pb=pb, n=n: e.activation(out=sg[:, pb, 0:n], in_=pgt[:, pb, 0:n], func=AF.Silu), w=[("pgt", pb), ("sg", pb)])
                            ph.op("dve", lambda e, pb=pb, ws=ws, j=j, t0=t0, n=n: e.tensor_tensor(out=actT[:, ws, j, t0:t0 + n], in0=put[:, pb, 0:n], in1=sg[:, pb, 0:n], op=ALU.mult),
                                  r=[("sg", pb)], w=[("put", pb), ("actT", ws, j, t0)])
                        if pending:
                            emit_down(*pending.pop(0))
                        pending.append((fg, ws, t0, n))
                while pending:
                    emit_down(*pending.pop(0))
                ph.run()

        def bcast_row(ph, dst_ap, src_1d, n, reps, key, wres):
            ph.dma("sp", dst_ap, src_1d.rearrange("(o h n) -> o h n", o=1, h=1).to_broadcast([128, reps, n]), w=[wres], key=key)

        def rms_rows(ph, src, nb, H, dh, gain, out, sq, ss, tag, src_res, out_res, gain_res):
            ph.op("act", lambda e: e.activation(out=sq[:nb, 0:H * dh].rearrange("p (h d) -> p h d", h=H), in_=src, func=AF.Square), r=[], w=list(src_res) + [("sq", tag)])
            ph.op("dve", lambda e: e.tensor_reduce(out=ss[:nb, 0:H], in_=sq[:nb, 0:H * dh].rearrange("p (h d) -> p h d", h=H), axis=AX.X, op=ALU.add), r=[("sq", tag)], w=[("ss", tag)])
            ph.op("act", lambda e: e.activation(out=ss[:nb, 0:H], in_=ss[:nb, 0:H], func=AF.Sqrt, scale=1.0 / dh, bias=EPS), r=[("ss", tag)], w=[("ss", tag)])
            ph.op("dve", lambda e: e.reciprocal(out=ss[:nb, 0:H], in_=ss[:nb, 0:H]), r=[("ss", tag)], w=[("ss", tag)])
            ph.op("dve", lambda e: e.tensor_tensor(out=out, in0=src, in1=ss[:nb, 0:H].unsqueeze(2).to_broadcast([nb, H, dh]), op=ALU.mult), r=[("ss", tag)], w=list(src_res) + list(out_res))
            ph.op("dve", lambda e: e.tensor_tensor(out=out, in0=out, in1=gain, op=ALU.mult), r=list(out_res) + list(gain_res), w=list(out_res))

        def attn_core(ph, sp, pT, qT, nq, tiles, E, outp, out_res, cnt):
            ngr = (len(tiles) + 3) // 4
            bufs = []
            for gi in range(ngr):
                bufs.append(cnt[0] % 2)
                cnt[0] += 1

            def emit_S(gi):
                b = bufs[gi]
                for ti, (kT, v, nk, corrs, kres, vres) in enumerate(tiles[gi * 4:(gi + 1) * 4]):
                    ph.op("pe", lambda e, kT=kT, ti=ti, nk=nk, last=(len(corrs) == 0): e.matmul(sp[:nk, b, ti * 128:ti * 128 + nq], lhsT=kT, rhs=qT, start=True, stop=last),
                          r=list(kres) + ["qt"], w=[("sp", b)])
                    for ci, cr in enumerate(corrs):
                        ph.op("pe", lambda e, cr=cr, ti=ti, nk=nk, last=(ci == len(corrs) - 1): e.matmul(sp[:nk, b, ti * 128:ti * 128 + nq], lhsT=idb[:nk, :nk], rhs=cr, start=False, stop=last),
                              r=["corr", "idb"], w=[("sp", b)])

            def emit_rest(gi):
                b = bufs[gi]
                grp = tiles[gi * 4:(gi + 1) * 4]
                ng = len(grp)
                ph.op("act", lambda e: e.activation(out=pT[:, b, 0:ng, 0:nq], in_=sp[:, b, 0:ng * 128].rearrange("p (g q) -> p g q", g=ng)[:, :, 0:nq], func=AF.Exp),
                      w=[("sp", b), ("pT", b)])
                for ti, (kT, v, nk, corrs, kres, vres) in enumerate(grp):
                    first = (gi == 0 and ti == 0)
                    lastmm = (gi == ngr - 1) and (ti == ng - 1)
                    ph.op("pe", lambda e, v=v, ti=ti, nk=nk, first=first, lastmm=lastmm: e.matmul(outp, lhsT=pT[:nk, b, ti, 0:nq], rhs=v, start=first, stop=lastmm),
                          r=[("pT", b)] + list(vres), w=list(out_res))

            emit_S(0)
            for gi in range(ngr):
                if gi + 1 < ngr:
                    emit_S(gi + 1)
                emit_rest(gi)

        def attn_T(ph, sp, pTb, opp, a, qflat, ncols, tiles, MP, cnt, kres, vres, zero_init=False, qres="qt"):
            nt = len(tiles)
            nbuf = sp.shape[1]
            bufs = []
            for ti in range(nt):
                bufs.append(cnt[0] % nbuf)
                cnt[0] += 1

            if zero_init:
                R0 = qflat.shape[0]
                for k_ in range(2):
                    ph.op("pe", lambda e, k_=k_: e.matmul(opp[:MP, 2 * a + k_, 0:ncols], lhsT=zerob[0:R0, 0:MP], rhs=qflat[:, 0:ncols], start=True, stop=False),
                          r=["zerob", qres], w=[("op", 2 * a + k_)])

            def emit_S(ti):
                kT, vl, nk, c_lo, corrs = tiles[ti][:5]
                c_hi = tiles[ti][5] if len(tiles[ti]) > 5 else ncols
                b = bufs[ti]
                ph.op("pe", lambda e: e.matmul(sp[:nk, b, c_lo:c_hi], lhsT=kT, rhs=qflat[:, c_lo:c_hi], start=True, stop=(len(corrs) == 0)),
                      r=list(kres) + [qres], w=[("sp", b)])
                for ci, (lo, hi, cap) in enumerate(corrs):
                    ph.op("pe", lambda e, lo=lo, hi=hi, cap=cap, last=(ci == len(corrs) - 1): e.matmul(sp[:nk, b, lo:hi], lhsT=idb[:nk, :nk], rhs=cap, start=False, stop=last),
                          r=["corr", "idb"], w=[("sp", b)])

            def emit_rest(ti):
                kT, vl, nk, c_lo, corrs = tiles[ti][:5]
                c_hi = tiles[ti][5] if len(tiles[ti]) > 5 else ncols
                b = bufs[ti]
                st_ = (ti == 0) and not zero_init
                ph.op("act", lambda e: e.activation(out=pTb[:nk, b, c_lo:c_hi], in_=sp[:nk, b, c_lo:c_hi], func=AF.Exp), w=[("sp", b), ("pT", b)])
                ph.op("pe", lambda e: e.matmul(opp[:MP, 2 * a, c_lo:c_hi], lhsT=vl, rhs=pTb[:nk, b, c_lo:c_hi], start=st_, stop=(ti == nt - 1)),
                      r=[("pT", b)] + list(vres), w=[("op", 2 * a)])
                ph.op("pe", lambda e: e.matmul(opp[:MP, 2 * a + 1, c_lo:c_hi], lhsT=ones[:nk, :MP], rhs=pTb[:nk, b, c_lo:c_hi], start=st_, stop=(ti == nt - 1)),
                      r=[("pT", b), "ones"], w=[("op", 2 * a + 1)])

            for ti in range(min(nbuf - 1, nt)):
                emit_S(ti)
            for ti in range(nt):
                if ti + nbuf - 1 < nt:
                    emit_S(ti + nbuf - 1)
                emit_rest(ti)

        def load_kv(ph, kt, vt, R, E, off_k, off_v, h, nhk, HV, res):
            HK = {O_KA: 4, O_KB: 8, O_KC: 4}[off_k]
            for jj in range(nhk):
                ph.dma("sp", kt[0:R, jj, :, :, :], rec_ap(kdst, off_k + (h * nhk + jj) * 128, [[HK * 128, R], [RBE, NKB], [1, 128]]), w=[res + "k"], key=("Kk", jj))
            ph.dma("act", vt[:, :, :, 0:E], rec_ap(kdst, off_v + h * E, [[HV * E, 128], [RBE, NKB], [1, E]]), w=[res + "v"], key="Kv")

        def load_kv_s(ph, kts, vts, R, E, off_k, off_v, h, nhk, HV, res, b0, nbk):
            HK = {O_KA: 4, O_KB: 8, O_KC: 4}[off_k]
            for jj in range(nhk):
                ph.dma("sp", kts[0:R, jj, 0:nbk, :], rec_ap(srec, b0 * RBE + off_k + (h * nhk + jj) * 128, [[HK * 128, R], [RBE, nbk], [1, 128]]), w=[res + "ks"], key=("Kks", jj))
            ph.dma("act", vts[:, 0:nbk, 0:E], rec_ap(srec, b0 * RBE + off_v + h * E, [[HV * E, 128], [RBE, nbk], [1, E]]), w=[res + "vs"], key="Kvs")

        def attention(l, Oall):
            with ExitStack() as es4:
                rc = es4.enter_context(SB("rc", [128, 8], F32))
                sp3 = es4.enter_context(PS("sp", [128, 3, 512], F32))
                sp = sp3[:, 0:2, :]
                opp = es4.enter_context(PS("opp", [128, 4, 512], F32))
                pss = es4.enter_context(PS("pss", [128, 1, 512], F32))
                OT = es4.enter_context(SB("OT", [128, 8, T], BF16))
                with ExitStack() as es5:
                    tb = es5.enter_context(SB("tb", [4, 257], F32))
                    ng = es5.enter_context(SB("ng", [4, 1], F32))
                    ext = es5.enter_context(SB("ext", [4, 1024], F32))
                    extb = es5.enter_context(SB("extb", [128, 1024], F32))
                    t7 = es5.enter_context(SB("t7", [128, 7, 128], F32))
                    tmpa = es5.enter_context(SB("tmpa", [128, 6, 128], F32))
                    mka = es5.enter_context(SB("mka", [128, 6, 128], F32))
                    w01 = es5.enter_context(SB("w01", [128, 2], F32))
                    biasA = es5.enter_context(SB("biasA", [128, 4, 6, 128], BF16))
                    biasS = es5.enter_context(SB("biasS", [128, 4, 5, 64], BF16))
                    kt = es5.enter_context(SB("kt", [64, NKB, 128], BF16))
                    vtp = es5.enter_context(SB("vtp", [128, 2, NKB, 128], BF16))
                    kts = es5.enter_context(SB("kts", [64, 5, 128], BF16))
                    vtsp = es5.enter_context(SB("vtsp", [128, 2, 5, 128], BF16))
                    qt = es5.enter_context(SB("qt", [64, NB * 128], BF16))
                    pTb = es5.enter_context(SB("pTb", [128, 3, 512], BF16))
                    rsA = es5.enter_context(SB("rsA", [128, 2, 512], F32))
                    ph = Phase(g)
                    ph.dma("sp", tb[:, :], W["a_rel_bias"][l, :, :], w=["tb"], key="tb")
                    ph.dma("sp", mka[:, :, :], c_maskA[:, :, :], w=["mka"], key="mka")
                    ph.dma("sp", w01[:, :], c_w01[:, :], w=["w01"], key="w01")
                    ph.op("dve", lambda e: e.tensor_scalar(out=ng[:, :], in0=tb[:, 256:257], scalar1=-1.0, scalar2=None, op0=ALU.mult), r=["tb"], w=["ng"])
                    ph.op("pool", lambda e: e.memset(ext[:, :], 0.0), w=["ext"])
                    ph.op("dve", lambda e: e.tensor_scalar(out=ext[:, 127:384], in0=tb[:, :], scalar1=ng[:, 0:1], scalar2=None, op0=ALU.add), r=["tb", "ng"], w=["ext"])
                    ph.op("dve", lambda e: e.tensor_copy(out=ext[:, 0:127], in_=ext[:, 127:128].to_broadcast([4, 127])), r=["ext"], w=["ext"])
                    ph.dma("sp", Etoe[:, :], ext[:, :], r=["ext"], w=["Etoe"], key="Etoe")
                    for h in range(4):
                        ph.dma("sp", extb[:, :], Etoe[h:h + 1, :].to_broadcast([128, 1024]), r=["Etoe"], w=["extb"], key="extb")
                        ph.dma("sp", Rtoe[h, :, :], extb[:, :], r=["extb"], w=[("Rtoe", h)], key="Rtoe")
                    ph.op("pool", lambda e: e.memset(vtp[:, :, :, :], 0.0), w=["Av"])
                    ph.op("pool", lambda e: e.memset(vtsp[:, :, :, :], 0.0), w=["Avs"])
                    cnt = [0]
                    acc = 0
                    qgroupsA = [list(range(b0_, min(b0_ + 4, NBLK))) for b0_ in range(0, NBLK, 4)] + [[NBLK]]
                    for h in range(4):
                        ph.dma("sp", t7[:, :, :], rec_ap(Rtoe, h * 128 * 1024 + 127, [[1023, 128], [128, 7], [1, 128]]), r=[("Rtoe", h)], w=["t7"], key="t7")
                        ph.op("dve", lambda e: e.tensor_scalar(out=tmpa[:, :, :], in0=t7[:, 0:6, :], scalar1=w01[:, 0:1], scalar2=None, op0=ALU.mult), r=["t7", "w01"], w=["tmpa"])
                        ph.op("dve", lambda e: e.scalar_tensor_tensor(out=tmpa[:, :, :], in0=t7[:, 1:7, :], scalar=w01[:, 1:2], in1=tmpa[:, :, :], op0=ALU.mult, op1=ALU.add), r=["t7", "w01", "tmpa"], w=["tmpa"])
                        ph.op("dve", lambda e, h=h: e.tensor_tensor(out=biasA[:, h, :, :], in0=tmpa[:, :, :], in1=mka[:, :, :], op=ALU.add), r=["tmpa", "mka"], w=["corr"])
                        ph.op("dve", lambda e, h=h: e.tensor_copy(out=biasS[:, h, :, :], in_=t7[:, 1:6, 0:64]), r=["t7"], w=["corr"])
                        par = h % 2
                        ph.dma("sp", kt[0:64, :, :], rec_ap(kdst, O_KA + h * 128, [[4 * 128, 64], [RBE, NKB], [1, 128]]), w=["Ak"], key=("Kk", 0))
                        ph.dma("sp", kts[0:64, :, :], rec_ap(srec, (NCB - 4) * RBE + O_KA + h * 128, [[4 * 128, 64], [RBE, 5], [1, 128]]), w=["Aks"], key=("Kks", 0))
                        ph.dma("sp", qt[:, :].rearrange("p (b t) -> p b t", b=NB), qa_d[:, :, h, :].rearrange("b d t -> d b t"), w=["qt"], key=("Kq", 0))
                        ph.dma("act", vtp[:, par, :, par * 64:par * 64 + 64], rec_ap(kdst, O_VA + h * 64, [[256, 128], [RBE, NKB], [1, 64]]), w=["Av"], key="Kv")
                        ph.dma("act", vtsp[:, par, :, par * 64:par * 64 + 64], rec_ap(srec, (NCB - 4) * RBE + O_VA + h * 64, [[256, 128], [RBE, 5], [1, 64]]), w=["Avs"], key="Kvs")
                        for qg in qgroupsA:
                            b0, b1 = qg[0], qg[-1]
                            tiles = []
                            if b0 < NBLK:
                                ncols = len(qg) * 128
                                q0 = b0 * 128
                                for kb in range(max(0, 2 * b0 - 4), 2 * b1 + 2):
                                    ilo = max(b0, kb // 2)
                                    ihi = min(b1, (kb + 4) // 2)
                                    corrs = []
                                    for i in range(ilo, ihi + 1):
                                        j = kb - (2 * i - 4)
                                        corrs.append(((i - b0) * 128, (i - b0 + 1) * 128, biasA[:, h, 5 - j, :]))
                                    tiles.append((kt[0:64, kb, :], vtp[:, par, kb, :], 128, (ilo - b0) * 128, corrs, (ihi - b0 + 1) * 128))
                                kres, vres = ["Ak"], ["Av"]
                            else:
                                ncols = 64
                                q0 = SC
                                for cbi in range(4):
                                    tiles.append((kts[0:64, cbi, :], vtsp[:, par, cbi, :], 128, 0, [(0, 64, biasS[:, h, 4 - cbi, :])]))
                                tiles.append((kts[0:64, 4, 0:64], vtsp[0:64, par, 4, :], 64, 0, [(0, 64, biasS[0:64, h, 0, :])]))
                                kres, vres = ["Aks"], ["Avs"]
                            a = acc % 2
                            acc += 1
                            attn_T(ph, sp3, pTb, opp, a, qt[0:64, q0:q0 + ncols], ncols, tiles, 128, cnt, kres, vres, zero_init=True)
                            ph.op("dve", lambda e, a=a, ncols=ncols: e.reciprocal(out=rsA[:, a, 0:ncols], in_=opp[:, 2 * a + 1, 0:ncols]), w=[("op", 2 * a + 1), ("rs", a)])
                            ph.op("dve", lambda e, a=a, ncols=ncols, q0=q0, h=h, par=par: e.tensor_tensor(out=OT[par * 64:par * 64 + 64, h // 2, q0:q0 + ncols], in0=opp[par * 64:par * 64 + 64, 2 * a, 0:ncols],
                                                                                                          in1=rsA[par * 64:par * 64 + 64, a, 0:ncols], op=ALU.mult),
                                  r=[("rs", a)], w=[("op", 2 * a), ("OT", h // 2)])
                    ph.run()
                if STAGE < 5:
                    return
                qgroups = [list(range(b0, min(b0 + 4, NBLK))) for b0 in range(0, NBLK, 4)] + [[NBLK]]
                with ExitStack() as es6:
                    kt2 = es6.enter_context(SB("kt", [68, 4, NKB, 128], BF16))
                    vt2 = es6.enter_context(SB("vt", [128, 2, NKB, 128], BF16))
                    kts = es6.enter_context(SB("kts", [68, 2, NSB, 128], BF16))
                    vts = es6.enter_context(SB("vts", [128, NSB, 128], BF16))
                    qt2 = es6.enter_context(SB("qt", [68, 4, NB * 128], BF16))
                    pTb = es6.enter_context(SB("pTb", [128, 3, 512], BF16))
                    corrB = es6.enter_context(SB("corrB", [128, 2, 4, 128], BF16))
                    corrBs = es6.enter_context(SB("corrBs", [64, 4, 64], BF16))
                    lamb = es6.enter_context(SB("lamb", [128, 4, 64], F32))
                    lp = es6.enter_context(SB("lp", [128, 2, 64], F32))
                    lv = es6.enter_context(SB("lv", [128, 4], F32))
                    gsc = es6.enter_context(SB("gsc", [128, 1], F32))
                    rs = es6.enter_context(SB("rs", [128, 2, 512], F32))
                    t0b = es6.enter_context(SB("t0", [128, 2, 512], F32))
                    t1b = es6.enter_context(SB("t1", [128, 2, 512], F32))
                    sqb = es6.enter_context(SB("sqb", [128, 512], BF16))
                    ph = Phase(g)
                    ph.dma("pool", corrB[:, :, :, :], c_corrB[:, :, :, :], w=["corr"], key="corrB")
                    ph.dma("pool", corrBs[:, :, :], c_corrBs[:, :, :], w=["corr"], key="corrBs")
                    ph.dma("sp", lamb[:, :, :], W["b_lambda"][l, :, :].rearrange("(o a) n -> o a n", o=1).to_broadcast([128, 4, 64]), w=["lamb"], key="lamb")
                    ph.dma("sp", gsc[:, :], W["b_sub_norm"][l, :].rearrange("(p o) -> p o", o=1), w=["gsc"], key="gsub")
                    ph.op("dve", lambda e: e.tensor_scalar(out=gsc[:, :], in0=gsc[:, :], scalar1=lamc[:, 2 * l + 1:2 * l + 2], scalar2=None, op0=ALU.mult), r=["gsc", "lamc"], w=["gsc"])
                    for k in range(2):
                        ph.op("dve", lambda e, k=k: e.tensor_tensor(out=lp[:, k, :], in0=lamb[:, 2 * k, :], in1=lamb[:, 2 * k + 1, :], op=ALU.mult), r=["lamb"], w=["lp"])
                    ph.op("dve", lambda e: e.tensor_reduce(out=lv[:, 0:2], in_=lp[:, :, :], axis=AX.X, op=ALU.add), r=["lp"], w=["lv"])
                    ph.op("act", lambda e: e.activation(out=lv[:, 0:2], in_=lv[:, 0:2], func=AF.Exp), r=["lv"], w=["lv"])
                    ph.op("dve", lambda e: e.tensor_tensor(out=lv[:, 2:3], in0=lv[:, 1:2], in1=lv[:, 0:1], op=ALU.subtract), r=["lv"], w=["lv2"])
                    ph.op("dve", lambda e: e.tensor_tensor(out=lv[:, 3:4], in0=lv[:, 2:3], in1=lamc[:, 2 * l:2 * l + 1], op=ALU.subtract), r=["lv2", "lamc"], w=["lv3"])
                    cnt = [0]
                    acc = 0
                    pendingB = []
                    gcount = [0]

                    def finalB(gb, ncols, q0, h):
                        t0, t1 = t0b[:, gb, :], t1b[:, gb, :]
                        ph.op("pool", lambda e: e.tensor_tensor(out=t0[:, 0:ncols], in0=t0[:, 0:ncols], in1=t1[:, 0:ncols], op=ALU.add), r=[("t0", gb), ("t1", gb)], w=[("t0", gb)])
                        ph.op("act", lambda e: e.activation(out=sqb[:, 0:ncols], in_=t0[:, 0:ncols], func=AF.Square), r=[("t0", gb)], w=["sqb"])
                        ph.op("pe", lambda e: e.matmul(pss[:, 0, 0:ncols], lhsT=ones[:, :], rhs=sqb[:, 0:ncols], start=True, stop=True), r=["sqb", "ones"], w=["pss"])
                        ph.op("act", lambda e: e.activation(out=t1[:, 0:ncols], in_=pss[:, 0, 0:ncols], func=AF.Sqrt, scale=1.0 / 128, bias=EPS), w=["pss", ("t1", gb)])
                        ph.op("dve", lambda e: e.reciprocal(out=t1[:, 0:ncols], in_=t1[:, 0:ncols]), r=[("t1", gb)], w=[("t1", gb)])
                        ph.op("dve", lambda e: e.scalar_tensor_tensor(out=OT[:, 2 + h, q0:q0 + ncols], in0=t0[:, 0:ncols], scalar=gsc[:, 0:1], in1=t1[:, 0:ncols], op0=ALU.mult, op1=ALU.mult),
                              r=[("t0", gb), ("t1", gb), "gsc"], w=[("OT", 2 + h)])

                    def loadB(h):
                        sl = h % 2
                        for jj in range(2):
                            ph.dma("sp", kt2[0:68, 2 * sl + jj, :, :], rec_ap(kdst, O_KB + (h * 2 + jj) * 128, [[8 * 128, 68], [RBE, NKB], [1, 128]]), w=[("Bk", sl)], key=("Kk", jj, sl))
                            ph.dma("sp", qt2[:, 2 * sl + jj, :].rearrange("p (b t) -> p b t", b=NB), qb_d[:, :, 2 * h + jj, :].rearrange("b d t -> d b t"), w=[("qt", sl)], key=("Kq", jj, sl))
                        ph.dma("sp", vt2[:, sl, :, :], rec_ap(kdst, O_VB + h * 128, [[512, 128], [RBE, NKB], [1, 128]]), w=[("Bv", sl)], key=("Kv", sl))

                    loadB(0)
                    for h in range(4):
                        sl = h % 2
                        kt = kt2[:, 2 * sl:2 * sl + 2, :, :]
                        vt = vt2[:, sl, :, :]
                        qt = qt2[:, 2 * sl:2 * sl + 2, :]
                        if h + 1 < 4:
                            loadB(h + 1)
                        for jj in range(2):
                            ph.dma("sp", kts[0:68, jj, :, :], rec_ap(srec, O_KB + (h * 2 + jj) * 128, [[8 * 128, 68], [RBE, NSB], [1, 128]]), w=["Bks"], key=("Kks", jj))
                        ph.dma("sp", vts[:, :, :], rec_ap(srec, O_VB + h * 128, [[512, 128], [RBE, NSB], [1, 128]]), w=["Bvs"], key="Kvs")
                        for qg in qgroups:
                            b0 = qg[0]
                            if b0 < NBLK:
                                ncols = len(qg) * 128
                                q0 = b0 * 128
                            else:
                                ncols = 64
                                q0 = SC
                            gb = gcount[0] % 2
                            gcount[0] += 1
                            t0, t1 = t0b[:, gb, :], t1b[:, gb, :]
                            for j in range(2):
                                tiles = []
                                if b0 < NBLK:
                                    for kb in range(2 * qg[-1] + 2):
                                        imin = max(b0, kb // 2)
                                        c_lo = (imin - b0) * 128
                                        corrs = []
                                        if kb // 2 >= b0:
                                            lo = (kb // 2 - b0) * 128
                                            corrs = [(lo, lo + 128, corrB[:, kb % 2, h, :])]
                                        tiles.append((kt[0:68, j, kb, :], vt[:, kb, :], 128, c_lo, corrs))
                                    kres, vres = [("Bk", sl)], [("Bv", sl)]
                                else:
                                    for cbi in range(NCB):
                                        tiles.append((kts[0:68, j, cbi, :], vts[:, cbi, :], 128, 0, []))
                                    tiles.append((kts[0:68, j, NCB, 0:64], vts[0:64, NCB, :], 64, 0, [(0, 64, corrBs[:, h, :])]))
                                    kres, vres = ["Bks"], ["Bvs"]
                                a = acc % 2
                                acc += 1
                                attn_T(ph, sp3, pTb, opp, a, qt[0:68, j, q0:q0 + ncols], ncols, tiles, 128, cnt, kres, vres, qres=("qt", sl))
                                ph.op("dve", lambda e, a=a, j=j, ncols=ncols: e.reciprocal(out=rs[:, j, 0:ncols], in_=opp[:, 2 * a + 1, 0:ncols]), w=[("op", 2 * a + 1), ("rs", j)])
                                if j == 0:
                                    ph.op("dve", lambda e, a=a, ncols=ncols, t0=t0: e.tensor_tensor(out=t0[:, 0:ncols], in0=opp[:, 2 * a, 0:ncols], in1=rs[:, 0, 0:ncols], op=ALU.mult), r=[("rs", 0)], w=[("op", 2 * a), ("t0", gb)])
                                    if pendingB:
                                        finalB(*pendingB.pop(0))
                                else:
                                    ph.op("dve", lambda e, ncols=ncols: e.tensor_scalar(out=rs[:, 1, 0:ncols], in0=rs[:, 1, 0:ncols], scalar1=lv[:, 3:4], scalar2=None, op0=ALU.mult), r=[("rs", 1), "lv3"], w=[("rs", 1)])
                                    ph.op("dve", lambda e, a=a, ncols=ncols, t1=t1: e.tensor_tensor(out=t1[:, 0:ncols], in0=opp[:, 2 * a, 0:ncols], in1=rs[:, 1, 0:ncols], op=ALU.mult), r=[("rs", 1)], w=[("op", 2 * a), ("t1", gb)])
                            pendingB.append((gb, ncols, q0, h))
                    while pendingB:
                        finalB(*pendingB.pop(0))
                    ph.run()
                if STAGE < 6:
                    return
                wout = es4.enter_context(SB("wout", [128, 8, D], BF16))
                with ExitStack() as es7:
                    kt = es7.enter_context(SB("kt", [96, NKB, 128], BF16))
                    vtp = es7.enter_context(SB("vtp", [128, 2, NKB, 128], BF16))
                    kts = es7.enter_context(SB("kts", [96, NSB, 128], BF16))
                    vtsp = es7.enter_context(SB("vtsp", [128, 2, NSB, 128], BF16))
                    qt = es7.enter_context(SB("qt", [96, NB * 128], BF16))
                    pTb = es7.enter_context(SB("pTb", [128, 3, 512], BF16))
                    maskC = es7.enter_context(SB("maskC", [128, 2, 128], BF16))
                    rs = es7.enter_context(SB("rs", [128, 2, 512], F32))
                    ph = Phase(g)
                    ph.dma("pool", maskC[:, :, :], c_maskC[:, :, :], w=["corr"], key="maskC")
                    for c4 in range(4):
                        ph.dma("pool", wout[:, 2 * c4:2 * c4 + 2, :], W["w_out"][l, c4 * 256:(c4 + 1) * 256, :].rearrange("(c p) n -> p c n", p=128), w=["wout"], key=("wout", c4))
                    ph.op("pool", lambda e: e.memset(vtp[:, :, :, :], 0.0), w=["Cv"])
                    ph.op("pool", lambda e: e.memset(vtsp[:, :, :, :], 0.0), w=["Cvs"])
                    cnt = [0]
                    acc = 0
                    for h in range(4):
                        par = h % 2
                        ph.dma("sp", kt[0:96, :, :], rec_ap(kdst, O_KC + h * 128, [[4 * 128, 96], [RBE, NKB], [1, 128]]), w=["Ck"], key=("Kk", 0))
                        ph.dma("sp", kts[0:96, :, :], rec_ap(srec, O_KC + h * 128, [[4 * 128, 96], [RBE, NSB], [1, 128]]), w=["Cks"], key=("Kks", 0))
                        ph.dma("sp", qt[:, :].rearrange("p (b t) -> p b t", b=NB), qc_d[:, :, h, :].rearrange("b d t -> d b t"), w=["qt"], key=("Kq", 0))
                        ph.dma("act", vtp[:, par, :, par * 64:par * 64 + 64], rec_ap(kdst, O_VC + h * 64, [[256, 128], [RBE, NKB], [1, 64]]), w=["Cv"], key="Kv")
                        ph.dma("act", vtsp[:, par, :, par * 64:par * 64 + 64], rec_ap(srec, O_VC + h * 64, [[256, 128], [RBE, NSB], [1, 64]]), w=["Cvs"], key="Kvs")
                        for qg in qgroups:
                            b0 = qg[0]
                            tiles = []
                            if b0 < NBLK:
                                ncols = len(qg) * 128
                                q0 = b0 * 128
                                for kb in range(2 * qg[-1] + 2):
                                    imin = max(b0, kb // 2)
                                    c_lo = (imin - b0) * 128
                                    corrs = []
                                    if kb // 2 >= b0:
                                        lo = (kb // 2 - b0) * 128
                                        corrs = [(lo, lo + 128, maskC[:, kb % 2, :])]
                                    tiles.append((kt[0:96, kb, :], vtp[:, par, kb, :], 128, c_lo, corrs))
                                kres, vres = ["Ck"], ["Cv"]
                            else:
                                ncols = 64
                                q0 = SC
                                for cbi in range(NCB):
                                    tiles.append((kts[0:96, cbi, :], vtsp[:, par, cbi, :], 128, 0, []))
                                tiles.append((kts[0:96, NCB, 0:64], vtsp[0:64, par, NCB, :], 64, 0, []))
                                kres, vres = ["Cks"], ["Cvs"]
                            a = acc % 2
                            acc += 1
                            attn_T(ph, sp3, pTb, opp, a, qt[0:96, q0:q0 + ncols], ncols, tiles, 128, cnt, kres, vres)
                            ph.op("dve", lambda e, a=a, ncols=ncols: e.reciprocal(out=rs[:, a, 0:ncols], in_=opp[:, 2 * a + 1, 0:ncols]), w=[("op", 2 * a + 1), ("rs", a)])
                            ph.op("dve", lambda e, a=a, ncols=ncols, q0=q0, h=h, par=par: e.tensor_tensor(out=OT[par * 64:par * 64 + 64, 6 + h // 2, q0:q0 + ncols], in0=opp[par * 64:par * 64 + 64, 2 * a, 0:ncols],
                                                                                                          in1=rs[par * 64:par * 64 + 64, a, 0:ncols], op=ALU.mult),
                                  r=[("rs", a)], w=[("op", 2 * a), ("OT", 6 + h // 2)])
                    ph.run()
                if STAGE < 7:
                    return
                ph = Phase(g)
                for gi, (t0_, n) in enumerate(groups):
                    for dc in range(8):
                        ob = dc % 4
                        for c in range(8):
                            ph.op("pe", lambda e, c=c, dc=dc, ob=ob, n=n, t0_=t0_: e.matmul(opp[:, ob, 0:n], lhsT=wout[:, c, dc * 128:(dc + 1) * 128], rhs=OT[:, c, t0_:t0_ + n], start=(c == 0), stop=(c == 7)),
                                  r=["wout"], w=[("op", ob)])
                        ph.op("dve", lambda e, dc=dc, ob=ob, t0_=t0_, n=n: e.tensor_tensor(out=xT[:, dc, t0_:t0_ + n], in0=opp[:, ob, 0:n], in1=xT[:, dc, t0_:t0_ + n], op=ALU.add),
                              r=[("xT", dc)], w=[("op", ob), ("xT", dc)])
                ph.run()

        def mem_attention(l):
            with ExitStack() as es8:
                hT = es8.enter_context(SB("hT", [128, 8, 512], BF16))
                xsq = es8.enter_context(SB("xsq", [128, 1, 8, 512], BF16))
                rinv = es8.enter_context(SB("rinv", [128, 1, 512], F32))
                wq = es8.enter_context(SB("wq", [128, 8, 512], BF16))
                wk = es8.enter_context(SB("wk", [128, 8, 512], BF16))
                wv = es8.enter_context(SB("wv", [128, 8, 512], BF16))
                wo = es8.enter_context(SB("wo", [128, 4, D], BF16))
                gm = es8.enter_context(SB("gm", [128, 1, D], F32))
                gkn = es8.enter_context(SB("gkn", [128, 4, 128], F32))
                mtok = es8.enter_context(SB("mtok", [128, 2, D], F32))
                mnb = es8.enter_context(SB("mnb", [128, 2, D], BF16))
                mnT = es8.enter_context(SB("mnT", [128, 8, 256], BF16))
                kf = es8.enter_context(SB("kf", [128, 2, 512], F32))
                vf = es8.enter_context(SB("vf", [128, 2, 512], F32))
                kb16 = es8.enter_context(SB("kb16", [128, 512], BF16))
                mK = es8.enter_context(SB("mK", [128, 2, 4, 256], BF16))
                mV = es8.enter_context(SB("mV", [128, 2, 2, 4, 129], BF16))
                sq = es8.enter_context(SB("sq", [128, 1024], F32))
                ss = es8.enter_context(SB("ss", [128, 8], F32))
                qTn = es8.enter_context(SB("qTn", [128, 4, T], BF16))
                pTb = es8.enter_context(SB("pTb", [128, 2, 512], BF16))
                omT = es8.enter_context(SB("omT", [128, 4, 512], BF16))
                sqh = es8.enter_context(SB("sqh", [128, 512], BF16))
                rn = es8.enter_context(SB("rn", [128, 512], F32))
                rsm = es8.enter_context(SB("rsm", [128, 512], F32))
                gqc = es8.enter_context(SB("gqc", [128, 1], F32))
                pn = es8.enter_context(PS("pn", [128, 2, 512], F32))
                pq = es8.enter_context(PS("pq", [128, 2, 512], F32))
                sp = es8.enter_context(PS("sp", [128, 2, 512], F32))
                opp = es8.enter_context(PS("opp", [128, 2, 512], F32))
                ph = Phase(g)
                for nm, t in [("mem_w_q", wq), ("mem_w_k", wk), ("mem_w_v", wv)]:
                    for c4 in range(2):
                        ph.dma("pool", t[:, 4 * c4:4 * c4 + 4, :], W[nm][l, c4 * 512:(c4 + 1) * 512, :].rearrange("(c p) n -> p c n", p=128), w=[nm], key=(nm, c4))
                ph.dma("pool", wo[:, :, :], W["mem_w_o"][l, :, :].rearrange("(c p) n -> p c n", p=128), w=["wo"], key="wo")
                bcast_row(ph, gm[:, :, :], W["mem_norm_m"][l, :], D, 1, "gm", "gm")
                ph.dma("sp", gqc[:, :], W["mem_q_norm"][l, :].rearrange("(p o) -> p o", o=1), w=["gqc"], key="gqn")
                bcast_row(ph, gkn[:, :, :], W["mem_k_norm"][l, :], 128, 4, "gkn", "gkn")
                ph.op("dve", lambda e: e.tensor_scalar(out=gqc[:, :], in0=gqc[:, :], scalar1=128.0 ** -0.5, scalar2=None, op0=ALU.mult), r=["gqc"], w=["gqc"])
                cnt = [0]
                ph.op("pool", lambda e: e.memset(mV[:, :, :, :, 128:129], 1.0), w=["mV"])
                spb = sp[:, :, :].bitcast(BF16)
                for mb in range(2):
                    ph.dma("sp", mtok[:, mb, :], mem[mb * 128:(mb + 1) * 128, :], w=[("mtok", mb)], key=("mtok", mb))
                    rms_rows(ph, mtok[:, mb, :].rearrange("p (h d) -> p h d", h=1), 128, 1, D, gm[:, :, :], mtok[:, mb, :].rearrange("p (h d) -> p h d", h=1), sq, ss, "m", [("mtok", mb)], [("mtok", mb)], ["gm"])
                    ph.op("act", lambda e, mb=mb: e.copy(out=mnb[:, mb, :], in_=mtok[:, mb, :]), r=[("mtok", mb)], w=[("mnb", mb)])
                    for c in range(8):
                        b = c % 2
                        ph.op("pe", lambda e, c=c, b=b, mb=mb: e.transpose(out=spb[:, b, 0:128], in_=mnb[:, mb, c * 128:(c + 1) * 128], identity=idb[:, :]), r=[("mnb", mb), "idb"], w=[("sp", b)])
                        ph.op("dve", lambda e, c=c, b=b, mb=mb: e.tensor_copy(out=mnT[:, c, mb * 128:(mb + 1) * 128], in_=spb[:, b, 0:128]), w=[("sp", b), "mnT"])
                    for c in range(8):
                        ph.op("pe", lambda e, c=c, mb=mb: e.matmul(pq[:, 0, :], lhsT=mnT[:, c, mb * 128:(mb + 1) * 128], rhs=wk[:, c, :], start=(c == 0), stop=(c == 7)), r=["mnT", "mem_w_k"], w=[("pq", 0)])
                    for c in range(8):
                        ph.op("pe", lambda e, c=c, mb=mb: e.matmul(pq[:, 1, :], lhsT=mnT[:, c, mb * 128:(mb + 1) * 128], rhs=wv[:, c, :], start=(c == 0), stop=(c == 7)), r=["mnT", "mem_w_v"], w=[("pq", 1)])
                    rms_rows(ph, pq[:, 0, :].rearrange("p (h d) -> p h d", h=4), 128, 4, 128, gkn[:, :, :], kf[:, mb, :].rearrange("p (h d) -> p h d", h=4), sq, ss, "mk", [("pq", 0)], [("kf", mb)], ["gkn"])
                    ph.op("act", lambda e, mb=mb: e.copy(out=vf[:, mb, :], in_=pq[:, 1, :]), w=[("pq", 1), ("vf", mb)])
                    ph.dma("sp", o_mk[l, mb * 128:(mb + 1) * 128, :], kf[:, mb, :], r=[("kf", mb)], key=("o_mk", mb))
                    ph.dma("sp", o_mv[l, mb * 128:(mb + 1) * 128, :], vf[:, mb, :], r=[("vf", mb)], key=("o_mv", mb))
                for st in range(2):
                    for mb in range(2):
                        if st == 1:
                            ph.dma("sp", kf[:, mb, :], cm_k[l, mb * 128:(mb + 1) * 128, :], w=[("kf", mb)], key=("cmk", mb))
                            ph.dma("sp", vf[:, mb, :], cm_v[l, mb * 128:(mb + 1) * 128, :], w=[("vf", mb)], key=("cmv", mb))
                        ph.op("act", lambda e, mb=mb: e.copy(out=kb16[:, :], in_=kf[:, mb, :]), r=[("kf", mb)], w=["kb16"])
                        for h in range(4):
                            b = h % 2
                            ph.op("pe", lambda e, h=h, b=b: e.transpose(out=spb[:, b, 0:128], in_=kb16[:, h * 128:(h + 1) * 128], identity=idb[:, :]), r=["kb16", "idb"], w=[("sp", b)])
                            ph.op("dve", lambda e, h=h, b=b, st=st, mb=mb: e.tensor_copy(out=mK[:, st, h, mb * 128:(mb + 1) * 128], in_=spb[:, b, 0:128]), w=[("sp", b), "mK"])
                        ph.op("act", lambda e, st=st, mb=mb: e.copy(out=mV[:, st, mb, :, 0:128], in_=vf[:, mb, :].rearrange("p (h d) -> p h d", h=4)), r=[("vf", mb)], w=["mV"])
                ph.stream = 1
                for gi, (t0_, n) in enumerate(groups):
                    norm_to_hT(ph, hT, xsq, rinv, pn, l * 4 + 2, only=gi)
                    for h in range(4):
                        pb = h % 2
                        for c in range(8):
                            ph.op("pe", lambda e, c=c, pb=pb, h=h, n=n: e.matmul(opp[:, pb, 0:n], lhsT=wq[:, c, h * 128:(h + 1) * 128], rhs=hT[:, c, 0:n], start=(c == 0), stop=(c == 7)),
                                  r=[("hT", c), "mem_w_q"], w=[("op", pb)])
                        ph.op("act", lambda e, pb=pb, n=n: e.activation(out=sqh[:, 0:n], in_=opp[:, pb, 0:n], func=AF.Square), w=[("op", pb), "sqh"])
                        ph.op("pe", lambda e, n=n: e.matmul(pn[:, 0, 0:n], lhsT=ones[:, :], rhs=sqh[:, 0:n], start=True, stop=True), r=["sqh", "ones"], w=[("pn", 0)])
                        ph.op("act", lambda e, n=n: e.activation(out=rn[:, 0:n], in_=pn[:, 0, 0:n], func=AF.Sqrt, scale=1.0 / 128, bias=EPS), w=[("pn", 0), "rn"])
                        ph.op("dve", lambda e, n=n: e.reciprocal(out=rn[:, 0:n], in_=rn[:, 0:n]), r=["rn"], w=["rn"])
                        ph.op("dve", lambda e, pb=pb, h=h, n=n, t0_=t0_: e.scalar_tensor_tensor(out=qTn[:, h, t0_:t0_ + n], in0=opp[:, pb, 0:n], scalar=gqc[:, 0:1], in1=rn[:, 0:n], op0=ALU.mult, op1=ALU.mult),
                              r=["rn", "gqc"], w=[("op", pb), ("qt", gi)])
                ph.stream = 0
                ph.segment = 1
                for gi, (t0_, n) in enumerate(groups):
                    st = 0 if t0_ < SC else 1
                    for h in range(4):
                        tiles = [(mK[:, st, h, mb * 128:(mb + 1) * 128], mV[:, st, mb, h, 0:128], 128, 0, []) for mb in range(2)]
                        attn_T(ph, sp, pTb, opp, 0, qTn[:, h, t0_:t0_ + n], n, tiles, 128, cnt, ["mK"], ["mV"], qres=("qt", gi))
                        ph.op("dve", lambda e, n=n: e.reciprocal(out=rsm[:, 0:n], in_=opp[:, 1, 0:n]), w=[("op", 1), "rsm"])
                        ph.op("dve", lambda e, h=h, n=n: e.tensor_tensor(out=omT[:, h, 0:n], in0=opp[:, 0, 0:n], in1=rsm[:, 0:n], op=ALU.mult), r=["rsm"], w=[("op", 0), "omT"])
                    for dc in range(8):
                        pb2 = dc % 2
                        for c in range(4):
                            ph.op("pe", lambda e, c=c, dc=dc, pb2=pb2, n=n: e.matmul(pn[:, pb2, 0:n], lhsT=wo[:, c, dc * 128:(dc + 1) * 128], rhs=omT[:, c, 0:n], start=(c == 0), stop=(c == 3)), r=["omT", "wo"], w=[("pn", pb2)])
                        ph.op("dve", lambda e, dc=dc, pb2=pb2, t0_=t0_, n=n: e.tensor_tensor(out=xT[:, dc, t0_:t0_ + n], in0=pn[:, pb2, 0:n], in1=xT[:, dc, t0_:t0_ + n], op=ALU.add), r=[("xT", dc)], w=[("pn", pb2), ("xT", dc)])
                ph.run()

        def rms_rows_multi(ph, items):
            for (src, nb, H, dh, gain, out, sqv, ssv, tag, src_res, out_res, gain_res) in items:
                ph.op("act", lambda e, src=src, sqv=sqv, nb=nb, H=H, dh=dh: e.activation(out=sqv[:nb, 0:H * dh].rearrange("p (h d) -> p h d", h=H), in_=src, func=AF.Square), r=[], w=list(src_res) + [("sq", tag)])
            for (src, nb, H, dh, gain, out, sqv, ssv, tag, src_res, out_res, gain_res) in items:
                ph.op("dve", lambda e, sqv=sqv, ssv=ssv, nb=nb, H=H, dh=dh: e.tensor_reduce(out=ssv[:nb, 0:H], in_=sqv[:nb, 0:H * dh].rearrange("p (h d) -> p h d", h=H), axis=AX.X, op=ALU.add), r=[("sq", tag)], w=[("ss", tag)])
            for (src, nb, H, dh, gain, out, sqv, ssv, tag, src_res, out_res, gain_res) in items:
                ph.op("act", lambda e, ssv=ssv, nb=nb, H=H, dh=dh: e.activation(out=ssv[:nb, 0:H], in_=ssv[:nb, 0:H], func=AF.Sqrt, scale=1.0 / dh, bias=EPS), r=[("ss", tag)], w=[("ss", tag)])
            for (src, nb, H, dh, gain, out, sqv, ssv, tag, src_res, out_res, gain_res) in items:
                ph.op("dve", lambda e, ssv=ssv, nb=nb, H=H: e.reciprocal(out=ssv[:nb, 0:H], in_=ssv[:nb, 0:H]), r=[("ss", tag)], w=[("ss", tag)])
            for (src, nb, H, dh, gain, out, sqv, ssv, tag, src_res, out_res, gain_res) in items:
                ph.op("dve", lambda e, src=src, out=out, ssv=ssv, nb=nb, H=H, dh=dh: e.tensor_tensor(out=out, in0=src, in1=ssv[:nb, 0:H].unsqueeze(2).to_broadcast([nb, H, dh]), op=ALU.mult), r=[("ss", tag)], w=list(src_res) + list(out_res))
            for (src, nb, H, dh, gain, out, sqv, ssv, tag, src_res, out_res, gain_res) in items:
                ph.op("dve", lambda e, out=out, gain=gain: e.tensor_tensor(out=out, in0=out, in1=gain, op=ALU.mult), r=list(out_res) + list(gain_res), w=list(out_res))

        for l in range(DEPTH):
            es_win = ExitStack()
            win = es_win.enter_context(SB("win", [128, 8, IN_COLS], BF16))

            def pf_win(ph, c4, l=l, win=win):
                ph.dma("pool", win[:, 2 * c4:2 * c4 + 2, :], W["w_in"][l, c4 * 256:(c4 + 1) * 256, :].rearrange("(c p) n -> p c n", p=128), w=["win"], key=("win", c4))
            ffn(l, 1, prefetch=pf_win)
            if STAGE < 2:
                continue

            with ExitStack() as es9:
                hT = es9.enter_context(SB("hT", [128, 8, 512], BF16))
                wuq = es9.enter_context(SB("wuq", [128, 3, 384], BF16))
                wukv = es9.enter_context(SB("wukv", [128, 2, 512], BF16))
                gall = es9.enter_context(SB("gall", [128, 28, 64], F32))
                gcq = es9.enter_context(SB("gcq", [128, 1, 384], F32))
                gckv = es9.enter_context(SB("gckv", [128, 1, 256], F32))
                gq96 = es9.enter_context(SB("gq96", [128, 4, 96], F32))
                gk96 = es9.enter_context(SB("gk96", [128, 4, 96], F32))
                Nf = es9.enter_context(SB("Nf", [128, 1, IN_COLS], F32))
                tokb2 = es9.enter_context(SB("tokb2", [128, 2464], BF16))
                stg2 = es9.enter_context(SB("stg2", [128, 2048], BF16))
                latT2 = es9.enter_context(SB("latT2", [128, 2, 128], BF16))
                kcf2 = es9.enter_context(SB("kcf2", [128, 4, 96], F32))
                sq2 = es9.enter_context(SB("sq2", [128, 384], F32))
                ss2 = es9.enter_context(SB("ss2", [128, 8], F32))
                ak2 = es9.enter_context(SB("ak2", [128, 32], F32))
                kp2 = es9.enter_context(SB("kp2", [128, 32], F32))
                sq = es9.enter_context(SB("sq", [128, 2560], F32))
                ss = es9.enter_context(SB("ss", [128, 32], F32))
                cs = es9.enter_context(SB("cs", [128, 32], F32))
                aq = es9.enter_context(SB("aq", [128, 32], F32))
                ak = es9.enter_context(SB("ak", [128, 32], F32))
                tokb = es9.enter_context(SB("tokb", [128, 3360], BF16))
                latT = es9.enter_context(SB("latT", [128, 5, 128], BF16))
                qcf = es9.enter_context(SB("qcf", [128, 4, 96], F32))
                kcf = es9.enter_context(SB("kcf", [128, 4, 96], F32))
                rt = es9.enter_context(SB("rt", [128, 4, 16], F32))
                rt2 = es9.enter_context(SB("rt2", [128, 4, 16], F32))
                stg = es9.enter_context(SB("stg", [128, 2, 2816], BF16))
                xsq = es9.enter_context(SB("xsq", [128, 1, 8, 512], BF16))
                rinv = es9.enter_context(SB("rinv", [128, 1, 512], F32))
                pu = es9.enter_context(PS("pu", [128, 6, 512], F32))
                pn = es9.enter_context(PS("pn", [128, 2, 512], F32))
                ptr = pu[:, :, :].bitcast(BF16)
                ph = Phase(g)
                ph.dma("pool", wuq[:, :, :], W["c_w_uq"][l, :, :].rearrange("(c p) n -> p c n", p=128), w=["wuq"], key="wuq")
                ph.dma("pool", wukv[:, :, :], W["c_w_ukv"][l, :, :].rearrange("(c p) n -> p c n", p=128), w=["wukv"], key="wukv")
                ph.op("pool", lambda e: e.memset(gall[:, 8:12, :], 1.0), w=["gall"])
                bcast_row(ph, gall[:, 0:4, :], W["a_q_norm"][l, :], 64, 4, "g0", "gall")
                bcast_row(ph, gall[:, 4:8, :], W["a_k_norm"][l, :], 64, 4, "g1", "gall")
                bcast_row(ph, gall[:, 12:20, :], W["b_q_norm"][l, :], 64, 8, "g2", "gall")
                bcast_row(ph, gall[:, 20:28, :], W["b_k_norm"][l, :], 64, 8, "g3", "gall")
                bcast_row(ph, gcq[:, :, :], W["c_q_lat_norm"][l, :], 384, 1, "g4", "gcq")
                bcast_row(ph, gckv[:, :, :], W["c_kv_lat_norm"][l, :], 256, 1, "g5", "gckv")
                bcast_row(ph, gq96[:, :, :], W["c_q_norm"][l, :], 96, 4, "g6", "gq96")
                bcast_row(ph, gk96[:, :, :], W["c_k_norm"][l, :], 96, 4, "g7", "gk96")
                ph.op("dve", lambda e: e.tensor_scalar(out=gall[:, 0:4, :], in0=gall[:, 0:4, :], scalar1=0.125, scalar2=None, op0=ALU.mult), r=["gall"], w=["gall"])
                ph.op("dve", lambda e: e.tensor_scalar(out=gall[:, 12:20, :], in0=gall[:, 12:20, :], scalar1=0.125, scalar2=None, op0=ALU.mult), r=["gall"], w=["gall"])
                ph.op("dve", lambda e: e.tensor_scalar(out=gq96[:, :, :], in0=gq96[:, :, :], scalar1=96.0 ** -0.5, scalar2=None, op0=ALU.mult), r=["gq96"], w=["gq96"])

                def kv_tail(ph, S, si, nb, rec_t, rec_base, bs, rr=None):
                    P_ = bs["pfx"]
                    tokb_, stg_, latT_, kcf_, sq_, ss_, ak_ = bs["tokb"], bs["stg"], bs["latT"], bs["kcf"], bs["sq"], bs["ss"], bs["ak"]
                    pA, rA = bs["pA"]
                    (pB0, rB0), (pB1, rB1) = bs["pB"]
                    pL, rL = bs["pL"]
                    pKV, rKV = bs["pKV"]
                    pC, rC = bs["pC"]
                    kpsrc = bs["kp"]
                    ks = bs["ksfx"]
                    if S is not None:
                        srcr = [("src", id(S), si)]
                        ph.op("act", lambda e: e.copy(out=tokb_[:nb, 0:512], in_=S[:nb, si, 256:768]), r=srcr, w=[P_ + "tokb_a"])
                        ph.op("act", lambda e: e.copy(out=tokb_[:nb, 512:512 + 544].rearrange("p (a d) -> p a d", a=8)[:, :, 0:64], in_=S[:nb, si, 1280:1792].rearrange("p (a d) -> p a d", a=8)), r=srcr, w=[P_ + "tokb_b"])
                        ph.op("act", lambda e: e.copy(out=tokb_[:nb, 1056:1568], in_=S[:nb, si, 1792:2304]), r=srcr, w=[P_ + "tokb_bv"])
                        ph.op("act", lambda e: e.copy(out=tokb_[:nb, 1568:1824], in_=S[:nb, si, 2688:2944]), r=srcr, w=[P_ + "tokb_c"])
                    for h in range(4):
                        ph.op("pe", lambda e, h=h: e.transpose(out=pA[0:64, h * 128:h * 128 + nb], in_=tokb_[:nb, h * 64:(h + 1) * 64], identity=idb[:nb, :nb]),
                              r=[P_ + "tokb_a", "idb"], w=[rA])
                    ph.op("dve", lambda e: e.tensor_copy(out=stg_[0:64, 0:512].rearrange("p (h t) -> p h t", h=4)[:, :, 0:nb], in_=pA[0:64, 0:512].rearrange("p (h t) -> p h t", h=4)[:, :, 0:nb]),
                          w=[rA, P_ + "stg_ka"])
                    ph.dma("sp", rec_ap(rec_t, rec_base + O_KA, [[512, 64], [128, 4], [1, nb]]), stg_[0:64, 0:512].rearrange("p (h t) -> p h t", h=4)[:, :, 0:nb], r=[P_ + "stg_ka"], w=([(rr, 0)] if rr is not None else []), key="st_ka" + ks)
                    ph.dma("act", rec_ap(rec_t, rec_base + O_VA, [[256, nb], [1, 256]]), tokb_[:nb, 256:512], r=[P_ + "tokb_a"], w=([(rr, 1)] if rr is not None else []), key="st_va" + ks)
                    ph.op("dve", lambda e: e.tensor_copy(out=tokb_[:nb, 512:512 + 544].rearrange("p (a d) -> p a d", a=8)[:, :, 64:68], in_=ak_[:nb, :].rearrange("p (a d) -> p a d", a=8)), r=[P_ + "ak"], w=[P_ + "tokb_b"])
                    for hj in range(8):
                        pBx, rBx = (pB0, rB0) if hj < 4 else (pB1, rB1)
                        col = (hj % 4) * 128
                        ph.op("pe", lambda e, hj=hj, pBx=pBx, col=col: e.transpose(out=pBx[0:68, col:col + nb], in_=tokb_[:nb, 512 + hj * 68:512 + (hj + 1) * 68], identity=idb[:nb, :nb]),
                              r=[P_ + "tokb_b", "idb"], w=[rBx])
                    for half, (pBx, rBx) in enumerate([(pB0, rB0), (pB1, rB1)]):
                        ph.op("dve", lambda e, half=half, pBx=pBx: e.tensor_copy(out=stg_[0:68, 512 + half * 512:1024 + half * 512].rearrange("p (h t) -> p h t", h=4)[:, :, 0:nb],
                                                                                  in_=pBx[0:68, 0:512].rearrange("p (h t) -> p h t", h=4)[:, :, 0:nb]),
                              w=[rBx, P_ + "stg_kb"])
                    ph.dma("sp", rec_ap(rec_t, rec_base + O_KB, [[1024, 68], [128, 8], [1, nb]]), stg_[0:68, 512:1536].rearrange("p (h t) -> p h t", h=8)[:, :, 0:nb], r=[P_ + "stg_kb"], w=([(rr, 2)] if rr is not None else []), key="st_kb" + ks)
                    ph.dma("act", rec_ap(rec_t, rec_base + O_VB, [[512, nb], [1, 512]]), tokb_[:nb, 1056:1568], r=[P_ + "tokb_bv"], w=([(rr, 3)] if rr is not None else []), key="st_vb" + ks)
                    for c in range(2):
                        ph.op("pe", lambda e, c=c: e.transpose(out=pL[:, c * 128:c * 128 + nb], in_=tokb_[:nb, 1568 + c * 128:1568 + (c + 1) * 128], identity=idb[:nb, :nb]),
                              r=[P_ + "tokb_c", "idb"], w=[rL])
                    ph.op("dve", lambda e: e.tensor_copy(out=latT_[:, 0:2, 0:nb], in_=pL[:, 0:256].rearrange("p (c t) -> p c t", c=2)[:, :, 0:nb]), w=[rL, P_ + "latT_kv"])
                    for c in range(2):
                        ph.op("pe", lambda e, c=c: e.matmul(pKV[:nb, 0:512], lhsT=latT_[:, c, 0:nb], rhs=wukv[:, c, :], start=(c == 0), stop=(c == 1)),
                              r=[P_ + "latT_kv", "wukv"], w=[rKV])
                    ph.op("act", lambda e: e.copy(out=kcf_[:nb, :, 0:64], in_=pKV[:nb, 0:512].rearrange("p (h d) -> p h d", h=4)[:, :, 0:64]), w=[rKV, P_ + "kcf"])
                    ph.op("pool", lambda e: e.tensor_copy(out=kcf_[:nb, :, 64:96], in_=kpsrc.unsqueeze(1).to_broadcast([nb, 4, 32])), r=bs["kpres"], w=[P_ + "kcf"])
                    ph.op("act", lambda e: e.copy(out=tokb_[:nb, 1824:2080].rearrange("p (h d) -> p h d", h=4), in_=pKV[:nb, 0:512].rearrange("p (h d) -> p h d", h=4)[:, :, 64:128]), w=[rKV, P_ + "tokb_cv"])
                    rms_rows(ph, kcf_[:nb, :, :], nb, 4, 96, gk96[:nb, :, :], kcf_[:nb, :, :], sq_, ss_, bs["sqtag"], [P_ + "kcf"], [P_ + "kcf"], ["gk96"])
                    ph.op("act", lambda e: e.copy(out=tokb_[:nb, 2080:2464].rearrange("p (h d) -> p h d", h=4), in_=kcf_[:nb, :, :]), r=[P_ + "kcf"], w=[P_ + "tokb_ck"])
                    for h in range(4):
                        ph.op("pe", lambda e, h=h: e.transpose(out=pC[0:96, h * 128:h * 128 + nb], in_=tokb_[:nb, 2080 + h * 96:2080 + (h + 1) * 96], identity=idb[:nb, :nb]),
                              r=[P_ + "tokb_ck", "idb"], w=[rC])
                    ph.op("dve", lambda e: e.tensor_copy(out=stg_[0:96, 1536:2048].rearrange("p (h t) -> p h t", h=4)[:, :, 0:nb], in_=pC[0:96, 0:512].rearrange("p (h t) -> p h t", h=4)[:, :, 0:nb]),
                          w=[rC, P_ + "stg_kc"])
                    ph.dma("sp", rec_ap(rec_t, rec_base + O_KC, [[512, 96], [128, 4], [1, nb]]), stg_[0:96, 1536:2048].rearrange("p (h t) -> p h t", h=4)[:, :, 0:nb], r=[P_ + "stg_kc"], w=([(rr, 4)] if rr is not None else []), key="st_kc" + ks)
                    ph.dma("act", rec_ap(rec_t, rec_base + O_VC, [[256, nb], [1, 256]]), tokb_[:nb, 1824:2080], r=[P_ + "tokb_cv"], w=([(rr, 5)] if rr is not None else []), key="st_vc" + ks)

                pnb = pn[:, :, :].bitcast(BF16)
                bs1 = dict(pfx="s1", tokb=tokb, stg=stg[:, 0, :], latT=latT, kcf=kcf, sq=sq[:, 2176:2560], ss=ss[:, 26:30], sqtag="x1", ak=ak, ksfx="",
                           pA=(ptr[:, 0, 0:512], ("pu", 0)), pB=((ptr[:, 1, 0:512], ("pu", 1)), (ptr[:, 2, 0:512], ("pu", 2))),
                           pL=(ptr[:, 3, 0:256], ("pu", 3)), pKV=(pu[:, 3, 0:512], ("pu", 3)), pC=(ptr[:, 4, 0:512], ("pu", 4)))
                bs2 = dict(pfx="s2", tokb=tokb2, stg=stg2, latT=latT2, kcf=kcf2, sq=sq2, ss=ss2, sqtag="s2k", ak=ak2, ksfx="2",
                           pA=(pnb[:, 0, 0:512], ("pn", 0)), pB=((pnb[:, 1, 0:512], ("pn", 1)), (pnb[:, 1, 512:1024], ("pn", 1))),
                           pL=(pnb[:, 0, 512:768], ("pn", 0)), pKV=(pn[:, 0, 0:512], ("pn", 0)), pC=(pnb[:, 1, 0:512], ("pn", 1)),
                           kp=kp2[:, :], kpres=["s2kp"])

                def proj_block(bi, hc0):
                    c0, nb = blk_cols(bi)
                    si = 0
                    nres = [("src", id(Nf), si)]
                    ph.dma("act", cs[:nb, :], c_cs[c0:c0 + nb, :], w=["cs"], key="cs")
                    ph.dma("act", aq[:nb, :], c_augq[c0:c0 + nb, :], w=["aq"], key="aq")
                    ph.dma("act", ak[:nb, :], c_augk[c0:c0 + nb, :], w=["s1ak"], key="ak")
                    for c in range(8):
                        for cg in range(6):
                            w0 = cg * 512
                            wn = min(512, IN_COLS - w0)
                            ph.op("pe", lambda e, c=c, cg=cg, w0=w0, wn=wn: e.matmul(pu[:nb, cg, 0:wn], lhsT=hT[:, c, hc0:hc0 + nb], rhs=win[:, c, w0:w0 + wn], start=(c == 0), stop=(c == 7)),
                                  r=[("hT", c), "win"], w=[("pu", cg)])
                    pur = [("pu", k) for k in range(6)]
                    puf = pu[:nb, :, :].rearrange("p a b -> p (a b)")
                    ph.op("act", lambda e: e.copy(out=Nf[:nb, si, :], in_=puf[:, 0:IN_COLS]), w=pur + nres)
                    rms_rows_multi(ph, [
                        (puf[:, 0:512].rearrange("p (h d) -> p h d", h=8), nb, 8, 64, gall[:nb, 0:8, :], Nf[:nb, si, 0:512].rearrange("p (h d) -> p h d", h=8), sq[:, 0:512], ss[:, 0:8], "a", pur, nres, ["gall"]),
                        (puf[:, 768:1792].rearrange("p (h d) -> p h d", h=16), nb, 16, 64, gall[:nb, 12:28, :], Nf[:nb, si, 768:1792].rearrange("p (h d) -> p h d", h=16), sq[:, 512:1536], ss[:, 8:24], "b", pur, nres, ["gall"]),
                        (puf[:, 2304:2688].rearrange("p (h d) -> p h d", h=1), nb, 1, 384, gcq[:nb, :, :], Nf[:nb, si, 2304:2688].rearrange("p (h d) -> p h d", h=1), sq[:, 1536:1920], ss[:, 24:25], "cq", pur, nres, ["gcq"]),
                        (puf[:, 2688:2944].rearrange("p (h d) -> p h d", h=1), nb, 1, 256, gckv[:nb, :, :], Nf[:nb, si, 2688:2944].rearrange("p (h d) -> p h d", h=1), sq[:, 1920:2176], ss[:, 25:26], "ckv", pur, nres, ["gckv"]),
                    ])
                    kp = puf[:, 2944:2976]
                    ph.op("dve", lambda e: e.tensor_tensor(out=rt[:nb, 0, :], in0=kp[:, 0:16], in1=cs[:nb, 0:16], op=ALU.mult), r=["cs"], w=pur + ["rt"])
                    ph.op("dve", lambda e: e.tensor_tensor(out=rt[:nb, 1, :], in0=kp[:, 16:32], in1=cs[:nb, 16:32], op=ALU.mult), r=["cs"], w=pur + ["rt"])
                    ph.op("dve", lambda e: e.tensor_tensor(out=Nf[:nb, si, 2944:2960], in0=rt[:nb, 0, :], in1=rt[:nb, 1, :], op=ALU.subtract), r=["rt"], w=nres)
                    ph.op("dve", lambda e: e.tensor_tensor(out=rt[:nb, 2, :], in0=kp[:, 0:16], in1=cs[:nb, 16:32], op=ALU.mult), r=["cs"], w=pur + ["rt"])
                    ph.op("dve", lambda e: e.tensor_tensor(out=rt[:nb, 3, :], in0=kp[:, 16:32], in1=cs[:nb, 0:16], op=ALU.mult), r=["cs"], w=pur + ["rt"])
                    ph.op("dve", lambda e: e.tensor_tensor(out=Nf[:nb, si, 2960:2976], in0=rt[:nb, 2, :], in1=rt[:nb, 3, :], op=ALU.add), r=["rt"], w=nres)
                    ph.dma("sp", o_bk[l, c0:c0 + nb, :], Nf[:nb, si, 1280:1792], r=nres, key="o_bk")
                    ph.dma("sp", o_bv[l, c0:c0 + nb, :], Nf[:nb, si, 1792:2304], r=nres, key="o_bv")
                    ph.dma("sp", o_ckv[l, c0:c0 + nb, :], Nf[:nb, si, 2688:2944], r=nres, key="o_ckv")
                    ph.dma("sp", o_kpe[l, c0:c0 + nb, :], Nf[:nb, si, 2944:2976], r=nres, key="o_kpe")
                    if bi >= NBLK - 2:
                        r0 = (bi - (NBLK - 2)) * 128
                        ph.dma("sp", o_ak[l, r0:r0 + nb, :], Nf[:nb, si, 256:512], r=nres, key="o_ak")
                        ph.dma("sp", o_av[l, r0:r0 + nb, :], Nf[:nb, si, 512:768], r=nres, key="o_av")
                    ph.op("act", lambda e: e.copy(out=tokb[:nb, 2464:2720], in_=Nf[:nb, si, 0:256]), r=nres, w=["tokb_qa"])
                    for h in range(4):
                        ph.op("pe", lambda e, h=h: e.transpose(out=ptr[0:64, 5, h * 128:h * 128 + nb], in_=tokb[:nb, 2464 + h * 64:2464 + (h + 1) * 64], identity=idb[:nb, :nb]),
                              r=["tokb_qa", "idb"], w=[("pu", 5)])
                    ph.op("dve", lambda e: e.tensor_copy(out=stg[0:64, 1, 0:512].rearrange("p (h t) -> p h t", h=4)[:, :, 0:nb], in_=ptr[0:64, 5, 0:512].rearrange("p (h t) -> p h t", h=4)[:, :, 0:nb]),
                          w=[("pu", 5), "stg_qa"])
                    ph.dma("sp", qa_d[bi, :, :, 0:nb], stg[0:64, 1, 0:512].rearrange("p (h t) -> p h t", h=4)[:, :, 0:nb], r=["stg_qa"], key="st_qa")
                    ph.op("act", lambda e: e.copy(out=tokb[:nb, 2720:3264].rearrange("p (a d) -> p a d", a=8)[:, :, 0:64], in_=Nf[:nb, si, 768:1280].rearrange("p (a d) -> p a d", a=8)), r=nres, w=["tokb_qb"])
                    ph.op("dve", lambda e: e.tensor_copy(out=tokb[:nb, 2720:3264].rearrange("p (a d) -> p a d", a=8)[:, :, 64:68], in_=aq[:nb, :].rearrange("p (a d) -> p a d", a=8)), r=["aq"], w=["tokb_qb"])
                    for hj in range(8):
                        bank, col = hj // 4, (hj % 4) * 128
                        ph.op("pe", lambda e, hj=hj, bank=bank, col=col: e.transpose(out=ptr[0:68, bank, col:col + nb], in_=tokb[:nb, 2720 + hj * 68:2720 + (hj + 1) * 68], identity=idb[:nb, :nb]),
                              r=["tokb_qb", "idb"], w=[("pu", bank)])
                    for half in range(2):
                        ph.op("dve", lambda e, half=half: e.tensor_copy(out=stg[0:68, 1, 512 + half * 512:1024 + half * 512].rearrange("p (h t) -> p h t", h=4)[:, :, 0:nb],
                                                                         in_=ptr[0:68, half, 0:512].rearrange("p (h t) -> p h t", h=4)[:, :, 0:nb]),
                              w=[("pu", half), "stg_qb"])
                    ph.dma("sp", qb_d[bi, :, :, 0:nb], stg[0:68, 1, 512:1536].rearrange("p (h t) -> p h t", h=8)[:, :, 0:nb], r=["stg_qb"], key="st_qb")
                    ph.op("act", lambda e: e.copy(out=stg[:nb, 1, 2048:2432], in_=Nf[:nb, si, 2304:2688]), r=nres, w=["cq_b"])
                    for c in range(3):
                        ph.op("pe", lambda e, c=c: e.transpose(out=ptr[:, 2, c * 128:c * 128 + nb], in_=stg[:nb, 1, 2048 + c * 128:2048 + (c + 1) * 128], identity=idb[:nb, :nb]),
                              r=["cq_b", "idb"], w=[("pu", 2)])
                    ph.op("dve", lambda e: e.tensor_copy(out=latT[:, 2:5, 0:nb], in_=ptr[:, 2, 0:384].rearrange("p (c t) -> p c t", c=3)[:, :, 0:nb]), w=[("pu", 2), "latT_q"])
                    for c in range(3):
                        ph.op("pe", lambda e, c=c: e.matmul(pu[:nb, 2, 0:384], lhsT=latT[:, 2 + c, 0:nb], rhs=wuq[:, c, :], start=(c == 0), stop=(c == 2)),
                              r=["latT_q", "wuq"], w=[("pu", 2)])
                    pq = pu[:nb, 2, 0:384].rearrange("p (h d) -> p h d", h=4)
                    ph.op("act", lambda e: e.copy(out=qcf[:nb, :, 0:64], in_=pq[:, :, 0:64]), w=[("pu", 2), "qcf"])
                    cosb = cs[:nb, 0:16].unsqueeze(1).to_broadcast([nb, 4, 16])
                    sinb = cs[:nb, 16:32].unsqueeze(1).to_broadcast([nb, 4, 16])
                    ph.op("dve", lambda e: e.tensor_tensor(out=rt[:nb, :, :], in0=pq[:, :, 64:80], in1=cosb, op=ALU.mult), r=["cs"], w=[("pu", 2), "rt"])
                    ph.op("dve", lambda e: e.tensor_tensor(out=rt2[:nb, :, :], in0=pq[:, :, 80:96], in1=sinb, op=ALU.mult), r=["cs"], w=[("pu", 2), "rt2"])
                    ph.op("dve", lambda e: e.tensor_tensor(out=qcf[:nb, :, 64:80], in0=rt[:nb, :, :], in1=rt2[:nb, :, :], op=ALU.subtract), r=["rt", "rt2"], w=["qcf"])
                    ph.op("dve", lambda e: e.tensor_tensor(out=rt[:nb, :, :], in0=pq[:, :, 64:80], in1=sinb, op=ALU.mult), r=["cs"], w=[("pu", 2), "rt"])
                    ph.op("dve", lambda e: e.tensor_tensor(out=rt2[:nb, :, :], in0=pq[:, :, 80:96], in1=cosb, op=ALU.mult), r=["cs"], w=[("pu", 2), "rt2"])
                    ph.op("dve", lambda e: e.tensor_tensor(out=qcf[:nb, :, 80:96], in0=rt[:nb, :, :], in1=rt2[:nb, :, :], op=ALU.add), r=["rt", "rt2"], w=["qcf"])
                    rms_rows(ph, qcf[:nb, :, :], nb, 4, 96, gq96[:nb, :, :], qcf[:nb, :, :], sq[:, 2176:2560], ss[:, 26:30], "x1", ["qcf"], ["qcf"], ["gq96"])
                    ph.op("act", lambda e: e.copy(out=stg[:nb, 1, 2432:2816].rearrange("p (h d) -> p h d", h=4), in_=qcf[:nb, :, :]), r=["qcf"], w=["qc_b"])
                    for h in range(4):
                        ph.op("pe", lambda e, h=h: e.transpose(out=ptr[0:96, 5, h * 128:h * 128 + nb], in_=stg[:nb, 1, 2432 + h * 96:2432 + (h + 1) * 96], identity=idb[:nb, :nb]),
                              r=["qc_b", "idb"], w=[("pu", 5)])
                    ph.op("dve", lambda e: e.tensor_copy(out=stg[0:96, 0, 2048:2560].rearrange("p (h t) -> p h t", h=4)[:, :, 0:nb], in_=ptr[0:96, 5, 0:512].rearrange("p (h t) -> p h t", h=4)[:, :, 0:nb]),
                          w=[("pu", 5), "stg_qc"])
                    ph.dma("sp", qc_d[bi, :, :, 0:nb], stg[0:96, 0, 2048:2560].rearrange("p (h t) -> p h t", h=4)[:, :, 0:nb], r=["stg_qc"], key="st_qc")
                    bs1["kp"] = Nf[:nb, si, 2944:2976]
                    bs1["kpres"] = nres
                    bs1["ak"] = ak
                    if bi < NBLK:
                        kv_tail(ph, Nf, si, nb, ksrc, bi * RBE, dict(bs1), rr=("rec", bi))
                        RB = RBE // 128
                        ph.custom("pool", lambda e: e.collective_compute("AllGather", ALU.bypass, replica_groups=[[0, 1], [2, 3], [4, 5], [6, 7]],
                                                                         ins=[ksrc[bi * RB:(bi + 1) * RB, :].opt()], outs=[kdst[2 * bi * RB:(2 * bi + 2) * RB, :].opt()]),
                                  r=[(("rec", bi), k) for k in range(6)], key="cc", inc=1)
                    else:
                        kv_tail(ph, Nf, si, nb, srec, NCB * RBE, dict(bs1))
                for gi, (gt0, gn) in enumerate(groups):
                    norm_to_hT(ph, hT, xsq, rinv, pu[:, 4:6, :], l * 4 + 1, only=gi, pres=lambda b: ("pu", 4 + b))
                    for bi in ([NBLK] if gt0 >= SC else range(gt0 // 128, (gt0 + gn) // 128)):
                        proj_block(bi, blk_cols(bi)[0] - gt0)
                ph.stream = 1
                for cb in range(NCB):
                    r0, r1 = cb * 128, (cb + 1) * 128
                    ab = cb - (NCB - 4)
                    if ab >= 0:
                        ph.dma("pool", tokb2[:, 0:256], ca_k[l, ab * 128:(ab + 1) * 128, :], w=["s2tokb_a"], key="ci0")
                        ph.dma("pool", tokb2[:, 256:512], ca_v[l, ab * 128:(ab + 1) * 128, :], w=["s2tokb_a"], key="ci1")
                    ph.dma("pool", tokb2[:, 512:512 + 544].rearrange("p (a d) -> p a d", a=8)[:, :, 0:64], cb_k[l, r0:r1, :].rearrange("p (a d) -> p a d", a=8), w=["s2tokb_b"], key="ci2")
                    ph.dma("pool", tokb2[:, 1056:1568], cb_v[l, r0:r1, :], w=["s2tokb_bv"], key="ci3")
                    ph.dma("pool", tokb2[:, 1568:1824], cc_kv[l, r0:r1, :], w=["s2tokb_c"], key="ci4")
                    ph.dma("sp", kp2[:, :], cc_kpe[l, r0:r1, :], w=["s2kp"], key="ci5")
                    ph.dma("act", ak2[:, :], c_augkc[r0:r1, :], w=["s2ak"], key="ak2")
                    kv_tail(ph, None, 0, 128, srec, cb * RBE, bs2)
                ph.stream = 0
                ph.run()
            es_win.close()

            if STAGE < 3:
                continue
            if STAGE < 4:
                continue
            attention(l, None)
            if STAGE < 8:
                continue
            mem_attention(l)
            if STAGE < 9:
                continue
            ffn(l, 2)

        with ExitStack() as es10:
            ytok = es10.enter_context(SB("ytok", [128, 2, D], F32))
            pt2 = es10.enter_context(PS("pt2", [128, 2, 512], F32))
            ph = Phase(g)
            for bi in range(NB):
                c0, nb = blk_cols(bi)
                s = bi % 2
                for c in range(8):
                    ps = c % 2
                    ph.op("pe", lambda e, c=c, ps=ps, c0=c0, nb=nb: e.transpose(out=pt2[:nb, ps, 0:128], in_=xT[:, c, c0:c0 + nb], identity=idf[:, :]),
                          r=[("xT", c), "idf"], w=[("pt2", ps)])
                    ph.op("act", lambda e, s=s, c=c, ps=ps, nb=nb: e.copy(out=ytok[:nb, s, c * 128:(c + 1) * 128], in_=pt2[:nb, ps, 0:128]),
                          w=[("pt2", ps), ("ytok", s)])
                ph.dma("sp", y[c0:c0 + nb, :], ytok[:nb, s, :], r=[("ytok", s)], key=("yst", s))
            ph.run()
    return nc


WNAMES = ["ffn1_norm", "ffn1_w_gate", "ffn1_w_up", "ffn1_w_down", "mix_norm", "w_in", "a_q_norm", "a_k_norm",
          "a_rel_bias", "b_q_norm", "b_k_norm", "b_lambda", "b_sub_norm", "c_q_lat_norm", "c_kv_lat_norm",
          "c_w_uq", "c_w_ukv", "c_q_norm", "c_k_norm", "w_out", "mem_norm_x", "mem_w_q", "mem_q_norm",
          "mem_norm_m", "mem_w_k", "mem_w_v", "mem_k_norm", "mem_w_o", "ffn2_norm", "ffn2_w_gate",
          "ffn2_w_up", "ffn2_w_down"]


def _consts(p, NBLK, DEPTH, PAST):
    SC = NBLK * 128
    T = SC + 64
    t = np.arange(SC)
    pos = np.concatenate([(2 * (t // 128) + p) * 128 + t % 128, PAST + np.arange(64)]).astype(np.int64)
    inv = (10000.0 ** (-np.arange(16, dtype=np.float32) / 16)).astype(np.float32)
    ang = pos.astype(np.float32)[:, None] * inv[None, :]
    cs = np.concatenate([np.cos(ang), np.sin(ang)], axis=1).astype(np.float32)
    slopes = np.exp2(-8.0 * (np.arange(4, dtype=np.float32) + 1.0) / 4).astype(np.float32)

    def aug(posv, qside):
        lo = (posv % 128).astype(np.float32)
        hi = (posv - posv % 128).astype(np.float32)
        out = np.zeros((len(posv), 8, 4), np.float32)
        for h in range(4):
            for j in range(2):
                if qside:
                    out[:, 2 * h + j] = np.stack([-slopes[h] * hi, -slopes[h] * lo, np.ones_like(lo), np.ones_like(lo)], 1)
                else:
                    out[:, 2 * h + j] = np.stack([np.ones_like(lo), np.ones_like(lo), slopes[h] * hi, slopes[h] * lo], 1)
        return out.reshape(len(posv), 32)

    k = np.arange(128)[:, None]
    q = np.arange(128)[None, :]
    kc, qc = k // 64, q // 64
    diag = np.zeros((4, 128, 128), np.float32)
    for h in range(4):
        d = np.where((kc == qc) & (k > q), -2.0 * slopes[h] * (k - q), 0.0)
        diag[h] = np.where(kc > qc, NEG, d)
    dmask = np.where(kc > qc, NEG, 0.0).astype(np.float32)
    full = np.full((128, 128), NEG, np.float32)
    zero = np.zeros((128, 128), np.float32)
    corrB = np.zeros((128, 2, 4, 128), np.float32)
    maskC = np.zeros((128, 2, 128), np.float32)
    for h in range(4):
        corrB[:, 0, h, :] = diag[h] if p == 0 else zero
        corrB[:, 1, h, :] = full if p == 0 else diag[h]
    maskC[:, 0, :] = dmask if p == 0 else zero
    maskC[:, 1, :] = full if p == 0 else dmask
    k64 = np.arange(64)[:, None]
    q64 = np.arange(64)[None, :]
    corrBs = np.zeros((64, 4, 64), np.float32)
    for h in range(4):
        corrBs[:, h, :] = np.where(k64 > q64, -2.0 * slopes[h] * (k64 - q64), 0.0)
    maskA = np.zeros((128, 6, 128), np.float32)
    for r in range(6):
        delta = r - 1 + p
        rel = -2 * delta + kc - qc
        maskA[:, r, :] = np.where((rel <= 0) & (rel >= -8), 0.0, NEG)
    w01 = np.tile(np.array([[1.0 - p, float(p)]], np.float32), (128, 1))
    lam = np.zeros((128, 2 * DEPTH), np.float32)
    for l in range(DEPTH):
        li = 0.8 - 0.6 * math.exp(-0.3 * l)
        lam[:, 2 * l] = li
        lam[:, 2 * l + 1] = 1.0 - li
    return dict(c_ident=np.eye(128, dtype=np.float32), c_cs=cs, c_augq=aug(pos, True), c_augk=aug(pos, False),
                c_augkc=aug(np.arange(PAST), False), c_corrB=corrB, c_corrBs=corrBs, c_maskC=maskC,
                c_maskA=maskA, c_w01=w01, c_lam=lam)


_CACHE = {}


def kernel(**inputs):
    x_prompt = np.asarray(inputs["x_prompt"], np.float32)
    x_sample = np.asarray(inputs["x_sample"], np.float32)
    B, SEQ, _ = x_prompt.shape
    DB = x_sample.shape[0]
    DEPTH = inputs["w_in"].shape[0]
    PAST = inputs["cache_b_k"].shape[2]
    assert B * 2 == 8 and DB == 8 and x_sample.shape[1] == 64 and inputs["cache_a_k"].shape[2] == 512
    NBLK = SEQ // 256
    SC = NBLK * 128
    key = (NBLK, DEPTH, PAST)
    if key not in _CACHE:
        _CACHE[key] = build(NBLK, DEPTH, PAST)
    nc = _CACHE[key]
    wts = {nm: np.ascontiguousarray(np.asarray(inputs[nm], np.float32)) for nm in WNAMES}
    in_maps = []
    for c in range(8):
        b, p = c // 2, c % 2
        xb = x_prompt[b].reshape(SEQ // 128, 128, D)[p::2].reshape(SC, D)
        m = dict(wts)
        m["xin"] = np.ascontiguousarray(np.concatenate([xb, x_sample[c]], axis=0))
        m["mem"] = np.ascontiguousarray(np.asarray(inputs["mem_prompt"], np.float32)[b])
        m["ca_k"] = np.ascontiguousarray(np.asarray(inputs["cache_a_k"], np.float32)[:, c].reshape(DEPTH, 512, 256))
        m["ca_v"] = np.ascontiguousarray(np.asarray(inputs["cache_a_v"], np.float32)[:, c].reshape(DEPTH, 512, 256))
        m["cb_k"] = np.ascontiguousarray(np.asarray(inputs["cache_b_k"], np.float32)[:, c].reshape(DEPTH, PAST, 512))
        m["cb_v"] = np.ascontiguousarray(np.asarray(inputs["cache_b_v"], np.float32)[:, c].reshape(DEPTH, PAST, 512))
        m["cc_kv"] = np.ascontiguousarray(np.asarray(inputs["cache_c_kv"], np.float32)[:, c])
        m["cc_kpe"] = np.ascontiguousarray(np.asarray(inputs["cache_c_kpe"], np.float32)[:, c])
        m["cm_k"] = np.ascontiguousarray(np.asarray(inputs["cache_mem_k"], np.float32)[:, c].reshape(DEPTH, NMEM, 512))
        m["cm_v"] = np.ascontiguousarray(np.asarray(inputs["cache_mem_v"], np.float32)[:, c].reshape(DEPTH, NMEM, 512))
        m.update(_consts(p, NBLK, DEPTH, PAST))
        in_maps.append(m)
    res = run_bass_kernel_spmd(nc, in_maps, core_ids=list(range(8))).results

    def unzig(name, width):
        out = np.zeros((DEPTH, B, SEQ // 128, 128, width), np.float32)
        for c in range(8):
            b, p = c // 2, c % 2
            out[:, b, p::2] = res[c][name][:, :SC].reshape(DEPTH, NBLK, 128, width)
        return out.reshape(DEPTH, B, SEQ, width)

    yp = np.zeros((B, SEQ // 128, 128, D), np.float32)
    ys = np.zeros((DB, 64, D), np.float32)
    for c in range(8):
        b, p = c // 2, c % 2
        yp[b, p::2] = res[c]["y"][:SC].reshape(NBLK, 128, D)
        ys[c] = res[c]["y"][SC:]
    yp = yp.reshape(B, SEQ, D)
    pak = np.zeros((DEPTH, B, 4, 128, 256), np.float32)
    pav = np.zeros((DEPTH, B, 4, 128, 256), np.float32)
    for c in range(8):
        b, p = c // 2, c % 2
        for mb in range(4):
            if mb % 2 == p:
                pak[:, b, mb] = res[c]["o_ak"][:, (mb // 2) * 128:(mb // 2 + 1) * 128]
                pav[:, b, mb] = res[c]["o_av"][:, (mb // 2) * 128:(mb // 2 + 1) * 128]
    pak = pak.reshape(DEPTH, B, 512, 4, 64)
    pav = pav.reshape(DEPTH, B, 512, 4, 64)
    pbk = unzig("o_bk", 512).reshape(DEPTH, B, SEQ, 4, 2, 64)
    pbv = unzig("o_bv", 512).reshape(DEPTH, B, SEQ, 4, 128)
    pckv = unzig("o_ckv", 256)
    pkpe = unzig("o_kpe", 32)
    pmk = np.stack([res[2 * b]["o_mk"] for b in range(B)], axis=1).reshape(DEPTH, B, NMEM, 4, 128)
    pmv = np.stack([res[2 * b]["o_mv"] for b in range(B)], axis=1).reshape(DEPTH, B, NMEM, 4, 128)
    sak = np.stack([res[c]["o_ak"][:, 256:320] for c in range(8)], axis=1).reshape(DEPTH, DB, 64, 4, 64)
    sav = np.stack([res[c]["o_av"][:, 256:320] for c in range(8)], axis=1).reshape(DEPTH, DB, 64, 4, 64)
    sbk = np.stack([res[c]["o_bk"][:, SC:] for c in range(8)], axis=1).reshape(DEPTH, DB, 64, 4, 2, 64)
    sbv = np.stack([res[c]["o_bv"][:, SC:] for c in range(8)], axis=1).reshape(DEPTH, DB, 64, 4, 128)
    sckv = np.stack([res[c]["o_ckv"][:, SC:] for c in range(8)], axis=1)
    skpe = np.stack([res[c]["o_kpe"][:, SC:] for c in range(8)], axis=1)
    return (yp, ys, pak, pav, pbk, pbv, pckv, pkpe, pmk, pmv, sak, sav, sbk, sbv, sckv, skpe)
```

```python
import math
import os
from contextlib import ExitStack
import numpy as np
import concourse.bass as bass
import concourse.mybir as mybir
from concourse.bass_utils import run_bass_kernel_spmd

F32 = mybir.dt.float32
BF16 = mybir.dt.bfloat16
ALU = mybir.AluOpType
AF = mybir.ActivationFunctionType
AX = mybir.AxisListType

D = 1024
DFF = 2816
HD = 64
NMEM = 256
EPS = 1e-6
IN_COLS = 2976
NEG = -30000.0
ENGS = ("pe", "act", "dve", "pool", "sp")

O_KA, O_VA, O_KB, O_VB, O_KC, O_VC, RBE = 0, 32768, 65536, 135168, 200704, 249856, 282624


class GSync:
    def __init__(self, nc):
        self.nc = nc
        self.esem = {e: nc.alloc_semaphore(name=f"es_{e}") for e in ("pe", "act", "dve", "pool")}
        self.ecnt = {e: 0 for e in self.esem}
        self.dsem = {}
        self.dcnt = {}
        self.kmap = {}
        self.nops = 0

    def kid(self, key):
        if key not in self.kmap:
            self.kmap[key] = len(self.kmap) % 56
        return self.kmap[key]

    def dkey(self, key):
        if key not in self.dsem:
            self.dsem[key] = self.nc.alloc_semaphore(name=f"ds_{len(self.dsem)}")
            self.dcnt[key] = 0
        return self.dsem[key]


class Phase:
    def __init__(self, g):
        self.g = g
        self.raw = []
        self.stream = 0
        self.segment = 0
        self.ops = []

    def _add(self, eng, fn, r, w, dma_key=None, inc=16):
        if dma_key is not None:
            dma_key = self.g.kid(dma_key)
        self.raw.append(dict(eng=eng, fn=fn, r=r, w=w, dma=dma_key, inc=inc, stream=self.stream, seg=self.segment))
        return len(self.raw) - 1

    def _finalize(self):
        order = []
        for seg in sorted({o["seg"] for o in self.raw}):
            streams = {}
            for o in self.raw:
                if o["seg"] == seg:
                    streams.setdefault(o["stream"], []).append(o)
            keys = sorted(streams)
            if len(keys) == 1:
                order.extend(streams[keys[0]])
                continue
            pos = {k: 0 for k in keys}
            tot = {k: len(streams[k]) for k in keys}
            while any(pos[k] < tot[k] for k in keys):
                k = min((k for k in keys if pos[k] < tot[k]), key=lambda k: (pos[k] + 1) / tot[k])
                order.append(streams[k][pos[k]])
                pos[k] += 1
        lastw, readers, lastdma = {}, {}, {}
        self.ops = []
        for raw in order:
            idx = len(self.ops)
            eng, dma_key = raw["eng"], raw["dma"]
            deps = {}
            for x in raw["r"]:
                if x in lastw:
                    deps.setdefault(lastw[x], set()).add("raw")
            for x in raw["w"]:
                if x in lastw:
                    deps.setdefault(lastw[x], set()).add("waw")
                for rd in readers.get(x, ()):
                    deps.setdefault(rd, set()).add("war")
            if dma_key is not None and dma_key in lastdma:
                deps.setdefault(lastdma[dma_key], set()).add("raw")
            op = dict(idx=idx, eng=eng, fn=raw["fn"], dma=dma_key, waits=[], signal=False, cnt=None, inc=raw["inc"])
            for p, kinds in deps.items():
                P = self.ops[p]
                if P["dma"] is not None:
                    op["waits"].append(p)
                elif P["eng"] == eng and dma_key is None and "raw" not in kinds:
                    continue
                else:
                    P["signal"] = True
                    op["waits"].append(p)
            self.ops.append(op)
            for x in raw["r"]:
                readers.setdefault(x, []).append(idx)
            for x in raw["w"]:
                lastw[x] = idx
                readers[x] = []
            if dma_key is not None:
                lastdma[dma_key] = idx

    def op(self, eng, fn, r=(), w=()):
        return self._add(eng, fn, tuple(r), tuple(w))

    def dma(self, q, out, in_, r=(), w=(), key=None, **kw):
        return self._add(q, lambda e: e.dma_start(out=out, in_=in_, **kw), tuple(r), tuple(w), dma_key=key)

    def custom(self, q, fn, r=(), w=(), key=None, inc=16):
        return self._add(q, fn, tuple(r), tuple(w), dma_key=key, inc=inc)

    def run(self):
        g = self.g
        nc = g.nc
        self._finalize()
        for op in self.ops:
            if op["dma"] is not None:
                g.dkey(op["dma"])
                g.dcnt[op["dma"]] += op["inc"]
                op["cnt"] = g.dcnt[op["dma"]]
            elif op["signal"]:
                g.ecnt[op["eng"]] += 1
                op["cnt"] = g.ecnt[op["eng"]]
        per = {e: [o for o in self.ops if o["eng"] == e] for e in ENGS}
        ops = self.ops
        g.nops += len(ops)

        def emit(e, lst):
            def body(engine):
                waited = {}
                lastkeys = {}
                for o in lst:
                    need = {}
                    for p in o["waits"]:
                        P = ops[p]
                        s = g.dsem[P["dma"]] if P["dma"] is not None else g.esem[P["eng"]]
                        k = id(s)
                        if k not in need or need[k][1] < P["cnt"]:
                            need[k] = (s, P["cnt"])
                    for k, (s, v) in need.items():
                        if waited.get(k, -1) >= v:
                            continue
                        engine.wait_ge(s, v)
                        waited[k] = v
                    ins = o["fn"](engine)
                    if o["dma"] is not None:
                        ins.then_inc(g.dsem[o["dma"]], o["inc"])
                        lastkeys[o["dma"]] = o["cnt"]
                    elif o["signal"]:
                        ins.then_inc(g.esem[e], 1)
                for key, v in lastkeys.items():
                    engine.wait_ge(g.dsem[key], v)
            return body

        with nc.Block() as block:
            for e in ENGS:
                if not per[e]:
                    continue
                reg = {"pe": block.tensor, "act": block.scalar, "dve": block.vector,
                       "pool": block.gpsimd, "sp": block.sync}[e]
                reg(emit(e, per[e]))


def build(NBLK, DEPTH, PAST):
    STAGE = int(os.environ.get('KSTAGE', '99'))
    NB = NBLK + 1
    T = NBLK * 128 + 64
    NCB = PAST // 128
    NSB = NCB + 1
    NKB = 2 * NBLK
    SC = NBLK * 128
    nc = bass.Bass("TRN2", target_bir_lowering=False)
    g = GSync(nc)

    def din(name, shape, dt=F32):
        return nc.dram_tensor(name, list(shape), dt, kind="ExternalInput")

    def dout(name, shape):
        return nc.dram_tensor(name, list(shape), F32, kind="ExternalOutput")

    xin = din("xin", [T, D])
    mem = din("mem", [NMEM, D])
    ca_k = din("ca_k", [DEPTH, 512, 256]); ca_v = din("ca_v", [DEPTH, 512, 256])
    cb_k = din("cb_k", [DEPTH, PAST, 512]); cb_v = din("cb_v", [DEPTH, PAST, 512])
    cc_kv = din("cc_kv", [DEPTH, PAST, 256]); cc_kpe = din("cc_kpe", [DEPTH, PAST, 32])
    cm_k = din("cm_k", [DEPTH, NMEM, 512]); cm_v = din("cm_v", [DEPTH, NMEM, 512])
    W = {}
    for nm, shp in [("ffn1_norm", [DEPTH, D]), ("ffn1_w_gate", [DEPTH, D, DFF]), ("ffn1_w_up", [DEPTH, D, DFF]),
                    ("ffn1_w_down", [DEPTH, DFF, D]), ("mix_norm", [DEPTH, D]), ("w_in", [DEPTH, D, IN_COLS]),
                    ("a_q_norm", [DEPTH, 64]), ("a_k_norm", [DEPTH, 64]), ("a_rel_bias", [DEPTH, 4, 257]),
                    ("b_q_norm", [DEPTH, 64]), ("b_k_norm", [DEPTH, 64]), ("b_lambda", [DEPTH, 4, 64]),
                    ("b_sub_norm", [DEPTH, 128]), ("c_q_lat_norm", [DEPTH, 384]), ("c_kv_lat_norm", [DEPTH, 256]),
                    ("c_w_uq", [DEPTH, 384, 384]), ("c_w_ukv", [DEPTH, 256, 512]), ("c_q_norm", [DEPTH, 96]),
                    ("c_k_norm", [DEPTH, 96]), ("w_out", [DEPTH, D, D]), ("mem_norm_x", [DEPTH, D]),
                    ("mem_w_q", [DEPTH, D, 512]), ("mem_q_norm", [DEPTH, 128]), ("mem_norm_m", [DEPTH, D]),
                    ("mem_w_k", [DEPTH, D, 512]), ("mem_w_v", [DEPTH, D, 512]), ("mem_k_norm", [DEPTH, 128]),
                    ("mem_w_o", [DEPTH, 512, D]), ("ffn2_norm", [DEPTH, D]), ("ffn2_w_gate", [DEPTH, D, DFF]),
                    ("ffn2_w_up", [DEPTH, D, DFF]), ("ffn2_w_down", [DEPTH, DFF, D])]:
        W[nm] = din(nm, shp)
    c_ident = din("c_ident", [128, 128])
    c_cs = din("c_cs", [T, 32])
    c_augq = din("c_augq", [T, 32])
    c_augk = din("c_augk", [T, 32])
    c_augkc = din("c_augkc", [PAST, 32])
    c_corrB = din("c_corrB", [128, 2, 4, 128])
    c_corrBs = din("c_corrBs", [64, 4, 64])
    c_maskC = din("c_maskC", [128, 2, 128])
    c_maskA = din("c_maskA", [128, 6, 128])
    c_w01 = din("c_w01", [128, 2])
    c_lam = din("c_lam", [128, 2 * DEPTH])
    y = dout("y", [T, D])
    o_ak = dout("o_ak", [DEPTH, 320, 256]); o_av = dout("o_av", [DEPTH, 320, 256])
    o_bk = dout("o_bk", [DEPTH, T, 512]); o_bv = dout("o_bv", [DEPTH, T, 512])
    o_ckv = dout("o_ckv", [DEPTH, T, 256]); o_kpe = dout("o_kpe", [DEPTH, T, 32])
    o_mk = dout("o_mk", [DEPTH, NMEM, 512]); o_mv = dout("o_mv", [DEPTH, NMEM, 512])
    qa_d = nc.dram_tensor("qa_d", [NB, 64, 4, 128], BF16)
    qb_d = nc.dram_tensor("qb_d", [NB, 68, 8, 128], BF16)
    qc_d = nc.dram_tensor("qc_d", [NB, 96, 4, 128], BF16)
    ksrc = nc.dram_tensor("ksrc", [NBLK * RBE // 128, 128], BF16)
    kdst = nc.dram_tensor("kdst", [2 * NBLK * RBE // 128, 128], BF16)
    srec = nc.dram_tensor("srec", [NSB * RBE // 128, 128], BF16)
    Rtoe = nc.dram_tensor("Rtoe", [4, 128, 1024], F32)
    Etoe = nc.dram_tensor("Etoe", [4, 1024], F32)

    uid = [0]

    def SB(name, shape, dt):
        uid[0] += 1
        return nc.sbuf_tensor(f"{name}_{uid[0]}", shape, dt)

    def PS(name, shape, dt):
        uid[0] += 1
        return nc.psum_tensor(f"{name}_{uid[0]}", shape, dt)

    def rec_ap(tensor, base, dims):
        return bass.AP(tensor, base, [list(d) for d in dims])

    def blk_cols(bi):
        return (bi * 128, 128) if bi < NBLK else (SC, 64)

    groups = [(t0, min(512, SC - t0)) for t0 in range(0, SC, 512)] + [(SC, 64)]
    groups_default = groups
    nfg_ = (T + 511) // 512
    base_ = ((T + nfg_ - 1) // nfg_ + 7) // 8 * 8
    ffn_groups = []
    t_ = 0
    while t_ < T:
        ffn_groups.append((t_, min(base_, T - t_)))
        t_ += base_

    with ExitStack() as es1:
        xT = es1.enter_context(SB("xT", [128, 8, T], F32))
        idf = es1.enter_context(SB("idf", [128, 128], F32))
        idb = es1.enter_context(SB("idb", [128, 128], BF16))
        ones = es1.enter_context(SB("ones", [128, 128], BF16))
        zerob = es1.enter_context(SB("zerob", [128, 128], BF16))
        gcols = es1.enter_context(SB("gcols", [128, 4 * DEPTH, 8], F32))
        lamc = es1.enter_context(SB("lamc", [128, 2 * DEPTH], F32))

        with ExitStack() as es2:
            xtok = es2.enter_context(SB("xtok", [128, 2, D], F32))
            grow = es2.enter_context(SB("grow", [4 * DEPTH * 8, 128], F32))
            pt = es2.enter_context(PS("pt", [128, 2, 512], F32))
            ph = Phase(g)
            ph.dma("sp", idf[:, :], c_ident[:, :], w=["idf"], key="c0")
            ph.dma("pool", idb[:, :], c_ident[:, :], w=["idb"], key="c1")
            ph.dma("sp", lamc[:, :], c_lam[:, :], w=["lamc"], key="c2")
            ph.op("pool", lambda e: e.memset(ones[:, :], 1.0), w=["ones"])
            ph.op("pool", lambda e: e.memset(zerob[:, :], 0.0), w=["zerob"])
            for k, nm in enumerate(["ffn1_norm", "mix_norm", "mem_norm_x", "ffn2_norm"]):
                for l in range(DEPTH):
                    r0 = (l * 4 + k) * 8
                    ph.dma("sp", grow[r0:r0 + 8, :], W[nm][l, :].rearrange("(c p) -> c p", p=128), w=["grow"], key="c3")
            nr = 4 * DEPTH * 8
            ph.op("pe", lambda e: e.transpose(out=pt[:, 0, 0:nr], in_=grow[:, :], identity=idf[0:nr, 0:nr]), r=["grow", "idf"], w=[("pt", 0)])
            ph.op("dve", lambda e: e.tensor_copy(out=gcols[:, :, :].rearrange("p a b -> p (a b)"), in_=pt[:, 0, 0:nr]), w=[("pt", 0), "gcols"])
            for bi in range(NB):
                c0, nb = blk_cols(bi)
                s = bi % 2
                ph.dma("sp", xtok[:nb, s, :], xin[c0:c0 + nb, :], w=[("xtok", s)], key=("xtok", s))
                for c in range(8):
                    ps = c % 2
                    ph.op("pe", lambda e, s=s, c=c, ps=ps, nb=nb: e.transpose(out=pt[:, ps, 0:nb], in_=xtok[:nb, s, c * 128:(c + 1) * 128], identity=idf[:nb, :nb]),
                          r=[("xtok", s), "idf"], w=[("pt", ps)])
                    ph.op("dve", lambda e, c=c, ps=ps, c0=c0, nb=nb: e.tensor_copy(out=xT[:, c, c0:c0 + nb], in_=pt[:, ps, 0:nb]),
                          w=[("pt", ps), ("xT", c)])
            ph.run()

        def norm_to_hT(ph, hT, xsq, rinv, pn, gidx, only=None, pres=lambda b: ("pn", b), groups=None):
            groups = groups if groups is not None else groups_default
            for gi, (t0, n) in enumerate(groups):
                if only is not None and gi != only:
                    continue
                h0 = 0 if only is not None else t0
                s = 0
                ps_ = gi % 2
                ph.op("act", lambda e, t0=t0, n=n, s=s: e.activation(out=xsq[:, s, :, 0:n], in_=xT[:, :, t0:t0 + n], func=AF.Square),
                      r=[("xT", c) for c in range(8)], w=[("xsq", s)])
                for c in range(8):
                    ph.op("pe", lambda e, c=c, n=n, s=s, ps_=ps_: e.matmul(pn[:, ps_, 0:n], lhsT=ones[:, :], rhs=xsq[:, s, c, 0:n], start=(c == 0), stop=(c == 7)),
                          r=[("xsq", s), "ones"], w=[pres(ps_)])
                ph.op("act", lambda e, n=n, s=s, ps_=ps_: e.activation(out=rinv[:, s, 0:n], in_=pn[:, ps_, 0:n], func=AF.Sqrt, scale=1.0 / D, bias=EPS),
                      w=[pres(ps_), ("rinv", s)])
                ph.op("dve", lambda e, n=n, s=s: e.reciprocal(out=rinv[:, s, 0:n], in_=rinv[:, s, 0:n]), r=[("rinv", s)], w=[("rinv", s)])
                for c in range(8):
                    ph.op("dve", lambda e, c=c, t0=t0, n=n, s=s, h0=h0: e.scalar_tensor_tensor(out=hT[:, c, h0:h0 + n], in0=xT[:, c, t0:t0 + n], scalar=gcols[:, gidx, c:c + 1],
                                                                                         in1=rinv[:, s, 0:n], op0=ALU.mult, op1=ALU.mult),
                          r=[("rinv", s), "gcols", ("xT", c)], w=[("hT", c)])

        def ffn(l, which, prefetch=None):
            wg_d, wu_d, wd_d = W[f"ffn{which}_w_gate"], W[f"ffn{which}_w_up"], W[f"ffn{which}_w_down"]
            gidx = l * 4 + (0 if which == 1 else 3)
            NFG = DFF // 256
            with ExitStack() as es3:
                hT = es3.enter_context(SB("hT", [128, 8, T], BF16))
                xsq = es3.enter_context(SB("xsq", [128, 1, 8, 512], BF16))
                rinv = es3.enter_context(SB("rinv", [128, 1, 512], F32))
                wgs = es3.enter_context(SB("wgs", [128, 2, 8, 256], BF16))
                wus = es3.enter_context(SB("wus", [128, 2, 8, 256], BF16))
                wds = es3.enter_context(SB("wds", [128, 2, 2, D], BF16))
                sg = es3.enter_context(SB("sg", [128, 2, 512], F32))
                actT = es3.enter_context(SB("actT", [128, 2, 2, T], BF16))
                pn = es3.enter_context(PS("pn", [128, 2, 512], F32))
                pgt = es3.enter_context(PS("pgt", [128, 2, 512], F32))
                put = es3.enter_context(PS("put", [128, 2, 512], F32))
                pd = es3.enter_context(PS("pd", [128, 2, 512], F32))
                ph = Phase(g)
                norm_to_hT(ph, hT, xsq, rinv, pn, gidx, groups=ffn_groups)
                it = 0
                dit = [0]
                pending = []

                def emit_down(fg, ws, t0, n):
                    for dc in range(8):
                        pb = dit[0] % 2
                        dit[0] += 1
                        for j in range(2):
                            ph.op("pe", lambda e, j=j, dc=dc, pb=pb: e.matmul(pd[:, pb, 0:n], lhsT=wds[:, ws, j, dc * 128:(dc + 1) * 128], rhs=actT[:, ws, j, t0:t0 + n], start=(j == 0), stop=(j == 1)),
                                  r=[("wds", ws), ("actT", ws, j, t0)], w=[("pd", pb)])
                        ph.op("dve", lambda e, dc=dc, pb=pb: e.scalar_tensor_tensor(out=xT[:, dc, t0:t0 + n], in0=pd[:, pb, 0:n], scalar=0.5, in1=xT[:, dc, t0:t0 + n], op0=ALU.mult, op1=ALU.add),
                              r=[("xT", dc)], w=[("pd", pb), ("xT", dc)])

                for fg in range(NFG):
                    ws = fg % 2
                    if prefetch is not None and fg in (2, 4, 6, 8):
                        prefetch(ph, fg // 2 - 1)
                    ph.dma("pool", wgs[:, ws, :, :], wg_d[l, :, fg * 256:(fg + 1) * 256].rearrange("(c p) n -> p c n", p=128), w=[("wgs", ws)], key=("wgs", ws))
                    ph.dma("pool", wus[:, ws, :, :], wu_d[l, :, fg * 256:(fg + 1) * 256].rearrange("(c p) n -> p c n", p=128), w=[("wus", ws)], key=("wus", ws))
                    ph.dma("pool", wds[:, ws, :, :], wd_d[l, fg * 256:(fg + 1) * 256, :].rearrange("(c p) n -> p c n", p=128), w=[("wds", ws)], key=("wds", ws))
                    for (t0, n) in ffn_groups:
                        for j in range(2):
                            pb = it % 2
                            it += 1
                            for c in range(8):
                                ph.op("pe", lambda e, c=c, j=j, pb=pb, ws=ws, t0=t0, n=n: e.matmul(pgt[:, pb, 0:n], lhsT=wgs[:, ws, c, j * 128:(j + 1) * 128], rhs=hT[:, c, t0:t0 + n], start=(c == 0), stop=(c == 7)),
                                      r=[("wgs", ws), ("hT", c)], w=[("pgt", pb)])
                            for c in range(8):
                                ph.op("pe", lambda e, c=c, j=j, pb=pb, ws=ws, t0=t0, n=n: e.matmul(put[:, pb, 0:n], lhsT=wus[:, ws, c, j * 128:(j + 1) * 128], rhs=hT[:, c, t0:t0 + n], start=(c == 0), stop=(c == 7)),
                                      r=[("wus", ws), ("hT", c)], w=[("put", pb)])
                            ph.op("act", lambda e, pb=pb, n=n: e.activation(out=sg[:, pb, 0:n], in_=pgt[:, pb, 0:n], func=AF.Silu), w=[("pgt", pb), ("sg", pb)])
                            ph.op("dve", lambda e, pb=pb, ws=ws, j=j, t0=t0, n=n: e.tensor_tensor(out=actT[:, ws, j, t0:t0 + n], in0=put[:, pb, 0:n], in1=sg[:, pb, 0:n], op=ALU.mult),
                                  r=[("sg", pb)], w=[("put", pb), ("actT", ws, j, t0)])
                        if pending:
                            emit_down(*pending.pop(0))
                        pending.append((fg, ws, t0, n))
                while pending:
                    emit_down(*pending.pop(0))
                ph.run()

        def bcast_row(ph, dst_ap, src_1d, n, reps, key, wres):
            ph.dma("sp", dst_ap, src_1d.rearrange("(o h n) -> o h n", o=1, h=1).to_broadcast([128, reps, n]), w=[wres], key=key)

        def rms_rows(ph, src, nb, H, dh, gain, out, sq, ss, tag, src_res, out_res, gain_res):
            ph.op("act", lambda e: e.activation(out=sq[:nb, 0:H * dh].rearrange("p (h d) -> p h d", h=H), in_=src, func=AF.Square), r=[], w=list(src_res) + [("sq", tag)])
            ph.op("dve", lambda e: e.tensor_reduce(out=ss[:nb, 0:H], in_=sq[:nb, 0:H * dh].rearrange("p (h d) -> p h d", h=H), axis=AX.X, op=ALU.add), r=[("sq", tag)], w=[("ss", tag)])
            ph.op("act", lambda e: e.activation(out=ss[:nb, 0:H], in_=ss[:nb, 0:H], func=AF.Sqrt, scale=1.0 / dh, bias=EPS), r=[("ss", tag)], w=[("ss", tag)])
            ph.op("dve", lambda e: e.reciprocal(out=ss[:nb, 0:H], in_=ss[:nb, 0:H]), r=[("ss", tag)], w=[("ss", tag)])
            ph.op("dve", lambda e: e.tensor_tensor(out=out, in0=src, in1=ss[:nb, 0:H].unsqueeze(2).to_broadcast([nb, H, dh]), op=ALU.mult), r=[("ss", tag)], w=list(src_res) + list(out_res))
            ph.op("dve", lambda e: e.tensor_tensor(out=out, in0=out, in1=gain, op=ALU.mult), r=list(out_res) + list(gain_res), w=list(out_res))

        def attn_core(ph, sp, pT, qT, nq, tiles, E, outp, out_res, cnt):
            ngr = (len(tiles) + 3) // 4
            bufs = []
            for gi in range(ngr):
                bufs.append(cnt[0] % 2)
                cnt[0] += 1

            def emit_S(gi):
                b = bufs[gi]
                for ti, (kT, v, nk, corrs, kres, vres) in enumerate(tiles[gi * 4:(gi + 1) * 4]):
                    ph.op("pe", lambda e, kT=kT, ti=ti, nk=nk, last=(len(corrs) == 0): e.matmul(sp[:nk, b, ti * 128:ti * 128 + nq], lhsT=kT, rhs=qT, start=True, stop=last),
                          r=list(kres) + ["qt"], w=[("sp", b)])
                    for ci, cr in enumerate(corrs):
                        ph.op("pe", lambda e, cr=cr, ti=ti, nk=nk, last=(ci == len(corrs) - 1): e.matmul(sp[:nk, b, ti * 128:ti * 128 + nq], lhsT=idb[:nk, :nk], rhs=cr, start=False, stop=last),
                              r=["corr", "idb"], w=[("sp", b)])

            def emit_rest(gi):
                b = bufs[gi]
                grp = tiles[gi * 4:(gi + 1) * 4]
                ng = len(grp)
                ph.op("act", lambda e: e.activation(out=pT[:, b, 0:ng, 0:nq], in_=sp[:, b, 0:ng * 128].rearrange("p (g q) -> p g q", g=ng)[:, :, 0:nq], func=AF.Exp),
                      w=[("sp", b), ("pT", b)])
                for ti, (kT, v, nk, corrs, kres, vres) in enumerate(grp):
                    first = (gi == 0 and ti == 0)
                    lastmm = (gi == ngr - 1) and (ti == ng - 1)
                    ph.op("pe", lambda e, v=v, ti=ti, nk=nk, first=first, lastmm=lastmm: e.matmul(outp, lhsT=pT[:nk, b, ti, 0:nq], rhs=v, start=first, stop=lastmm),
                          r=[("pT", b)] + list(vres), w=list(out_res))

            emit_S(0)
            for gi in range(ngr):
                if gi + 1 < ngr:
                    emit_S(gi + 1)
                emit_rest(gi)

        def attn_T(ph, sp, pTb, opp, a, qflat, ncols, tiles, MP, cnt, kres, vres, zero_init=False, qres="qt"):
            nt = len(tiles)
            nbuf = sp.shape[1]
            bufs = []
            for ti in range(nt):
                bufs.append(cnt[0] % nbuf)
                cnt[0] += 1

            if zero_init:
                R0 = qflat.shape[0]
                for k_ in range(2):
                    ph.op("pe", lambda e, k_=k_: e.matmul(opp[:MP, 2 * a + k_, 0:ncols], lhsT=zerob[0:R0, 0:MP], rhs=qflat[:, 0:ncols], start=True, stop=False),
                          r=["zerob", qres], w=[("op", 2 * a + k_)])

            def emit_S(ti):
                kT, vl, nk, c_lo, corrs = tiles[ti][:5]
                c_hi = tiles[ti][5] if len(tiles[ti]) > 5 else ncols
                b = bufs[ti]
                ph.op("pe", lambda e: e.matmul(sp[:nk, b, c_lo:c_hi], lhsT=kT, rhs=qflat[:, c_lo:c_hi], start=True, stop=(len(corrs) == 0)),
                      r=list(kres) + [qres], w=[("sp", b)])
                for ci, (lo, hi, cap) in enumerate(corrs):
                    ph.op("pe", lambda e, lo=lo, hi=hi, cap=cap, last=(ci == len(corrs) - 1): e.matmul(sp[:nk, b, lo:hi], lhsT=idb[:nk, :nk], rhs=cap, start=False, stop=last),
                          r=["corr", "idb"], w=[("sp", b)])

            def emit_rest(ti):
                kT, vl, nk, c_lo, corrs = tiles[ti][:5]
                c_hi = tiles[ti][5] if len(tiles[ti]) > 5 else ncols
                b = bufs[ti]
                st_ = (ti == 0) and not zero_init
                ph.op("act", lambda e: e.activation(out=pTb[:nk, b, c_lo:c_hi], in_=sp[:nk, b, c_lo:c_hi], func=AF.Exp), w=[("sp", b), ("pT", b)])
                ph.op("pe", lambda e: e.matmul(opp[:MP, 2 * a, c_lo:c_hi], lhsT=vl, rhs=pTb[:nk, b, c_lo:c_hi], start=st_, stop=(ti == nt - 1)),
                      r=[("pT", b)] + list(vres), w=[("op", 2 * a)])
                ph.op("pe", lambda e: e.matmul(opp[:MP, 2 * a + 1, c_lo:c_hi], lhsT=ones[:nk, :MP], rhs=pTb[:nk, b, c_lo:c_hi], start=st_, stop=(ti == nt - 1)),
                      r=[("pT", b), "ones"], w=[("op", 2 * a + 1)])

            for ti in range(min(nbuf - 1, nt)):
                emit_S(ti)
            for ti in range(nt):
                if ti + nbuf - 1 < nt:
                    emit_S(ti + nbuf - 1)
                emit_rest(ti)

        def load_kv(ph, kt, vt, R, E, off_k, off_v, h, nhk, HV, res):
            HK = {O_KA: 4, O_KB: 8, O_KC: 4}[off_k]
            for jj in range(nhk):
                ph.dma("sp", kt[0:R, jj, :, :, :], rec_ap(kdst, off_k + (h * nhk + jj) * 128, [[HK * 128, R], [RBE, NKB], [1, 128]]), w=[res + "k"], key=("Kk", jj))
            ph.dma("act", vt[:, :, :, 0:E], rec_ap(kdst, off_v + h * E, [[HV * E, 128], [RBE, NKB], [1, E]]), w=[res + "v"], key="Kv")

        def load_kv_s(ph, kts, vts, R, E, off_k, off_v, h, nhk, HV, res, b0, nbk):
            HK = {O_KA: 4, O_KB: 8, O_KC: 4}[off_k]
            for jj in range(nhk):
                ph.dma("sp", kts[0:R, jj, 0:nbk, :], rec_ap(srec, b0 * RBE + off_k + (h * nhk + jj) * 128, [[HK * 128, R], [RBE, nbk], [1, 128]]), w=[res + "ks"], key=("Kks", jj))
            ph.dma("act", vts[:, 0:nbk, 0:E], rec_ap(srec, b0 * RBE + off_v + h * E, [[HV * E, 128], [RBE, nbk], [1, E]]), w=[res + "vs"], key="Kvs")

        def attention(l, Oall):
            with ExitStack() as es4:
                rc = es4.enter_context(SB("rc", [128, 8], F32))
                sp3 = es4.enter_context(PS("sp", [128, 3, 512], F32))
                sp = sp3[:, 0:2, :]
                opp = es4.enter_context(PS("opp", [128, 4, 512], F32))
                pss = es4.enter_context(PS("pss", [128, 1, 512], F32))
                OT = es4.enter_context(SB("OT", [128, 8, T], BF16))
                with ExitStack() as es5:
                    tb = es5.enter_context(SB("tb", [4, 257], F32))
                    ng = es5.enter_context(SB("ng", [4, 1], F32))
                    ext = es5.enter_context(SB("ext", [4, 1024], F32))
                    extb = es5.enter_context(SB("extb", [128, 1024], F32))
                    t7 = es5.enter_context(SB("t7", [128, 7, 128], F32))
                    tmpa = es5.enter_context(SB("tmpa", [128, 6, 128], F32))
                    mka = es5.enter_context(SB("mka", [128, 6, 128], F32))
                    w01 = es5.enter_context(SB("w01", [128, 2], F32))
                    biasA = es5.enter_context(SB("biasA", [128, 4, 6, 128], BF16))
                    biasS = es5.enter_context(SB("biasS", [128, 4, 5, 64], BF16))
                    kt = es5.enter_context(SB("kt", [64, NKB, 128], BF16))
                    vtp = es5.enter_context(SB("vtp", [128, 2, NKB, 128], BF16))
                    kts = es5.enter_context(SB("kts", [64, 5, 128], BF16))
                    vtsp = es5.enter_context(SB("vtsp", [128, 2, 5, 128], BF16))
                    qt = es5.enter_context(SB("qt", [64, NB * 128], BF16))
                    pTb = es5.enter_context(SB("pTb", [128, 3, 512], BF16))
                    rsA = es5.enter_context(SB("rsA", [128, 2, 512], F32))
                    ph = Phase(g)
                    ph.dma("sp", tb[:, :], W["a_rel_bias"][l, :, :], w=["tb"], key="tb")
                    ph.dma("sp", mka[:, :, :], c_maskA[:, :, :], w=["mka"], key="mka")
                    ph.dma("sp", w01[:, :], c_w01[:, :], w=["w01"], key="w01")
                    ph.op("dve", lambda e: e.tensor_scalar(out=ng[:, :], in0=tb[:, 256:257], scalar1=-1.0, scalar2=None, op0=ALU.mult), r=["tb"], w=["ng"])
                    ph.op("pool", lambda e: e.memset(ext[:, :], 0.0), w=["ext"])
                    ph.op("dve", lambda e: e.tensor_scalar(out=ext[:, 127:384], in0=tb[:, :], scalar1=ng[:, 0:1], scalar2=None, op0=ALU.add), r=["tb", "ng"], w=["ext"])
                    ph.op("dve", lambda e: e.tensor_copy(out=ext[:, 0:127], in_=ext[:, 127:128].to_broadcast([4, 127])), r=["ext"], w=["ext"])
                    ph.dma("sp", Etoe[:, :], ext[:, :], r=["ext"], w=["Etoe"], key="Etoe")
                    for h in range(4):
                        ph.dma("sp", extb[:, :], Etoe[h:h + 1, :].to_broadcast([128, 1024]), r=["Etoe"], w=["extb"], key="extb")
                        ph.dma("sp", Rtoe[h, :, :], extb[:, :], r=["extb"], w=[("Rtoe", h)], key="Rtoe")
                    ph.op("pool", lambda e: e.memset(vtp[:, :, :, :], 0.0), w=["Av"])
                    ph.op("pool", lambda e: e.memset(vtsp[:, :, :, :], 0.0), w=["Avs"])
                    cnt = [0]
                    acc = 0
                    qgroupsA = [list(range(b0_, min(b0_ + 4, NBLK))) for b0_ in range(0, NBLK, 4)] + [[NBLK]]
                    for h in range(4):
                        ph.dma("sp", t7[:, :, :], rec_ap(Rtoe, h * 128 * 1024 + 127, [[1023, 128], [128, 7], [1, 128]]), r=[("Rtoe", h)], w=["t7"], key="t7")
                        ph.op("dve", lambda e: e.tensor_scalar(out=tmpa[:, :, :], in0=t7[:, 0:6, :], scalar1=w01[:, 0:1], scalar2=None, op0=ALU.mult), r=["t7", "w01"], w=["tmpa"])
                        ph.op("dve", lambda e: e.scalar_tensor_tensor(out=tmpa[:, :, :], in0=t7[:, 1:7, :], scalar=w01[:, 1:2], in1=tmpa[:, :, :], op0=ALU.mult, op1=ALU.add), r=["t7", "w01", "tmpa"], w=["tmpa"])
                        ph.op("dve", lambda e, h=h: e.tensor_tensor(out=biasA[:, h, :, :], in0=tmpa[:, :, :], in1=mka[:, :, :], op=ALU.add), r=["tmpa", "mka"], w=["corr"])
                        ph.op("dve", lambda e, h=h: e.tensor_copy(out=biasS[:, h, :, :], in_=t7[:, 1:6, 0:64]), r=["t7"], w=["corr"])
                        par = h % 2
                        ph.dma("sp", kt[0:64, :, :], rec_ap(kdst, O_KA + h * 128, [[4 * 128, 64], [RBE, NKB], [1, 128]]), w=["Ak"], key=("Kk", 0))
                        ph.dma("sp", kts[0:64, :, :], rec_ap(srec, (NCB - 4) * RBE + O_KA + h * 128, [[4 * 128, 64], [RBE, 5], [1, 128]]), w=["Aks"], key=("Kks", 0))
                        ph.dma("sp", qt[:, :].rearrange("p (b t) -> p b t", b=NB), qa_d[:, :, h, :].rearrange("b d t -> d b t"), w=["qt"], key=("Kq", 0))
                        ph.dma("act", vtp[:, par, :, par * 64:par * 64 + 64], rec_ap(kdst, O_VA + h * 64, [[256, 128], [RBE, NKB], [1, 64]]), w=["Av"], key="Kv")
                        ph.dma("act", vtsp[:, par, :, par * 64:par * 64 + 64], rec_ap(srec, (NCB - 4) * RBE + O_VA + h * 64, [[256, 128], [RBE, 5], [1, 64]]), w=["Avs"], key="Kvs")
                        for qg in qgroupsA:
                            b0, b1 = qg[0], qg[-1]
                            tiles = []
                            if b0 < NBLK:
                                ncols = len(qg) * 128
                                q0 = b0 * 128
                                for kb in range(max(0, 2 * b0 - 4), 2 * b1 + 2):
                                    ilo = max(b0, kb // 2)
                                    ihi = min(b1, (kb + 4) // 2)
                                    corrs = []
                                    for i in range(ilo, ihi + 1):
                                        j = kb - (2 * i - 4)
                                        corrs.append(((i - b0) * 128, (i - b0 + 1) * 128, biasA[:, h, 5 - j, :]))
                                    tiles.append((kt[0:64, kb, :], vtp[:, par, kb, :], 128, (ilo - b0) * 128, corrs, (ihi - b0 + 1) * 128))
                                kres, vres = ["Ak"], ["Av"]
                            else:
                                ncols = 64
                                q0 = SC
                                for cbi in range(4):
                                    tiles.append((kts[0:64, cbi, :], vtsp[:, par, cbi, :], 128, 0, [(0, 64, biasS[:, h, 4 - cbi, :])]))
                                tiles.append((kts[0:64, 4, 0:64], vtsp[0:64, par, 4, :], 64, 0, [(0, 64, biasS[0:64, h, 0, :])]))
                                kres, vres = ["Aks"], ["Avs"]
                            a = acc % 2
                            acc += 1
                            attn_T(ph, sp3, pTb, opp, a, qt[0:64, q0:q0 + ncols], ncols, tiles, 128, cnt, kres, vres, zero_init=True)
                            ph.op("dve", lambda e, a=a, ncols=ncols: e.reciprocal(out=rsA[:, a, 0:ncols], in_=opp[:, 2 * a + 1, 0:ncols]), w=[("op", 2 * a + 1), ("rs", a)])
                            ph.op("dve", lambda e, a=a, ncols=ncols, q0=q0, h=h, par=par: e.tensor_tensor(out=OT[par * 64:par * 64 + 64, h // 2, q0:q0 + ncols], in0=opp[par * 64:par * 64 + 64, 2 * a, 0:ncols],
                                                                                                          in1=rsA[par * 64:par * 64 + 64, a, 0:ncols], op=ALU.mult),
                                  r=[("rs", a)], w=[("op", 2 * a), ("OT", h // 2)])
                    ph.run()
                if STAGE < 5:
                    return
                qgroups = [list(range(b0, min(b0 + 4, NBLK))) for b0 in range(0, NBLK, 4)] + [[NBLK]]
                with ExitStack() as es6:
                    kt2 = es6.enter_context(SB("kt", [68, 4, NKB, 128], BF16))
                    vt2 = es6.enter_context(SB("vt", [128, 2, NKB, 128], BF16))
                    kts = es6.enter_context(SB("kts", [68, 2, NSB, 128], BF16))
                    vts = es6.enter_context(SB("vts", [128, NSB, 128], BF16))
                    qt2 = es6.enter_context(SB("qt", [68, 4, NB * 128], BF16))
                    pTb = es6.enter_context(SB("pTb", [128, 3, 512], BF16))
                    corrB = es6.enter_context(SB("corrB", [128, 2, 4, 128], BF16))
                    corrBs = es6.enter_context(SB("corrBs", [64, 4, 64], BF16))
                    lamb = es6.enter_context(SB("lamb", [128, 4, 64], F32))
                    lp = es6.enter_context(SB("lp", [128, 2, 64], F32))
                    lv = es6.enter_context(SB("lv", [128, 4], F32))
                    gsc = es6.enter_context(SB("gsc", [128, 1], F32))
                    rs = es6.enter_context(SB("rs", [128, 2, 512], F32))
                    t0b = es6.enter_context(SB("t0", [128, 2, 512], F32))
                    t1b = es6.enter_context(SB("t1", [128, 2, 512], F32))
                    sqb = es6.enter_context(SB("sqb", [128, 512], BF16))
                    ph = Phase(g)
                    ph.dma("pool", corrB[:, :, :, :], c_corrB[:, :, :, :], w=["corr"], key="corrB")
                    ph.dma("pool", corrBs[:, :, :], c_corrBs[:, :, :], w=["corr"], key="corrBs")
                    ph.dma("sp", lamb[:, :, :], W["b_lambda"][l, :, :].rearrange("(o a) n -> o a n", o=1).to_broadcast([128, 4, 64]), w=["lamb"], key="lamb")
                    ph.dma("sp", gsc[:, :], W["b_sub_norm"][l, :].rearrange("(p o) -> p o", o=1), w=["gsc"], key="gsub")
                    ph.op("dve", lambda e: e.tensor_scalar(out=gsc[:, :], in0=gsc[:, :], scalar1=lamc[:, 2 * l + 1:2 * l + 2], scalar2=None, op0=ALU.mult), r=["gsc", "lamc"], w=["gsc"])
                    for k in range(2):
                        ph.op("dve", lambda e, k=k: e.tensor_tensor(out=lp[:, k, :], in0=lamb[:, 2 * k, :], in1=lamb[:, 2 * k + 1, :], op=ALU.mult), r=["lamb"], w=["lp"])
                    ph.op("dve", lambda e: e.tensor_reduce(out=lv[:, 0:2], in_=lp[:, :, :], axis=AX.X, op=ALU.add), r=["lp"], w=["lv"])
                    ph.op("act", lambda e: e.activation(out=lv[:, 0:2], in_=lv[:, 0:2], func=AF.Exp), r=["lv"], w=["lv"])
                    ph.op("dve", lambda e: e.tensor_tensor(out=lv[:, 2:3], in0=lv[:, 1:2], in1=lv[:, 0:1], op=ALU.subtract), r=["lv"], w=["lv2"])
                    ph.op("dve", lambda e: e.tensor_tensor(out=lv[:, 3:4], in0=lv[:, 2:3], in1=lamc[:, 2 * l:2 * l + 1], op=ALU.subtract), r=["lv2", "lamc"], w=["lv3"])
                    cnt = [0]
                    acc = 0
                    pendingB = []
                    gcount = [0]

                    def finalB(gb, ncols, q0, h):
                        t0, t1 = t0b[:, gb, :], t1b[:, gb, :]
                        ph.op("pool", lambda e: e.tensor_tensor(out=t0[:, 0:ncols], in0=t0[:, 0:ncols], in1=t1[:, 0:ncols], op=ALU.add), r=[("t0", gb), ("t1", gb)], w=[("t0", gb)])
                        ph.op("act", lambda e: e.activation(out=sqb[:, 0:ncols], in_=t0[:, 0:ncols], func=AF.Square), r=[("t0", gb)], w=["sqb"])
                        ph.op("pe", lambda e: e.matmul(pss[:, 0, 0:ncols], lhsT=ones[:, :], rhs=sqb[:, 0:ncols], start=True, stop=True), r=["sqb", "ones"], w=["pss"])
                        ph.op("act", lambda e: e.activation(out=t1[:, 0:ncols], in_=pss[:, 0, 0:ncols], func=AF.Sqrt, scale=1.0 / 128, bias=EPS), w=["pss", ("t1", gb)])
                        ph.op("dve", lambda e: e.reciprocal(out=t1[:, 0:ncols], in_=t1[:, 0:ncols]), r=[("t1", gb)], w=[("t1", gb)])
                        ph.op("dve", lambda e: e.scalar_tensor_tensor(out=OT[:, 2 + h, q0:q0 + ncols], in0=t0[:, 0:ncols], scalar=gsc[:, 0:1], in1=t1[:, 0:ncols], op0=ALU.mult, op1=ALU.mult),
                              r=[("t0", gb), ("t1", gb), "gsc"], w=[("OT", 2 + h)])

                    def loadB(h):
                        sl = h % 2
                        for jj in range(2):
                            ph.dma("sp", kt2[0:68, 2 * sl + jj, :, :], rec_ap(kdst, O_KB + (h * 2 + jj) * 128, [[8 * 128, 68], [RBE, NKB], [1, 128]]), w=[("Bk", sl)], key=("Kk", jj, sl))
                            ph.dma("sp", qt2[:, 2 * sl + jj, :].rearrange("p (b t) -> p b t", b=NB), qb_d[:, :, 2 * h + jj, :].rearrange("b d t -> d b t"), w=[("qt", sl)], key=("Kq", jj, sl))
                        ph.dma("sp", vt2[:, sl, :, :], rec_ap(kdst, O_VB + h * 128, [[512, 128], [RBE, NKB], [1, 128]]), w=[("Bv", sl)], key=("Kv", sl))

                    loadB(0)
                    for h in range(4):
                        sl = h % 2
                        kt = kt2[:, 2 * sl:2 * sl + 2, :, :]
                        vt = vt2[:, sl, :, :]
                        qt = qt2[:, 2 * sl:2 * sl + 2, :]
                        if h + 1 < 4:
                            loadB(h + 1)
                        for jj in range(2):
                            ph.dma("sp", kts[0:68, jj, :, :], rec_ap(srec, O_KB + (h * 2 + jj) * 128, [[8 * 128, 68], [RBE, NSB], [1, 128]]), w=["Bks"], key=("Kks", jj))
                        ph.dma("sp", vts[:, :, :], rec_ap(srec, O_VB + h * 128, [[512, 128], [RBE, NSB], [1, 128]]), w=["Bvs"], key="Kvs")
                        for qg in qgroups:
                            b0 = qg[0]
                            if b0 < NBLK:
                                ncols = len(qg) * 128
                                q0 = b0 * 128
                            else:
                                ncols = 64
                                q0 = SC
                            gb = gcount[0] % 2
                            gcount[0] += 1
                            t0, t1 = t0b[:, gb, :], t1b[:, gb, :]
                            for j in range(2):
                                tiles = []
                                if b0 < NBLK:
                                    for kb in range(2 * qg[-1] + 2):
                                        imin = max(b0, kb // 2)
                                        c_lo = (imin - b0) * 128
                                        corrs = []
                                        if kb // 2 >= b0:
                                            lo = (kb // 2 - b0) * 128
                                            corrs = [(lo, lo + 128, corrB[:, kb % 2, h, :])]
                                        tiles.append((kt[0:68, j, kb, :], vt[:, kb, :], 128, c_lo, corrs))
                                    kres, vres = [("Bk", sl)], [("Bv", sl)]
                                else:
                                    for cbi in range(NCB):
                                        tiles.append((kts[0:68, j, cbi, :], vts[:, cbi, :], 128, 0, []))
                                    tiles.append((kts[0:68, j, NCB, 0:64], vts[0:64, NCB, :], 64, 0, [(0, 64, corrBs[:, h, :])]))
                                    kres, vres = ["Bks"], ["Bvs"]
                                a = acc % 2
                                acc += 1
                                attn_T(ph, sp3, pTb, opp, a, qt[0:68, j, q0:q0 + ncols], ncols, tiles, 128, cnt, kres, vres, qres=("qt", sl))
                                ph.op("dve", lambda e, a=a, j=j, ncols=ncols: e.reciprocal(out=rs[:, j, 0:ncols], in_=opp[:, 2 * a + 1, 0:ncols]), w=[("op", 2 * a + 1), ("rs", j)])
                                if j == 0:
                                    ph.op("dve", lambda e, a=a, ncols=ncols, t0=t0: e.tensor_tensor(out=t0[:, 0:ncols], in0=opp[:, 2 * a, 0:ncols], in1=rs[:, 0, 0:ncols], op=ALU.mult), r=[("rs", 0)], w=[("op", 2 * a), ("t0", gb)])
                                    if pendingB:
                                        finalB(*pendingB.pop(0))
                                else:
                                    ph.op("dve", lambda e, ncols=ncols: e.tensor_scalar(out=rs[:, 1, 0:ncols], in0=rs[:, 1, 0:ncols], scalar1=lv[:, 3:4], scalar2=None, op0=ALU.mult), r=[("rs", 1), "lv3"], w=[("rs", 1)])
                                    ph.op("dve", lambda e, a=a, ncols=ncols, t1=t1: e.tensor_tensor(out=t1[:, 0:ncols], in0=opp[:, 2 * a, 0:ncols], in1=rs[:, 1, 0:ncols], op=ALU.mult), r=[("rs", 1)], w=[("op", 2 * a), ("t1", gb)])
                            pendingB.append((gb, ncols, q0, h))
                    while pendingB:
                        finalB(*pendingB.pop(0))
                    ph.run()
                if STAGE < 6:
                    return
                wout = es4.enter_context(SB("wout", [128, 8, D], BF16))
                with ExitStack() as es7:
                    kt = es7.enter_context(SB("kt", [96, NKB, 128], BF16))
                    vtp = es7.enter_context(SB("vtp", [128, 2, NKB, 128], BF16))
                    kts = es7.enter_context(SB("kts", [96, NSB, 128], BF16))
                    vtsp = es7.enter_context(SB("vtsp", [128, 2, NSB, 128], BF16))
                    qt = es7.enter_context(SB("qt", [96, NB * 128], BF16))
                    pTb = es7.enter_context(SB("pTb", [128, 3, 512], BF16))
                    maskC = es7.enter_context(SB("maskC", [128, 2, 128], BF16))
                    rs = es7.enter_context(SB("rs", [128, 2, 512], F32))
                    ph = Phase(g)
                    ph.dma("pool", maskC[:, :, :], c_maskC[:, :, :], w=["corr"], key="maskC")
                    for c4 in range(4):
                        ph.dma("pool", wout[:, 2 * c4:2 * c4 + 2, :], W["w_out"][l, c4 * 256:(c4 + 1) * 256, :].rearrange("(c p) n -> p c n", p=128), w=["wout"], key=("wout", c4))
                    ph.op("pool", lambda e: e.memset(vtp[:, :, :, :], 0.0), w=["Cv"])
                    ph.op("pool", lambda e: e.memset(vtsp[:, :, :, :], 0.0), w=["Cvs"])
                    cnt = [0]
                    acc = 0
                    for h in range(4):
                        par = h % 2
                        ph.dma("sp", kt[0:96, :, :], rec_ap(kdst, O_KC + h * 128, [[4 * 128, 96], [RBE, NKB], [1, 128]]), w=["Ck"], key=("Kk", 0))
                        ph.dma("sp", kts[0:96, :, :], rec_ap(srec, O_KC + h * 128, [[4 * 128, 96], [RBE, NSB], [1, 128]]), w=["Cks"], key=("Kks", 0))
                        ph.dma("sp", qt[:, :].rearrange("p (b t) -> p b t", b=NB), qc_d[:, :, h, :].rearrange("b d t -> d b t"), w=["qt"], key=("Kq", 0))
                        ph.dma("act", vtp[:, par, :, par * 64:par * 64 + 64], rec_ap(kdst, O_VC + h * 64, [[256, 128], [RBE, NKB], [1, 64]]), w=["Cv"], key="Kv")
                        ph.dma("act", vtsp[:, par, :, par * 64:par * 64 + 64], rec_ap(srec, O_VC + h * 64, [[256, 128], [RBE, NSB], [1, 64]]), w=["Cvs"], key="Kvs")
                        for qg in qgroups:
                            b0 = qg[0]
                            tiles = []
                            if b0 < NBLK:
                                ncols = len(qg) * 128
                                q0 = b0 * 128
                                for kb in range(2 * qg[-1] + 2):
                                    imin = max(b0, kb // 2)
                                    c_lo = (imin - b0) * 128
                                    corrs = []
                                    if kb // 2 >= b0:
                                        lo = (kb // 2 - b0) * 128
                                        corrs = [(lo, lo + 128, maskC[:, kb % 2, :])]
                                    tiles.append((kt[0:96, kb, :], vtp[:, par, kb, :], 128, c_lo, corrs))
                                kres, vres = ["Ck"], ["Cv"]
                            else:
                                ncols = 64
                                q0 = SC
                                for cbi in range(NCB):
                                    tiles.append((kts[0:96, cbi, :], vtsp[:, par, cbi, :], 128, 0, []))
                                tiles.append((kts[0:96, NCB, 0:64], vtsp[0:64, par, NCB, :], 64, 0, []))
                                kres, vres = ["Cks"], ["Cvs"]
                            a = acc % 2
                            acc += 1
                            attn_T(ph, sp3, pTb, opp, a, qt[0:96, q0:q0 + ncols], ncols, tiles, 128, cnt, kres, vres)
                            ph.op("dve", lambda e, a=a, ncols=ncols: e.reciprocal(out=rs[:, a, 0:ncols], in_=opp[:, 2 * a + 1, 0:ncols]), w=[("op", 2 * a + 1), ("rs", a)])
                            ph.op("dve", lambda e, a=a, ncols=ncols, q0=q0, h=h, par=par: e.tensor_tensor(out=OT[par * 64:par * 64 + 64, 6 + h // 2, q0:q0 + ncols], in0=opp[par * 64:par * 64 + 64, 2 * a, 0:ncols],
                                                                                                          in1=rs[par * 64:par * 64 + 64, a, 0:ncols], op=ALU.mult),
                                  r=[("rs", a)], w=[("op", 2 * a), ("OT", 6 + h // 2)])
                    ph.run()
                if STAGE < 7:
                    return
                ph = Phase(g)
                for gi, (t0_, n) in enumerate(groups):
                    for dc in range(8):
                        ob = dc % 4
                        for c in range(8):
                            ph.op("pe", lambda e, c=c, dc=dc, ob=ob, n=n, t0_=t0_: e.matmul(opp[:, ob, 0:n], lhsT=wout[:, c, dc * 128:(dc + 1) * 128], rhs=OT[:, c, t0_:t0_ + n], start=(c == 0), stop=(c == 7)),
                                  r=["wout"], w=[("op", ob)])
                        ph.op("dve", lambda e, dc=dc, ob=ob, t0_=t0_, n=n: e.tensor_tensor(out=xT[:, dc, t0_:t0_ + n], in0=opp[:, ob, 0:n], in1=xT[:, dc, t0_:t0_ + n], op=ALU.add),
                              r=[("xT", dc)], w=[("op", ob), ("xT", dc)])
                ph.run()

        def mem_attention(l):
            with ExitStack() as es8:
                hT = es8.enter_context(SB("hT", [128, 8, 512], BF16))
                xsq = es8.enter_context(SB("xsq", [128, 1, 8, 512], BF16))
                rinv = es8.enter_context(SB("rinv", [128, 1, 512], F32))
                wq = es8.enter_context(SB("wq", [128, 8, 512], BF16))
                wk = es8.enter_context(SB("wk", [128, 8, 512], BF16))
                wv = es8.enter_context(SB("wv", [128, 8, 512], BF16))
                wo = es8.enter_context(SB("wo", [128, 4, D], BF16))
                gm = es8.enter_context(SB("gm", [128, 1, D], F32))
                gkn = es8.enter_context(SB("gkn", [128, 4, 128], F32))
                mtok = es8.enter_context(SB("mtok", [128, 2, D], F32))
                mnb = es8.enter_context(SB("mnb", [128, 2, D], BF16))
                mnT = es8.enter_context(SB("mnT", [128, 8, 256], BF16))
                kf = es8.enter_context(SB("kf", [128, 2, 512], F32))
                vf = es8.enter_context(SB("vf", [128, 2, 512], F32))
                kb16 = es8.enter_context(SB("kb16", [128, 512], BF16))
                mK = es8.enter_context(SB("mK", [128, 2, 4, 256], BF16))
                mV = es8.enter_context(SB("mV", [128, 2, 2, 4, 129], BF16))
                sq = es8.enter_context(SB("sq", [128, 1024], F32))
                ss = es8.enter_context(SB("ss", [128, 8], F32))
                qTn = es8.enter_context(SB("qTn", [128, 4, T], BF16))
                pTb = es8.enter_context(SB("pTb", [128, 2, 512], BF16))
                omT = es8.enter_context(SB("omT", [128, 4, 512], BF16))
                sqh = es8.enter_context(SB("sqh", [128, 512], BF16))
                rn = es8.enter_context(SB("rn", [128, 512], F32))
                rsm = es8.enter_context(SB("rsm", [128, 512], F32))
                gqc = es8.enter_context(SB("gqc", [128, 1], F32))
                pn = es8.enter_context(PS("pn", [128, 2, 512], F32))
                pq = es8.enter_context(PS("pq", [128, 2, 512], F32))
                sp = es8.enter_context(PS("sp", [128, 2, 512], F32))
                opp = es8.enter_context(PS("opp", [128, 2, 512], F32))
                ph = Phase(g)
                for nm, t in [("mem_w_q", wq), ("mem_w_k", wk), ("mem_w_v", wv)]:
                    for c4 in range(2):
                        ph.dma("pool", t[:, 4 * c4:4 * c4 + 4, :], W[nm][l, c4 * 512:(c4 + 1) * 512, :].rearrange("(c p) n -> p c n", p=128), w=[nm], key=(nm, c4))
                ph.dma("pool", wo[:, :, :], W["mem_w_o"][l, :, :].rearrange("(c p) n -> p c n", p=128), w=["wo"], key="wo")
                bcast_row(ph, gm[:, :, :], W["mem_norm_m"][l, :], D, 1, "gm", "gm")
                ph.dma("sp", gqc[:, :], W["mem_q_norm"][l, :].rearrange("(p o) -> p o", o=1), w=["gqc"], key="gqn")
                bcast_row(ph, gkn[:, :, :], W["mem_k_norm"][l, :], 128, 4, "gkn", "gkn")
                ph.op("dve", lambda e: e.tensor_scalar(out=gqc[:, :], in0=gqc[:, :], scalar1=128.0 ** -0.5, scalar2=None, op0=ALU.mult), r=["gqc"], w=["gqc"])
                cnt = [0]
                ph.op("pool", lambda e: e.memset(mV[:, :, :, :, 128:129], 1.0), w=["mV"])
                spb = sp[:, :, :].bitcast(BF16)
                for mb in range(2):
                    ph.dma("sp", mtok[:, mb, :], mem[mb * 128:(mb + 1) * 128, :], w=[("mtok", mb)], key=("mtok", mb))
                    rms_rows(ph, mtok[:, mb, :].rearrange("p (h d) -> p h d", h=1), 128, 1, D, gm[:, :, :], mtok[:, mb, :].rearrange("p (h d) -> p h d", h=1), sq, ss, "m", [("mtok", mb)], [("mtok", mb)], ["gm"])
                    ph.op("act", lambda e, mb=mb: e.copy(out=mnb[:, mb, :], in_=mtok[:, mb, :]), r=[("mtok", mb)], w=[("mnb", mb)])
                    for c in range(8):
                        b = c % 2
                        ph.op("pe", lambda e, c=c, b=b, mb=mb: e.transpose(out=spb[:, b, 0:128], in_=mnb[:, mb, c * 128:(c + 1) * 128], identity=idb[:, :]), r=[("mnb", mb), "idb"], w=[("sp", b)])
                        ph.op("dve", lambda e, c=c, b=b, mb=mb: e.tensor_copy(out=mnT[:, c, mb * 128:(mb + 1) * 128], in_=spb[:, b, 0:128]), w=[("sp", b), "mnT"])
                    for c in range(8):
                        ph.op("pe", lambda e, c=c, mb=mb: e.matmul(pq[:, 0, :], lhsT=mnT[:, c, mb * 128:(mb + 1) * 128], rhs=wk[:, c, :], start=(c == 0), stop=(c == 7)), r=["mnT", "mem_w_k"], w=[("pq", 0)])
                    for c in range(8):
                        ph.op("pe", lambda e, c=c, mb=mb: e.matmul(pq[:, 1, :], lhsT=mnT[:, c, mb * 128:(mb + 1) * 128], rhs=wv[:, c, :], start=(c == 0), stop=(c == 7)), r=["mnT", "mem_w_v"], w=[("pq", 1)])
                    rms_rows(ph, pq[:, 0, :].rearrange("p (h d) -> p h d", h=4), 128, 4, 128, gkn[:, :, :], kf[:, mb, :].rearrange("p (h d) -> p h d", h=4), sq, ss, "mk", [("pq", 0)], [("kf", mb)], ["gkn"])
                    ph.op("act", lambda e, mb=mb: e.copy(out=vf[:, mb, :], in_=pq[:, 1, :]), w=[("pq", 1), ("vf", mb)])
                    ph.dma("sp", o_mk[l, mb * 128:(mb + 1) * 128, :], kf[:, mb, :], r=[("kf", mb)], key=("o_mk", mb))
                    ph.dma("sp", o_mv[l, mb * 128:(mb + 1) * 128, :], vf[:, mb, :], r=[("vf", mb)], key=("o_mv", mb))
                for st in range(2):
                    for mb in range(2):
                        if st == 1:
                            ph.dma("sp", kf[:, mb, :], cm_k[l, mb * 128:(mb + 1) * 128, :], w=[("kf", mb)], key=("cmk", mb))
                            ph.dma("sp", vf[:, mb, :], cm_v[l, mb * 128:(mb + 1) * 128, :], w=[("vf", mb)], key=("cmv", mb))
                        ph.op("act", lambda e, mb=mb: e.copy(out=kb16[:, :], in_=kf[:, mb, :]), r=[("kf", mb)], w=["kb16"])
                        for h in range(4):
                            b = h % 2
                            ph.op("pe", lambda e, h=h, b=b: e.transpose(out=spb[:, b, 0:128], in_=kb16[:, h * 128:(h + 1) * 128], identity=idb[:, :]), r=["kb16", "idb"], w=[("sp", b)])
                            ph.op("dve", lambda e, h=h, b=b, st=st, mb=mb: e.tensor_copy(out=mK[:, st, h, mb * 128:(mb + 1) * 128], in_=spb[:, b, 0:128]), w=[("sp", b), "mK"])
                        ph.op("act", lambda e, st=st, mb=mb: e.copy(out=mV[:, st, mb, :, 0:128], in_=vf[:, mb, :].rearrange("p (h d) -> p h d", h=4)), r=[("vf", mb)], w=["mV"])
                ph.stream = 1
                for gi, (t0_, n) in enumerate(groups):
                    norm_to_hT(ph, hT, xsq, rinv, pn, l * 4 + 2, only=gi)
                    for h in range(4):
                        pb = h % 2
                        for c in range(8):
                            ph.op("pe", lambda e, c=c, pb=pb, h=h, n=n: e.matmul(opp[:, pb, 0:n], lhsT=wq[:, c, h * 128:(h + 1) * 128], rhs=hT[:, c, 0:n], start=(c == 0), stop=(c == 7)),
                                  r=[("hT", c), "mem_w_q"], w=[("op", pb)])
                        ph.op("act", lambda e, pb=pb, n=n: e.activation(out=sqh[:, 0:n], in_=opp[:, pb, 0:n], func=AF.Square), w=[("op", pb), "sqh"])
                        ph.op("pe", lambda e, n=n: e.matmul(pn[:, 0, 0:n], lhsT=ones[:, :], rhs=sqh[:, 0:n], start=True, stop=True), r=["sqh", "ones"], w=[("pn", 0)])
                        ph.op("act", lambda e, n=n: e.activation(out=rn[:, 0:n], in_=pn[:, 0, 0:n], func=AF.Sqrt, scale=1.0 / 128, bias=EPS), w=[("pn", 0), "rn"])
                        ph.op("dve", lambda e, n=n: e.reciprocal(out=rn[:, 0:n], in_=rn[:, 0:n]), r=["rn"], w=["rn"])
                        ph.op("dve", lambda e, pb=pb, h=h, n=n, t0_=t0_: e.scalar_tensor_tensor(out=qTn[:, h, t0_:t0_ + n], in0=opp[:, pb, 0:n], scalar=gqc[:, 0:1], in1=rn[:, 0:n], op0=ALU.mult, op1=ALU.mult),
                              r=["rn", "gqc"], w=[("op", pb), ("qt", gi)])
                ph.stream = 0
                ph.segment = 1
                for gi, (t0_, n) in enumerate(groups):
                    st = 0 if t0_ < SC else 1
                    for h in range(4):
                        tiles = [(mK[:, st, h, mb * 128:(mb + 1) * 128], mV[:, st, mb, h, 0:128], 128, 0, []) for mb in range(2)]
                        attn_T(ph, sp, pTb, opp, 0, qTn[:, h, t0_:t0_ + n], n, tiles, 128, cnt, ["mK"], ["mV"], qres=("qt", gi))
                        ph.op("dve", lambda e, n=n: e.reciprocal(out=rsm[:, 0:n], in_=opp[:, 1, 0:n]), w=[("op", 1), "rsm"])
                        ph.op("dve", lambda e, h=h, n=n: e.tensor_tensor(out=omT[:, h, 0:n], in0=opp[:, 0, 0:n], in1=rsm[:, 0:n], op=ALU.mult), r=["rsm"], w=[("op", 0), "omT"])
                    for dc in range(8):
                        pb2 = dc % 2
                        for c in range(4):
                            ph.op("pe", lambda e, c=c, dc=dc, pb2=pb2, n=n: e.matmul(pn[:, pb2, 0:n], lhsT=wo[:, c, dc * 128:(dc + 1) * 128], rhs=omT[:, c, 0:n], start=(c == 0), stop=(c == 3)), r=["omT", "wo"], w=[("pn", pb2)])
                        ph.op("dve", lambda e, dc=dc, pb2=pb2, t0_=t0_, n=n: e.tensor_tensor(out=xT[:, dc, t0_:t0_ + n], in0=pn[:, pb2, 0:n], in1=xT[:, dc, t0_:t0_ + n], op=ALU.add), r=[("xT", dc)], w=[("pn", pb2), ("xT", dc)])
                ph.run()

        def rms_rows_multi(ph, items):
            for (src, nb, H, dh, gain, out, sqv, ssv, tag, src_res, out_res, gain_res) in items:
                ph.op("act", lambda e, src=src, sqv=sqv, nb=nb, H=H, dh=dh: e.activation(out=sqv[:nb, 0:H * dh].rearrange("p (h d) -> p h d", h=H), in_=src, func=AF.Square), r=[], w=list(src_res) + [("sq", tag)])
            for (src, nb, H, dh, gain, out, sqv, ssv, tag, src_res, out_res, gain_res) in items:
                ph.op("dve", lambda e, sqv=sqv, ssv=ssv, nb=nb, H=H, dh=dh: e.tensor_reduce(out=ssv[:nb, 0:H], in_=sqv[:nb, 0:H * dh].rearrange("p (h d) -> p h d", h=H), axis=AX.X, op=ALU.add), r=[("sq", tag)], w=[("ss", tag)])
            for (src, nb, H, dh, gain, out, sqv, ssv, tag, src_res, out_res, gain_res) in items:
                ph.op("act", lambda e, ssv=ssv, nb=nb, H=H, dh=dh: e.activation(out=ssv[:nb, 0:H], in_=ssv[:nb, 0:H], func=AF.Sqrt, scale=1.0 / dh, bias=EPS), r=[("ss", tag)], w=[("ss", tag)])
            for (src, nb, H, dh, gain, out, sqv, ssv, tag, src_res, out_res, gain_res) in items:
                ph.op("dve", lambda e, ssv=ssv, nb=nb, H=H: e.reciprocal(out=ssv[:nb, 0:H], in_=ssv[:nb, 0:H]), r=[("ss", tag)], w=[("ss", tag)])
            for (src, nb, H, dh, gain, out, sqv, ssv, tag, src_res, out_res, gain_res) in items:
                ph.op("dve", lambda e, src=src, out=out, ssv=ssv, nb=nb, H=H, dh=dh: e.tensor_tensor(out=out, in0=src, in1=ssv[:nb, 0:H].unsqueeze(2).to_broadcast([nb, H, dh]), op=ALU.mult), r=[("ss", tag)], w=list(src_res) + list(out_res))
            for (src, nb, H, dh, gain, out, sqv, ssv, tag, src_res, out_res, gain_res) in items:
                ph.op("dve", lambda e, out=out, gain=gain: e.tensor_tensor(out=out, in0=out, in1=gain, op=ALU.mult), r=list(out_res) + list(gain_res), w=list(out_res))

        for l in range(DEPTH):
            es_win = ExitStack()
            win = es_win.enter_context(SB("win", [128, 8, IN_COLS], BF16))

            def pf_win(ph, c4, l=l, win=win):
                ph.dma("pool", win[:, 2 * c4:2 * c4 + 2, :], W["w_in"][l, c4 * 256:(c4 + 1) * 256, :].rearrange("(c p) n -> p c n", p=128), w=["win"], key=("win", c4))
            ffn(l, 1, prefetch=pf_win)
            if STAGE < 2:
                continue

            with ExitStack() as es9:
                hT = es9.enter_context(SB("hT", [128, 8, 512], BF16))
                wuq = es9.enter_context(SB("wuq", [128, 3, 384], BF16))
                wukv = es9.enter_context(SB("wukv", [128, 2, 512], BF16))
                gall = es9.enter_context(SB("gall", [128, 28, 64], F32))
                gcq = es9.enter_context(SB("gcq", [128, 1, 384], F32))
                gckv = es9.enter_context(SB("gckv", [128, 1, 256], F32))
                gq96 = es9.enter_context(SB("gq96", [128, 4, 96], F32))
                gk96 = es9.enter_context(SB("gk96", [128, 4, 96], F32))
                Nf = es9.enter_context(SB("Nf", [128, 1, IN_COLS], F32))
                tokb2 = es9.enter_context(SB("tokb2", [128, 2464], BF16))
                stg2 = es9.enter_context(SB("stg2", [128, 2048], BF16))
                latT2 = es9.enter_context(SB("latT2", [128, 2, 128], BF16))
                kcf2 = es9.enter_context(SB("kcf2", [128, 4, 96], F32))
                sq2 = es9.enter_context(SB("sq2", [128, 384], F32))
                ss2 = es9.enter_context(SB("ss2", [128, 8], F32))
                ak2 = es9.enter_context(SB("ak2", [128, 32], F32))
                kp2 = es9.enter_context(SB("kp2", [128, 32], F32))
                sq = es9.enter_context(SB("sq", [128, 2560], F32))
                ss = es9.enter_context(SB("ss", [128, 32], F32))
                cs = es9.enter_context(SB("cs", [128, 32], F32))
                aq = es9.enter_context(SB("aq", [128, 32], F32))
                ak = es9.enter_context(SB("ak", [128, 32], F32))
                tokb = es9.enter_context(SB("tokb", [128, 3360], BF16))
                latT = es9.enter_context(SB("latT", [128, 5, 128], BF16))
                qcf = es9.enter_context(SB("qcf", [128, 4, 96], F32))
                kcf = es9.enter_context(SB("kcf", [128, 4, 96], F32))
                rt = es9.enter_context(SB("rt", [128, 4, 16], F32))
                rt2 = es9.enter_context(SB("rt2", [128, 4, 16], F32))
                stg = es9.enter_context(SB("stg", [128, 2, 2816], BF16))
                xsq = es9.enter_context(SB("xsq", [128, 1, 8, 512], BF16))
                rinv = es9.enter_context(SB("rinv", [128, 1, 512], F32))
                pu = es9.enter_context(PS("pu", [128, 6, 512], F32))
                pn = es9.enter_context(PS("pn", [128, 2, 512], F32))
                ptr = pu[:, :, :].bitcast(BF16)
                ph = Phase(g)
                ph.dma("pool", wuq[:, :, :], W["c_w_uq"][l, :, :].rearrange("(c p) n -> p c n", p=128), w=["wuq"], key="wuq")
                ph.dma("pool", wukv[:, :, :], W["c_w_ukv"][l, :, :].rearrange("(c p) n -> p c n", p=128), w=["wukv"], key="wukv")
                ph.op("pool", lambda e: e.memset(gall[:, 8:12, :], 1.0), w=["gall"])
                bcast_row(ph, gall[:, 0:4, :], W["a_q_norm"][l, :], 64, 4, "g0", "gall")
                bcast_row(ph, gall[:, 4:8, :], W["a_k_norm"][l, :], 64, 4, "g1", "gall")
                bcast_row(ph, gall[:, 12:20, :], W["b_q_norm"][l, :], 64, 8, "g2", "gall")
                bcast_row(ph, gall[:, 20:28, :], W["b_k_norm"][l, :], 64, 8, "g3", "gall")
                bcast_row(ph, gcq[:, :, :], W["c_q_lat_norm"][l, :], 384, 1, "g4", "gcq")
                bcast_row(ph, gckv[:, :, :], W["c_kv_lat_norm"][l, :], 256, 1, "g5", "gckv")
                bcast_row(ph, gq96[:, :, :], W["c_q_norm"][l, :], 96, 4, "g6", "gq96")
                bcast_row(ph, gk96[:, :, :], W["c_k_norm"][l, :], 96, 4, "g7", "gk96")
                ph.op("dve", lambda e: e.tensor_scalar(out=gall[:, 0:4, :], in0=gall[:, 0:4, :], scalar1=0.125, scalar2=None, op0=ALU.mult), r=["gall"], w=["gall"])
                ph.op("dve", lambda e: e.tensor_scalar(out=gall[:, 12:20, :], in0=gall[:, 12:20, :], scalar1=0.125, scalar2=None, op0=ALU.mult), r=["gall"], w=["gall"])
                ph.op("dve", lambda e: e.tensor_scalar(out=gq96[:, :, :], in0=gq96[:, :, :], scalar1=96.0 ** -0.5, scalar2=None, op0=ALU.mult), r=["gq96"], w=["gq96"])

                def kv_tail(ph, S, si, nb, rec_t, rec_base, bs, rr=None):
                    P_ = bs["pfx"]
                    tokb_, stg_, latT_, kcf_, sq_, ss_, ak_ = bs["tokb"], bs["stg"], bs["latT"], bs["kcf"], bs["sq"], bs["ss"], bs["ak"]
                    pA, rA = bs["pA"]
                    (pB0, rB0), (pB1, rB1) = bs["pB"]
                    pL, rL = bs["pL"]
                    pKV, rKV = bs["pKV"]
                    pC, rC = bs["pC"]
                    kpsrc = bs["kp"]
                    ks = bs["ksfx"]
                    if S is not None:
                        srcr = [("src", id(S), si)]
                        ph.op("act", lambda e: e.copy(out=tokb_[:nb, 0:512], in_=S[:nb, si, 256:768]), r=srcr, w=[P_ + "tokb_a"])
                        ph.op("act", lambda e: e.copy(out=tokb_[:nb, 512:512 + 544].rearrange("p (a d) -> p a d", a=8)[:, :, 0:64], in_=S[:nb, si, 1280:1792].rearrange("p (a d) -> p a d", a=8)), r=srcr, w=[P_ + "tokb_b"])
                        ph.op("act", lambda e: e.copy(out=tokb_[:nb, 1056:1568], in_=S[:nb, si, 1792:2304]), r=srcr, w=[P_ + "tokb_bv"])
                        ph.op("act", lambda e: e.copy(out=tokb_[:nb, 1568:1824], in_=S[:nb, si, 2688:2944]), r=srcr, w=[P_ + "tokb_c"])
                    if bs.get("do_a", True):
                        for h in range(4):
                            ph.op("pe", lambda e, h=h: e.transpose(out=pA[0:64, h * 128:h * 128 + nb], in_=tokb_[:nb, h * 64:(h + 1) * 64], identity=idb[:nb, :nb]),
                                  r=[P_ + "tokb_a", "idb"], w=[rA])
                        ph.op("dve", lambda e: e.tensor_copy(out=stg_[0:64, 0:512].rearrange("p (h t) -> p h t", h=4)[:, :, 0:nb], in_=pA[0:64, 0:512].rearrange("p (h t) -> p h t", h=4)[:, :, 0:nb]),
                              w=[rA, P_ + "stg_ka"])
                        ph.dma("sp", rec_ap(rec_t, rec_base + O_KA, [[512, 64], [128, 4], [1, nb]]), stg_[0:64, 0:512].rearrange("p (h t) -> p h t", h=4)[:, :, 0:nb], r=[P_ + "stg_ka"], w=([(rr, 0)] if rr is not None else []), key="st_ka" + ks)
                        ph.dma("act", rec_ap(rec_t, rec_base + O_VA, [[256, nb], [1, 256]]), tokb_[:nb, 256:512], r=[P_ + "tokb_a"], w=([(rr, 1)] if rr is not None else []), key="st_va" + ks)
                    ph.op("dve", lambda e: e.tensor_copy(out=tokb_[:nb, 512:512 + 544].rearrange("p (a d) -> p a d", a=8)[:, :, 64:68], in_=ak_[:nb, :].rearrange("p (a d) -> p a d", a=8)), r=[P_ + "ak"], w=[P_ + "tokb_b"])
                    for hj in range(8):
                        pBx, rBx = (pB0, rB0) if hj < 4 else (pB1, rB1)
                        col = (hj % 4) * 128
                        ph.op("pe", lambda e, hj=hj, pBx=pBx, col=col: e.transpose(out=pBx[0:68, col:col + nb], in_=tokb_[:nb, 512 + hj * 68:512 + (hj + 1) * 68], identity=idb[:nb, :nb]),
                              r=[P_ + "tokb_b", "idb"], w=[rBx])
                    for half, (pBx, rBx) in enumerate([(pB0, rB0), (pB1, rB1)]):
                        ph.op("dve", lambda e, half=half, pBx=pBx: e.tensor_copy(out=stg_[0:68, 512 + half * 512:1024 + half * 512].rearrange("p (h t) -> p h t", h=4)[:, :, 0:nb],
                                                                                  in_=pBx[0:68, 0:512].rearrange("p (h t) -> p h t", h=4)[:, :, 0:nb]),
                              w=[rBx, P_ + "stg_kb"])
                    ph.dma("sp", rec_ap(rec_t, rec_base + O_KB, [[1024, 68], [128, 8], [1, nb]]), stg_[0:68, 512:1536].rearrange("p (h t) -> p h t", h=8)[:, :, 0:nb], r=[P_ + "stg_kb"], w=([(rr, 2)] if rr is not None else []), key="st_kb" + ks)
                    ph.dma("act", rec_ap(rec_t, rec_base + O_VB, [[512, nb], [1, 512]]), tokb_[:nb, 1056:1568], r=[P_ + "tokb_bv"], w=([(rr, 3)] if rr is not None else []), key="st_vb" + ks)
                    for c in range(2):
                        ph.op("pe", lambda e, c=c: e.transpose(out=pL[:, c * 128:c * 128 + nb], in_=tokb_[:nb, 1568 + c * 128:1568 + (c + 1) * 128], identity=idb[:nb, :nb]),
                              r=[P_ + "tokb_c", "idb"], w=[rL])
                    ph.op("dve", lambda e: e.tensor_copy(out=latT_[:, 0:2, 0:nb], in_=pL[:, 0:256].rearrange("p (c t) -> p c t", c=2)[:, :, 0:nb]), w=[rL, P_ + "latT_kv"])
                    for c in range(2):
                        ph.op("pe", lambda e, c=c: e.matmul(pKV[:nb, 0:512], lhsT=latT_[:, c, 0:nb], rhs=wukv[:, c, :], start=(c == 0), stop=(c == 1)),
                              r=[P_ + "latT_kv", "wukv"], w=[rKV])
                    ph.op("act", lambda e: e.copy(out=kcf_[:nb, :, 0:64], in_=pKV[:nb, 0:512].rearrange("p (h d) -> p h d", h=4)[:, :, 0:64]), w=[rKV, P_ + "kcf"])
                    ph.op("pool", lambda e: e.tensor_copy(out=kcf_[:nb, :, 64:96], in_=kpsrc.unsqueeze(1).to_broadcast([nb, 4, 32])), r=bs["kpres"], w=[P_ + "kcf"])
                    ph.op("act", lambda e: e.copy(out=tokb_[:nb, 1824:2080].rearrange("p (h d) -> p h d", h=4), in_=pKV[:nb, 0:512].rearrange("p (h d) -> p h d", h=4)[:, :, 64:128]), w=[rKV, P_ + "tokb_cv"])
                    rms_rows(ph, kcf_[:nb, :, :], nb, 4, 96, gk96[:nb, :, :], kcf_[:nb, :, :], sq_, ss_, bs["sqtag"], [P_ + "kcf"], [P_ + "kcf"], ["gk96"])
                    ph.op("act", lambda e: e.copy(out=tokb_[:nb, 2080:2464].rearrange("p (h d) -> p h d", h=4), in_=kcf_[:nb, :, :]), r=[P_ + "kcf"], w=[P_ + "tokb_ck"])
                    for h in range(4):
                        ph.op("pe", lambda e, h=h: e.transpose(out=pC[0:96, h * 128:h * 128 + nb], in_=tokb_[:nb, 2080 + h * 96:2080 + (h + 1) * 96], identity=idb[:nb, :nb]),
                              r=[P_ + "tokb_ck", "idb"], w=[rC])
                    ph.op("dve", lambda e: e.tensor_copy(out=stg_[0:96, 1536:2048].rearrange("p (h t) -> p h t", h=4)[:, :, 0:nb], in_=pC[0:96, 0:512].rearrange("p (h t) -> p h t", h=4)[:, :, 0:nb]),
                          w=[rC, P_ + "stg_kc"])
                    ph.dma("sp", rec_ap(rec_t, rec_base + O_KC, [[512, 96], [128, 4], [1, nb]]), stg_[0:96, 1536:2048].rearrange("p (h t) -> p h t", h=4)[:, :, 0:nb], r=[P_ + "stg_kc"], w=([(rr, 4)] if rr is not None else []), key="st_kc" + ks)
                    ph.dma("act", rec_ap(rec_t, rec_base + O_VC, [[256, nb], [1, 256]]), tokb_[:nb, 1824:2080], r=[P_ + "tokb_cv"], w=([(rr, 5)] if rr is not None else []), key="st_vc" + ks)

                pnb = pn[:, :, :].bitcast(BF16)
                bs1 = dict(pfx="s1", tokb=tokb, stg=stg[:, 0, :], latT=latT, kcf=kcf, sq=sq[:, 2176:2560], ss=ss[:, 26:30], sqtag="x1", ak=ak, ksfx="",
                           pA=(ptr[:, 0, 0:512], ("pu", 0)), pB=((ptr[:, 1, 0:512], ("pu", 1)), (ptr[:, 2, 0:512], ("pu", 2))),
                           pL=(ptr[:, 3, 0:256], ("pu", 3)), pKV=(pu[:, 3, 0:512], ("pu", 3)), pC=(ptr[:, 4, 0:512], ("pu", 4)))
                bs2 = dict(pfx="s2", tokb=tokb2, stg=stg2, latT=latT2, kcf=kcf2, sq=sq2, ss=ss2, sqtag="s2k", ak=ak2, ksfx="2",
                           pA=(pnb[:, 0, 0:512], ("pn", 0)), pB=((pnb[:, 1, 0:512], ("pn", 1)), (pnb[:, 1, 512:1024], ("pn", 1))),
                           pL=(pnb[:, 0, 512:768], ("pn", 0)), pKV=(pn[:, 0, 0:512], ("pn", 0)), pC=(pnb[:, 1, 0:512], ("pn", 1)),
                           kp=kp2[:, :], kpres=["s2kp"])

                def proj_block(bi, hc0):
                    c0, nb = blk_cols(bi)
                    si = 0
                    nres = [("src", id(Nf), si)]
                    ph.dma("act", cs[:nb, :], c_cs[c0:c0 + nb, :], w=["cs"], key="cs")
                    ph.dma("act", aq[:nb, :], c_augq[c0:c0 + nb, :], w=["aq"], key="aq")
                    ph.dma("act", ak[:nb, :], c_augk[c0:c0 + nb, :], w=["s1ak"], key="ak")
                    for c in range(8):
                        for cg in range(6):
                            w0 = cg * 512
                            wn = min(512, IN_COLS - w0)
                            ph.op("pe", lambda e, c=c, cg=cg, w0=w0, wn=wn: e.matmul(pu[:nb, cg, 0:wn], lhsT=hT[:, c, hc0:hc0 + nb], rhs=win[:, c, w0:w0 + wn], start=(c == 0), stop=(c == 7)),
                                  r=[("hT", c), "win"], w=[("pu", cg)])
                    pur = [("pu", k) for k in range(6)]
                    puf = pu[:nb, :, :].rearrange("p a b -> p (a b)")
                    ph.op("act", lambda e: e.copy(out=Nf[:nb, si, :], in_=puf[:, 0:IN_COLS]), w=pur + nres)
                    rms_rows_multi(ph, [
                        (puf[:, 0:512].rearrange("p (h d) -> p h d", h=8), nb, 8, 64, gall[:nb, 0:8, :], Nf[:nb, si, 0:512].rearrange("p (h d) -> p h d", h=8), sq[:, 0:512], ss[:, 0:8], "a", pur, nres, ["gall"]),
                        (puf[:, 768:1792].rearrange("p (h d) -> p h d", h=16), nb, 16, 64, gall[:nb, 12:28, :], Nf[:nb, si, 768:1792].rearrange("p (h d) -> p h d", h=16), sq[:, 512:1536], ss[:, 8:24], "b", pur, nres, ["gall"]),
                        (puf[:, 2304:2688].rearrange("p (h d) -> p h d", h=1), nb, 1, 384, gcq[:nb, :, :], Nf[:nb, si, 2304:2688].rearrange("p (h d) -> p h d", h=1), sq[:, 1536:1920], ss[:, 24:25], "cq", pur, nres, ["gcq"]),
                        (puf[:, 2688:2944].rearrange("p (h d) -> p h d", h=1), nb, 1, 256, gckv[:nb, :, :], Nf[:nb, si, 2688:2944].rearrange("p (h d) -> p h d", h=1), sq[:, 1920:2176], ss[:, 25:26], "ckv", pur, nres, ["gckv"]),
                    ])
                    kp = puf[:, 2944:2976]
                    ph.op("dve", lambda e: e.tensor_tensor(out=rt[:nb, 0, :], in0=kp[:, 0:16], in1=cs[:nb, 0:16], op=ALU.mult), r=["cs"], w=pur + ["rt"])
                    ph.op("dve", lambda e: e.tensor_tensor(out=rt[:nb, 1, :], in0=kp[:, 16:32], in1=cs[:nb, 16:32], op=ALU.mult), r=["cs"], w=pur + ["rt"])
                    ph.op("dve", lambda e: e.tensor_tensor(out=Nf[:nb, si, 2944:2960], in0=rt[:nb, 0, :], in1=rt[:nb, 1, :], op=ALU.subtract), r=["rt"], w=nres)
                    ph.op("dve", lambda e: e.tensor_tensor(out=rt[:nb, 2, :], in0=kp[:, 0:16], in1=cs[:nb, 16:32], op=ALU.mult), r=["cs"], w=pur + ["rt"])
                    ph.op("dve", lambda e: e.tensor_tensor(out=rt[:nb, 3, :], in0=kp[:, 16:32], in1=cs[:nb, 0:16], op=ALU.mult), r=["cs"], w=pur + ["rt"])
                    ph.op("dve", lambda e: e.tensor_tensor(out=Nf[:nb, si, 2960:2976], in0=rt[:nb, 2, :], in1=rt[:nb, 3, :], op=ALU.add), r=["rt"], w=nres)
                    ph.dma("sp", o_bk[l, c0:c0 + nb, :], Nf[:nb, si, 1280:1792], r=nres, key="o_bk")
                    ph.dma("sp", o_bv[l, c0:c0 + nb, :], Nf[:nb, si, 1792:2304], r=nres, key="o_bv")
                    ph.dma("sp", o_ckv[l, c0:c0 + nb, :], Nf[:nb, si, 2688:2944], r=nres, key="o_ckv")
                    ph.dma("sp", o_kpe[l, c0:c0 + nb, :], Nf[:nb, si, 2944:2976], r=nres, key="o_kpe")
                    if bi >= NBLK - 2:
                        r0 = (bi - (NBLK - 2)) * 128
                        ph.dma("sp", o_ak[l, r0:r0 + nb, :], Nf[:nb, si, 256:512], r=nres, key="o_ak")
                        ph.dma("sp", o_av[l, r0:r0 + nb, :], Nf[:nb, si, 512:768], r=nres, key="o_av")
                    ph.op("act", lambda e: e.copy(out=tokb[:nb, 2464:2720], in_=Nf[:nb, si, 0:256]), r=nres, w=["tokb_qa"])
                    for h in range(4):
                        ph.op("pe", lambda e, h=h: e.transpose(out=ptr[0:64, 5, h * 128:h * 128 + nb], in_=tokb[:nb, 2464 + h * 64:2464 + (h + 1) * 64], identity=idb[:nb, :nb]),
                              r=["tokb_qa", "idb"], w=[("pu", 5)])
                    ph.op("dve", lambda e: e.tensor_copy(out=stg[0:64, 1, 0:512].rearrange("p (h t) -> p h t", h=4)[:, :, 0:nb], in_=ptr[0:64, 5, 0:512].rearrange("p (h t) -> p h t", h=4)[:, :, 0:nb]),
                          w=[("pu", 5), "stg_qa"])
                    ph.dma("sp", qa_d[bi, :, :, 0:nb], stg[0:64, 1, 0:512].rearrange("p (h t) -> p h t", h=4)[:, :, 0:nb], r=["stg_qa"], key="st_qa")
                    ph.op("act", lambda e: e.copy(out=tokb[:nb, 2720:3264].rearrange("p (a d) -> p a d", a=8)[:, :, 0:64], in_=Nf[:nb, si, 768:1280].rearrange("p (a d) -> p a d", a=8)), r=nres, w=["tokb_qb"])
                    ph.op("dve", lambda e: e.tensor_copy(out=tokb[:nb, 2720:3264].rearrange("p (a d) -> p a d", a=8)[:, :, 64:68], in_=aq[:nb, :].rearrange("p (a d) -> p a d", a=8)), r=["aq"], w=["tokb_qb"])
                    for hj in range(8):
                        bank, col = hj // 4, (hj % 4) * 128
                        ph.op("pe", lambda e, hj=hj, bank=bank, col=col: e.transpose(out=ptr[0:68, bank, col:col + nb], in_=tokb[:nb, 2720 + hj * 68:2720 + (hj + 1) * 68], identity=idb[:nb, :nb]),
                              r=["tokb_qb", "idb"], w=[("pu", bank)])
                    for half in range(2):
                        ph.op("dve", lambda e, half=half: e.tensor_copy(out=stg[0:68, 1, 512 + half * 512:1024 + half * 512].rearrange("p (h t) -> p h t", h=4)[:, :, 0:nb],
                                                                         in_=ptr[0:68, half, 0:512].rearrange("p (h t) -> p h t", h=4)[:, :, 0:nb]),
                              w=[("pu", half), "stg_qb"])
                    ph.dma("sp", qb_d[bi, :, :, 0:nb], stg[0:68, 1, 512:1536].rearrange("p (h t) -> p h t", h=8)[:, :, 0:nb], r=["stg_qb"], key="st_qb")
                    ph.op("act", lambda e: e.copy(out=stg[:nb, 1, 2048:2432], in_=Nf[:nb, si, 2304:2688]), r=nres, w=["cq_b"])
                    for c in range(3):
                        ph.op("pe", lambda e, c=c: e.transpose(out=ptr[:, 2, c * 128:c * 128 + nb], in_=stg[:nb, 1, 2048 + c * 128:2048 + (c + 1) * 128], identity=idb[:nb, :nb]),
                              r=["cq_b", "idb"], w=[("pu", 2)])
                    ph.op("dve", lambda e: e.tensor_copy(out=latT[:, 2:5, 0:nb], in_=ptr[:, 2, 0:384].rearrange("p (c t) -> p c t", c=3)[:, :, 0:nb]), w=[("pu", 2), "latT_q"])
                    for c in range(3):
                        ph.op("pe", lambda e, c=c: e.matmul(pu[:nb, 2, 0:384], lhsT=latT[:, 2 + c, 0:nb], rhs=wuq[:, c, :], start=(c == 0), stop=(c == 2)),
                              r=["latT_q", "wuq"], w=[("pu", 2)])
                    pq = pu[:nb, 2, 0:384].rearrange("p (h d) -> p h d", h=4)
                    ph.op("act", lambda e: e.copy(out=qcf[:nb, :, 0:64], in_=pq[:, :, 0:64]), w=[("pu", 2), "qcf"])
                    cosb = cs[:nb, 0:16].unsqueeze(1).to_broadcast([nb, 4, 16])
                    sinb = cs[:nb, 16:32].unsqueeze(1).to_broadcast([nb, 4, 16])
                    ph.op("dve", lambda e: e.tensor_tensor(out=rt[:nb, :, :], in0=pq[:, :, 64:80], in1=cosb, op=ALU.mult), r=["cs"], w=[("pu", 2), "rt"])
                    ph.op("dve", lambda e: e.tensor_tensor(out=rt2[:nb, :, :], in0=pq[:, :, 80:96], in1=sinb, op=ALU.mult), r=["cs"], w=[("pu", 2), "rt2"])
                    ph.op("dve", lambda e: e.tensor_tensor(out=qcf[:nb, :, 64:80], in0=rt[:nb, :, :], in1=rt2[:nb, :, :], op=ALU.subtract), r=["rt", "rt2"], w=["qcf"])
                    ph.op("dve", lambda e: e.tensor_tensor(out=rt[:nb, :, :], in0=pq[:, :, 64:80], in1=sinb, op=ALU.mult), r=["cs"], w=[("pu", 2), "rt"])
                    ph.op("dve", lambda e: e.tensor_tensor(out=rt2[:nb, :, :], in0=pq[:, :, 80:96], in1=cosb, op=ALU.mult), r=["cs"], w=[("pu", 2), "rt2"])
                    ph.op("dve", lambda e: e.tensor_tensor(out=qcf[:nb, :, 80:96], in0=rt[:nb, :, :], in1=rt2[:nb, :, :], op=ALU.add), r=["rt", "rt2"], w=["qcf"])
                    rms_rows(ph, qcf[:nb, :, :], nb, 4, 96, gq96[:nb, :, :], qcf[:nb, :, :], sq[:, 2176:2560], ss[:, 26:30], "x1", ["qcf"], ["qcf"], ["gq96"])
                    ph.op("act", lambda e: e.copy(out=stg[:nb, 1, 2432:2816].rearrange("p (h d) -> p h d", h=4), in_=qcf[:nb, :, :]), r=["qcf"], w=["qc_b"])
                    for h in range(4):
                        ph.op("pe", lambda e, h=h: e.transpose(out=ptr[0:96, 5, h * 128:h * 128 + nb], in_=stg[:nb, 1, 2432 + h * 96:2432 + (h + 1) * 96], identity=idb[:nb, :nb]),
                              r=["qc_b", "idb"], w=[("pu", 5)])
                    ph.op("dve", lambda e: e.tensor_copy(out=stg[0:96, 0, 2048:2560].rearrange("p (h t) -> p h t", h=4)[:, :, 0:nb], in_=ptr[0:96, 5, 0:512].rearrange("p (h t) -> p h t", h=4)[:, :, 0:nb]),
                          w=[("pu", 5), "stg_qc"])
                    ph.dma("sp", qc_d[bi, :, :, 0:nb], stg[0:96, 0, 2048:2560].rearrange("p (h t) -> p h t", h=4)[:, :, 0:nb], r=["stg_qc"], key="st_qc")
                    bs1["kp"] = Nf[:nb, si, 2944:2976]
                    bs1["kpres"] = nres
                    bs1["ak"] = ak
                    if bi < NBLK:
                        kv_tail(ph, Nf, si, nb, ksrc, bi * RBE, dict(bs1), rr=("rec", bi))
                        RB = RBE // 128
                        ph.custom("pool", lambda e: e.collective_compute("AllGather", ALU.bypass, replica_groups=[[0, 1], [2, 3], [4, 5], [6, 7]],
                                                                         ins=[ksrc[bi * RB:(bi + 1) * RB, :].opt()], outs=[kdst[2 * bi * RB:(2 * bi + 2) * RB, :].opt()]),
                                  r=[(("rec", bi), k) for k in range(6)], key="cc", inc=1)
                    else:
                        kv_tail(ph, Nf, si, nb, srec, NCB * RBE, dict(bs1))
                for gi, (gt0, gn) in enumerate(groups):
                    norm_to_hT(ph, hT, xsq, rinv, pu[:, 4:6, :], l * 4 + 1, only=gi, pres=lambda b: ("pu", 4 + b))
                    for bi in ([NBLK] if gt0 >= SC else range(gt0 // 128, (gt0 + gn) // 128)):
                        proj_block(bi, blk_cols(bi)[0] - gt0)
                ph.stream = 1
                for cb in range(NCB):
                    r0, r1 = cb * 128, (cb + 1) * 128
                    ab = cb - (NCB - 4)
                    if ab >= 0:
                        ph.dma("pool", tokb2[:, 0:256], ca_k[l, ab * 128:(ab + 1) * 128, :], w=["s2tokb_a"], key="ci0")
                        ph.dma("pool", tokb2[:, 256:512], ca_v[l, ab * 128:(ab + 1) * 128, :], w=["s2tokb_a"], key="ci1")
                    ph.dma("pool", tokb2[:, 512:512 + 544].rearrange("p (a d) -> p a d", a=8)[:, :, 0:64], cb_k[l, r0:r1, :].rearrange("p (a d) -> p a d", a=8), w=["s2tokb_b"], key="ci2")
                    ph.dma("pool", tokb2[:, 1056:1568], cb_v[l, r0:r1, :], w=["s2tokb_bv"], key="ci3")
                    ph.dma("pool", tokb2[:, 1568:1824], cc_kv[l, r0:r1, :], w=["s2tokb_c"], key="ci4")
                    ph.dma("sp", kp2[:, :], cc_kpe[l, r0:r1, :], w=["s2kp"], key="ci5")
                    ph.dma("act", ak2[:, :], c_augkc[r0:r1, :], w=["s2ak"], key="ak2")
                    kv_tail(ph, None, 0, 128, srec, cb * RBE, dict(bs2, do_a=(ab >= 0)))
                ph.stream = 0
                ph.run()
            es_win.close()

            if STAGE < 3:
                continue
            if STAGE < 4:
                continue
            attention(l, None)
            if STAGE < 8:
                continue
            mem_attention(l)
            if STAGE < 9:
                continue
            ffn(l, 2)

        with ExitStack() as es10:
            ytok = es10.enter_context(SB("ytok", [128, 2, D], F32))
            pt2 = es10.enter_context(PS("pt2", [128, 2, 512], F32))
            ph = Phase(g)
            for bi in range(NB):
                c0, nb = blk_cols(bi)
                s = bi % 2
                for c in range(8):
                    ps = c % 2
                    ph.op("pe", lambda e, c=c, ps=ps, c0=c0, nb=nb: e.transpose(out=pt2[:nb, ps, 0:128], in_=xT[:, c, c0:c0 + nb], identity=idf[:, :]),
                          r=[("xT", c), "idf"], w=[("pt2", ps)])
                    ph.op("act", lambda e, s=s, c=c, ps=ps, nb=nb: e.copy(out=ytok[:nb, s, c * 128:(c + 1) * 128], in_=pt2[:nb, ps, 0:128]),
                          w=[("pt2", ps), ("ytok", s)])
                ph.dma("sp", y[c0:c0 + nb, :], ytok[:nb, s, :], r=[("ytok", s)], key=("yst", s))
            ph.run()
    return nc


WNAMES = ["ffn1_norm", "ffn1_w_gate", "ffn1_w_up", "ffn1_w_down", "mix_norm", "w_in", "a_q_norm", "a_k_norm",
          "a_rel_bias", "b_q_norm", "b_k_norm", "b_lambda", "b_sub_norm", "c_q_lat_norm", "c_kv_lat_norm",
          "c_w_uq", "c_w_ukv", "c_q_norm", "c_k_norm", "w_out", "mem_norm_x", "mem_w_q", "mem_q_norm",
          "mem_norm_m", "mem_w_k", "mem_w_v", "mem_k_norm", "mem_w_o", "ffn2_norm", "ffn2_w_gate",
          "ffn2_w_up", "ffn2_w_down"]


def _consts(p, NBLK, DEPTH, PAST):
    SC = NBLK * 128
    T = SC + 64
    t = np.arange(SC)
    pos = np.concatenate([(2 * (t // 128) + p) * 128 + t % 128, PAST + np.arange(64)]).astype(np.int64)
    inv = (10000.0 ** (-np.arange(16, dtype=np.float32) / 16)).astype(np.float32)
    ang = pos.astype(np.float32)[:, None] * inv[None, :]
    cs = np.concatenate([np.cos(ang), np.sin(ang)], axis=1).astype(np.float32)
    slopes = np.exp2(-8.0 * (np.arange(4, dtype=np.float32) + 1.0) / 4).astype(np.float32)

    def aug(posv, qside):
        lo = (posv % 128).astype(np.float32)
        hi = (posv - posv % 128).astype(np.float32)
        out = np.zeros((len(posv), 8, 4), np.float32)
        for h in range(4):
            for j in range(2):
                if qside:
                    out[:, 2 * h + j] = np.stack([-slopes[h] * hi, -slopes[h] * lo, np.ones_like(lo), np.ones_like(lo)], 1)
                else:
                    out[:, 2 * h + j] = np.stack([np.ones_like(lo), np.ones_like(lo), slopes[h] * hi, slopes[h] * lo], 1)
        return out.reshape(len(posv), 32)

    k = np.arange(128)[:, None]
    q = np.arange(128)[None, :]
    kc, qc = k // 64, q // 64
    diag = np.zeros((4, 128, 128), np.float32)
    for h in range(4):
        d = np.where((kc == qc) & (k > q), -2.0 * slopes[h] * (k - q), 0.0)
        diag[h] = np.where(kc > qc, NEG, d)
    dmask = np.where(kc > qc, NEG, 0.0).astype(np.float32)
    full = np.full((128, 128), NEG, np.float32)
    zero = np.zeros((128, 128), np.float32)
    corrB = np.zeros((128, 2, 4, 128), np.float32)
    maskC = np.zeros((128, 2, 128), np.float32)
    for h in range(4):
        corrB[:, 0, h, :] = diag[h] if p == 0 else zero
        corrB[:, 1, h, :] = full if p == 0 else diag[h]
    maskC[:, 0, :] = dmask if p == 0 else zero
    maskC[:, 1, :] = full if p == 0 else dmask
    k64 = np.arange(64)[:, None]
    q64 = np.arange(64)[None, :]
    corrBs = np.zeros((64, 4, 64), np.float32)
    for h in range(4):
        corrBs[:, h, :] = np.where(k64 > q64, -2.0 * slopes[h] * (k64 - q64), 0.0)
    maskA = np.zeros((128, 6, 128), np.float32)
    for r in range(6):
        delta = r - 1 + p
        rel = -2 * delta + kc - qc
        maskA[:, r, :] = np.where((rel <= 0) & (rel >= -8), 0.0, NEG)
    w01 = np.tile(np.array([[1.0 - p, float(p)]], np.float32), (128, 1))
    lam = np.zeros((128, 2 * DEPTH), np.float32)
    for l in range(DEPTH):
        li = 0.8 - 0.6 * math.exp(-0.3 * l)
        lam[:, 2 * l] = li
        lam[:, 2 * l + 1] = 1.0 - li
    return dict(c_ident=np.eye(128, dtype=np.float32), c_cs=cs, c_augq=aug(pos, True), c_augk=aug(pos, False),
                c_augkc=aug(np.arange(PAST), False), c_corrB=corrB, c_corrBs=corrBs, c_maskC=maskC,
                c_maskA=maskA, c_w01=w01, c_lam=lam)


_CACHE = {}


def kernel(**inputs):
    x_prompt = np.asarray(inputs["x_prompt"], np.float32)
    x_sample = np.asarray(inputs["x_sample"], np.float32)
    B, SEQ, _ = x_prompt.shape
    DB = x_sample.shape[0]
    DEPTH = inputs["w_in"].shape[0]
    PAST = inputs["cache_b_k"].shape[2]
    assert B * 2 == 8 and DB == 8 and x_sample.shape[1] == 64 and inputs["cache_a_k"].shape[2] == 512
    NBLK = SEQ // 256
    SC = NBLK * 128
    key = (NBLK, DEPTH, PAST)
    if key not in _CACHE:
        _CACHE[key] = build(NBLK, DEPTH, PAST)
    nc = _CACHE[key]
    wts = {nm: np.ascontiguousarray(np.asarray(inputs[nm], np.float32)) for nm in WNAMES}
    in_maps = []
    for c in range(8):
        b, p = c // 2, c % 2
        xb = x_prompt[b].reshape(SEQ // 128, 128, D)[p::2].reshape(SC, D)
        m = dict(wts)
        m["xin"] = np.ascontiguousarray(np.concatenate([xb, x_sample[c]], axis=0))
        m["mem"] = np.ascontiguousarray(np.asarray(inputs["mem_prompt"], np.float32)[b])
        m["ca_k"] = np.ascontiguousarray(np.asarray(inputs["cache_a_k"], np.float32)[:, c].reshape(DEPTH, 512, 256))
        m["ca_v"] = np.ascontiguousarray(np.asarray(inputs["cache_a_v"], np.float32)[:, c].reshape(DEPTH, 512, 256))
        m["cb_k"] = np.ascontiguousarray(np.asarray(inputs["cache_b_k"], np.float32)[:, c].reshape(DEPTH, PAST, 512))
        m["cb_v"] = np.ascontiguousarray(np.asarray(inputs["cache_b_v"], np.float32)[:, c].reshape(DEPTH, PAST, 512))
        m["cc_kv"] = np.ascontiguousarray(np.asarray(inputs["cache_c_kv"], np.float32)[:, c])
        m["cc_kpe"] = np.ascontiguousarray(np.asarray(inputs["cache_c_kpe"], np.float32)[:, c])
        m["cm_k"] = np.ascontiguousarray(np.asarray(inputs["cache_mem_k"], np.float32)[:, c].reshape(DEPTH, NMEM, 512))
        m["cm_v"] = np.ascontiguousarray(np.asarray(inputs["cache_mem_v"], np.float32)[:, c].reshape(DEPTH, NMEM, 512))
        m.update(_consts(p, NBLK, DEPTH, PAST))
        in_maps.append(m)
    res = run_bass_kernel_spmd(nc, in_maps, core_ids=list(range(8))).results

    def unzig(name, width):
        out = np.zeros((DEPTH, B, SEQ // 128, 128, width), np.float32)
        for c in range(8):
            b, p = c // 2, c % 2
            out[:, b, p::2] = res[c][name][:, :SC].reshape(DEPTH, NBLK, 128, width)
        return out.reshape(DEPTH, B, SEQ, width)

    yp = np.zeros((B, SEQ // 128, 128, D), np.float32)
    ys = np.zeros((DB, 64, D), np.float32)
    for c in range(8):
        b, p = c // 2, c % 2
        yp[b, p::2] = res[c]["y"][:SC].reshape(NBLK, 128, D)
        ys[c] = res[c]["y"][SC:]
    yp = yp.reshape(B, SEQ, D)
    pak = np.zeros((DEPTH, B, 4, 128, 256), np.float32)
    pav = np.zeros((DEPTH, B, 4, 128, 256), np.float32)
    for c in range(8):
        b, p = c // 2, c % 2
        for mb in range(4):
            if mb % 2 == p:
                pak[:, b, mb] = res[c]["o_ak"][:, (mb // 2) * 128:(mb // 2 + 1) * 128]
                pav[:, b, mb] = res[c]["o_av"][:, (mb // 2) * 128:(mb // 2 + 1) * 128]
    pak = pak.reshape(DEPTH, B, 512, 4, 64)
    pav = pav.reshape(DEPTH, B, 512, 4, 64)
    pbk = unzig("o_bk", 512).reshape(DEPTH, B, SEQ, 4, 2, 64)
    pbv = unzig("o_bv", 512).reshape(DEPTH, B, SEQ, 4, 128)
    pckv = unzig("o_ckv", 256)
    pkpe = unzig("o_kpe", 32)
    pmk = np.stack([res[2 * b]["o_mk"] for b in range(B)], axis=1).reshape(DEPTH, B, NMEM, 4, 128)
    pmv = np.stack([res[2 * b]["o_mv"] for b in range(B)], axis=1).reshape(DEPTH, B, NMEM, 4, 128)
    sak = np.stack([res[c]["o_ak"][:, 256:320] for c in range(8)], axis=1).reshape(DEPTH, DB, 64, 4, 64)
    sav = np.stack([res[c]["o_av"][:, 256:320] for c in range(8)], axis=1).reshape(DEPTH, DB, 64, 4, 64)
    sbk = np.stack([res[c]["o_bk"][:, SC:] for c in range(8)], axis=1).reshape(DEPTH, DB, 64, 4, 2, 64)
    sbv = np.stack([res[c]["o_bv"][:, SC:] for c in range(8)], axis=1).reshape(DEPTH, DB, 64, 4, 128)
    sckv = np.stack([res[c]["o_ckv"][:, SC:] for c in range(8)], axis=1)
    skpe = np.stack([res[c]["o_kpe"][:, SC:] for c in range(8)], axis=1)
    return (yp, ys, pak, pav, pbk, pbv, pckv, pkpe, pmk, pmv, sak, sav, sbk, sbv, sckv, skpe)
```

```python
import math
import os
from contextlib import ExitStack
import numpy as np
import concourse.bass as bass
import concourse.mybir as mybir
from concourse.bass_utils import run_bass_kernel_spmd

F32 = mybir.dt.float32
BF16 = mybir.dt.bfloat16
ALU = mybir.AluOpType
AF = mybir.ActivationFunctionType
AX = mybir.AxisListType

D = 1024
DFF = 2816
HD = 64
NMEM = 256
EPS = 1e-6
IN_COLS = 2976
NEG = -30000.0
ENGS = ("pe", "act", "dve", "pool", "sp")

O_KA, O_VA, O_KB, O_VB, O_KC, O_VC, RBE = 0, 32768, 65536, 135168, 200704, 249856, 282624


class GSync:
    def __init__(self, nc):
        self.nc = nc
        self.esem = {e: nc.alloc_semaphore(name=f"es_{e}") for e in ("pe", "act", "dve", "pool")}
        self.ecnt = {e: 0 for e in self.esem}
        self.dsem = {}
        self.dcnt = {}
        self.kmap = {}
        self.nops = 0

    def kid(self, key):
        if key not in self.kmap:
            self.kmap[key] = len(self.kmap) % 56
        return self.kmap[key]

    def dkey(self, key):
        if key not in self.dsem:
            self.dsem[key] = self.nc.alloc_semaphore(name=f"ds_{len(self.dsem)}")
            self.dcnt[key] = 0
        return self.dsem[key]


class Phase:
    def __init__(self, g):
        self.g = g
        self.raw = []
        self.stream = 0
        self.segment = 0
        self.ops = []

    def _add(self, eng, fn, r, w, dma_key=None, inc=16):
        if dma_key is not None:
            dma_key = self.g.kid(dma_key)
        self.raw.append(dict(eng=eng, fn=fn, r=r, w=w, dma=dma_key, inc=inc, stream=self.stream, seg=self.segment))
        return len(self.raw) - 1

    def _finalize(self):
        order = []
        for seg in sorted({o["seg"] for o in self.raw}):
            streams = {}
            for o in self.raw:
                if o["seg"] == seg:
                    streams.setdefault(o["stream"], []).append(o)
            keys = sorted(streams)
            if len(keys) == 1:
                order.extend(streams[keys[0]])
                continue
            pos = {k: 0 for k in keys}
            tot = {k: len(streams[k]) for k in keys}
            while any(pos[k] < tot[k] for k in keys):
                k = min((k for k in keys if pos[k] < tot[k]), key=lambda k: (pos[k] + 1) / tot[k])
                order.append(streams[k][pos[k]])
                pos[k] += 1
        lastw, readers, lastdma = {}, {}, {}
        self.ops = []
        for raw in order:
            idx = len(self.ops)
            eng, dma_key = raw["eng"], raw["dma"]
            deps = {}
            for x in raw["r"]:
                if x in lastw:
                    deps.setdefault(lastw[x], set()).add("raw")
            for x in raw["w"]:
                if x in lastw:
                    deps.setdefault(lastw[x], set()).add("waw")
                for rd in readers.get(x, ()):
                    deps.setdefault(rd, set()).add("war")
            if dma_key is not None and dma_key in lastdma:
                deps.setdefault(lastdma[dma_key], set()).add("raw")
            op = dict(idx=idx, eng=eng, fn=raw["fn"], dma=dma_key, waits=[], signal=False, cnt=None, inc=raw["inc"])
            for p, kinds in deps.items():
                P = self.ops[p]
                if P["dma"] is not None:
                    op["waits"].append(p)
                elif P["eng"] == eng and dma_key is None and "raw" not in kinds:
                    continue
                else:
                    P["signal"] = True
                    op["waits"].append(p)
            self.ops.append(op)
            for x in raw["r"]:
                readers.setdefault(x, []).append(idx)
            for x in raw["w"]:
                lastw[x] = idx
                readers[x] = []
            if dma_key is not None:
                lastdma[dma_key] = idx

    def op(self, eng, fn, r=(), w=()):
        return self._add(eng, fn, tuple(r), tuple(w))

    def dma(self, q, out, in_, r=(), w=(), key=None, **kw):
        return self._add(q, lambda e: e.dma_start(out=out, in_=in_, **kw), tuple(r), tuple(w), dma_key=key)

    def custom(self, q, fn, r=(), w=(), key=None, inc=16):
        return self._add(q, fn, tuple(r), tuple(w), dma_key=key, inc=inc)

    def run(self):
        g = self.g
        nc = g.nc
        self._finalize()
        for op in self.ops:
            if op["dma"] is not None:
                g.dkey(op["dma"])
                g.dcnt[op["dma"]] += op["inc"]
                op["cnt"] = g.dcnt[op["dma"]]
            elif op["signal"]:
                g.ecnt[op["eng"]] += 1
                op["cnt"] = g.ecnt[op["eng"]]
        per = {e: [o for o in self.ops if o["eng"] == e] for e in ENGS}
        ops = self.ops
        g.nops += len(ops)

        def emit(e, lst):
            def body(engine):
                waited = {}
                lastkeys = {}
                for o in lst:
                    need = {}
                    for p in o["waits"]:
                        P = ops[p]
                        s = g.dsem[P["dma"]] if P["dma"] is not None else g.esem[P["eng"]]
                        k = id(s)
                        if k not in need or need[k][1] < P["cnt"]:
                            need[k] = (s, P["cnt"])
                    for k, (s, v) in need.items():
                        if waited.get(k, -1) >= v:
                            continue
                        engine.wait_ge(s, v)
                        waited[k] = v
                    ins = o["fn"](engine)
                    if o["dma"] is not None:
                        ins.then_inc(g.dsem[o["dma"]], o["inc"])
                        lastkeys[o["dma"]] = o["cnt"]
                    elif o["signal"]:
                        ins.then_inc(g.esem[e], 1)
                for key, v in lastkeys.items():
                    engine.wait_ge(g.dsem[key], v)
            return body

        with nc.Block() as block:
            for e in ENGS:
                if not per[e]:
                    continue
                reg = {"pe": block.tensor, "act": block.scalar, "dve": block.vector,
                       "pool": block.gpsimd, "sp": block.sync}[e]
                reg(emit(e, per[e]))


def build(NBLK, DEPTH, PAST):
    STAGE = int(os.environ.get('KSTAGE', '99'))
    NB = NBLK + 1
    T = NBLK * 128 + 64
    NCB = PAST // 128
    NSB = NCB + 1
    NKB = 2 * NBLK
    SC = NBLK * 128
    nc = bass.Bass("TRN2", target_bir_lowering=False)
    g = GSync(nc)

    def din(name, shape, dt=F32):
        return nc.dram_tensor(name, list(shape), dt, kind="ExternalInput")

    def dout(name, shape):
        return nc.dram_tensor(name, list(shape), F32, kind="ExternalOutput")

    xin = din("xin", [T, D])
    mem = din("mem", [NMEM, D])
    ca_k = din("ca_k", [DEPTH, 512, 256]); ca_v = din("ca_v", [DEPTH, 512, 256])
    cb_k = din("cb_k", [DEPTH, PAST, 512]); cb_v = din("cb_v", [DEPTH, PAST, 512])
    cc_kv = din("cc_kv", [DEPTH, PAST, 256]); cc_kpe = din("cc_kpe", [DEPTH, PAST, 32])
    cm_k = din("cm_k", [DEPTH, NMEM, 512]); cm_v = din("cm_v", [DEPTH, NMEM, 512])
    W = {}
    for nm, shp in [("ffn1_norm", [DEPTH, D]), ("ffn1_w_gate", [DEPTH, D, DFF]), ("ffn1_w_up", [DEPTH, D, DFF]),
                    ("ffn1_w_down", [DEPTH, DFF, D]), ("mix_norm", [DEPTH, D]), ("w_in", [DEPTH, D, IN_COLS]),
                    ("a_q_norm", [DEPTH, 64]), ("a_k_norm", [DEPTH, 64]), ("a_rel_bias", [DEPTH, 4, 257]),
                    ("b_q_norm", [DEPTH, 64]), ("b_k_norm", [DEPTH, 64]), ("b_lambda", [DEPTH, 4, 64]),
                    ("b_sub_norm", [DEPTH, 128]), ("c_q_lat_norm", [DEPTH, 384]), ("c_kv_lat_norm", [DEPTH, 256]),
                    ("c_w_uq", [DEPTH, 384, 384]), ("c_w_ukv", [DEPTH, 256, 512]), ("c_q_norm", [DEPTH, 96]),
                    ("c_k_norm", [DEPTH, 96]), ("w_out", [DEPTH, D, D]), ("mem_norm_x", [DEPTH, D]),
                    ("mem_w_q", [DEPTH, D, 512]), ("mem_q_norm", [DEPTH, 128]), ("mem_norm_m", [DEPTH, D]),
                    ("mem_w_k", [DEPTH, D, 512]), ("mem_w_v", [DEPTH, D, 512]), ("mem_k_norm", [DEPTH, 128]),
                    ("mem_w_o", [DEPTH, 512, D]), ("ffn2_norm", [DEPTH, D]), ("ffn2_w_gate", [DEPTH, D, DFF]),
                    ("ffn2_w_up", [DEPTH, D, DFF]), ("ffn2_w_down", [DEPTH, DFF, D])]:
        W[nm] = din(nm, shp)
    c_ident = din("c_ident", [128, 128])
    c_cs = din("c_cs", [T, 32])
    c_augq = din("c_augq", [T, 32])
    c_augk = din("c_augk", [T, 32])
    c_augkc = din("c_augkc", [PAST, 32])
    c_corrB = din("c_corrB", [128, 2, 4, 128])
    c_corrBs = din("c_corrBs", [64, 4, 64])
    c_maskC = din("c_maskC", [128, 2, 128])
    c_maskA = din("c_maskA", [128, 6, 128])
    c_w01 = din("c_w01", [128, 2])
    c_lam = din("c_lam", [128, 2 * DEPTH])
    y = dout("y", [T, D])
    o_ak = dout("o_ak", [DEPTH, 320, 256]); o_av = dout("o_av", [DEPTH, 320, 256])
    o_bk = dout("o_bk", [DEPTH, T, 512]); o_bv = dout("o_bv", [DEPTH, T, 512])
    o_ckv = dout("o_ckv", [DEPTH, T, 256]); o_kpe = dout("o_kpe", [DEPTH, T, 32])
    o_mk = dout("o_mk", [DEPTH, NMEM, 512]); o_mv = dout("o_mv", [DEPTH, NMEM, 512])
    qa_d = nc.dram_tensor("qa_d", [NB, 64, 4, 128], BF16)
    qb_d = nc.dram_tensor("qb_d", [NB, 68, 8, 128], BF16)
    qc_d = nc.dram_tensor("qc_d", [NB, 96, 4, 128], BF16)
    ksrc = nc.dram_tensor("ksrc", [NBLK * RBE // 128, 128], BF16)
    kdst = nc.dram_tensor("kdst", [2 * NBLK * RBE // 128, 128], BF16)
    srec = nc.dram_tensor("srec", [NSB * RBE // 128, 128], BF16)
    Rtoe = nc.dram_tensor("Rtoe", [4, 128, 1024], F32)
    Etoe = nc.dram_tensor("Etoe", [4, 1024], F32)

    uid = [0]

    def SB(name, shape, dt):
        uid[0] += 1
        return nc.sbuf_tensor(f"{name}_{uid[0]}", shape, dt)

    def PS(name, shape, dt):
        uid[0] += 1
        return nc.psum_tensor(f"{name}_{uid[0]}", shape, dt)

    def rec_ap(tensor, base, dims):
        return bass.AP(tensor, base, [list(d) for d in dims])

    def blk_cols(bi):
        return (bi * 128, 128) if bi < NBLK else (SC, 64)

    groups = [(t0, min(512, SC - t0)) for t0 in range(0, SC, 512)] + [(SC, 64)]
    groups_default = groups
    nfg_ = (T + 511) // 512
    base_ = ((T + nfg_ - 1) // nfg_ + 7) // 8 * 8
    ffn_groups = []
    t_ = 0
    while t_ < T:
        ffn_groups.append((t_, min(base_, T - t_)))
        t_ += base_

    with ExitStack() as es1:
        xT = es1.enter_context(SB("xT", [128, 8, T], F32))
        idf = es1.enter_context(SB("idf", [128, 128], F32))
        idb = es1.enter_context(SB("idb", [128, 128], BF16))
        ones = es1.enter_context(SB("ones", [128, 128], BF16))
        zerob = es1.enter_context(SB("zerob", [128, 128], BF16))
        gcols = es1.enter_context(SB("gcols", [128, 4 * DEPTH, 8], F32))
        lamc = es1.enter_context(SB("lamc", [128, 2 * DEPTH], F32))

        with ExitStack() as es2:
            xtok = es2.enter_context(SB("xtok", [128, 2, D], F32))
            grow = es2.enter_context(SB("grow", [4 * DEPTH * 8, 128], F32))
            pt = es2.enter_context(PS("pt", [128, 2, 512], F32))
            ph = Phase(g)
            ph.dma("sp", idf[:, :], c_ident[:, :], w=["idf"], key="c0")
            ph.dma("pool", idb[:, :], c_ident[:, :], w=["idb"], key="c1")
            ph.dma("sp", lamc[:, :], c_lam[:, :], w=["lamc"], key="c2")
            ph.op("pool", lambda e: e.memset(ones[:, :], 1.0), w=["ones"])
            ph.op("pool", lambda e: e.memset(zerob[:, :], 0.0), w=["zerob"])
            for k, nm in enumerate(["ffn1_norm", "mix_norm", "mem_norm_x", "ffn2_norm"]):
                for l in range(DEPTH):
                    r0 = (l * 4 + k) * 8
                    ph.dma("sp", grow[r0:r0 + 8, :], W[nm][l, :].rearrange("(c p) -> c p", p=128), w=["grow"], key="c3")
            nr = 4 * DEPTH * 8
            ph.op("pe", lambda e: e.transpose(out=pt[:, 0, 0:nr], in_=grow[:, :], identity=idf[0:nr, 0:nr]), r=["grow", "idf"], w=[("pt", 0)])
            ph.op("dve", lambda e: e.tensor_copy(out=gcols[:, :, :].rearrange("p a b -> p (a b)"), in_=pt[:, 0, 0:nr]), w=[("pt", 0), "gcols"])
            for bi in range(NB):
                c0, nb = blk_cols(bi)
                s = bi % 2
                ph.dma("sp", xtok[:nb, s, :], xin[c0:c0 + nb, :], w=[("xtok", s)], key=("xtok", s))
                for c in range(8):
                    ps = c % 2
                    ph.op("pe", lambda e, s=s, c=c, ps=ps, nb=nb: e.transpose(out=pt[:, ps, 0:nb], in_=xtok[:nb, s, c * 128:(c + 1) * 128], identity=idf[:nb, :nb]),
                          r=[("xtok", s), "idf"], w=[("pt", ps)])
                    ph.op("dve", lambda e, c=c, ps=ps, c0=c0, nb=nb: e.tensor_copy(out=xT[:, c, c0:c0 + nb], in_=pt[:, ps, 0:nb]),
                          w=[("pt", ps), ("xT", c)])
            ph.run()

        def norm_to_hT(ph, hT, xsq, rinv, pn, gidx, only=None, pres=lambda b: ("pn", b), groups=None):
            groups = groups if groups is not None else groups_default
            for gi, (t0, n) in enumerate(groups):
                if only is not None and gi != only:
                    continue
                h0 = 0 if only is not None else t0
                s = 0
                ps_ = gi % 2
                ph.op("act", lambda e, t0=t0, n=n, s=s: e.activation(out=xsq[:, s, :, 0:n], in_=xT[:, :, t0:t0 + n], func=AF.Square),
                      r=[("xT", c) for c in range(8)], w=[("xsq", s)])
                for c in range(8):
                    ph.op("pe", lambda e, c=c, n=n, s=s, ps_=ps_: e.matmul(pn[:, ps_, 0:n], lhsT=ones[:, :], rhs=xsq[:, s, c, 0:n], start=(c == 0), stop=(c == 7)),
                          r=[("xsq", s), "ones"], w=[pres(ps_)])
                ph.op("act", lambda e, n=n, s=s, ps_=ps_: e.activation(out=rinv[:, s, 0:n], in_=pn[:, ps_, 0:n], func=AF.Sqrt, scale=1.0 / D, bias=EPS),
                      w=[pres(ps_), ("rinv", s)])
                ph.op("dve", lambda e, n=n, s=s: e.reciprocal(out=rinv[:, s, 0:n], in_=rinv[:, s, 0:n]), r=[("rinv", s)], w=[("rinv", s)])
                for c in range(8):
                    ph.op("dve", lambda e, c=c, t0=t0, n=n, s=s, h0=h0: e.scalar_tensor_tensor(out=hT[:, c, h0:h0 + n], in0=xT[:, c, t0:t0 + n], scalar=gcols[:, gidx, c:c + 1],
                                                                                         in1=rinv[:, s, 0:n], op0=ALU.mult, op1=ALU.mult),
                          r=[("rinv", s), "gcols", ("xT", c)], w=[("hT", c)])

        def ffn(l, which, prefetch=None):
            wg_d, wu_d, wd_d = W[f"ffn{which}_w_gate"], W[f"ffn{which}_w_up"], W[f"ffn{which}_w_down"]
            gidx = l * 4 + (0 if which == 1 else 3)
            NFG = DFF // 256
            with ExitStack() as es3:
                hT = es3.enter_context(SB("hT", [128, 8, T], BF16))
                xsq = es3.enter_context(SB("xsq", [128, 1, 8, 512], BF16))
                rinv = es3.enter_context(SB("rinv", [128, 1, 512], F32))
                wgs = es3.enter_context(SB("wgs", [128, 2, 8, 256], BF16))
                wus = es3.enter_context(SB("wus", [128, 2, 8, 256], BF16))
                wds = es3.enter_context(SB("wds", [128, 2, 2, D], BF16))
                sg = es3.enter_context(SB("sg", [128, 2, 512], F32))
                actT = es3.enter_context(SB("actT", [128, 2, 2, T], BF16))
                pn = es3.enter_context(PS("pn", [128, 2, 512], F32))
                pgt = es3.enter_context(PS("pgt", [128, 2, 512], F32))
                put = es3.enter_context(PS("put", [128, 2, 512], F32))
                pd = es3.enter_context(PS("pd", [128, 2, 512], F32))
                ph = Phase(g)
                norm_to_hT(ph, hT, xsq, rinv, pn, gidx, groups=ffn_groups)
                it = 0
                dit = [0]
                pending = []

                def emit_down(fg, ws, t0, n):
                    for dc in range(8):
                        pb = dit[0] % 2
                        dit[0] += 1
                        for j in range(2):
                            ph.op("pe", lambda e, j=j, dc=dc, pb=pb: e.matmul(pd[:, pb, 0:n], lhsT=wds[:, ws, j, dc * 128:(dc + 1) * 128], rhs=actT[:, ws, j, t0:t0 + n], start=(j == 0), stop=(j == 1)),
                                  r=[("wds", ws), ("actT", ws, j, t0)], w=[("pd", pb)])
                        ph.op("dve", lambda e, dc=dc, pb=pb: e.scalar_tensor_tensor(out=xT[:, dc, t0:t0 + n], in0=pd[:, pb, 0:n], scalar=0.5, in1=xT[:, dc, t0:t0 + n], op0=ALU.mult, op1=ALU.add),
                              r=[("xT", dc)], w=[("pd", pb), ("xT", dc)])

                for fg in range(NFG):
                    ws = fg % 2
                    if prefetch is not None and fg in (2, 4, 6, 8):
                        prefetch(ph, fg // 2 - 1)
                    ph.dma("pool", wgs[:, ws, :, :], wg_d[l, :, fg * 256:(fg + 1) * 256].rearrange("(c p) n -> p c n", p=128), w=[("wgs", ws)], key=("wgs", ws))
                    ph.dma("pool", wus[:, ws, :, :], wu_d[l, :, fg * 256:(fg + 1) * 256].rearrange("(c p) n -> p c n", p=128), w=[("wus", ws)], key=("wus", ws))
                    ph.dma("pool", wds[:, ws, :, :], wd_d[l, fg * 256:(fg + 1) * 256, :].rearrange("(c p) n -> p c n", p=128), w=[("wds", ws)], key=("wds", ws))
                    for (t0, n) in ffn_groups:
                        for j in range(2):
                            pb = it % 2
                            it += 1
                            for c in range(8):
                                ph.op("pe", lambda e, c=c, j=j, pb=pb, ws=ws, t0=t0, n=n: e.matmul(pgt[:, pb, 0:n], lhsT=wgs[:, ws, c, j * 128:(j + 1) * 128], rhs=hT[:, c, t0:t0 + n], start=(c == 0), stop=(c == 7)),
                                      r=[("wgs", ws), ("hT", c)], w=[("pgt", pb)])
                            for c in range(8):
                                ph.op("pe", lambda e, c=c, j=j, pb=pb, ws=ws, t0=t0, n=n: e.matmul(put[:, pb, 0:n], lhsT=wus[:, ws, c, j * 128:(j + 1) * 128], rhs=hT[:, c, t0:t0 + n], start=(c == 0), stop=(c == 7)),
                                      r=[("wus", ws), ("hT", c)], w=[("put", pb)])
                            ph.op("act", lambda e, pb=pb, n=n: e.activation(out=sg[:, pb, 0:n], in_=pgt[:, pb, 0:n], func=AF.Silu), w=[("pgt", pb), ("sg", pb)])
                            ph.op("dve", lambda e, pb=pb, ws=ws, j=j, t0=t0, n=n: e.tensor_tensor(out=actT[:, ws, j, t0:t0 + n], in0=put[:, pb, 0:n], in1=sg[:, pb, 0:n], op=ALU.mult),
                                  r=[("sg", pb)], w=[("put", pb), ("actT", ws, j, t0)])
                        if pending:
                            emit_down(*pending.pop(0))
                        pending.append((fg, ws, t0, n))
                while pending:
                    emit_down(*pending.pop(0))
                ph.run()

        def bcast_row(ph, dst_ap, src_1d, n, reps, key, wres):
            ph.dma("sp", dst_ap, src_1d.rearrange("(o h n) -> o h n", o=1, h=1).to_broadcast([128, reps, n]), w=[wres], key=key)

        def rms_rows(ph, src, nb, H, dh, gain, out, sq, ss, tag, src_res, out_res, gain_res):
            ph.op("act", lambda e: e.activation(out=sq[:nb, 0:H * dh].rearrange("p (h d) -> p h d", h=H), in_=src, func=AF.Square), r=[], w=list(src_res) + [("sq", tag)])
            ph.op("dve", lambda e: e.tensor_reduce(out=ss[:nb, 0:H], in_=sq[:nb, 0:H * dh].rearrange("p (h d) -> p h d", h=H), axis=AX.X, op=ALU.add), r=[("sq", tag)], w=[("ss", tag)])
            ph.op("act", lambda e: e.activation(out=ss[:nb, 0:H], in_=ss[:nb, 0:H], func=AF.Sqrt, scale=1.0 / dh, bias=EPS), r=[("ss", tag)], w=[("ss", tag)])
            ph.op("dve", lambda e: e.reciprocal(out=ss[:nb, 0:H], in_=ss[:nb, 0:H]), r=[("ss", tag)], w=[("ss", tag)])
            ph.op("dve", lambda e: e.tensor_tensor(out=out, in0=src, in1=ss[:nb, 0:H].unsqueeze(2).to_broadcast([nb, H, dh]), op=ALU.mult), r=[("ss", tag)], w=list(src_res) + list(out_res))
            ph.op("dve", lambda e: e.tensor_tensor(out=out, in0=out, in1=gain, op=ALU.mult), r=list(out_res) + list(gain_res), w=list(out_res))

        def attn_core(ph, sp, pT, qT, nq, tiles, E, outp, out_res, cnt):
            ngr = (len(tiles) + 3) // 4
            bufs = []
            for gi in range(ngr):
                bufs.append(cnt[0] % 2)
                cnt[0] += 1

            def emit_S(gi):
                b = bufs[gi]
                for ti, (kT, v, nk, corrs, kres, vres) in enumerate(tiles[gi * 4:(gi + 1) * 4]):
                    ph.op("pe", lambda e, kT=kT, ti=ti, nk=nk, last=(len(corrs) == 0): e.matmul(sp[:nk, b, ti * 128:ti * 128 + nq], lhsT=kT, rhs=qT, start=True, stop=last),
                          r=list(kres) + ["qt"], w=[("sp", b)])
                    for ci, cr in enumerate(corrs):
                        ph.op("pe", lambda e, cr=cr, ti=ti, nk=nk, last=(ci == len(corrs) - 1): e.matmul(sp[:nk, b, ti * 128:ti * 128 + nq], lhsT=idb[:nk, :nk], rhs=cr, start=False, stop=last),
                              r=["corr", "idb"], w=[("sp", b)])

            def emit_rest(gi):
                b = bufs[gi]
                grp = tiles[gi * 4:(gi + 1) * 4]
                ng = len(grp)
                ph.op("act", lambda e: e.activation(out=pT[:, b, 0:ng, 0:nq], in_=sp[:, b, 0:ng * 128].rearrange("p (g q) -> p g q", g=ng)[:, :, 0:nq], func=AF.Exp),
                      w=[("sp", b), ("pT", b)])
                for ti, (kT, v, nk, corrs, kres, vres) in enumerate(grp):
                    first = (gi == 0 and ti == 0)
                    lastmm = (gi == ngr - 1) and (ti == ng - 1)
                    ph.op("pe", lambda e, v=v, ti=ti, nk=nk, first=first, lastmm=lastmm: e.matmul(outp, lhsT=pT[:nk, b, ti, 0:nq], rhs=v, start=first, stop=lastmm),
                          r=[("pT", b)] + list(vres), w=list(out_res))

            emit_S(0)
            for gi in range(ngr):
                if gi + 1 < ngr:
                    emit_S(gi + 1)
                emit_rest(gi)

        def attn_T(ph, sp, pTb, opp, a, qflat, ncols, tiles, MP, cnt, kres, vres, zero_init=False, qres="qt"):
            nt = len(tiles)
            nbuf = sp.shape[1]
            bufs = []
            for ti in range(nt):
                bufs.append(cnt[0] % nbuf)
                cnt[0] += 1

            if zero_init:
                R0 = qflat.shape[0]
                for k_ in range(2):
                    ph.op("pe", lambda e, k_=k_: e.matmul(opp[:MP, 2 * a + k_, 0:ncols], lhsT=zerob[0:R0, 0:MP], rhs=qflat[:, 0:ncols], start=True, stop=False),
                          r=["zerob", qres], w=[("op", 2 * a + k_)])

            def emit_S(ti):
                kT, vl, nk, c_lo, corrs = tiles[ti][:5]
                c_hi = tiles[ti][5] if len(tiles[ti]) > 5 else ncols
                b = bufs[ti]
                ph.op("pe", lambda e: e.matmul(sp[:nk, b, c_lo:c_hi], lhsT=kT, rhs=qflat[:, c_lo:c_hi], start=True, stop=(len(corrs) == 0)),
                      r=list(kres) + [qres], w=[("sp", b)])
                for ci, (lo, hi, cap) in enumerate(corrs):
                    ph.op("pe", lambda e, lo=lo, hi=hi, cap=cap, last=(ci == len(corrs) - 1): e.matmul(sp[:nk, b, lo:hi], lhsT=idb[:nk, :nk], rhs=cap, start=False, stop=last),
                          r=["corr", "idb"], w=[("sp", b)])

            def emit_rest(ti):
                kT, vl, nk, c_lo, corrs = tiles[ti][:5]
                c_hi = tiles[ti][5] if len(tiles[ti]) > 5 else ncols
                b = bufs[ti]
                st_ = (ti == 0) and not zero_init
                ph.op("act", lambda e: e.activation(out=pTb[:nk, b, c_lo:c_hi], in_=sp[:nk, b, c_lo:c_hi], func=AF.Exp), w=[("sp", b), ("pT", b)])
                ph.op("pe", lambda e: e.matmul(opp[:MP, 2 * a, c_lo:c_hi], lhsT=vl, rhs=pTb[:nk, b, c_lo:c_hi], start=st_, stop=(ti == nt - 1)),
                      r=[("pT", b)] + list(vres), w=[("op", 2 * a)])
                ph.op("pe", lambda e: e.matmul(opp[:MP, 2 * a + 1, c_lo:c_hi], lhsT=ones[:nk, :MP], rhs=pTb[:nk, b, c_lo:c_hi], start=st_, stop=(ti == nt - 1)),
                      r=[("pT", b), "ones"], w=[("op", 2 * a + 1)])

            for ti in range(min(nbuf - 1, nt)):
                emit_S(ti)
            for ti in range(nt):
                if ti + nbuf - 1 < nt:
                    emit_S(ti + nbuf - 1)
                emit_rest(ti)

        def load_kv(ph, kt, vt, R, E, off_k, off_v, h, nhk, HV, res):
            HK = {O_KA: 4, O_KB: 8, O_KC: 4}[off_k]
            for jj in range(nhk):
                ph.dma("sp", kt[0:R, jj, :, :, :], rec_ap(kdst, off_k + (h * nhk + jj) * 128, [[HK * 128, R], [RBE, NKB], [1, 128]]), w=[res + "k"], key=("Kk", jj))
            ph.dma("act", vt[:, :, :, 0:E], rec_ap(kdst, off_v + h * E, [[HV * E, 128], [RBE, NKB], [1, E]]), w=[res + "v"], key="Kv")

        def load_kv_s(ph, kts, vts, R, E, off_k, off_v, h, nhk, HV, res, b0, nbk):
            HK = {O_KA: 4, O_KB: 8, O_KC: 4}[off_k]
            for jj in range(nhk):
                ph.dma("sp", kts[0:R, jj, 0:nbk, :], rec_ap(srec, b0 * RBE + off_k + (h * nhk + jj) * 128, [[HK * 128, R], [RBE, nbk], [1, 128]]), w=[res + "ks"], key=("Kks", jj))
            ph.dma("act", vts[:, 0:nbk, 0:E], rec_ap(srec, b0 * RBE + off_v + h * E, [[HV * E, 128], [RBE, nbk], [1, E]]), w=[res + "vs"], key="Kvs")

        def attention(l, Oall):
            with ExitStack() as es4:
                rc = es4.enter_context(SB("rc", [128, 8], F32))
                sp3 = es4.enter_context(PS("sp", [128, 3, 512], F32))
                sp = sp3[:, 0:2, :]
                opp = es4.enter_context(PS("opp", [128, 4, 512], F32))
                pss = es4.enter_context(PS("pss", [128, 1, 512], F32))
                OT = es4.enter_context(SB("OT", [128, 8, T], BF16))
                with ExitStack() as es5:
                    tb = es5.enter_context(SB("tb", [4, 257], F32))
                    ng = es5.enter_context(SB("ng", [4, 1], F32))
                    ext = es5.enter_context(SB("ext", [4, 1024], F32))
                    extb = es5.enter_context(SB("extb", [128, 1024], F32))
                    t7 = es5.enter_context(SB("t7", [128, 7, 128], F32))
                    tmpa = es5.enter_context(SB("tmpa", [128, 6, 128], F32))
                    mka = es5.enter_context(SB("mka", [128, 6, 128], F32))
                    w01 = es5.enter_context(SB("w01", [128, 2], F32))
                    biasA = es5.enter_context(SB("biasA", [128, 4, 6, 128], BF16))
                    biasS = es5.enter_context(SB("biasS", [128, 4, 5, 64], BF16))
                    kt = es5.enter_context(SB("kt", [64, NKB, 128], BF16))
                    vtp = es5.enter_context(SB("vtp", [128, 2, NKB, 128], BF16))
                    kts = es5.enter_context(SB("kts", [64, 5, 128], BF16))
                    vtsp = es5.enter_context(SB("vtsp", [128, 2, 5, 128], BF16))
                    qt = es5.enter_context(SB("qt", [64, NB * 128], BF16))
                    pTb = es5.enter_context(SB("pTb", [128, 3, 512], BF16))
                    rsA = es5.enter_context(SB("rsA", [128, 2, 512], F32))
                    ph = Phase(g)
                    ph.dma("sp", tb[:, :], W["a_rel_bias"][l, :, :], w=["tb"], key="tb")
                    ph.dma("sp", mka[:, :, :], c_maskA[:, :, :], w=["mka"], key="mka")
                    ph.dma("sp", w01[:, :], c_w01[:, :], w=["w01"], key="w01")
                    ph.op("dve", lambda e: e.tensor_scalar(out=ng[:, :], in0=tb[:, 256:257], scalar1=-1.0, scalar2=None, op0=ALU.mult), r=["tb"], w=["ng"])
                    ph.op("pool", lambda e: e.memset(ext[:, :], 0.0), w=["ext"])
                    ph.op("dve", lambda e: e.tensor_scalar(out=ext[:, 127:384], in0=tb[:, :], scalar1=ng[:, 0:1], scalar2=None, op0=ALU.add), r=["tb", "ng"], w=["ext"])
                    ph.op("dve", lambda e: e.tensor_copy(out=ext[:, 0:127], in_=ext[:, 127:128].to_broadcast([4, 127])), r=["ext"], w=["ext"])
                    ph.dma("sp", Etoe[:, :], ext[:, :], r=["ext"], w=["Etoe"], key="Etoe")
                    for h in range(4):
                        ph.dma("sp", extb[:, :], Etoe[h:h + 1, :].to_broadcast([128, 1024]), r=["Etoe"], w=["extb"], key="extb")
                        ph.dma("sp", Rtoe[h, :, :], extb[:, :], r=["extb"], w=[("Rtoe", h)], key="Rtoe")
                    ph.op("pool", lambda e: e.memset(vtp[:, :, :, :], 0.0), w=["Av"])
                    ph.op("pool", lambda e: e.memset(vtsp[:, :, :, :], 0.0), w=["Avs"])
                    cnt = [0]
                    acc = 0
                    qgroupsA = [list(range(b0_, min(b0_ + 4, NBLK))) for b0_ in range(0, NBLK, 4)] + [[NBLK]]
                    for h in range(4):
                        ph.dma("sp", t7[:, :, :], rec_ap(Rtoe, h * 128 * 1024 + 127, [[1023, 128], [128, 7], [1, 128]]), r=[("Rtoe", h)], w=["t7"], key="t7")
                        ph.op("dve", lambda e: e.tensor_scalar(out=tmpa[:, :, :], in0=t7[:, 0:6, :], scalar1=w01[:, 0:1], scalar2=None, op0=ALU.mult), r=["t7", "w01"], w=["tmpa"])
                        ph.op("dve", lambda e: e.scalar_tensor_tensor(out=tmpa[:, :, :], in0=t7[:, 1:7, :], scalar=w01[:, 1:2], in1=tmpa[:, :, :], op0=ALU.mult, op1=ALU.add), r=["t7", "w01", "tmpa"], w=["tmpa"])
                        ph.op("dve", lambda e, h=h: e.tensor_tensor(out=biasA[:, h, :, :], in0=tmpa[:, :, :], in1=mka[:, :, :], op=ALU.add), r=["tmpa", "mka"], w=["corr"])
                        ph.op("dve", lambda e, h=h: e.tensor_copy(out=biasS[:, h, :, :], in_=t7[:, 1:6, 0:64]), r=["t7"], w=["corr"])
                        par = h % 2
                        ph.dma("sp", kt[0:64, :, :], rec_ap(kdst, O_KA + h * 128, [[4 * 128, 64], [RBE, NKB], [1, 128]]), w=["Ak"], key=("Kk", 0))
                        ph.dma("sp", kts[0:64, :, :], rec_ap(srec, (NCB - 4) * RBE + O_KA + h * 128, [[4 * 128, 64], [RBE, 5], [1, 128]]), w=["Aks"], key=("Kks", 0))
                        ph.dma("sp", qt[:, :].rearrange("p (b t) -> p b t", b=NB), qa_d[:, :, h, :].rearrange("b d t -> d b t"), w=["qt"], key=("Kq", 0))
                        ph.dma("act", vtp[:, par, :, par * 64:par * 64 + 64], rec_ap(kdst, O_VA + h * 64, [[256, 128], [RBE, NKB], [1, 64]]), w=["Av"], key="Kv")
                        ph.dma("act", vtsp[:, par, :, par * 64:par * 64 + 64], rec_ap(srec, (NCB - 4) * RBE + O_VA + h * 64, [[256, 128], [RBE, 5], [1, 64]]), w=["Avs"], key="Kvs")
                        for qg in qgroupsA:
                            b0, b1 = qg[0], qg[-1]
                            tiles = []
                            if b0 < NBLK:
                                ncols = len(qg) * 128
                                q0 = b0 * 128
                                for kb in range(max(0, 2 * b0 - 4), 2 * b1 + 2):
                                    ilo = max(b0, kb // 2)
                                    ihi = min(b1, (kb + 4) // 2)
                                    corrs = []
                                    for i in range(ilo, ihi + 1):
                                        j = kb - (2 * i - 4)
                                        corrs.append(((i - b0) * 128, (i - b0 + 1) * 128, biasA[:, h, 5 - j, :]))
                                    tiles.append((kt[0:64, kb, :], vtp[:, par, kb, :], 128, (ilo - b0) * 128, corrs, (ihi - b0 + 1) * 128))
                                kres, vres = ["Ak"], ["Av"]
                            else:
                                ncols = 64
                                q0 = SC
                                for cbi in range(4):
                                    tiles.append((kts[0:64, cbi, :], vtsp[:, par, cbi, :], 128, 0, [(0, 64, biasS[:, h, 4 - cbi, :])]))
                                tiles.append((kts[0:64, 4, 0:64], vtsp[0:64, par, 4, :], 64, 0, [(0, 64, biasS[0:64, h, 0, :])]))
                                kres, vres = ["Aks"], ["Avs"]
                            a = acc % 2
                            acc += 1
                            attn_T(ph, sp3, pTb, opp, a, qt[0:64, q0:q0 + ncols], ncols, tiles, 128, cnt, kres, vres, zero_init=True)
                            ph.op("dve", lambda e, a=a, ncols=ncols: e.reciprocal(out=rsA[:, a, 0:ncols], in_=opp[:, 2 * a + 1, 0:ncols]), w=[("op", 2 * a + 1), ("rs", a)])
                            ph.op("dve", lambda e, a=a, ncols=ncols, q0=q0, h=h, par=par: e.tensor_tensor(out=OT[par * 64:par * 64 + 64, h // 2, q0:q0 + ncols], in0=opp[par * 64:par * 64 + 64, 2 * a, 0:ncols],
                                                                                                          in1=rsA[par * 64:par * 64 + 64, a, 0:ncols], op=ALU.mult),
                                  r=[("rs", a)], w=[("op", 2 * a), ("OT", h // 2)])
                    ph.run()
                if STAGE < 5:
                    return
                qgroups = [list(range(b0, min(b0 + 4, NBLK))) for b0 in range(0, NBLK, 4)] + [[NBLK]]
                with ExitStack() as es6:
                    kt2 = es6.enter_context(SB("kt", [68, 4, NKB, 128], BF16))
                    vt2 = es6.enter_context(SB("vt", [128, 2, NKB, 128], BF16))
                    kts = es6.enter_context(SB("kts", [68, 2, NSB, 128], BF16))
                    vts = es6.enter_context(SB("vts", [128, NSB, 128], BF16))
                    qt2 = es6.enter_context(SB("qt", [68, 4, NB * 128], BF16))
                    pTb = es6.enter_context(SB("pTb", [128, 3, 512], BF16))
                    corrB = es6.enter_context(SB("corrB", [128, 2, 4, 128], BF16))
                    corrBs = es6.enter_context(SB("corrBs", [64, 4, 64], BF16))
                    lamb = es6.enter_context(SB("lamb", [128, 4, 64], F32))
                    lp = es6.enter_context(SB("lp", [128, 2, 64], F32))
                    lv = es6.enter_context(SB("lv", [128, 4], F32))
                    gsc = es6.enter_context(SB("gsc", [128, 1], F32))
                    rs = es6.enter_context(SB("rs", [128, 2, 512], F32))
                    t0b = es6.enter_context(SB("t0", [128, 2, 512], F32))
                    t1b = es6.enter_context(SB("t1", [128, 2, 512], F32))
                    sqb = es6.enter_context(SB("sqb", [128, 512], BF16))
                    ph = Phase(g)
                    ph.dma("pool", corrB[:, :, :, :], c_corrB[:, :, :, :], w=["corr"], key="corrB")
                    ph.dma("pool", corrBs[:, :, :], c_corrBs[:, :, :], w=["corr"], key="corrBs")
                    ph.dma("sp", lamb[:, :, :], W["b_lambda"][l, :, :].rearrange("(o a) n -> o a n", o=1).to_broadcast([128, 4, 64]), w=["lamb"], key="lamb")
                    ph.dma("sp", gsc[:, :], W["b_sub_norm"][l, :].rearrange("(p o) -> p o", o=1), w=["gsc"], key="gsub")
                    ph.op("dve", lambda e: e.tensor_scalar(out=gsc[:, :], in0=gsc[:, :], scalar1=lamc[:, 2 * l + 1:2 * l + 2], scalar2=None, op0=ALU.mult), r=["gsc", "lamc"], w=["gsc"])
                    for k in range(2):
                        ph.op("dve", lambda e, k=k: e.tensor_tensor(out=lp[:, k, :], in0=lamb[:, 2 * k, :], in1=lamb[:, 2 * k + 1, :], op=ALU.mult), r=["lamb"], w=["lp"])
                    ph.op("dve", lambda e: e.tensor_reduce(out=lv[:, 0:2], in_=lp[:, :, :], axis=AX.X, op=ALU.add), r=["lp"], w=["lv"])
                    ph.op("act", lambda e: e.activation(out=lv[:, 0:2], in_=lv[:, 0:2], func=AF.Exp), r=["lv"], w=["lv"])
                    ph.op("dve", lambda e: e.tensor_tensor(out=lv[:, 2:3], in0=lv[:, 1:2], in1=lv[:, 0:1], op=ALU.subtract), r=["lv"], w=["lv2"])
                    ph.op("dve", lambda e: e.tensor_tensor(out=lv[:, 3:4], in0=lv[:, 2:3], in1=lamc[:, 2 * l:2 * l + 1], op=ALU.subtract), r=["lv2", "lamc"], w=["lv3"])
                    cnt = [0]
                    acc = 0
                    pendingB = []
                    gcount = [0]

                    def finalB(gb, ncols, q0, h):
                        t0, t1 = t0b[:, gb, :], t1b[:, gb, :]
                        ph.op("pool", lambda e: e.tensor_tensor(out=t0[:, 0:ncols], in0=t0[:, 0:ncols], in1=t1[:, 0:ncols], op=ALU.add), r=[("t0", gb), ("t1", gb)], w=[("t0", gb)])
                        ph.op("act", lambda e: e.activation(out=sqb[:, 0:ncols], in_=t0[:, 0:ncols], func=AF.Square), r=[("t0", gb)], w=["sqb"])
                        ph.op("pe", lambda e: e.matmul(pss[:, 0, 0:ncols], lhsT=ones[:, :], rhs=sqb[:, 0:ncols], start=True, stop=True), r=["sqb", "ones"], w=["pss"])
                        ph.op("act", lambda e: e.activation(out=t1[:, 0:ncols], in_=pss[:, 0, 0:ncols], func=AF.Sqrt, scale=1.0 / 128, bias=EPS), w=["pss", ("t1", gb)])
                        ph.op("dve", lambda e: e.reciprocal(out=t1[:, 0:ncols], in_=t1[:, 0:ncols]), r=[("t1", gb)], w=[("t1", gb)])
                        ph.op("dve", lambda e: e.scalar_tensor_tensor(out=OT[:, 2 + h, q0:q0 + ncols], in0=t0[:, 0:ncols], scalar=gsc[:, 0:1], in1=t1[:, 0:ncols], op0=ALU.mult, op1=ALU.mult),
                              r=[("t0", gb), ("t1", gb), "gsc"], w=[("OT", 2 + h)])

                    def loadB(h):
                        sl = h % 2
                        for jj in range(2):
                            ph.dma("sp", kt2[0:68, 2 * sl + jj, :, :], rec_ap(kdst, O_KB + (h * 2 + jj) * 128, [[8 * 128, 68], [RBE, NKB], [1, 128]]), w=[("Bk", sl)], key=("Kk", jj, sl))
                            ph.dma("sp", qt2[:, 2 * sl + jj, :].rearrange("p (b t) -> p b t", b=NB), qb_d[:, :, 2 * h + jj, :].rearrange("b d t -> d b t"), w=[("qt", sl)], key=("Kq", jj, sl))
                        ph.dma("sp", vt2[:, sl, :, :], rec_ap(kdst, O_VB + h * 128, [[512, 128], [RBE, NKB], [1, 128]]), w=[("Bv", sl)], key=("Kv", sl))

                    loadB(0)
                    for h in range(4):
                        sl = h % 2
                        kt = kt2[:, 2 * sl:2 * sl + 2, :, :]
                        vt = vt2[:, sl, :, :]
                        qt = qt2[:, 2 * sl:2 * sl + 2, :]
                        if h + 1 < 4:
                            loadB(h + 1)
                        for jj in range(2):
                            ph.dma("sp", kts[0:68, jj, :, :], rec_ap(srec, O_KB + (h * 2 + jj) * 128, [[8 * 128, 68], [RBE, NSB], [1, 128]]), w=["Bks"], key=("Kks", jj))
                        ph.dma("sp", vts[:, :, :], rec_ap(srec, O_VB + h * 128, [[512, 128], [RBE, NSB], [1, 128]]), w=["Bvs"], key="Kvs")
                        for qg in qgroups:
                            b0 = qg[0]
                            if b0 < NBLK:
                                ncols = len(qg) * 128
                                q0 = b0 * 128
                            else:
                                ncols = 64
                                q0 = SC
                            gb = gcount[0] % 2
                            gcount[0] += 1
                            t0, t1 = t0b[:, gb, :], t1b[:, gb, :]
                            for j in range(2):
                                tiles = []
                                if b0 < NBLK:
                                    for kb in range(2 * qg[-1] + 2):
                                        imin = max(b0, kb // 2)
                                        c_lo = (imin - b0) * 128
                                        corrs = []
                                        if kb // 2 >= b0:
                                            lo = (kb // 2 - b0) * 128
                                            corrs = [(lo, lo + 128, corrB[:, kb % 2, h, :])]
                                        tiles.append((kt[0:68, j, kb, :], vt[:, kb, :], 128, c_lo, corrs))
                                    kres, vres = [("Bk", sl)], [("Bv", sl)]
                                else:
                                    for cbi in range(NCB):
                                        tiles.append((kts[0:68, j, cbi, :], vts[:, cbi, :], 128, 0, []))
                                    tiles.append((kts[0:68, j, NCB, 0:64], vts[0:64, NCB, :], 64, 0, [(0, 64, corrBs[:, h, :])]))
                                    kres, vres = ["Bks"], ["Bvs"]
                                a = acc % 2
                                acc += 1
                                attn_T(ph, sp3, pTb, opp, a, qt[0:68, j, q0:q0 + ncols], ncols, tiles, 128, cnt, kres, vres, qres=("qt", sl))
                                ph.op("dve", lambda e, a=a, j=j, ncols=ncols: e.reciprocal(out=rs[:, j, 0:ncols], in_=opp[:, 2 * a + 1, 0:ncols]), w=[("op", 2 * a + 1), ("rs", j)])
                                if j == 0:
                                    ph.op("dve", lambda e, a=a, ncols=ncols, t0=t0: e.tensor_tensor(out=t0[:, 0:ncols], in0=opp[:, 2 * a, 0:ncols], in1=rs[:, 0, 0:ncols], op=ALU.mult), r=[("rs", 0)], w=[("op", 2 * a), ("t0", gb)])
                                    if pendingB:
                                        finalB(*pendingB.pop(0))
                                else:
                                    ph.op("dve", lambda e, ncols=ncols: e.tensor_scalar(out=rs[:, 1, 0:ncols], in0=rs[:, 1, 0:ncols], scalar1=lv[:, 3:4], scalar2=None, op0=ALU.mult), r=[("rs", 1), "lv3"], w=[("rs", 1)])
                                    ph.op("dve", lambda e, a=a, ncols=ncols, t1=t1: e.tensor_tensor(out=t1[:, 0:ncols], in0=opp[:, 2 * a, 0:ncols], in1=rs[:, 1, 0:ncols], op=ALU.mult), r=[("rs", 1)], w=[("op", 2 * a), ("t1", gb)])
                            pendingB.append((gb, ncols, q0, h))
                    while pendingB:
                        finalB(*pendingB.pop(0))
                    ph.run()
                if STAGE < 6:
                    return
                wout = es4.enter_context(SB("wout", [128, 8, D], BF16))
                with ExitStack() as es7:
                    kt2 = es7.enter_context(SB("kt", [96, 2, NKB, 128], BF16))
                    vtp = es7.enter_context(SB("vtp", [128, 2, NKB, 128], BF16))
                    kts = es7.enter_context(SB("kts", [96, NSB, 128], BF16))
                    vtsp = es7.enter_context(SB("vtsp", [128, 2, NSB, 128], BF16))
                    qt2 = es7.enter_context(SB("qt", [96, 2, NB * 128], BF16))
                    pTb = es7.enter_context(SB("pTb", [128, 3, 512], BF16))
                    maskC = es7.enter_context(SB("maskC", [128, 2, 128], BF16))
                    rs = es7.enter_context(SB("rs", [128, 2, 512], F32))
                    ph = Phase(g)
                    ph.dma("pool", maskC[:, :, :], c_maskC[:, :, :], w=["corr"], key="maskC")
                    for c4 in range(4):
                        ph.dma("pool", wout[:, 2 * c4:2 * c4 + 2, :], W["w_out"][l, c4 * 256:(c4 + 1) * 256, :].rearrange("(c p) n -> p c n", p=128), w=["wout"], key=("wout", c4))
                    ph.op("pool", lambda e: e.memset(vtp[:, :, :, :], 0.0), w=[("Cv", 0), ("Cv", 1)])
                    ph.op("pool", lambda e: e.memset(vtsp[:, :, :, :], 0.0), w=["Cvs"])
                    cnt = [0]
                    acc = 0

                    def loadC(h):
                        sl = h % 2
                        ph.dma("sp", kt2[0:96, sl, :, :], rec_ap(kdst, O_KC + h * 128, [[4 * 128, 96], [RBE, NKB], [1, 128]]), w=[("Ck", sl)], key=("Kk", 0, sl))
                        ph.dma("sp", qt2[:, sl, :].rearrange("p (b t) -> p b t", b=NB), qc_d[:, :, h, :].rearrange("b d t -> d b t"), w=[("qt", sl)], key=("Kq", 0, sl))
                        ph.dma("sp", vtp[:, sl, :, sl * 64:sl * 64 + 64], rec_ap(kdst, O_VC + h * 64, [[256, 128], [RBE, NKB], [1, 64]]), w=[("Cv", sl)], key=("Kv", sl))

                    loadC(0)
                    for h in range(4):
                        par = h % 2
                        kt = kt2[:, par, :, :]
                        qt = qt2[:, par, :]
                        if h + 1 < 4:
                            loadC(h + 1)
                        ph.dma("sp", kts[0:96, :, :], rec_ap(srec, O_KC + h * 128, [[4 * 128, 96], [RBE, NSB], [1, 128]]), w=["Cks"], key=("Kks", 0))
                        ph.dma("sp", vtsp[:, par, :, par * 64:par * 64 + 64], rec_ap(srec, O_VC + h * 64, [[256, 128], [RBE, NSB], [1, 64]]), w=["Cvs"], key="Kvs")
                        for qg in qgroups:
                            b0 = qg[0]
                            tiles = []
                            if b0 < NBLK:
                                ncols = len(qg) * 128
                                q0 = b0 * 128
                                for kb in range(2 * qg[-1] + 2):
                                    imin = max(b0, kb // 2)
                                    c_lo = (imin - b0) * 128
                                    corrs = []
                                    if kb // 2 >= b0:
                                        lo = (kb // 2 - b0) * 128
                                        corrs = [(lo, lo + 128, maskC[:, kb % 2, :])]
                                    tiles.append((kt[0:96, kb, :], vtp[:, par, kb, :], 128, c_lo, corrs))
                                kres, vres = [("Ck", par)], [("Cv", par)]
                            else:
                                ncols = 64
                                q0 = SC
                                for cbi in range(NCB):
                                    tiles.append((kts[0:96, cbi, :], vtsp[:, par, cbi, :], 128, 0, []))
                                tiles.append((kts[0:96, NCB, 0:64], vtsp[0:64, par, NCB, :], 64, 0, []))
                                kres, vres = ["Cks"], ["Cvs"]
                            a = acc % 2
                            acc += 1
                            attn_T(ph, sp3, pTb, opp, a, qt[0:96, q0:q0 + ncols], ncols, tiles, 128, cnt, kres, vres, qres=("qt", par))
                            ph.op("dve", lambda e, a=a, ncols=ncols: e.reciprocal(out=rs[:, a, 0:ncols], in_=opp[:, 2 * a + 1, 0:ncols]), w=[("op", 2 * a + 1), ("rs", a)])
                            ph.op("dve", lambda e, a=a, ncols=ncols, q0=q0, h=h, par=par: e.tensor_tensor(out=OT[par * 64:par * 64 + 64, 6 + h // 2, q0:q0 + ncols], in0=opp[par * 64:par * 64 + 64, 2 * a, 0:ncols],
                                                                                                          in1=rs[par * 64:par * 64 + 64, a, 0:ncols], op=ALU.mult),
                                  r=[("rs", a)], w=[("op", 2 * a), ("OT", 6 + h // 2)])
                    ph.run()
                if STAGE < 7:
                    return
                ph = Phase(g)
                for gi, (t0_, n) in enumerate(groups):
                    for dc in range(8):
                        ob = dc % 4
                        for c in range(8):
                            ph.op("pe", lambda e, c=c, dc=dc, ob=ob, n=n, t0_=t0_: e.matmul(opp[:, ob, 0:n], lhsT=wout[:, c, dc * 128:(dc + 1) * 128], rhs=OT[:, c, t0_:t0_ + n], start=(c == 0), stop=(c == 7)),
                                  r=["wout"], w=[("op", ob)])
                        ph.op("dve", lambda e, dc=dc, ob=ob, t0_=t0_, n=n: e.tensor_tensor(out=xT[:, dc, t0_:t0_ + n], in0=opp[:, ob, 0:n], in1=xT[:, dc, t0_:t0_ + n], op=ALU.add),
                              r=[("xT", dc)], w=[("op", ob), ("xT", dc)])
                ph.run()

        def mem_attention(l):
            with ExitStack() as es8:
                hT = es8.enter_context(SB("hT", [128, 8, 512], BF16))
                xsq = es8.enter_context(SB("xsq", [128, 1, 8, 512], BF16))
                rinv = es8.enter_context(SB("rinv", [128, 1, 512], F32))
                wq = es8.enter_context(SB("wq", [128, 8, 512], BF16))
                wk = es8.enter_context(SB("wk", [128, 8, 512], BF16))
                wv = es8.enter_context(SB("wv", [128, 8, 512], BF16))
                wo = es8.enter_context(SB("wo", [128, 4, D], BF16))
                gm = es8.enter_context(SB("gm", [128, 1, D], F32))
                gkn = es8.enter_context(SB("gkn", [128, 4, 128], F32))
                mtok = es8.enter_context(SB("mtok", [128, 2, D], F32))
                mnb = es8.enter_context(SB("mnb", [128, 2, D], BF16))
                mnT = es8.enter_context(SB("mnT", [128, 8, 256], BF16))
                kf = es8.enter_context(SB("kf", [128, 2, 512], F32))
                vf = es8.enter_context(SB("vf", [128, 2, 512], F32))
                kb16 = es8.enter_context(SB("kb16", [128, 512], BF16))
                mK = es8.enter_context(SB("mK", [128, 2, 4, 256], BF16))
                mV = es8.enter_context(SB("mV", [128, 2, 2, 4, 129], BF16))
                sq = es8.enter_context(SB("sq", [128, 1024], F32))
                ss = es8.enter_context(SB("ss", [128, 8], F32))
                qTn = es8.enter_context(SB("qTn", [128, 4, T], BF16))
                pTb = es8.enter_context(SB("pTb", [128, 2, 512], BF16))
                omT = es8.enter_context(SB("omT", [128, 4, 512], BF16))
                sqh = es8.enter_context(SB("sqh", [128, 512], BF16))
                rn = es8.enter_context(SB("rn", [128, 512], F32))
                rsm = es8.enter_context(SB("rsm", [128, 512], F32))
                gqc = es8.enter_context(SB("gqc", [128, 1], F32))
                pn = es8.enter_context(PS("pn", [128, 2, 512], F32))
                pq = es8.enter_context(PS("pq", [128, 2, 512], F32))
                sp = es8.enter_context(PS("sp", [128, 2, 512], F32))
                opp = es8.enter_context(PS("opp", [128, 2, 512], F32))
                ph = Phase(g)
                for nm, t in [("mem_w_q", wq), ("mem_w_k", wk), ("mem_w_v", wv)]:
                    for c4 in range(2):
                        ph.dma("pool", t[:, 4 * c4:4 * c4 + 4, :], W[nm][l, c4 * 512:(c4 + 1) * 512, :].rearrange("(c p) n -> p c n", p=128), w=[nm], key=(nm, c4))
                ph.dma("pool", wo[:, :, :], W["mem_w_o"][l, :, :].rearrange("(c p) n -> p c n", p=128), w=["wo"], key="wo")
                bcast_row(ph, gm[:, :, :], W["mem_norm_m"][l, :], D, 1, "gm", "gm")
                ph.dma("sp", gqc[:, :], W["mem_q_norm"][l, :].rearrange("(p o) -> p o", o=1), w=["gqc"], key="gqn")
                bcast_row(ph, gkn[:, :, :], W["mem_k_norm"][l, :], 128, 4, "gkn", "gkn")
                ph.op("dve", lambda e: e.tensor_scalar(out=gqc[:, :], in0=gqc[:, :], scalar1=128.0 ** -0.5, scalar2=None, op0=ALU.mult), r=["gqc"], w=["gqc"])
                cnt = [0]
                ph.op("pool", lambda e: e.memset(mV[:, :, :, :, 128:129], 1.0), w=["mV"])
                spb = sp[:, :, :].bitcast(BF16)
                for mb in range(2):
                    ph.dma("sp", mtok[:, mb, :], mem[mb * 128:(mb + 1) * 128, :], w=[("mtok", mb)], key=("mtok", mb))
                    rms_rows(ph, mtok[:, mb, :].rearrange("p (h d) -> p h d", h=1), 128, 1, D, gm[:, :, :], mtok[:, mb, :].rearrange("p (h d) -> p h d", h=1), sq, ss, "m", [("mtok", mb)], [("mtok", mb)], ["gm"])
                    ph.op("act", lambda e, mb=mb: e.copy(out=mnb[:, mb, :], in_=mtok[:, mb, :]), r=[("mtok", mb)], w=[("mnb", mb)])
                    for c in range(8):
                        b = c % 2
                        ph.op("pe", lambda e, c=c, b=b, mb=mb: e.transpose(out=spb[:, b, 0:128], in_=mnb[:, mb, c * 128:(c + 1) * 128], identity=idb[:, :]), r=[("mnb", mb), "idb"], w=[("sp", b)])
                        ph.op("dve", lambda e, c=c, b=b, mb=mb: e.tensor_copy(out=mnT[:, c, mb * 128:(mb + 1) * 128], in_=spb[:, b, 0:128]), w=[("sp", b), "mnT"])
                    for c in range(8):
                        ph.op("pe", lambda e, c=c, mb=mb: e.matmul(pq[:, 0, :], lhsT=mnT[:, c, mb * 128:(mb + 1) * 128], rhs=wk[:, c, :], start=(c == 0), stop=(c == 7)), r=["mnT", "mem_w_k"], w=[("pq", 0)])
                    for c in range(8):
                        ph.op("pe", lambda e, c=c, mb=mb: e.matmul(pq[:, 1, :], lhsT=mnT[:, c, mb * 128:(mb + 1) * 128], rhs=wv[:, c, :], start=(c == 0), stop=(c == 7)), r=["mnT", "mem_w_v"], w=[("pq", 1)])
                    rms_rows(ph, pq[:, 0, :].rearrange("p (h d) -> p h d", h=4), 128, 4, 128, gkn[:, :, :], kf[:, mb, :].rearrange("p (h d) -> p h d", h=4), sq, ss, "mk", [("pq", 0)], [("kf", mb)], ["gkn"])
                    ph.op("act", lambda e, mb=mb: e.copy(out=vf[:, mb, :], in_=pq[:, 1, :]), w=[("pq", 1), ("vf", mb)])
                    ph.dma("sp", o_mk[l, mb * 128:(mb + 1) * 128, :], kf[:, mb, :], r=[("kf", mb)], key=("o_mk", mb))
                    ph.dma("sp", o_mv[l, mb * 128:(mb + 1) * 128, :], vf[:, mb, :], r=[("vf", mb)], key=("o_mv", mb))
                for st in range(2):
                    for mb in range(2):
                        if st == 1:
                            ph.dma("sp", kf[:, mb, :], cm_k[l, mb * 128:(mb + 1) * 128, :], w=[("kf", mb)], key=("cmk", mb))
                            ph.dma("sp", vf[:, mb, :], cm_v[l, mb * 128:(mb + 1) * 128, :], w=[("vf", mb)], key=("cmv", mb))
                        ph.op("act", lambda e, mb=mb: e.copy(out=kb16[:, :], in_=kf[:, mb, :]), r=[("kf", mb)], w=["kb16"])
                        for h in range(4):
                            b = h % 2
                            ph.op("pe", lambda e, h=h, b=b: e.transpose(out=spb[:, b, 0:128], in_=kb16[:, h * 128:(h + 1) * 128], identity=idb[:, :]), r=["kb16", "idb"], w=[("sp", b)])
                            ph.op("dve", lambda e, h=h, b=b, st=st, mb=mb: e.tensor_copy(out=mK[:, st, h, mb * 128:(mb + 1) * 128], in_=spb[:, b, 0:128]), w=[("sp", b), "mK"])
                        ph.op("act", lambda e, st=st, mb=mb: e.copy(out=mV[:, st, mb, :, 0:128], in_=vf[:, mb, :].rearrange("p (h d) -> p h d", h=4)), r=[("vf", mb)], w=["mV"])
                ph.stream = 1
                for gi, (t0_, n) in enumerate(groups):
                    norm_to_hT(ph, hT, xsq, rinv, pn, l * 4 + 2, only=gi)
                    for h in range(4):
                        pb = h % 2
                        for c in range(8):
                            ph.op("pe", lambda e, c=c, pb=pb, h=h, n=n: e.matmul(opp[:, pb, 0:n], lhsT=wq[:, c, h * 128:(h + 1) * 128], rhs=hT[:, c, 0:n], start=(c == 0), stop=(c == 7)),
                                  r=[("hT", c), "mem_w_q"], w=[("op", pb)])
                        ph.op("act", lambda e, pb=pb, n=n: e.activation(out=sqh[:, 0:n], in_=opp[:, pb, 0:n], func=AF.Square), w=[("op", pb), "sqh"])
                        ph.op("pe", lambda e, n=n: e.matmul(pn[:, 0, 0:n], lhsT=ones[:, :], rhs=sqh[:, 0:n], start=True, stop=True), r=["sqh", "ones"], w=[("pn", 0)])
                        ph.op("act", lambda e, n=n: e.activation(out=rn[:, 0:n], in_=pn[:, 0, 0:n], func=AF.Sqrt, scale=1.0 / 128, bias=EPS), w=[("pn", 0), "rn"])
                        ph.op("dve", lambda e, n=n: e.reciprocal(out=rn[:, 0:n], in_=rn[:, 0:n]), r=["rn"], w=["rn"])
                        ph.op("dve", lambda e, pb=pb, h=h, n=n, t0_=t0_: e.scalar_tensor_tensor(out=qTn[:, h, t0_:t0_ + n], in0=opp[:, pb, 0:n], scalar=gqc[:, 0:1], in1=rn[:, 0:n], op0=ALU.mult, op1=ALU.mult),
                              r=["rn", "gqc"], w=[("op", pb), ("qt", gi)])
                ph.stream = 0
                ph.segment = 1
                for gi, (t0_, n) in enumerate(groups):
                    st = 0 if t0_ < SC else 1
                    for h in range(4):
                        tiles = [(mK[:, st, h, mb * 128:(mb + 1) * 128], mV[:, st, mb, h, 0:128], 128, 0, []) for mb in range(2)]
                        attn_T(ph, sp, pTb, opp, 0, qTn[:, h, t0_:t0_ + n], n, tiles, 128, cnt, ["mK"], ["mV"], qres=("qt", gi))
                        ph.op("dve", lambda e, n=n: e.reciprocal(out=rsm[:, 0:n], in_=opp[:, 1, 0:n]), w=[("op", 1), "rsm"])
                        ph.op("dve", lambda e, h=h, n=n: e.tensor_tensor(out=omT[:, h, 0:n], in0=opp[:, 0, 0:n], in1=rsm[:, 0:n], op=ALU.mult), r=["rsm"], w=[("op", 0), "omT"])
                    for dc in range(8):
                        pb2 = dc % 2
                        for c in range(4):
                            ph.op("pe", lambda e, c=c, dc=dc, pb2=pb2, n=n: e.matmul(pn[:, pb2, 0:n], lhsT=wo[:, c, dc * 128:(dc + 1) * 128], rhs=omT[:, c, 0:n], start=(c == 0), stop=(c == 3)), r=["omT", "wo"], w=[("pn", pb2)])
                        ph.op("dve", lambda e, dc=dc, pb2=pb2, t0_=t0_, n=n: e.tensor_tensor(out=xT[:, dc, t0_:t0_ + n], in0=pn[:, pb2, 0:n], in1=xT[:, dc, t0_:t0_ + n], op=ALU.add), r=[("xT", dc)], w=[("pn", pb2), ("xT", dc)])
                ph.run()

        def rms_rows_multi(ph, items):
            for (src, nb, H, dh, gain, out, sqv, ssv, tag, src_res, out_res, gain_res) in items:
                ph.op("act", lambda e, src=src, sqv=sqv, nb=nb, H=H, dh=dh: e.activation(out=sqv[:nb, 0:H * dh].rearrange("p (h d) -> p h d", h=H), in_=src, func=AF.Square), r=[], w=list(src_res) + [("sq", tag)])
            for (src, nb, H, dh, gain, out, sqv, ssv, tag, src_res, out_res, gain_res) in items:
                ph.op("dve", lambda e, sqv=sqv, ssv=ssv, nb=nb, H=H, dh=dh: e.tensor_reduce(out=ssv[:nb, 0:H], in_=sqv[:nb, 0:H * dh].rearrange("p (h d) -> p h d", h=H), axis=AX.X, op=ALU.add), r=[("sq", tag)], w=[("ss", tag)])
            for (src, nb, H, dh, gain, out, sqv, ssv, tag, src_res, out_res, gain_res) in items:
                ph.op("act", lambda e, ssv=ssv, nb=nb, H=H, dh=dh: e.activation(out=ssv[:nb, 0:H], in_=ssv[:nb, 0:H], func=AF.Sqrt, scale=1.0 / dh, bias=EPS), r=[("ss", tag)], w=[("ss", tag)])
            for (src, nb, H, dh, gain, out, sqv, ssv, tag, src_res, out_res, gain_res) in items:
                ph.op("dve", lambda e, ssv=ssv, nb=nb, H=H: e.reciprocal(out=ssv[:nb, 0:H], in_=ssv[:nb, 0:H]), r=[("ss", tag)], w=[("ss", tag)])
            for (src, nb, H, dh, gain, out, sqv, ssv, tag, src_res, out_res, gain_res) in items:
                ph.op("dve", lambda e, src=src, out=out, ssv=ssv, nb=nb, H=H, dh=dh: e.tensor_tensor(out=out, in0=src, in1=ssv[:nb, 0:H].unsqueeze(2).to_broadcast([nb, H, dh]), op=ALU.mult), r=[("ss", tag)], w=list(src_res) + list(out_res))
            for (src, nb, H, dh, gain, out, sqv, ssv, tag, src_res, out_res, gain_res) in items:
                ph.op("dve", lambda e, out=out, gain=gain: e.tensor_tensor(out=out, in0=out, in1=gain, op=ALU.mult), r=list(out_res) + list(gain_res), w=list(out_res))

        for l in range(DEPTH):
            es_win = ExitStack()
            win = es_win.enter_context(SB("win", [128, 8, IN_COLS], BF16))

            def pf_win(ph, c4, l=l, win=win):
                ph.dma("pool", win[:, 2 * c4:2 * c4 + 2, :], W["w_in"][l, c4 * 256:(c4 + 1) * 256, :].rearrange("(c p) n -> p c n", p=128), w=["win"], key=("win", c4))
            ffn(l, 1, prefetch=pf_win)
            if STAGE < 2:
                continue

            with ExitStack() as es9:
                hT = es9.enter_context(SB("hT", [128, 8, 512], BF16))
                wuq = es9.enter_context(SB("wuq", [128, 3, 384], BF16))
                wukv = es9.enter_context(SB("wukv", [128, 2, 512], BF16))
                gall = es9.enter_context(SB("gall", [128, 28, 64], F32))
                gcq = es9.enter_context(SB("gcq", [128, 1, 384], F32))
                gckv = es9.enter_context(SB("gckv", [128, 1, 256], F32))
                gq96 = es9.enter_context(SB("gq96", [128, 4, 96], F32))
                gk96 = es9.enter_context(SB("gk96", [128, 4, 96], F32))
                Nf = es9.enter_context(SB("Nf", [128, 1, IN_COLS], F32))
                tokb2 = es9.enter_context(SB("tokb2", [128, 2464], BF16))
                stg2 = es9.enter_context(SB("stg2", [128, 2048], BF16))
                latT2 = es9.enter_context(SB("latT2", [128, 2, 128], BF16))
                kcf2 = es9.enter_context(SB("kcf2", [128, 4, 96], F32))
                sq2 = es9.enter_context(SB("sq2", [128, 384], F32))
                ss2 = es9.enter_context(SB("ss2", [128, 8], F32))
                ak2 = es9.enter_context(SB("ak2", [128, 32], F32))
                kp2 = es9.enter_context(SB("kp2", [128, 32], F32))
                sq = es9.enter_context(SB("sq", [128, 2560], F32))
                ss = es9.enter_context(SB("ss", [128, 32], F32))
                cs = es9.enter_context(SB("cs", [128, 32], F32))
                aq = es9.enter_context(SB("aq", [128, 32], F32))
                ak = es9.enter_context(SB("ak", [128, 32], F32))
                tokb = es9.enter_context(SB("tokb", [128, 3360], BF16))
                latT = es9.enter_context(SB("latT", [128, 5, 128], BF16))
                qcf = es9.enter_context(SB("qcf", [128, 4, 96], F32))
                kcf = es9.enter_context(SB("kcf", [128, 4, 96], F32))
                rt = es9.enter_context(SB("rt", [128, 4, 16], F32))
                rt2 = es9.enter_context(SB("rt2", [128, 4, 16], F32))
                stg = es9.enter_context(SB("stg", [128, 2, 2816], BF16))
                xsq = es9.enter_context(SB("xsq", [128, 1, 8, 512], BF16))
                rinv = es9.enter_context(SB("rinv", [128, 1, 512], F32))
                pu = es9.enter_context(PS("pu", [128, 6, 512], F32))
                pn = es9.enter_context(PS("pn", [128, 2, 512], F32))
                ptr = pu[:, :, :].bitcast(BF16)
                ph = Phase(g)
                ph.dma("pool", wuq[:, :, :], W["c_w_uq"][l, :, :].rearrange("(c p) n -> p c n", p=128), w=["wuq"], key="wuq")
                ph.dma("pool", wukv[:, :, :], W["c_w_ukv"][l, :, :].rearrange("(c p) n -> p c n", p=128), w=["wukv"], key="wukv")
                ph.op("pool", lambda e: e.memset(gall[:, 8:12, :], 1.0), w=["gall"])
                bcast_row(ph, gall[:, 0:4, :], W["a_q_norm"][l, :], 64, 4, "g0", "gall")
                bcast_row(ph, gall[:, 4:8, :], W["a_k_norm"][l, :], 64, 4, "g1", "gall")
                bcast_row(ph, gall[:, 12:20, :], W["b_q_norm"][l, :], 64, 8, "g2", "gall")
                bcast_row(ph, gall[:, 20:28, :], W["b_k_norm"][l, :], 64, 8, "g3", "gall")
                bcast_row(ph, gcq[:, :, :], W["c_q_lat_norm"][l, :], 384, 1, "g4", "gcq")
                bcast_row(ph, gckv[:, :, :], W["c_kv_lat_norm"][l, :], 256, 1, "g5", "gckv")
                bcast_row(ph, gq96[:, :, :], W["c_q_norm"][l, :], 96, 4, "g6", "gq96")
                bcast_row(ph, gk96[:, :, :], W["c_k_norm"][l, :], 96, 4, "g7", "gk96")
                ph.op("dve", lambda e: e.tensor_scalar(out=gall[:, 0:4, :], in0=gall[:, 0:4, :], scalar1=0.125, scalar2=None, op0=ALU.mult), r=["gall"], w=["gall"])
                ph.op("dve", lambda e: e.tensor_scalar(out=gall[:, 12:20, :], in0=gall[:, 12:20, :], scalar1=0.125, scalar2=None, op0=ALU.mult), r=["gall"], w=["gall"])
                ph.op("dve", lambda e: e.tensor_scalar(out=gq96[:, :, :], in0=gq96[:, :, :], scalar1=96.0 ** -0.5, scalar2=None, op0=ALU.mult), r=["gq96"], w=["gq96"])

                def kv_tail(ph, S, si, nb, rec_t, rec_base, bs, rr=None):
                    P_ = bs["pfx"]
                    tokb_, stg_, latT_, kcf_, sq_, ss_, ak_ = bs["tokb"], bs["stg"], bs["latT"], bs["kcf"], bs["sq"], bs["ss"], bs["ak"]
                    pA, rA = bs["pA"]
                    (pB0, rB0), (pB1, rB1) = bs["pB"]
                    pL, rL = bs["pL"]
                    pKV, rKV = bs["pKV"]
                    pC, rC = bs["pC"]
                    kpsrc = bs["kp"]
                    ks = bs["ksfx"]
                    if S is not None:
                        srcr = [("src", id(S), si)]
                        ph.op("act", lambda e: e.copy(out=tokb_[:nb, 0:512], in_=S[:nb, si, 256:768]), r=srcr, w=[P_ + "tokb_a"])
                        ph.op("act", lambda e: e.copy(out=tokb_[:nb, 512:512 + 544].rearrange("p (a d) -> p a d", a=8)[:, :, 0:64], in_=S[:nb, si, 1280:1792].rearrange("p (a d) -> p a d", a=8)), r=srcr, w=[P_ + "tokb_b"])
                        ph.op("act", lambda e: e.copy(out=tokb_[:nb, 1056:1568], in_=S[:nb, si, 1792:2304]), r=srcr, w=[P_ + "tokb_bv"])
                        ph.op("act", lambda e: e.copy(out=tokb_[:nb, 1568:1824], in_=S[:nb, si, 2688:2944]), r=srcr, w=[P_ + "tokb_c"])
                    if bs.get("do_a", True):
                        for h in range(4):
                            ph.op("pe", lambda e, h=h: e.transpose(out=pA[0:64, h * 128:h * 128 + nb], in_=tokb_[:nb, h * 64:(h + 1) * 64], identity=idb[:nb, :nb]),
                                  r=[P_ + "tokb_a", "idb"], w=[rA])
                        ph.op("dve", lambda e: e.tensor_copy(out=stg_[0:64, 0:512].rearrange("p (h t) -> p h t", h=4)[:, :, 0:nb], in_=pA[0:64, 0:512].rearrange("p (h t) -> p h t", h=4)[:, :, 0:nb]),
                              w=[rA, P_ + "stg_ka"])
                        ph.dma("sp", rec_ap(rec_t, rec_base + O_KA, [[512, 64], [128, 4], [1, nb]]), stg_[0:64, 0:512].rearrange("p (h t) -> p h t", h=4)[:, :, 0:nb], r=[P_ + "stg_ka"], w=([(rr, 0)] if rr is not None else []), key="st_ka" + ks)
                        ph.dma("act", rec_ap(rec_t, rec_base + O_VA, [[256, nb], [1, 256]]), tokb_[:nb, 256:512], r=[P_ + "tokb_a"], w=([(rr, 1)] if rr is not None else []), key="st_va" + ks)
                    ph.op("dve", lambda e: e.tensor_copy(out=tokb_[:nb, 512:512 + 544].rearrange("p (a d) -> p a d", a=8)[:, :, 64:68], in_=ak_[:nb, :].rearrange("p (a d) -> p a d", a=8)), r=[P_ + "ak"], w=[P_ + "tokb_b"])
                    for hj in range(8):
                        pBx, rBx = (pB0, rB0) if hj < 4 else (pB1, rB1)
                        col = (hj % 4) * 128
                        ph.op("pe", lambda e, hj=hj, pBx=pBx, col=col: e.transpose(out=pBx[0:68, col:col + nb], in_=tokb_[:nb, 512 + hj * 68:512 + (hj + 1) * 68], identity=idb[:nb, :nb]),
                              r=[P_ + "tokb_b", "idb"], w=[rBx])
                    for half, (pBx, rBx) in enumerate([(pB0, rB0), (pB1, rB1)]):
                        ph.op("dve", lambda e, half=half, pBx=pBx: e.tensor_copy(out=stg_[0:68, 512 + half * 512:1024 + half * 512].rearrange("p (h t) -> p h t", h=4)[:, :, 0:nb],
                                                                                  in_=pBx[0:68, 0:512].rearrange("p (h t) -> p h t", h=4)[:, :, 0:nb]),
                              w=[rBx, P_ + "stg_kb"])
                    ph.dma("sp", rec_ap(rec_t, rec_base + O_KB, [[1024, 68], [128, 8], [1, nb]]), stg_[0:68, 512:1536].rearrange("p (h t) -> p h t", h=8)[:, :, 0:nb], r=[P_ + "stg_kb"], w=([(rr, 2)] if rr is not None else []), key="st_kb" + ks)
                    ph.dma("act", rec_ap(rec_t, rec_base + O_VB, [[512, nb], [1, 512]]), tokb_[:nb, 1056:1568], r=[P_ + "tokb_bv"], w=([(rr, 3)] if rr is not None else []), key="st_vb" + ks)
                    for c in range(2):
                        ph.op("pe", lambda e, c=c: e.transpose(out=pL[:, c * 128:c * 128 + nb], in_=tokb_[:nb, 1568 + c * 128:1568 + (c + 1) * 128], identity=idb[:nb, :nb]),
                              r=[P_ + "tokb_c", "idb"], w=[rL])
                    ph.op("dve", lambda e: e.tensor_copy(out=latT_[:, 0:2, 0:nb], in_=pL[:, 0:256].rearrange("p (c t) -> p c t", c=2)[:, :, 0:nb]), w=[rL, P_ + "latT_kv"])
                    for c in range(2):
                        ph.op("pe", lambda e, c=c: e.matmul(pKV[:nb, 0:512], lhsT=latT_[:, c, 0:nb], rhs=wukv[:, c, :], start=(c == 0), stop=(c == 1)),
                              r=[P_ + "latT_kv", "wukv"], w=[rKV])
                    ph.op("act", lambda e: e.copy(out=kcf_[:nb, :, 0:64], in_=pKV[:nb, 0:512].rearrange("p (h d) -> p h d", h=4)[:, :, 0:64]), w=[rKV, P_ + "kcf"])
                    ph.op("pool", lambda e: e.tensor_copy(out=kcf_[:nb, :, 64:96], in_=kpsrc.unsqueeze(1).to_broadcast([nb, 4, 32])), r=bs["kpres"], w=[P_ + "kcf"])
                    ph.op("act", lambda e: e.copy(out=tokb_[:nb, 1824:2080].rearrange("p (h d) -> p h d", h=4), in_=pKV[:nb, 0:512].rearrange("p (h d) -> p h d", h=4)[:, :, 64:128]), w=[rKV, P_ + "tokb_cv"])
                    rms_rows(ph, kcf_[:nb, :, :], nb, 4, 96, gk96[:nb, :, :], kcf_[:nb, :, :], sq_, ss_, bs["sqtag"], [P_ + "kcf"], [P_ + "kcf"], ["gk96"])
                    ph.op("act", lambda e: e.copy(out=tokb_[:nb, 2080:2464].rearrange("p (h d) -> p h d", h=4), in_=kcf_[:nb, :, :]), r=[P_ + "kcf"], w=[P_ + "tokb_ck"])
                    for h in range(4):
                        ph.op("pe", lambda e, h=h: e.transpose(out=pC[0:96, h * 128:h * 128 + nb], in_=tokb_[:nb, 2080 + h * 96:2080 + (h + 1) * 96], identity=idb[:nb, :nb]),
                              r=[P_ + "tokb_ck", "idb"], w=[rC])
                    ph.op("dve", lambda e: e.tensor_copy(out=stg_[0:96, 1536:2048].rearrange("p (h t) -> p h t", h=4)[:, :, 0:nb], in_=pC[0:96, 0:512].rearrange("p (h t) -> p h t", h=4)[:, :, 0:nb]),
                          w=[rC, P_ + "stg_kc"])
                    ph.dma("sp", rec_ap(rec_t, rec_base + O_KC, [[512, 96], [128, 4], [1, nb]]), stg_[0:96, 1536:2048].rearrange("p (h t) -> p h t", h=4)[:, :, 0:nb], r=[P_ + "stg_kc"], w=([(rr, 4)] if rr is not None else []), key="st_kc" + ks)
                    ph.dma("act", rec_ap(rec_t, rec_base + O_VC, [[256, nb], [1, 256]]), tokb_[:nb, 1824:2080], r=[P_ + "tokb_cv"], w=([(rr, 5)] if rr is not None else []), key="st_vc" + ks)

                pnb = pn[:, :, :].bitcast(BF16)
                bs1 = dict(pfx="s1", tokb=tokb, stg=stg[:, 0, :], latT=latT, kcf=kcf, sq=sq[:, 2176:2560], ss=ss[:, 26:30], sqtag="x1", ak=ak, ksfx="",
                           pA=(ptr[:, 0, 0:512], ("pu", 0)), pB=((ptr[:, 1, 0:512], ("pu", 1)), (ptr[:, 2, 0:512], ("pu", 2))),
                           pL=(ptr[:, 3, 0:256], ("pu", 3)), pKV=(pu[:, 3, 0:512], ("pu", 3)), pC=(ptr[:, 4, 0:512], ("pu", 4)))
                bs2 = dict(pfx="s2", tokb=tokb2, stg=stg2, latT=latT2, kcf=kcf2, sq=sq2, ss=ss2, sqtag="s2k", ak=ak2, ksfx="2",
                           pA=(pnb[:, 0, 0:512], ("pn", 0)), pB=((pnb[:, 1, 0:512], ("pn", 1)), (pnb[:, 1, 512:1024], ("pn", 1))),
                           pL=(pnb[:, 0, 512:768], ("pn", 0)), pKV=(pn[:, 0, 0:512], ("pn", 0)), pC=(pnb[:, 1, 0:512], ("pn", 1)),
                           kp=kp2[:, :], kpres=["s2kp"])

                def proj_block(bi, hc0):
                    c0, nb = blk_cols(bi)
                    si = 0
                    nres = [("src", id(Nf), si)]
                    ph.dma("act", cs[:nb, :], c_cs[c0:c0 + nb, :], w=["cs"], key="cs")
                    ph.dma("act", aq[:nb, :], c_augq[c0:c0 + nb, :], w=["aq"], key="aq")
                    ph.dma("act", ak[:nb, :], c_augk[c0:c0 + nb, :], w=["s1ak"], key="ak")
                    for c in range(8):
                        for cg in range(6):
                            w0 = cg * 512
                            wn = min(512, IN_COLS - w0)
                            ph.op("pe", lambda e, c=c, cg=cg, w0=w0, wn=wn: e.matmul(pu[:nb, cg, 0:wn], lhsT=hT[:, c, hc0:hc0 + nb], rhs=win[:, c, w0:w0 + wn], start=(c == 0), stop=(c == 7)),
                                  r=[("hT", c), "win"], w=[("pu", cg)])
                    pur = [("pu", k) for k in range(6)]
                    puf = pu[:nb, :, :].rearrange("p a b -> p (a b)")
                    ph.op("act", lambda e: e.copy(out=Nf[:nb, si, :], in_=puf[:, 0:IN_COLS]), w=pur + nres)
                    rms_rows_multi(ph, [
                        (puf[:, 0:512].rearrange("p (h d) -> p h d", h=8), nb, 8, 64, gall[:nb, 0:8, :], Nf[:nb, si, 0:512].rearrange("p (h d) -> p h d", h=8), sq[:, 0:512], ss[:, 0:8], "a", pur, nres, ["gall"]),
                        (puf[:, 768:1792].rearrange("p (h d) -> p h d", h=16), nb, 16, 64, gall[:nb, 12:28, :], Nf[:nb, si, 768:1792].rearrange("p (h d) -> p h d", h=16), sq[:, 512:1536], ss[:, 8:24], "b", pur, nres, ["gall"]),
                        (puf[:, 2304:2688].rearrange("p (h d) -> p h d", h=1), nb, 1, 384, gcq[:nb, :, :], Nf[:nb, si, 2304:2688].rearrange("p (h d) -> p h d", h=1), sq[:, 1536:1920], ss[:, 24:25], "cq", pur, nres, ["gcq"]),
                        (puf[:, 2688:2944].rearrange("p (h d) -> p h d", h=1), nb, 1, 256, gckv[:nb, :, :], Nf[:nb, si, 2688:2944].rearrange("p (h d) -> p h d", h=1), sq[:, 1920:2176], ss[:, 25:26], "ckv", pur, nres, ["gckv"]),
                    ])
                    kp = puf[:, 2944:2976]
                    ph.op("dve", lambda e: e.tensor_tensor(out=rt[:nb, 0, :], in0=kp[:, 0:16], in1=cs[:nb, 0:16], op=ALU.mult), r=["cs"], w=pur + ["rt"])
                    ph.op("dve", lambda e: e.tensor_tensor(out=rt[:nb, 1, :], in0=kp[:, 16:32], in1=cs[:nb, 16:32], op=ALU.mult), r=["cs"], w=pur + ["rt"])
                    ph.op("dve", lambda e: e.tensor_tensor(out=Nf[:nb, si, 2944:2960], in0=rt[:nb, 0, :], in1=rt[:nb, 1, :], op=ALU.subtract), r=["rt"], w=nres)
                    ph.op("dve", lambda e: e.tensor_tensor(out=rt[:nb, 2, :], in0=kp[:, 0:16], in1=cs[:nb, 16:32], op=ALU.mult), r=["cs"], w=pur + ["rt"])
                    ph.op("dve", lambda e: e.tensor_tensor(out=rt[:nb, 3, :], in0=kp[:, 16:32], in1=cs[:nb, 0:16], op=ALU.mult), r=["cs"], w=pur + ["rt"])
                    ph.op("dve", lambda e: e.tensor_tensor(out=Nf[:nb, si, 2960:2976], in0=rt[:nb, 2, :], in1=rt[:nb, 3, :], op=ALU.add), r=["rt"], w=nres)
                    ph.dma("sp", o_bk[l, c0:c0 + nb, :], Nf[:nb, si, 1280:1792], r=nres, key="o_bk")
                    ph.dma("sp", o_bv[l, c0:c0 + nb, :], Nf[:nb, si, 1792:2304], r=nres, key="o_bv")
                    ph.dma("sp", o_ckv[l, c0:c0 + nb, :], Nf[:nb, si, 2688:2944], r=nres, key="o_ckv")
                    ph.dma("sp", o_kpe[l, c0:c0 + nb, :], Nf[:nb, si, 2944:2976], r=nres, key="o_kpe")
                    if bi >= NBLK - 2:
                        r0 = (bi - (NBLK - 2)) * 128
                        ph.dma("sp", o_ak[l, r0:r0 + nb, :], Nf[:nb, si, 256:512], r=nres, key="o_ak")
                        ph.dma("sp", o_av[l, r0:r0 + nb, :], Nf[:nb, si, 512:768], r=nres, key="o_av")
                    ph.op("act", lambda e: e.copy(out=tokb[:nb, 2464:2720], in_=Nf[:nb, si, 0:256]), r=nres, w=["tokb_qa"])
                    for h in range(4):
                        ph.op("pe", lambda e, h=h: e.transpose(out=ptr[0:64, 5, h * 128:h * 128 + nb], in_=tokb[:nb, 2464 + h * 64:2464 + (h + 1) * 64], identity=idb[:nb, :nb]),
                              r=["tokb_qa", "idb"], w=[("pu", 5)])
                    ph.op("dve", lambda e: e.tensor_copy(out=stg[0:64, 1, 0:512].rearrange("p (h t) -> p h t", h=4)[:, :, 0:nb], in_=ptr[0:64, 5, 0:512].rearrange("p (h t) -> p h t", h=4)[:, :, 0:nb]),
                          w=[("pu", 5), "stg_qa"])
                    ph.dma("sp", qa_d[bi, :, :, 0:nb], stg[0:64, 1, 0:512].rearrange("p (h t) -> p h t", h=4)[:, :, 0:nb], r=["stg_qa"], key="st_qa")
                    ph.op("act", lambda e: e.copy(out=tokb[:nb, 2720:3264].rearrange("p (a d) -> p a d", a=8)[:, :, 0:64], in_=Nf[:nb, si, 768:1280].rearrange("p (a d) -> p a d", a=8)), r=nres, w=["tokb_qb"])
                    ph.op("dve", lambda e: e.tensor_copy(out=tokb[:nb, 2720:3264].rearrange("p (a d) -> p a d", a=8)[:, :, 64:68], in_=aq[:nb, :].rearrange("p (a d) -> p a d", a=8)), r=["aq"], w=["tokb_qb"])
                    for hj in range(8):
                        bank, col = hj // 4, (hj % 4) * 128
                        ph.op("pe", lambda e, hj=hj, bank=bank, col=col: e.transpose(out=ptr[0:68, bank, col:col + nb], in_=tokb[:nb, 2720 + hj * 68:2720 + (hj + 1) * 68], identity=idb[:nb, :nb]),
                              r=["tokb_qb", "idb"], w=[("pu", bank)])
                    for half in range(2):
                        ph.op("dve", lambda e, half=half: e.tensor_copy(out=stg[0:68, 1, 512 + half * 512:1024 + half * 512].rearrange("p (h t) -> p h t", h=4)[:, :, 0:nb],
                                                                         in_=ptr[0:68, half, 0:512].rearrange("p (h t) -> p h t", h=4)[:, :, 0:nb]),
                              w=[("pu", half), "stg_qb"])
                    ph.dma("sp", qb_d[bi, :, :, 0:nb], stg[0:68, 1, 512:1536].rearrange("p (h t) -> p h t", h=8)[:, :, 0:nb], r=["stg_qb"], key="st_qb")
                    ph.op("act", lambda e: e.copy(out=stg[:nb, 1, 2048:2432], in_=Nf[:nb, si, 2304:2688]), r=nres, w=["cq_b"])
                    for c in range(3):
                        ph.op("pe", lambda e, c=c: e.transpose(out=ptr[:, 2, c * 128:c * 128 + nb], in_=stg[:nb, 1, 2048 + c * 128:2048 + (c + 1) * 128], identity=idb[:nb, :nb]),
                              r=["cq_b", "idb"], w=[("pu", 2)])
                    ph.op("dve", lambda e: e.tensor_copy(out=latT[:, 2:5, 0:nb], in_=ptr[:, 2, 0:384].rearrange("p (c t) -> p c t", c=3)[:, :, 0:nb]), w=[("pu", 2), "latT_q"])
                    for c in range(3):
                        ph.op("pe", lambda e, c=c: e.matmul(pu[:nb, 2, 0:384], lhsT=latT[:, 2 + c, 0:nb], rhs=wuq[:, c, :], start=(c == 0), stop=(c == 2)),
                              r=["latT_q", "wuq"], w=[("pu", 2)])
                    pq = pu[:nb, 2, 0:384].rearrange("p (h d) -> p h d", h=4)
                    ph.op("act", lambda e: e.copy(out=qcf[:nb, :, 0:64], in_=pq[:, :, 0:64]), w=[("pu", 2), "qcf"])
                    cosb = cs[:nb, 0:16].unsqueeze(1).to_broadcast([nb, 4, 16])
                    sinb = cs[:nb, 16:32].unsqueeze(1).to_broadcast([nb, 4, 16])
                    ph.op("dve", lambda e: e.tensor_tensor(out=rt[:nb, :, :], in0=pq[:, :, 64:80], in1=cosb, op=ALU.mult), r=["cs"], w=[("pu", 2), "rt"])
                    ph.op("dve", lambda e: e.tensor_tensor(out=rt2[:nb, :, :], in0=pq[:, :, 80:96], in1=sinb, op=ALU.mult), r=["cs"], w=[("pu", 2), "rt2"])
                    ph.op("dve", lambda e: e.tensor_tensor(out=qcf[:nb, :, 64:80], in0=rt[:nb, :, :], in1=rt2[:nb, :, :], op=ALU.subtract), r=["rt", "rt2"], w=["qcf"])
                    ph.op("dve", lambda e: e.tensor_tensor(out=rt[:nb, :, :], in0=pq[:, :, 64:80], in1=sinb, op=ALU.mult), r=["cs"], w=[("pu", 2), "rt"])
                    ph.op("dve", lambda e: e.tensor_tensor(out=rt2[:nb, :, :], in0=pq[:, :, 80:96], in1=cosb, op=ALU.mult), r=["cs"], w=[("pu", 2), "rt2"])
                    ph.op("dve", lambda e: e.tensor_tensor(out=qcf[:nb, :, 80:96], in0=rt[:nb, :, :], in1=rt2[:nb, :, :], op=ALU.add), r=["rt", "rt2"], w=["qcf"])
                    rms_rows(ph, qcf[:nb, :, :], nb, 4, 96, gq96[:nb, :, :], qcf[:nb, :, :], sq[:, 2176:2560], ss[:, 26:30], "x1", ["qcf"], ["qcf"], ["gq96"])
                    ph.op("act", lambda e: e.copy(out=stg[:nb, 1, 2432:2816].rearrange("p (h d) -> p h d", h=4), in_=qcf[:nb, :, :]), r=["qcf"], w=["qc_b"])
                    for h in range(4):
                        ph.op("pe", lambda e, h=h: e.transpose(out=ptr[0:96, 5, h * 128:h * 128 + nb], in_=stg[:nb, 1, 2432 + h * 96:2432 + (h + 1) * 96], identity=idb[:nb, :nb]),
                              r=["qc_b", "idb"], w=[("pu", 5)])
                    ph.op("dve", lambda e: e.tensor_copy(out=stg[0:96, 0, 2048:2560].rearrange("p (h t) -> p h t", h=4)[:, :, 0:nb], in_=ptr[0:96, 5, 0:512].rearrange("p (h t) -> p h t", h=4)[:, :, 0:nb]),
                          w=[("pu", 5), "stg_qc"])
                    ph.dma("sp", qc_d[bi, :, :, 0:nb], stg[0:96, 0, 2048:2560].rearrange("p (h t) -> p h t", h=4)[:, :, 0:nb], r=["stg_qc"], key="st_qc")
                    bs1["kp"] = Nf[:nb, si, 2944:2976]
                    bs1["kpres"] = nres
                    bs1["ak"] = ak
                    if bi < NBLK:
                        kv_tail(ph, Nf, si, nb, ksrc, bi * RBE, dict(bs1), rr=("rec", bi))
                        RB = RBE // 128
                        ph.custom("pool", lambda e: e.collective_compute("AllGather", ALU.bypass, replica_groups=[[0, 1], [2, 3], [4, 5], [6, 7]],
                                                                         ins=[ksrc[bi * RB:(bi + 1) * RB, :].opt()], outs=[kdst[2 * bi * RB:(2 * bi + 2) * RB, :].opt()]),
                                  r=[(("rec", bi), k) for k in range(6)], key="cc", inc=1)
                    else:
                        kv_tail(ph, Nf, si, nb, srec, NCB * RBE, dict(bs1))
                for gi, (gt0, gn) in enumerate(groups):
                    norm_to_hT(ph, hT, xsq, rinv, pu[:, 4:6, :], l * 4 + 1, only=gi, pres=lambda b: ("pu", 4 + b))
                    for bi in ([NBLK] if gt0 >= SC else range(gt0 // 128, (gt0 + gn) // 128)):
                        proj_block(bi, blk_cols(bi)[0] - gt0)
                ph.stream = 1
                for cb in range(NCB):
                    r0, r1 = cb * 128, (cb + 1) * 128
                    ab = cb - (NCB - 4)
                    if ab >= 0:
                        ph.dma("pool", tokb2[:, 0:256], ca_k[l, ab * 128:(ab + 1) * 128, :], w=["s2tokb_a"], key="ci0")
                        ph.dma("pool", tokb2[:, 256:512], ca_v[l, ab * 128:(ab + 1) * 128, :], w=["s2tokb_a"], key="ci1")
                    ph.dma("pool", tokb2[:, 512:512 + 544].rearrange("p (a d) -> p a d", a=8)[:, :, 0:64], cb_k[l, r0:r1, :].rearrange("p (a d) -> p a d", a=8), w=["s2tokb_b"], key="ci2")
                    ph.dma("pool", tokb2[:, 1056:1568], cb_v[l, r0:r1, :], w=["s2tokb_bv"], key="ci3")
                    ph.dma("pool", tokb2[:, 1568:1824], cc_kv[l, r0:r1, :], w=["s2tokb_c"], key="ci4")
                    ph.dma("sp", kp2[:, :], cc_kpe[l, r0:r1, :], w=["s2kp"], key="ci5")
                    ph.dma("act", ak2[:, :], c_augkc[r0:r1, :], w=["s2ak"], key="ak2")
                    kv_tail(ph, None, 0, 128, srec, cb * RBE, dict(bs2, do_a=(ab >= 0)))
                ph.stream = 0
                ph.run()
            es_win.close()

            if STAGE < 3:
                continue
            if STAGE < 4:
                continue
            attention(l, None)
            if STAGE < 8:
                continue
            mem_attention(l)
            if STAGE < 9:
                continue
            ffn(l, 2)

        with ExitStack() as es10:
            ytok = es10.enter_context(SB("ytok", [128, 2, D], F32))
            pt2 = es10.enter_context(PS("pt2", [128, 2, 512], F32))
            ph = Phase(g)
            for bi in range(NB):
                c0, nb = blk_cols(bi)
                s = bi % 2
                for c in range(8):
                    ps = c % 2
                    ph.op("pe", lambda e, c=c, ps=ps, c0=c0, nb=nb: e.transpose(out=pt2[:nb, ps, 0:128], in_=xT[:, c, c0:c0 + nb], identity=idf[:, :]),
                          r=[("xT", c), "idf"], w=[("pt2", ps)])
                    ph.op("act", lambda e, s=s, c=c, ps=ps, nb=nb: e.copy(out=ytok[:nb, s, c * 128:(c + 1) * 128], in_=pt2[:nb, ps, 0:128]),
                          w=[("pt2", ps), ("ytok", s)])
                ph.dma("sp", y[c0:c0 + nb, :], ytok[:nb, s, :], r=[("ytok", s)], key=("yst", s))
            ph.run()
    return nc


WNAMES = ["ffn1_norm", "ffn1_w_gate", "ffn1_w_up", "ffn1_w_down", "mix_norm", "w_in", "a_q_norm", "a_k_norm",
          "a_rel_bias", "b_q_norm", "b_k_norm", "b_lambda", "b_sub_norm", "c_q_lat_norm", "c_kv_lat_norm",
          "c_w_uq", "c_w_ukv", "c_q_norm", "c_k_norm", "w_out", "mem_norm_x", "mem_w_q", "mem_q_norm",
          "mem_norm_m", "mem_w_k", "mem_w_v", "mem_k_norm", "mem_w_o", "ffn2_norm", "ffn2_w_gate",
          "ffn2_w_up", "ffn2_w_down"]


def _consts(p, NBLK, DEPTH, PAST):
    SC = NBLK * 128
    T = SC + 64
    t = np.arange(SC)
    pos = np.concatenate([(2 * (t // 128) + p) * 128 + t % 128, PAST + np.arange(64)]).astype(np.int64)
    inv = (10000.0 ** (-np.arange(16, dtype=np.float32) / 16)).astype(np.float32)
    ang = pos.astype(np.float32)[:, None] * inv[None, :]
    cs = np.concatenate([np.cos(ang), np.sin(ang)], axis=1).astype(np.float32)
    slopes = np.exp2(-8.0 * (np.arange(4, dtype=np.float32) + 1.0) / 4).astype(np.float32)

    def aug(posv, qside):
        lo = (posv % 128).astype(np.float32)
        hi = (posv - posv % 128).astype(np.float32)
        out = np.zeros((len(posv), 8, 4), np.float32)
        for h in range(4):
            for j in range(2):
                if qside:
                    out[:, 2 * h + j] = np.stack([-slopes[h] * hi, -slopes[h] * lo, np.ones_like(lo), np.ones_like(lo)], 1)
                else:
                    out[:, 2 * h + j] = np.stack([np.ones_like(lo), np.ones_like(lo), slopes[h] * hi, slopes[h] * lo], 1)
        return out.reshape(len(posv), 32)

    k = np.arange(128)[:, None]
    q = np.arange(128)[None, :]
    kc, qc = k // 64, q // 64
    diag = np.zeros((4, 128, 128), np.float32)
    for h in range(4):
        d = np.where((kc == qc) & (k > q), -2.0 * slopes[h] * (k - q), 0.0)
        diag[h] = np.where(kc > qc, NEG, d)
    dmask = np.where(kc > qc, NEG, 0.0).astype(np.float32)
    full = np.full((128, 128), NEG, np.float32)
    zero = np.zeros((128, 128), np.float32)
    corrB = np.zeros((128, 2, 4, 128), np.float32)
    maskC = np.zeros((128, 2, 128), np.float32)
    for h in range(4):
        corrB[:, 0, h, :] = diag[h] if p == 0 else zero
        corrB[:, 1, h, :] = full if p == 0 else diag[h]
    maskC[:, 0, :] = dmask if p == 0 else zero
    maskC[:, 1, :] = full if p == 0 else dmask
    k64 = np.arange(64)[:, None]
    q64 = np.arange(64)[None, :]
    corrBs = np.zeros((64, 4, 64), np.float32)
    for h in range(4):
        corrBs[:, h, :] = np.where(k64 > q64, -2.0 * slopes[h] * (k64 - q64), 0.0)
    maskA = np.zeros((128, 6, 128), np.float32)
    for r in range(6):
        delta = r - 1 + p
        rel = -2 * delta + kc - qc
        maskA[:, r, :] = np.where((rel <= 0) & (rel >= -8), 0.0, NEG)
    w01 = np.tile(np.array([[1.0 - p, float(p)]], np.float32), (128, 1))
    lam = np.zeros((128, 2 * DEPTH), np.float32)
    for l in range(DEPTH):
        li = 0.8 - 0.6 * math.exp(-0.3 * l)
        lam[:, 2 * l] = li
        lam[:, 2 * l + 1] = 1.0 - li
    return dict(c_ident=np.eye(128, dtype=np.float32), c_cs=cs, c_augq=aug(pos, True), c_augk=aug(pos, False),
                c_augkc=aug(np.arange(PAST), False), c_corrB=corrB, c_corrBs=corrBs, c_maskC=maskC,
                c_maskA=maskA, c_w01=w01, c_lam=lam)


_CACHE = {}


def kernel(**inputs):
    x_prompt = np.asarray(inputs["x_prompt"], np.float32)
    x_sample = np.asarray(inputs["x_sample"], np.float32)
    B, SEQ, _ = x_prompt.shape
    DB = x_sample.shape[0]
    DEPTH = inputs["w_in"].shape[0]
    PAST = inputs["cache_b_k"].shape[2]
    assert B * 2 == 8 and DB == 8 and x_sample.shape[1] == 64 and inputs["cache_a_k"].shape[2] == 512
    NBLK = SEQ // 256
    SC = NBLK * 128
    key = (NBLK, DEPTH, PAST)
    if key not in _CACHE:
        _CACHE[key] = build(NBLK, DEPTH, PAST)
    nc = _CACHE[key]
    wts = {nm: np.ascontiguousarray(np.asarray(inputs[nm], np.float32)) for nm in WNAMES}
    in_maps = []
    for c in range(8):
        b, p = c // 2, c % 2
        xb = x_prompt[b].reshape(SEQ // 128, 128, D)[p::2].reshape(SC, D)
        m = dict(wts)
        m["xin"] = np.ascontiguousarray(np.concatenate([xb, x_sample[c]], axis=0))
        m["mem"] = np.ascontiguousarray(np.asarray(inputs["mem_prompt"], np.float32)[b])
        m["ca_k"] = np.ascontiguousarray(np.asarray(inputs["cache_a_k"], np.float32)[:, c].reshape(DEPTH, 512, 256))
        m["ca_v"] = np.ascontiguousarray(np.asarray(inputs["cache_a_v"], np.float32)[:, c].reshape(DEPTH, 512, 256))
        m["cb_k"] = np.ascontiguousarray(np.asarray(inputs["cache_b_k"], np.float32)[:, c].reshape(DEPTH, PAST, 512))
        m["cb_v"] = np.ascontiguousarray(np.asarray(inputs["cache_b_v"], np.float32)[:, c].reshape(DEPTH, PAST, 512))
        m["cc_kv"] = np.ascontiguousarray(np.asarray(inputs["cache_c_kv"], np.float32)[:, c])
        m["cc_kpe"] = np.ascontiguousarray(np.asarray(inputs["cache_c_kpe"], np.float32)[:, c])
        m["cm_k"] = np.ascontiguousarray(np.asarray(inputs["cache_mem_k"], np.float32)[:, c].reshape(DEPTH, NMEM, 512))
        m["cm_v"] = np.ascontiguousarray(np.asarray(inputs["cache_mem_v"], np.float32)[:, c].reshape(DEPTH, NMEM, 512))
        m.update(_consts(p, NBLK, DEPTH, PAST))
        in_maps.append(m)
    res = run_bass_kernel_spmd(nc, in_maps, core_ids=list(range(8))).results

    def unzig(name, width):
        out = np.zeros((DEPTH, B, SEQ // 128, 128, width), np.float32)
        for c in range(8):
            b, p = c // 2, c % 2
            out[:, b, p::2] = res[c][name][:, :SC].reshape(DEPTH, NBLK, 128, width)
        return out.reshape(DEPTH, B, SEQ, width)

    yp = np.zeros((B, SEQ // 128, 128, D), np.float32)
    ys = np.zeros((DB, 64, D), np.float32)
    for c in range(8):
        b, p = c // 2, c % 2
        yp[b, p::2] = res[c]["y"][:SC].reshape(NBLK, 128, D)
        ys[c] = res[c]["y"][SC:]
    yp = yp.reshape(B, SEQ, D)
    pak = np.zeros((DEPTH, B, 4, 128, 256), np.float32)
    pav = np.zeros((DEPTH, B, 4, 128, 256), np.float32)
    for c in range(8):
        b, p = c // 2, c % 2
        for mb in range(4):
            if mb % 2 == p:
                pak[:, b, mb] = res[c]["o_ak"][:, (mb // 2) * 128:(mb // 2 + 1) * 128]
                pav[:, b, mb] = res[c]["o_av"][:, (mb // 2) * 128:(mb // 2 + 1) * 128]
    pak = pak.reshape(DEPTH, B, 512, 4, 64)
    pav = pav.reshape(DEPTH, B, 512, 4, 64)
    pbk = unzig("o_bk", 512).reshape(DEPTH, B, SEQ, 4, 2, 64)
    pbv = unzig("o_bv", 512).reshape(DEPTH, B, SEQ, 4, 128)
    pckv = unzig("o_ckv", 256)
    pkpe = unzig("o_kpe", 32)
    pmk = np.stack([res[2 * b]["o_mk"] for b in range(B)], axis=1).reshape(DEPTH, B, NMEM, 4, 128)
    pmv = np.stack([res[2 * b]["o_mv"] for b in range(B)], axis=1).reshape(DEPTH, B, NMEM, 4, 128)
    sak = np.stack([res[c]["o_ak"][:, 256:320] for c in range(8)], axis=1).reshape(DEPTH, DB, 64, 4, 64)
    sav = np.stack([res[c]["o_av"][:, 256:320] for c in range(8)], axis=1).reshape(DEPTH, DB, 64, 4, 64)
    sbk = np.stack([res[c]["o_bk"][:, SC:] for c in range(8)], axis=1).reshape(DEPTH, DB, 64, 4, 2, 64)
    sbv = np.stack([res[c]["o_bv"][:, SC:] for c in range(8)], axis=1).reshape(DEPTH, DB, 64, 4, 128)
    sckv = np.stack([res[c]["o_ckv"][:, SC:] for c in range(8)], axis=1)
    skpe = np.stack([res[c]["o_kpe"][:, SC:] for c in range(8)], axis=1)
    return (yp, ys, pak, pav, pbk, pbv, pckv, pkpe, pmk, pmv, sak, sav, sbk, sbv, sckv, skpe)
```
